# Optimizing a Trainium2 kernel written in Bass

```python
import jax
import jax.numpy as jnp
from jax import lax
import numpy as np

D_MODEL = 1024
BATCH = 8
SEQ = 4096
DEPTH = 2

N_MIXERS = 2
EPS = 1e-6

A_HEADS = 16
A_GROUPS = 4
A_HEAD_DIM = 64
A_WIDTH = A_HEADS * A_HEAD_DIM
CMP_LEN = 32
CMP_STRIDE = 16
CMP_HIDDEN = 256
SLC_LEN = 64
SLC_TOP = 16
WIN = 512
Q_BLOCK = 64
A_KV_COLS = 6 * A_GROUPS * A_HEAD_DIM
A_SPLITS = (A_WIDTH, A_WIDTH + A_KV_COLS, A_WIDTH + A_KV_COLS + 3 * A_HEADS)
A_IN_COLS = 2 * A_WIDTH + A_KV_COLS + 3 * A_HEADS

B_HEADS = 8
B_QK_DIM = 128
B_V_DIM = 256
B_WIDTH = B_HEADS * B_V_DIM
B_QK_COLS = 2 * B_HEADS * B_QK_DIM
CONV_WIDTH = 4
CHUNK = 64
B_SPLITS = (B_QK_COLS, B_QK_COLS + B_WIDTH, B_QK_COLS + B_WIDTH + B_HEADS,
            B_QK_COLS + B_WIDTH + 2 * B_HEADS, B_QK_COLS + 2 * B_WIDTH + 2 * B_HEADS)
B_IN_COLS = B_QK_COLS + 3 * B_WIDTH + 2 * B_HEADS

kernel_name = 'nsa_mlstm_hybrid_block'


def rmsnorm(x, g):
    xf = x.astype(jnp.float32)
    xf = xf * lax.rsqrt(jnp.mean(xf * xf, axis=-1, keepdims=True) + EPS)
    return xf.astype(x.dtype) * g


def alibi_slopes(n):
    return jnp.asarray(2.0 ** (-8.0 * np.arange(1, n + 1) / n), jnp.float32)


def masked_probs(s, mask):
    s = jnp.where(mask, s, -jnp.inf)
    m = jnp.max(s, axis=-1, keepdims=True)
    m = jnp.where(jnp.isfinite(m), m, 0.0)
    p = jnp.exp(s - m)
    return p / jnp.maximum(jnp.sum(p, axis=-1, keepdims=True), 1e-30)


def selection_overlap(seq):
    n_cmp = seq // CMP_STRIDE - CMP_LEN // CMP_STRIDE + 1
    n_sel = seq // SLC_LEN
    c0 = np.arange(n_cmp)[:, None] * CMP_STRIDE
    s0 = np.arange(n_sel)[None, :] * SLC_LEN
    ov = np.clip(np.minimum(c0 + CMP_LEN, s0 + SLC_LEN) - np.maximum(c0, s0), 0, None)
    return jnp.asarray(ov / CMP_LEN, jnp.float32)


def compress_tokens(kv, pe, w1, b1, w2, b2):
    B, S, G, DH = kv.shape
    r = CMP_LEN // CMP_STRIDE
    n_chunk = S // CMP_STRIDE
    n_cmp = n_chunk - r + 1
    chunks = kv.reshape(B, n_chunk, CMP_STRIDE, G, DH)
    blocks = jnp.concatenate([chunks[:, i:i + n_cmp] for i in range(r)], axis=2) + pe[:, None, :]
    flat = blocks.transpose(0, 1, 3, 2, 4).reshape(B, n_cmp, G, CMP_LEN * DH)
    return jax.nn.gelu(flat @ w1 + b1) @ w2 + b2


def nsa_mixer(h, w_in, cmp_pe, cmp_w1, cmp_b1, cmp_w2, cmp_b2, w_out):
    B, S, _ = h.shape
    G, HG, DH = A_GROUPS, A_HEADS // A_GROUPS, A_HEAD_DIM
    f32 = jnp.float32
    scale = DH ** -0.5
    slopes = alibi_slopes(A_HEADS).reshape(G, HG)
    q, kv, gate_logits, z = jnp.split(h @ w_in, A_SPLITS, axis=-1)
    q = q.reshape(B, S, G, HG, DH)
    kv = kv.reshape(B, S, 6, G, DH)
    t = jnp.arange(S)

    k_cmp = compress_tokens(kv[:, :, 0], cmp_pe[0], cmp_w1[0], cmp_b1[0], cmp_w2[0], cmp_b2[0])
    v_cmp = compress_tokens(kv[:, :, 1], cmp_pe[1], cmp_w1[1], cmp_b1[1], cmp_w2[1], cmp_b2[1])
    n_cmp = k_cmp.shape[1]
    end_c = jnp.arange(n_cmp) * CMP_STRIDE + (CMP_LEN - 1)
    dist_c = (t[:, None] - end_c[None, :]).astype(f32)
    s_cmp = jnp.einsum('bsghd,bcgd->bghsc', q, k_cmp).astype(f32) * scale - slopes[:, :, None, None] * dist_c
    p_cmp = masked_probs(s_cmp, dist_c >= 0)
    o_cmp = jnp.einsum('bghsc,bcgd->bsghd', p_cmp.astype(v_cmp.dtype), v_cmp)

    n_sel = S // SLC_LEN
    n_top = min(SLC_TOP, n_sel)
    imp = jnp.einsum('bghsc,cj->bgsj', p_cmp, selection_overlap(S))
    blk = jnp.arange(n_sel)[None, :]
    cur = (t // SLC_LEN)[:, None]
    forced = (blk == 0) | (blk == cur) | (blk == cur - 1)
    score = jnp.where(forced, jnp.inf, jnp.where(blk <= cur, imp, -jnp.inf))
    _, sel_idx = lax.top_k(score, n_top)
    k_slc = kv[:, :, 2].reshape(B, n_sel, SLC_LEN, G, DH).transpose(0, 3, 1, 2, 4)
    v_slc = kv[:, :, 3].reshape(B, n_sel, SLC_LEN, G, DH).transpose(0, 3, 1, 2, 4)
    gather = jax.vmap(jax.vmap(lambda blocks, ix: blocks[ix]))

    pad = ((0, 0), (WIN, 0), (0, 0), (0, 0))
    k_win = jnp.pad(kv[:, :, 4], pad)
    v_win = jnp.pad(kv[:, :, 5], pad)

    n_qb = S // Q_BLOCK
    q_blocks = jnp.moveaxis(q.reshape(B, n_qb, Q_BLOCK, G, HG, DH), 1, 0)
    idx_blocks = jnp.moveaxis(sel_idx.reshape(B, G, n_qb, Q_BLOCK, n_top), 2, 0)

    def query_block(args):
        j, qb, ib = args
        q0 = j * Q_BLOCK
        tq = q0 + jnp.arange(Q_BLOCK)
        ks = gather(k_slc, ib)
        vs = gather(v_slc, ib)
        key_pos = ib[..., None] * SLC_LEN + jnp.arange(SLC_LEN)
        dist = (tq[None, None, :, None, None] - key_pos).astype(f32)[:, :, None]
        s = jnp.einsum('bqghd,bgqnkd->bghqnk', qb, ks).astype(f32) * scale - slopes[None, :, :, None, None, None] * dist
        p = masked_probs(s.reshape(B, G, HG, Q_BLOCK, -1), (dist >= 0).reshape(B, G, 1, Q_BLOCK, -1))
        o_s = jnp.einsum('bghqm,bgqmd->bqghd', p.astype(vs.dtype), vs.reshape(B, G, Q_BLOCK, -1, DH))
        kw = lax.dynamic_slice_in_dim(k_win, q0, WIN + Q_BLOCK, axis=1)
        vw = lax.dynamic_slice_in_dim(v_win, q0, WIN + Q_BLOCK, axis=1)
        kpos = q0 - WIN + jnp.arange(WIN + Q_BLOCK)
        dist_w = tq[:, None] - kpos[None, :]
        mask_w = (dist_w >= 0) & (dist_w < WIN) & (kpos[None, :] >= 0)
        s = jnp.einsum('bqghd,bkgd->bghqk', qb, kw).astype(f32) * scale - slopes[None, :, :, None, None] * dist_w.astype(f32)
        p = masked_probs(s, mask_w)
        o_w = jnp.einsum('bghqk,bkgd->bqghd', p.astype(vw.dtype), vw)
        return o_s, o_w

    o_slc, o_win = lax.map(query_block, (jnp.arange(n_qb), q_blocks, idx_blocks))
    o_slc = jnp.moveaxis(o_slc, 0, 1).reshape(B, S, G, HG, DH)
    o_win = jnp.moveaxis(o_win, 0, 1).reshape(B, S, G, HG, DH)

    gates = jax.nn.sigmoid(gate_logits).reshape(B, S, 3, G, HG, 1)
    o = gates[:, :, 0] * o_cmp + gates[:, :, 1] * o_slc + gates[:, :, 2] * o_win
    y = o.reshape(B, S, A_WIDTH) * jax.nn.silu(z)
    return y @ w_out


def causal_conv(x, w, b):
    C = x.shape[-1]
    y = lax.conv_general_dilated(x, w[:, None, :], window_strides=(1,), padding=[(CONV_WIDTH - 1, 0)],
                                 dimension_numbers=('NWC', 'WIO', 'NWC'), feature_group_count=C)
    return y + b


def mlstm_chunkwise(q, k, v, i_pre, log_f):
    B, H, S, DK = q.shape
    DV = v.shape[-1]
    N, L = S // CHUNK, CHUNK
    f32 = jnp.float32
    q = q.reshape(B, H, N, L, DK)
    k = k.reshape(B, H, N, L, DK)
    v = v.reshape(B, H, N, L, DV)
    i_pre = i_pre.reshape(B, H, N, L)
    b = jnp.cumsum(log_f.reshape(B, H, N, L), axis=-1)
    g = b[..., -1]
    a = g[..., None] - b + i_pre
    m_loc = jnp.max(a, axis=-1)
    w = jnp.exp(a - m_loc[..., None])
    c_loc = jnp.einsum('bhnlk,bhnlv->bhnkv', k * w[..., None], v)
    n_loc = jnp.einsum('bhnl,bhnlk->bhnk', w, k)

    def step(carry, xs):
        c_st, n_st, m_st = carry
        c_l, n_l, m_l, g_l = xs
        m_new = jnp.maximum(g_l + m_st, m_l)
        s_old = jnp.exp(g_l + m_st - m_new)
        s_new = jnp.exp(m_l - m_new)
        c_next = s_old[..., None, None] * c_st + s_new[..., None, None] * c_l
        n_next = s_old[..., None] * n_st + s_new[..., None] * n_l
        return (c_next, n_next, m_new), (c_st, n_st, m_st)

    init = (jnp.zeros((B, H, DK, DV), f32), jnp.zeros((B, H, DK), f32), jnp.zeros((B, H), f32))
    xs = (jnp.moveaxis(c_loc, 2, 0), jnp.moveaxis(n_loc, 2, 0), jnp.moveaxis(m_loc, 2, 0), jnp.moveaxis(g, 2, 0))
    _, (c_prev, n_prev, m_prev) = lax.scan(step, init, xs)
    c_prev = jnp.moveaxis(c_prev, 0, 2)
    n_prev = jnp.moveaxis(n_prev, 0, 2)
    m_prev = jnp.moveaxis(m_prev, 0, 2)

    causal = jnp.tril(jnp.ones((L, L), bool))
    log_d = jnp.where(causal, b[..., :, None] - b[..., None, :] + i_pre[..., None, :], -jnp.inf)
    m_inter = b + m_prev[..., None]
    m_j = jnp.maximum(jnp.max(log_d, axis=-1), m_inter)
    s = jnp.einsum('bhnjk,bhnsk->bhnjs', q, k) * jnp.exp(log_d - m_j[..., None])
    w_inter = jnp.exp(m_inter - m_j)
    num = jnp.einsum('bhnjs,bhnsv->bhnjv', s, v) + w_inter[..., None] * jnp.einsum('bhnjk,bhnkv->bhnjv', q, c_prev)
    den = jnp.sum(s, axis=-1) + w_inter * jnp.einsum('bhnjk,bhnk->bhnj', q, n_prev)
    h = num / jnp.maximum(jnp.abs(den), jnp.exp(-m_j))[..., None]
    return h.reshape(B, H, S, DV)


def mlstm_mixer(h, w_in, conv_w, conv_b, gate_b, head_g, w_out):
    B, S, _ = h.shape
    H, DK, DV = B_HEADS, B_QK_DIM, B_V_DIM
    f32 = jnp.float32
    qk, v, gi, gf, og, z = jnp.split(h @ w_in, B_SPLITS, axis=-1)
    qk = jax.nn.silu(causal_conv(qk, conv_w, conv_b))
    q, k = jnp.split(qk, 2, axis=-1)
    q = q.reshape(B, S, H, DK).transpose(0, 2, 1, 3).astype(f32)
    k = k.reshape(B, S, H, DK).transpose(0, 2, 1, 3).astype(f32) * (DK ** -0.5)
    v = v.reshape(B, S, H, DV).transpose(0, 2, 1, 3).astype(f32)
    i_pre = (gi + gate_b[0]).astype(f32).transpose(0, 2, 1)
    log_f = jax.nn.log_sigmoid((gf + gate_b[1]).astype(f32)).transpose(0, 2, 1)
    hh = mlstm_chunkwise(q, k, v, i_pre, log_f)
    hh = hh * lax.rsqrt(jnp.mean(hh * hh, axis=-1, keepdims=True) + EPS)
    hh = hh.transpose(0, 2, 1, 3).reshape(B, S, B_WIDTH).astype(h.dtype) * head_g
    y = jax.nn.sigmoid(og) * hh * jax.nn.silu(z)
    return y @ w_out


def setup_inputs(seed: int = 0) -> dict:
    key = jax.random.key(seed)
    ks = jax.random.split(key, 20)
    D = D_MODEL
    n_a = (DEPTH + 1) // 2
    n_b = DEPTH // 2

    def nrm(k, shape, s):
        return jax.random.normal(k, shape, jnp.float32) * s

    x = nrm(ks[0], (BATCH, SEQ, D), 1.0)
    c = nrm(ks[1], (BATCH, D), 1.0)
    ada_w = nrm(ks[2], (DEPTH, D, 3 * D), 0.3 * D ** -0.5)
    ada_b = nrm(ks[3], (DEPTH, 3 * D), 0.01)
    norm_g = 1.0 + nrm(ks[4], (DEPTH, D), 0.02)
    final_g = 1.0 + nrm(ks[5], (D,), 0.02)
    a_w_in = nrm(ks[6], (n_a, D, A_IN_COLS), D ** -0.5)
    a_cmp_pe = nrm(ks[7], (n_a, 2, CMP_LEN, A_HEAD_DIM), 0.1)
    a_cmp_w1 = nrm(ks[8], (n_a, 2, CMP_LEN * A_HEAD_DIM, CMP_HIDDEN), (CMP_LEN * A_HEAD_DIM) ** -0.5)
    a_cmp_b1 = nrm(ks[9], (n_a, 2, CMP_HIDDEN), 0.01)
    a_cmp_w2 = nrm(ks[10], (n_a, 2, CMP_HIDDEN, A_HEAD_DIM), CMP_HIDDEN ** -0.5)
    a_cmp_b2 = nrm(ks[11], (n_a, 2, A_HEAD_DIM), 0.01)
    a_w_out = nrm(ks[12], (n_a, A_WIDTH, D), A_WIDTH ** -0.5)
    b_w_in = nrm(ks[13], (n_b, D, B_IN_COLS), D ** -0.5)
    b_conv_w = nrm(ks[14], (n_b, CONV_WIDTH, B_QK_COLS), CONV_WIDTH ** -0.5)
    b_conv_b = nrm(ks[15], (n_b, B_QK_COLS), 0.01)
    b_i_bias = nrm(ks[16], (n_b, B_HEADS), 0.1)
    b_f_bias = jnp.linspace(3.0, 6.0, B_HEADS, dtype=jnp.float32)[None, :] + nrm(ks[17], (n_b, B_HEADS), 0.1)
    b_gate_b = jnp.stack([b_i_bias, b_f_bias], axis=1)
    b_head_g = 1.0 + nrm(ks[18], (n_b, B_WIDTH), 0.02)
    b_w_out = nrm(ks[19], (n_b, B_WIDTH, D), B_WIDTH ** -0.5)
    return {'x': x, 'c': c, 'ada_w': ada_w, 'ada_b': ada_b, 'norm_g': norm_g, 'final_g': final_g,
            'a_w_in': a_w_in, 'a_cmp_pe': a_cmp_pe, 'a_cmp_w1': a_cmp_w1, 'a_cmp_b1': a_cmp_b1,
            'a_cmp_w2': a_cmp_w2, 'a_cmp_b2': a_cmp_b2, 'a_w_out': a_w_out,
            'b_w_in': b_w_in, 'b_conv_w': b_conv_w, 'b_conv_b': b_conv_b, 'b_gate_b': b_gate_b,
            'b_head_g': b_head_g, 'b_w_out': b_w_out}


def reference(x, c, ada_w, ada_b, norm_g, final_g,
              a_w_in, a_cmp_pe, a_cmp_w1, a_cmp_b1, a_cmp_w2, a_cmp_b2, a_w_out,
              b_w_in, b_conv_w, b_conv_b, b_gate_b, b_head_g, b_w_out):
    for i in range(DEPTH):
        shift, scale, gate = jnp.split(c @ ada_w[i] + ada_b[i], 3, axis=-1)
        h = rmsnorm(x, norm_g[i]) * (1.0 + scale[:, None, :]) + shift[:, None, :]
        li = i // N_MIXERS
        if i % N_MIXERS == 0:
            y = nsa_mixer(h, a_w_in[li], a_cmp_pe[li], a_cmp_w1[li], a_cmp_b1[li],
                          a_cmp_w2[li], a_cmp_b2[li], a_w_out[li])
        else:
            y = mlstm_mixer(h, b_w_in[li], b_conv_w[li], b_conv_b[li], b_gate_b[li],
                            b_head_g[li], b_w_out[li])
        x = x + gate[:, None, :] * y
    return rmsnorm(x, final_g)
```

```python
import math
from contextlib import ExitStack

import numpy as np
import ml_dtypes

import concourse.bass as bass
import concourse.mybir as mybir
from concourse.bass_utils import run_bass_kernel_spmd

F32 = mybir.dt.float32
BF16 = mybir.dt.bfloat16
AF = mybir.ActivationFunctionType
ALU = mybir.AluOpType
AX = mybir.AxisListType

S = 4096
D = 1024
NT = S // 128
NQ = S // 512
NEG = -30000.0
EPS = 1e-6
SLOPES = [2.0 ** (-8.0 * (i + 1) / 16) for i in range(16)]
LN_KSCALE = math.log(128 ** -0.5)
ZERO_CUT = 130.0

ENGS = ("pe", "act", "dve", "pool", "sp")
CENG = ("pe", "act", "dve", "pool")
N_DSEM = 90


class Tok:
    __slots__ = ("w", "r", "ds", "name")

    def __init__(self, name="", ds=None):
        self.w = None
        self.r = {}
        self.ds = ds
        self.name = name


class Prog:
    def __init__(self, nc, es):
        self.nc = nc
        self.q = {e: [] for e in ENGS}
        self.cnt = {e: 0 for e in CENG}
        self.seen = {e: {} for e in ENGS}
        self.sems = {}
        for e in CENG:
            self.sems[e] = es.enter_context(nc.semaphore("c_" + e))
        self.sems["bar"] = es.enter_context(nc.semaphore("c_bar"))
        self.bar = 0
        self.dsem_cnt = {}
        self.free = []
        for i in range(N_DSEM):
            k = "d%d" % i
            self.sems[k] = es.enter_context(nc.semaphore(k))
            self.dsem_cnt[k] = 0
            self.free.append(k)
        self.phase_ds = []

    def tok(self, name="", dma=False):
        ds = None
        if dma:
            ds = self.free.pop(0)
            self.phase_ds.append(ds)
        return Tok(name, ds)

    def _need(self, eng, ev):
        if ev is None:
            return
        key, val = ev
        if key == eng and eng == "pe":
            return
        if self.seen[eng].get(key, 0) >= val:
            return
        self.seen[eng][key] = val
        self.q[eng].append(("wait", key, val))

    def capture(self):
        self._cap = []

    def end_capture(self):
        lst, self._cap = self._cap, None
        return lst

    def replay_merged(self, lists):
        lists = [l for l in lists if l]
        pos = [0] * len(lists)
        while True:
            best, bf = None, None
            for i, l in enumerate(lists):
                if pos[i] < len(l):
                    f = pos[i] / len(l)
                    if bf is None or f < bf:
                        best, bf = i, f
            if best is None:
                break
            kind, args = lists[best][pos[best]]
            pos[best] += 1
            if kind == "op":
                self.op(*args)
            else:
                self.dma(*args)

    def op(self, eng, fn, reads=(), writes=()):
        if getattr(self, "_cap", None) is not None:
            self._cap.append(("op", (eng, fn, tuple(reads), tuple(writes))))
            return
        for t in reads:
            self._need(eng, t.w)
        for t in writes:
            self._need(eng, t.w)
            for k, v in t.r.items():
                self._need(eng, (k, v))
        self.cnt[eng] += 1
        n = self.cnt[eng]
        self.q[eng].append(("op", fn))
        for t in reads:
            if t.r.get(eng, 0) < n:
                t.r[eng] = n
        for t in writes:
            t.w = (eng, n)
            t.r = {}

    def dma(self, eng, fn, tok, load, reads=(), writes=()):
        assert tok.ds is not None, tok.name
        if getattr(self, "_cap", None) is not None:
            self._cap.append(("dma", (eng, fn, tok, load, tuple(reads), tuple(writes))))
            return
        self._need(eng, tok.w)
        if load:
            for k, v in tok.r.items():
                self._need(eng, (k, v))
        for t in reads:
            self._need(eng, t.w)
        for t in writes:
            self._need(eng, t.w)
            for k, v in t.r.items():
                self._need(eng, (k, v))
        self.dsem_cnt[tok.ds] += 16
        n = self.dsem_cnt[tok.ds]
        self.q[eng].append(("dma", fn, tok.ds))
        if load:
            tok.w = (tok.ds, n)
            tok.r = {}
        else:
            tok.r[tok.ds] = n
        for t in reads:
            t.r[tok.ds] = max(t.r.get(tok.ds, 0), n)
        for t in writes:
            t.w = (tok.ds, n)
            t.r = {}

    def barrier(self):
        for e in CENG:
            if self.cnt[e]:
                self._need("sp", (e, self.cnt[e]))
        for k, v in self.dsem_cnt.items():
            if v:
                self._need("sp", (k, v))
        self.bar += 1
        self.q["sp"].append(("inc", "bar", 1))
        for e in CENG:
            self.q[e].append(("wait", "bar", self.bar))
            for e2 in CENG:
                self.seen[e][e2] = max(self.seen[e].get(e2, 0), self.cnt[e2])
            for k, v in self.dsem_cnt.items():
                self.seen[e][k] = max(self.seen[e].get(k, 0), v)

    def end_phase(self):
        self.barrier()
        self.flush()
        self.free.extend(self.phase_ds)
        self.phase_ds = []

    def flush(self):
        nc = self.nc
        q = self.q
        sems = self.sems

        def emit(name, h):
            for it in q[name]:
                if it[0] == "wait":
                    h.wait_ge(sems[it[1]], it[2])
                elif it[0] == "op":
                    it[1](h).then_inc(sems[name], 1)
                elif it[0] == "dma":
                    it[1](h).then_inc(sems[it[2]], 16)
                elif it[0] == "inc":
                    h.sem_inc(sems[it[1]], it[2])

        with nc.Block() as block:
            @block.sync
            def _(h):
                emit("sp", h)

            @block.tensor
            def _(h):
                emit("pe", h)

            @block.scalar
            def _(h):
                emit("act", h)

            @block.vector
            def _(h):
                emit("dve", h)

            @block.gpsimd
            def _(h):
                emit("pool", h)
        self.q = {e: [] for e in ENGS}


class Rot:
    def __init__(self, items):
        self.items = items
        self.i = 0

    def next(self):
        it = self.items[self.i % len(self.items)]
        self.i += 1
        return it


def _bf(a):
    return np.ascontiguousarray(np.asarray(a, dtype=np.float32)).astype(ml_dtypes.bfloat16)


def make_consts():
    c = {}
    c["ident_f"] = np.eye(128, dtype=np.float32)
    c["ident_b"] = _bf(np.eye(128))
    c["ones_f"] = np.ones((128, 128), np.float32)
    key = np.arange(S)
    kaug_slc = np.zeros((64, S), np.float32)
    for j in range(63):
        kaug_slc[j] = (key // 64 == j)
    kaug_slc[63] = 1.0
    kaug_one = np.zeros((64, S), np.float32)
    kaug_one[63] = 1.0
    c["kaug_slc"] = _bf(kaug_slc)
    c["kaug_one"] = _bf(kaug_one)
    qq = np.arange(512, dtype=np.float64)
    c["vrow"] = _bf(np.stack([-8.0 * s * qq for s in SLOPES]))
    p = np.arange(128, dtype=np.float64)
    bt = np.zeros((128, 16, 32), np.float64)
    for h in range(16):
        for di in range(32):
            bt[:, h, di] = SLOPES[h] * (p + 128.0 * (di - 28))
    c["bias_tab"] = bt.reshape(128, 512).astype(np.float32)
    bc = np.zeros((128, 16, 8, 2), np.float64)
    for h in range(16):
        for t in range(8):
            for ct in range(2):
                bc[:, h, t, ct] = SLOPES[h] * (16.0 * (ct * 128 + p) + 31.0 - 512.0 * t)
    c["biasc_tab"] = bc.reshape(128, 256).astype(np.float32)
    cc = np.arange(256)
    tq = np.arange(S)
    ok = (16 * cc[:, None] + 31 <= tq[None, :]) & (cc[:, None] < 255)
    cm = np.where(ok, 0.0, NEG).astype(np.float32)
    c["cmask"] = _bf(cm.reshape(2, 128, S).transpose(1, 0, 2))
    kk = np.arange(128)[:, None]
    q2 = np.arange(128)[None, :]
    c["tri_causal"] = _bf(np.where(kk <= q2, 0.0, NEG))
    c["tri_band"] = _bf(np.where(kk > q2, 0.0, NEG))
    c["mask01"] = (kk <= q2).astype(np.float32)
    c0 = np.arange(256)[:, None] * 16
    s0 = np.arange(64)[None, :] * 64
    ov = np.clip(np.minimum(c0 + 32, s0 + 64) - np.maximum(c0, s0), 0, None) / 32.0
    ov[255] = 0.0
    c["ov"] = _bf(ov.reshape(2, 128, 64).transpose(1, 0, 2))
    blk = np.arange(64)[None, :]
    cur = (tq // 64)[:, None]
    forced = (blk == 0) | (blk == cur) | (blk == cur - 1)
    valid = blk <= cur
    vnf = (valid & ~forced).astype(np.float32)
    epsj = (64 - np.arange(64)).astype(np.float32)[None, :] * 1e-30
    fbig = (1e9 + 1e5 * np.arange(64)).astype(np.float32)[None, :]
    addc = np.where(forced, fbig, np.where(valid, epsj, -1.0)).astype(np.float32)

    def tm(a):
        return np.ascontiguousarray(a.reshape(32, 128, 64).transpose(1, 0, 2))

    c["sel_vnf"] = tm(vnf)
    c["sel_addc"] = tm(addc)
    c["sel_valid"] = tm(valid.astype(np.float32))
    return c


WA_PERM = np.concatenate([
    np.arange(0, 1024),
    np.arange(1024, 1536),
    np.arange(1536, 1792),
    np.arange(2048, 2304),
    np.arange(1792, 2048), np.arange(2304, 2560),
    np.arange(2608, 3632),
    np.arange(2560, 2608),
])
WB_PERM = np.concatenate(
    [np.arange(0, 2048), np.arange(2048, 4096)]
    + [np.concatenate([np.arange(4112 + i * 256, 4112 + (i + 1) * 256),
                       np.arange(6160 + i * 256, 6160 + (i + 1) * 256)]) for i in range(8)]
    + [np.arange(4096, 4112)])


def build_program(stop_after=99, dbg=False):
    nc = bass.Bass("TRN2", target_bir_lowering=False)
    consts = make_consts()

    def din(name, shape, dt=F32):
        return nc.dram_tensor(name, list(shape), dt, kind="ExternalInput").ap()

    def dscr(name, shape, dt):
        return nc.dram_tensor(name, list(shape), dt, kind=("ExternalOutput" if dbg else "Internal")).ap()

    x_d = din("x", [S, D])
    cT_d = din("cT", [128, 8])
    adaw_d = din("ada_w", [2, D, 3 * D])
    adabT_d = din("adabT", [128, 48])
    ngT_d = din("ngT", [128, 16])
    fing_d = din("fing", [1, D])
    wa_d = din("wa", [D, 3632])
    w1_d = din("cmp_w1", [2, 2048, 256])
    peT_d = din("peT", [64, 64])
    b1T_d = din("b1T", [128, 4])
    w2_d = din("cmp_w2", [2, 256, 64])
    b2T_d = din("b2T", [64, 2])
    woa_d = din("a_w_out", [D, D])
    wb_d = din("wb", [D, 8208])
    convwT_d = din("convwT", [128, 64])
    convbT_d = din("convbT", [128, 16])
    gateb_d = din("gateb", [1, 16])
    headg_d = din("headgT", [128, 16])
    wob_d = din("b_w_out", [2048, D])
    cd = {}
    for k, v in consts.items():
        cd[k] = din("k_" + k, v.shape, BF16 if v.dtype == ml_dtypes.bfloat16 else F32)
    out_d = nc.dram_tensor("out", [S, D], F32, kind="ExternalOutput").ap()

    qT_s = dscr("qT_s", [1024, S], BF16)
    kvT_s = dscr("kvT_s", [4, 256, S], BF16)
    vtok_s = dscr("vtok_s", [S, 512], BF16)
    zs_s = dscr("zs_s", [S, 1024], F32)
    y0_s = dscr("y0_s", [S, 1024], BF16)
    x1_s = dscr("x1_s", [S, D], F32)
    qkT1_s = dscr("qkT1_s", [16, 128, S], BF16)
    vtok1_s = dscr("vtok1_s", [S, 2048], BF16)
    ogz_s = dscr("ogz_s", [S, 2048], F32)

    es_all = ExitStack()
    P = Prog(nc, es_all)

    uid = {"n": 0}

    def sb(es, name, shape, dt):
        uid["n"] += 1
        return es.enter_context(nc.sbuf_tensor("s%d_%s" % (uid["n"], name), list(shape), dt))

    def ps(es, name, shape, dt=F32):
        uid["n"] += 1
        return es.enter_context(nc.psum_tensor("p%d_%s" % (uid["n"], name), list(shape), dt))

    def finish(dbg=None):
        P.end_phase()
        return nc

    ident_f = sb(es_all, "ident_f", [128, 128], F32)
    ident_b = sb(es_all, "ident_b", [128, 128], BF16)
    ones_f = sb(es_all, "ones_f", [128, 128], F32)
    cvec = sb(es_all, "cvec", [128, 4], F32)
    modT = sb(es_all, "modT", [128, 48], F32)
    gsc = sb(es_all, "gsc", [128, 16], F32)
    gate_bc = [sb(es_all, "gate_bc%d" % l, [128, D], F32) for l in range(2)]
    gate_sb = sb(es_all, "gate_sb", [128, NT, 48], F32)
    es_w1 = ExitStack()
    w1f = sb(es_w1, "w1f", [64, 32, 256], F32)
    t_w1f = Tok("w1f", P.free.pop(0))
    es_h = ExitStack()
    hT = sb(es_h, "hT", [128, 8, S], BF16)
    t_const = Tok("const", P.free.pop(0))
    t_cvec = P.tok("cvec")
    t_modT = P.tok("modT")
    t_gsc = P.tok("gsc")
    t_gbc = [P.tok("gbc0"), P.tok("gbc1")]
    t_gate = P.tok("gate_sb")
    t_hT = [P.tok("hT%d" % i) for i in range(NQ)]

    P.dma("sp", lambda e: e.dma_start(out=ident_f[:], in_=cd["ident_f"]), t_const, True)
    P.dma("sp", lambda e: e.dma_start(out=ident_b[:], in_=cd["ident_b"]), t_const, True)
    P.dma("sp", lambda e: e.dma_start(out=ones_f[:], in_=cd["ones_f"]), t_const, True)
    P.op("pool", lambda e: e.memset(cvec[:, 0:1], EPS), writes=[t_cvec])
    P.op("pool", lambda e: e.memset(cvec[:, 1:2], 1.0), writes=[t_cvec])
    P.op("pool", lambda e: e.memset(cvec[:, 2:3], LN_KSCALE), writes=[t_cvec])
    P.op("pool", lambda e: e.memset(cvec[:, 3:4], 0.0), writes=[t_cvec])

    with ExitStack() as es:
        cT = sb(es, "cT", [128, 8], F32)
        adab = sb(es, "adab", [128, 48], F32)
        ngT = sb(es, "ngT", [128, 16], F32)
        wst = [sb(es, "adaw%d" % i, [128, 8, 768], F32) for i in range(3)]
        dg = [sb(es, "dg%d" % i, [128, 128], F32) for i in range(2)]
        pmod = ps(es, "pmod", [128, 48], F32)
        pbc = [ps(es, "pbc%d" % i, [128, 512], F32) for i in range(2)]
        t_small = P.tok("small", dma=True)
        t_w = [P.tok("adaw%d" % i, dma=True) for i in range(3)]
        t_pmod = P.tok("pmod")
        t_dg = [P.tok("dg0"), P.tok("dg1")]
        t_pbc = [P.tok("pbc0"), P.tok("pbc1")]
        P.dma("sp", lambda e: e.dma_start(out=cT[:], in_=cT_d), t_small, True)
        P.dma("sp", lambda e: e.dma_start(out=adab[:], in_=adabT_d), t_small, True)
        P.dma("sp", lambda e: e.dma_start(out=ngT[:], in_=ngT_d), t_small, True)
        k = 0
        for l in range(2):
            for pc in range(4):
                w = wst[k % 3]
                tw = t_w[k % 3]
                k += 1
                src = adaw_d[l, :, pc * 768:(pc + 1) * 768].rearrange("(kc p) w -> p kc w", p=128)
                P.dma("sp", lambda e, w=w, src=src: e.dma_start(out=w[:], in_=src), tw, True)
                for jj in range(6):
                    col = l * 24 + pc * 6 + jj
                    for kc in range(8):
                        P.op("pe", lambda e, w=w, jj=jj, kc=kc, col=col: e.matmul(
                            pmod[:, col:col + 1], lhsT=w[:, kc, jj * 128:(jj + 1) * 128],
                            rhs=cT[:, kc:kc + 1], start=(kc == 0), stop=(kc == 7)),
                            reads=[tw, t_small], writes=[t_pmod])
        P.op("dve", lambda e: e.tensor_tensor(out=modT[:], in0=pmod[:], in1=adab[:], op=ALU.add),
             reads=[t_pmod, t_small], writes=[t_modT])
        for l in range(2):
            P.op("dve", lambda e, l=l: e.scalar_tensor_tensor(
                out=gsc[:, l * 8:(l + 1) * 8], in0=modT[:, l * 24 + 8:l * 24 + 16], scalar=1.0,
                in1=ngT[:, l * 8:(l + 1) * 8], op0=ALU.add, op1=ALU.mult),
                reads=[t_modT, t_small], writes=[t_gsc])
            for kc in range(8):
                d_, td = dg[kc % 2], t_dg[kc % 2]
                P.op("dve", lambda e, l=l, kc=kc, d_=d_: e.tensor_scalar(
                    out=d_[:], in0=ident_f[:], scalar1=modT[:, l * 24 + 16 + kc:l * 24 + 17 + kc],
                    scalar2=None, op0=ALU.mult), reads=[t_modT, t_const], writes=[td])
                P.op("pe", lambda e, kc=kc, d_=d_: e.matmul(
                    pbc[kc // 4][:, (kc % 4) * 128:(kc % 4 + 1) * 128], lhsT=ones_f[:], rhs=d_[:],
                    start=True, stop=True), reads=[td, t_const], writes=[t_pbc[kc // 4]])
            for hf in range(2):
                P.op("dve", lambda e, l=l, hf=hf: e.tensor_copy(
                    gate_bc[l][:, hf * 512:(hf + 1) * 512], pbc[hf][:]),
                    reads=[t_pbc[hf]], writes=[t_gbc[l]])
        P.end_phase()
    if stop_after == 0:
        return finish()

    nres = {"i": 0}

    def norm_res(es, full=True):
        R = {}
        k = nres["i"]
        nres["i"] += 1
        R["junk"] = Rot([(sb(es, "njunk%d_%d" % (k, i), [128, D], F32), P.tok("njunk")) for i in range(2 if full else 1)])
        R["stat"] = Rot([(sb(es, "nstat%d_%d" % (k, i), [128, 4], F32), P.tok("nstat")) for i in range(3)])
        if full:
            R["xn"] = Rot([(sb(es, "nxn%d_%d" % (k, i), [128, D], F32), P.tok("nxn")) for i in range(3)])
            R["ptr"] = Rot([(ps(es, "nptr%d_%d" % (k, i), [128, 512], F32), P.tok("nptr")) for i in range(2)])
        return R

    def rms_stats(xt_ap, t_xt, R):
        junk, t_junk = R["junk"].next()
        st, t_st = R["stat"].next()
        P.op("act", lambda e: e.activation(out=junk[:], in_=xt_ap, func=AF.Square, accum_out=st[:, 0:1]),
             reads=[t_xt], writes=[t_junk, t_st])
        P.op("act", lambda e: e.activation(out=st[:, 1:2], in_=st[:, 0:1], func=AF.Ln,
                                           scale=1.0 / D, bias=cvec[:, 0:1]),
             reads=[t_st, t_cvec], writes=[t_st])
        P.op("act", lambda e: e.activation(out=st[:, 2:3], in_=st[:, 1:2], func=AF.Exp, scale=-0.5),
             reads=[t_st], writes=[t_st])
        return st, t_st

    def norm_a(xt_ap, t_xt, R):
        st, t_st = rms_stats(xt_ap, t_xt, R)
        xn, t_xn = R["xn"].next()
        P.op("dve", lambda e: e.tensor_scalar(out=xn[:], in0=xt_ap, scalar1=st[:, 2:3], scalar2=None,
                                              op0=ALU.mult), reads=[t_xt, t_st], writes=[t_xn])
        return xn, t_xn

    def norm_b(l, tt, xn, t_xn, R):
        for hf in range(2):
            pt, t_pt = R["ptr"].next()
            for c4 in range(4):
                kc = hf * 4 + c4
                P.op("pe", lambda e, pt=pt, c4=c4, kc=kc: e.transpose(
                    pt[:, c4 * 128:(c4 + 1) * 128], xn[:, kc * 128:(kc + 1) * 128], ident_f[:]),
                    reads=[t_xn, t_const], writes=[t_pt])
            for c4 in range(4):
                kc = hf * 4 + c4
                P.op("act", lambda e, pt=pt, c4=c4, kc=kc: e.activation(
                    out=hT[:, kc, tt * 128:(tt + 1) * 128], in_=pt[:, c4 * 128:(c4 + 1) * 128],
                    func=AF.Identity, scale=gsc[:, l * 8 + kc:l * 8 + kc + 1],
                    bias=modT[:, l * 24 + kc:l * 24 + kc + 1]),
                    reads=[t_pt, t_gsc, t_modT], writes=[t_hT[tt // 4]])

    def norm_to_hT(l, tt, xt_ap, t_xt, R):
        xn, t_xn = norm_a(xt_ap, t_xt, R)
        norm_b(l, tt, xn, t_xn, R)

    def inproj(es, w_d, groups, fm_sink, tm_sink):
        wst = [sb(es, "wst%d" % i, [128, 8, 512], F32) for i in range(2)]
        wbf = [sb(es, "wbf%d" % i, [128, 8, 512], BF16) for i in range(2)]
        t_wst = [P.tok("wst%d" % i, dma=True) for i in range(2)]
        t_wbf = [P.tok("wbf%d" % i) for i in range(2)]
        pacc = Rot([(ps(es, "pacc%d" % i, [128, 512], F32), P.tok("pacc")) for i in range(4)])
        for gi, (kind, c0, width, meta) in enumerate(groups):
            w32, w16 = wst[gi % 2], wbf[gi % 2]
            tw32, tw16 = t_wst[gi % 2], t_wbf[gi % 2]
            src = w_d[:, c0:c0 + width].rearrange("(kc p) w -> p kc w", p=128)
            P.dma("sp", lambda e, w32=w32, src=src, width=width: e.dma_start(out=w32[:, :, 0:width], in_=src),
                  tw32, True)
            for kc in range(8):
                P.op("act", lambda e, w32=w32, w16=w16, kc=kc, width=width: e.activation(
                    out=w16[:, kc, 0:width], in_=w32[:, kc, 0:width], func=AF.Identity), reads=[tw32], writes=[tw16])
            if kind == "fm":
                for bi in range(width // 128):
                    for qt in range(NQ):
                        pa, t_pa = pacc.next()
                        for kc in range(8):
                            P.op("pe", lambda e, pa=pa, w16=w16, bi=bi, kc=kc, qt=qt: e.matmul(
                                pa[:], lhsT=w16[:, kc, bi * 128:(bi + 1) * 128],
                                rhs=hT[:, kc, qt * 512:(qt + 1) * 512], start=(kc == 0), stop=(kc == 7)),
                                reads=[tw16, t_hT[qt]], writes=[t_pa])
                        fm_sink(meta + bi, qt, pa, t_pa)
            else:
                for tt in range(NT):
                    pa, t_pa = pacc.next()
                    for kc in range(8):
                        P.op("pe", lambda e, pa=pa, w16=w16, kc=kc, tt=tt, width=width: e.matmul(
                            pa[:, 0:width], lhsT=hT[:, kc, tt * 128:(tt + 1) * 128],
                            rhs=w16[:, kc, 0:width], start=(kc == 0), stop=(kc == 7)),
                            reads=[tw16, t_hT[tt // 4]], writes=[t_pa])
                    tm_sink(meta, tt, pa, t_pa)

    with ExitStack() as es:
        R = norm_res(es)
        xin = Rot([(sb(es, "xin%d" % i, [128, D], F32), P.tok("xin%d" % i, dma=True)) for i in range(3)])
        xns = {}
        for tt in range(NT + 1):
            lists = []
            if tt < NT:
                xt, t_xt = xin.next()
                P.dma("sp", lambda e, xt=xt, tt=tt: e.dma_start(out=xt[:], in_=x_d[tt * 128:(tt + 1) * 128, :]),
                      t_xt, True)
                P.capture()
                xns[tt] = norm_a(xt[:], t_xt, R)
                lists.append(P.end_capture())
            if tt >= 1:
                P.capture()
                norm_b(0, tt - 1, *xns.pop(tt - 1), R)
                lists.append(P.end_capture())
            P.replay_merged(lists)
        P.end_phase()
    if stop_after == 1:
        return finish()

    with ExitStack() as es:
        frow = [sb(es, "frow%d" % i, [128, S], BF16) for i in range(2)]
        t_frow = [P.tok("frow%d" % i, dma=True) for i in range(2)]
        tstg_b = Rot([(sb(es, "tstb%d" % i, [128, 4, 512], BF16), P.tok("tstb%d" % i, dma=True)) for i in range(2)])
        tstg_f = Rot([(sb(es, "tstf%d" % i, [128, 4, 512], F32), P.tok("tstf%d" % i, dma=True)) for i in range(2)])
        FM_DEST = ([qT_s[i * 128:(i + 1) * 128, :] for i in range(8)]
                   + [kvT_s[j, i * 128:(i + 1) * 128, :] for j in range(4) for i in range(2)])

        def fm_sink(blk, qt, pa, t_pa):
            fr, tf = frow[blk % 2], t_frow[blk % 2]
            if qt % 2 == 0:
                P.op("act", lambda e: e.activation(out=fr[:, qt * 512:(qt + 1) * 512], in_=pa[:],
                                                   func=AF.Identity), reads=[t_pa], writes=[tf])
            else:
                P.op("dve", lambda e: e.tensor_copy(fr[:, qt * 512:(qt + 1) * 512], pa[:]),
                     reads=[t_pa], writes=[tf])
            if qt == NQ - 1:
                dst = FM_DEST[blk]
                P.dma("pool", lambda e: e.dma_start(out=dst, in_=fr[:]), tf, False)

        tm_cur = {}

        def tm_sink(meta, tt, pa, t_pa):
            if meta == "gates":
                P.op("act", lambda e: e.activation(out=gate_sb[:, tt, :], in_=pa[:, 0:48], func=AF.Sigmoid),
                     reads=[t_pa], writes=[t_gate])
                return
            if tt % 4 == 0:
                tm_cur["buf"] = tstg_b.next() if meta == "v" else tstg_f.next()
            st, t_s = tm_cur["buf"]
            if meta == "v":
                P.op("dve", lambda e: e.tensor_copy(st[:, tt % 4, :], pa[:]), reads=[t_pa], writes=[t_s])
                if tt % 4 == 3:
                    dst = vtok_s[(tt - 3) * 128:(tt + 1) * 128, :].rearrange("(t p) c -> p t c", p=128)
                    P.dma("pool", lambda e: e.dma_start(out=dst, in_=st[:]), t_s, False)
            else:
                zi = meta
                P.op("act", lambda e: e.activation(out=st[:, tt % 4, :], in_=pa[:], func=AF.Silu),
                     reads=[t_pa], writes=[t_s])
                if tt % 4 == 3:
                    dst = zs_s[(tt - 3) * 128:(tt + 1) * 128, zi * 512:(zi + 1) * 512].rearrange(
                        "(t p) c -> p t c", p=128)
                    P.dma("pool", lambda e: e.dma_start(out=dst, in_=st[:]), t_s, False)

        groups = [("fm", 0, 512, 0), ("fm", 512, 512, 4), ("fm", 1024, 512, 8), ("fm", 1536, 512, 12),
                  ("tm", 2048, 512, "v"), ("tm", 2560, 512, 0), ("tm", 3072, 512, 1), ("tm", 3584, 48, "gates")]
        inproj(es, wa_d, groups, fm_sink, tm_sink)
        P.dma("sp", lambda e: e.dma_start(out=w1f[:], in_=w1_d[0].rearrange("(l d) h -> d l h", d=64)), t_w1f, True)
        P.end_phase()
    es_h.close()
    if stop_after == 2:
        return finish()

    with ExitStack() as es:
        t_c3 = P.tok("c3", dma=True)
        cmask = sb(es, "cmask", [128, 2, S], BF16)
        tri_c = sb(es, "tri_c", [128, 128], BF16)
        tri_b = sb(es, "tri_b", [128, 128], BF16)
        bias_tab = sb(es, "bias_tab", [128, 512], F32)
        biasc_tab = sb(es, "biasc_tab", [128, 256], F32)
        ovt = sb(es, "ovt", [128, 2, 64], BF16)
        sel_vnf = sb(es, "sel_vnf", [128, 32, 64], F32)
        sel_addc = sb(es, "sel_addc", [128, 32, 64], F32)
        sel_valid = sb(es, "sel_valid", [128, 32, 64], F32)
        KC = sb(es, "KC", [128, 4, 256], BF16)
        VC = sb(es, "VC", [128, 4, 2, 65], BF16)
        t_KC = P.tok("KC", dma=True)
        t_VC = P.tok("VC")
        for g in range(4):
            P.dma("sp", lambda e, g=g: e.dma_start(out=KC[64:128, g, :], in_=cd["kaug_one"][:, 0:256]), t_KC, True)
        P.op("pool", lambda e: e.memset(VC[:], 1.0), writes=[t_VC])

        with ExitStack() as es2:
            w1b = [sb(es2, "w1b%d" % i, [64, 32, 256], BF16) for i in range(2)]
            kvs_ = sb(es2, "kvs", [64, 4, S], BF16)
            kvs = [kvs_, kvs_]
            kvd = sb(es2, "kvd", [64, 4, 16, 256], BF16)
            t_kvd = P.tok("kvd")
            pef = sb(es2, "pef", [64, 64], F32)
            peb = sb(es2, "peb", [64, 64], BF16)
            b1T = sb(es2, "b1T", [128, 4], F32)
            bias1 = sb(es2, "bias1", [128, 4], F32)
            w2f = sb(es2, "w2f", [128, 2, 2, 64], F32)
            w2b = sb(es2, "w2b", [128, 2, 2, 64], BF16)
            b2T = sb(es2, "b2T", [64, 2], F32)
            xh = sb(es2, "xh", [128, 256], F32)
            tt1 = sb(es2, "tt1", [128, 256], F32)
            sg = sb(es2, "sg", [128, 256], F32)
            gel = [sb(es2, "gel%d" % i, [128, 2, 256], BF16) for i in range(2)]
            vct = sb(es2, "vct", [64, 256], BF16)
            ph = [ps(es2, "ph%d" % i, [128, 256], F32) for i in range(2)]
            pb1 = ps(es2, "pb1", [128, 4], F32)
            pk = ps(es2, "pk", [64, 256], F32)
            pvt = ps(es2, "pvt", [128, 128], BF16)
            t_w1b = [P.tok("w1b0"), P.tok("w1b1")]
            t_kvs_ = P.tok("kvs", dma=True)
            t_kvs = [t_kvs_, t_kvs_]
            t_sm = P.tok("cmpsmall", dma=True)
            t_peb = P.tok("peb")
            t_bias1 = P.tok("bias1")
            t_w2b = P.tok("w2b")
            t_xh, t_tt1, t_sg = P.tok("xh"), P.tok("tt1"), P.tok("sg")
            t_gel = [P.tok("gel0"), P.tok("gel1")]
            t_vct = P.tok("vct")
            t_ph = [P.tok("ph0"), P.tok("ph1")]
            t_pb1, t_pk, t_pvt = P.tok("pb1"), P.tok("pk"), P.tok("pvt")
            P.dma("sp", lambda e: e.dma_start(out=pef[:], in_=peT_d), t_sm, True)
            P.dma("sp", lambda e: e.dma_start(out=b1T[:], in_=b1T_d), t_sm, True)
            P.dma("sp", lambda e: e.dma_start(out=b2T[:], in_=b2T_d), t_sm, True)
            for xx in range(2):
                P.dma("sp", lambda e, xx=xx: e.dma_start(
                    out=w2f[:, xx, :, :], in_=w2_d[xx].rearrange("(hh p) d -> p hh d", p=128)), t_sm, True)
            P.op("dve", lambda e: e.tensor_copy(peb[:], pef[:]), reads=[t_sm], writes=[t_peb])
            P.op("dve", lambda e: e.tensor_copy(w2b[:], w2f[:]), reads=[t_sm], writes=[t_w2b])
            P.op("pool", lambda e: e.memset(gel[0][:], 0.0), writes=[t_gel[0]])
            P.op("pool", lambda e: e.memset(gel[1][:], 0.0), writes=[t_gel[1]])

            def load_x(xx):
                if xx == 1:
                    P.dma("sp", lambda e: e.dma_start(
                        out=w1f[:], in_=w1_d[xx].rearrange("(l d) h -> d l h", d=64)), t_w1f, True)
                for l4 in range(4):
                    if l4 % 2 == 0:
                        P.op("act", lambda e, l4=l4: e.activation(
                            out=w1b[xx][:, l4 * 8:(l4 + 1) * 8, :], in_=w1f[:, l4 * 8:(l4 + 1) * 8, :],
                            func=AF.Identity), reads=[t_w1f], writes=[t_w1b[xx]])
                    else:
                        P.op("dve", lambda e, l4=l4: e.tensor_copy(
                            w1b[xx][:, l4 * 8:(l4 + 1) * 8, :], w1f[:, l4 * 8:(l4 + 1) * 8, :]),
                            reads=[t_w1f], writes=[t_w1b[xx]])
                P.dma("sp", lambda e: e.dma_start(
                    out=kvs[xx][:], in_=kvT_s[xx].rearrange("(g d) s -> d g s", d=64)), t_kvs[xx], True)
                for hh in range(2):
                    col = xx * 2 + hh
                    for l in range(32):
                        P.op("pe", lambda e, hh=hh, l=l, col=col: e.matmul(
                            pb1[:, col:col + 1], lhsT=w1b[xx][:, l, hh * 128:(hh + 1) * 128],
                            rhs=peb[:, xx * 32 + l:xx * 32 + l + 1], start=(l == 0), stop=(l == 31)),
                            reads=[t_w1b[xx], t_peb], writes=[t_pb1])
                P.op("dve", lambda e: e.tensor_tensor(out=bias1[:, 2 * xx:2 * xx + 2], in0=pb1[:, 2 * xx:2 * xx + 2],
                                                      in1=b1T[:, 2 * xx:2 * xx + 2], op=ALU.add),
                     reads=[t_pb1, t_sm], writes=[t_bias1])

            it = 0
            load_x(0)
            for dst, nm in ((cmask, "cmask"), (tri_c, "tri_causal"), (tri_b, "tri_band"), (bias_tab, "bias_tab"),
                            (biasc_tab, "biasc_tab"), (ovt, "ov"), (sel_vnf, "sel_vnf"), (sel_addc, "sel_addc"),
                            (sel_valid, "sel_valid")):
                P.dma("sp", lambda e, dst=dst, nm=nm: e.dma_start(out=dst[:], in_=cd[nm]), t_c3, True)

            for xx in range(2):
                if xx == 1:
                    load_x(1)
                for g in range(4):
                    P.op("dve", lambda e, g=g, xx=xx: e.tensor_copy(
                        kvd[:, g, :, :], kvs[xx][:, g, :].rearrange("d (c s) -> d s c", s=16)),
                        reads=[t_kvs[xx]], writes=[t_kvd])
                for g in range(4):
                    ge, t_ge = gel[(xx * 4 + g) % 2], t_gel[(xx * 4 + g) % 2]
                    for hh in range(2):
                        p_, t_p = ph[it % 2], t_ph[it % 2]
                        it += 1
                        for l in range(32):
                            P.op("pe", lambda e, p_=p_, xx=xx, g=g, hh=hh, l=l: e.matmul(
                                p_[:, 0:255], lhsT=w1b[xx][:, l, hh * 128:(hh + 1) * 128],
                                rhs=kvd[:, g, l % 16, l // 16:l // 16 + 255], start=(l == 0), stop=(l == 31)),
                                reads=[t_w1b[xx], t_kvd], writes=[t_p])
                        col = xx * 2 + hh
                        P.op("act", lambda e, p_=p_, col=col: e.activation(
                            out=xh[:, 0:255], in_=p_[:, 0:255], func=AF.Identity, bias=bias1[:, col:col + 1]),
                            reads=[t_p, t_bias1], writes=[t_xh])
                        P.op("dve", lambda e: e.tensor_tensor(out=tt1[:, 0:255], in0=xh[:, 0:255],
                                                              in1=xh[:, 0:255], op=ALU.mult),
                             reads=[t_xh], writes=[t_tt1])
                        P.op("dve", lambda e: e.tensor_scalar(out=tt1[:, 0:255], in0=tt1[:, 0:255],
                                                              scalar1=0.044715, scalar2=1.0,
                                                              op0=ALU.mult, op1=ALU.add),
                             reads=[t_tt1], writes=[t_tt1])
                        P.op("dve", lambda e: e.tensor_tensor(out=tt1[:, 0:255], in0=tt1[:, 0:255],
                                                              in1=xh[:, 0:255], op=ALU.mult),
                             reads=[t_tt1, t_xh], writes=[t_tt1])
                        P.op("act", lambda e: e.activation(out=sg[:, 0:255], in_=tt1[:, 0:255], func=AF.Sigmoid,
                                                           scale=1.5957691216057308),
                             reads=[t_tt1], writes=[t_sg])
                        P.op("dve", lambda e, ge=ge, hh=hh: e.tensor_tensor(
                            out=ge[:, hh, 0:255], in0=xh[:, 0:255], in1=sg[:, 0:255], op=ALU.mult),
                            reads=[t_xh, t_sg], writes=[t_ge])
                    for hh in range(2):
                        P.op("pe", lambda e, ge=ge, xx=xx, hh=hh: e.matmul(
                            pk[:], lhsT=w2b[:, xx, hh, :], rhs=ge[:, hh, :], start=(hh == 0), stop=(hh == 1)),
                            reads=[t_ge, t_w2b], writes=[t_pk])
                    if xx == 0:
                        P.op("act", lambda e, g=g: e.activation(
                            out=KC[0:64, g, :], in_=pk[:], func=AF.Identity, bias=b2T[:, 0:1]),
                            reads=[t_pk, t_sm], writes=[t_KC])
                    else:
                        P.op("act", lambda e: e.activation(
                            out=vct[:], in_=pk[:], func=AF.Identity, bias=b2T[:, 1:2]),
                            reads=[t_pk, t_sm], writes=[t_vct])
                        for ct in range(2):
                            P.op("pe", lambda e, ct=ct: e.transpose(
                                pvt[:, 0:64], vct[:, ct * 128:(ct + 1) * 128], ident_b[0:64, 0:64]),
                                reads=[t_vct, t_const], writes=[t_pvt])
                            P.op("dve", lambda e, g=g, ct=ct: e.tensor_copy(VC[:, g, ct, 0:64], pvt[:, 0:64]),
                                 reads=[t_pvt], writes=[t_VC])
            P.barrier()
            P.flush()
        if stop_after == 2.5:
            return finish()

        KS = [sb(es, "KS%d" % i, [128, S], BF16) for i in range(2)]
        KW = [sb(es, "KW%d" % i, [128, S], BF16) for i in range(2)]
        VS = [sb(es, "VS%d" % i, [128, 32, 65], BF16) for i in range(2)]
        VW = [sb(es, "VW%d" % i, [128, 32, 65], BF16) for i in range(2)]
        t_KS = [P.tok("KS%d" % i, dma=True) for i in range(2)]
        t_KW = [P.tok("KW%d" % i, dma=True) for i in range(2)]
        t_VS = [P.tok("VS%d" % i, dma=True) for i in range(2)]
        t_VW = [P.tok("VW%d" % i, dma=True) for i in range(2)]
        for i in range(2):
            P.op("pool", lambda e, i=i: e.memset(VS[i][:], 1.0), writes=[t_VS[i]])
            P.op("pool", lambda e, i=i: e.memset(VW[i][:], 1.0), writes=[t_VW[i]])
            P.dma("sp", lambda e, i=i: e.dma_start(out=KS[i][64:128, :], in_=cd["kaug_slc"]), t_KS[i], True)
            P.dma("sp", lambda e, i=i: e.dma_start(out=KW[i][64:128, :], in_=cd["kaug_one"]), t_KW[i], True)
        QA = [[sb(es, "QA%d_%d" % (hh, i), [128, 512], BF16) for i in range(2)] for hh in range(4)]
        t_QA = [[P.tok("QA%d_%d" % (hh, i), dma=True) for i in range(2)] for hh in range(4)]
        for hh in range(4):
            for i in range(2):
                P.op("pool", lambda e, hh=hh, i=i: e.memset(QA[hh][i][:], 0.0), writes=[t_QA[hh][i]])
        resA = {"PT": Rot([(sb(es, "PTA%d" % i, [128, 512], BF16), P.tok("PTA")) for i in range(2)]),
                "Sp": Rot([(ps(es, "SpA%d" % i, [128, 512], F32), P.tok("SpA")) for i in range(1)]),
                "Op": Rot([(ps(es, "OpA%d" % i, [128, 4, 128], F32), P.tok("OpA")) for i in range(1)])}
        resB = {"PT": Rot([(sb(es, "PTB%d" % i, [128, 512], BF16), P.tok("PTB")) for i in range(4)]),
                "Sp": Rot([(ps(es, "SpB%d" % i, [128, 512], F32), P.tok("SpB")) for i in range(2)]),
                "Op": Rot([(ps(es, "OpB%d" % i, [128, 4, 128], F32), P.tok("OpB")) for i in range(2)])}
        IMPp = ps(es, "IMPp", [128, 4, 64], F32)
        t_IMP = P.tok("IMP")
        negTp = ps(es, "negTp", [128, 512], BF16)
        t_negT = P.tok("negT")
        yacc = [sb(es, "yacc%d" % i, [128, 4, 256], F32) for i in range(2)]
        t_yacc = [P.tok("yacc0"), P.tok("yacc1")]
        ztile = [sb(es, "ztile%d" % i, [128, 4, 256], F32) for i in range(2)]
        t_zt = [P.tok("zt%d" % i, dma=True) for i in range(2)]
        ybf = [sb(es, "ybf%d" % i, [128, 4, 256], BF16) for i in range(2)]
        t_ybf = [P.tok("ybf%d" % i, dma=True) for i in range(2)]
        sc = sb(es, "sc", [128, 4, 64], F32)
        t_sc = P.tok("sc")
        for nm, r_ in (("A", resA), ("B", resB)):
            r_["small"] = Rot([(sb(es, "sm%s%d" % (nm, i), [128, 16], F32), P.tok("sm")) for i in range(3)])
            r_["tmpo"] = Rot([(sb(es, "tmpo%s%d" % (nm, i), [128, 4, 64], F32), P.tok("tmpo")) for i in range(2)])
        s1 = sb(es, "s1", [128, 64], F32)
        s2 = sb(es, "s2", [128, 64], F32)
        m8 = sb(es, "m8", [128, 16], F32)
        negpad = sb(es, "negpad", [128, 128], BF16)
        t_s1, t_s2, t_m8, t_negpad = P.tok("s1"), P.tok("s2"), P.tok("m8"), P.tok("negpad")
        P.op("pool", lambda e: e.memset(negpad[:], 0.0), writes=[t_negpad])
        zb = sb(es, "zb", [128, 512], BF16)
        t_zb = P.tok("zb")
        P.op("pool", lambda e: e.memset(zb[:], 0.0), writes=[t_zb])

        def zero_bank(ap2d, t_bank, n):
            P.op("pe", lambda e: e.matmul(ap2d, lhsT=zb[:, 0:128], rhs=zb[:, 0:n], start=True, stop=False),
                 reads=[t_zb], writes=[t_bank])

        def s_tile(res, branch, h, t, Q, t_Q, Ktile, t_K, kcol, col0, ncol, mask, bias_ap, t_bias):
            sp, t_sp = res["Sp"].next()
            c1 = col0 + ncol
            P.op("pe", lambda e: e.matmul(sp[:, col0:c1], lhsT=Ktile[:, kcol:kcol + 128], rhs=Q[:, col0:c1],
                                          start=True, stop=(mask is None)),
                 reads=[t_Q, t_K], writes=[t_sp])
            if mask is not None:
                mcol, mrhs, t_m = mask
                w = mrhs.shape[-1] if hasattr(mrhs, "shape") else 128
                P.op("pe", lambda e: e.matmul(sp[:, mcol:mcol + w], lhsT=ident_b[:], rhs=mrhs,
                                              start=False, stop=True),
                     reads=[t_const, t_m], writes=[t_sp])
            pt, t_pt = res["PT"].next()
            P.op("act", lambda e: e.activation(out=pt[:, col0:c1], in_=sp[:, col0:c1], func=AF.Exp,
                                               scale=0.125, bias=bias_ap),
                 reads=[t_sp, t_bias], writes=[t_pt])
            return pt, t_pt

        def finalize(res, br, h, hh, t, op_, t_op, ya, t_ya, first):
            sm, t_sm_ = res["small"].next()
            gcol = br * 16 + h
            P.op("dve", lambda e: e.tensor_scalar(out=sm[:, 0:4], in0=op_[:, :, 64], scalar1=1e-30,
                                                  scalar2=None, op0=ALU.max),
                 reads=[t_op], writes=[t_sm_])
            P.op("dve", lambda e: e.reciprocal(out=sm[:, 4:8], in_=sm[:, 0:4]), reads=[t_sm_], writes=[t_sm_])
            P.op("dve", lambda e: e.tensor_tensor(out=sm[:, 8:12], in0=sm[:, 4:8],
                                                  in1=gate_sb[:, 4 * t:4 * t + 4, gcol], op=ALU.mult),
                 reads=[t_sm_, t_gate], writes=[t_sm_])
            fb = sm[:, 8:12].unsqueeze(2).broadcast_to([128, 4, 64])
            if first:
                P.op("dve", lambda e: e.tensor_tensor(out=ya[:, :, hh * 64:(hh + 1) * 64], in0=op_[:, :, 0:64],
                                                      in1=fb, op=ALU.mult),
                     reads=[t_op, t_sm_], writes=[t_ya])
            else:
                to, t_to = res["tmpo"].next()
                P.op("dve", lambda e: e.tensor_tensor(out=to[:], in0=op_[:, :, 0:64], in1=fb, op=ALU.mult),
                     reads=[t_op, t_sm_], writes=[t_to])
                P.op("dve", lambda e: e.tensor_tensor(out=ya[:, :, hh * 64:(hh + 1) * 64],
                                                      in0=ya[:, :, hh * 64:(hh + 1) * 64], in1=to[:], op=ALU.add),
                     reads=[t_to, t_ya], writes=[t_ya])
            return sm, t_sm_

        class Pipe:
            def __init__(self, res, depth):
                self.pending = []
                self.depth = depth
                self.res = res

            def push(self, s_args, pv_fn, post_fn=None):
                pt, t_pt = s_tile(self.res, *s_args)
                self.pending.append((pt, t_pt, pv_fn, post_fn))
                while len(self.pending) > self.depth:
                    self._pop()

            def _pop(self):
                pt, t_pt, pv_fn, post_fn = self.pending.pop(0)
                pv_fn(pt, t_pt)
                if post_fn is not None:
                    post_fn()

            def flush(self):
                while self.pending:
                    self._pop()

        pipeA = Pipe(resA, 1)
        pipeB = Pipe(resB, 2)

        def load_group(g):
            gi = g % 2
            P.dma("sp", lambda e: e.dma_start(
                out=KS[gi][0:64, :], in_=kvT_s[2, g * 64:(g + 1) * 64, :]), t_KS[gi], True)
            P.dma("sp", lambda e: e.dma_start(
                out=KW[gi][0:64, :], in_=kvT_s[3, g * 64:(g + 1) * 64, :]), t_KW[gi], True)
            for half in range(2):
                P.dma("sp", lambda e, half=half: e.dma_start(
                    out=VS[gi][:, half * 16:(half + 1) * 16, 0:64],
                    in_=vtok_s[half * 2048:(half + 1) * 2048, g * 64:(g + 1) * 64].rearrange(
                        "(kt p) d -> p kt d", p=128)), t_VS[gi], True)
                P.dma("sp", lambda e, half=half: e.dma_start(
                    out=VW[gi][:, half * 16:(half + 1) * 16, 0:64],
                    in_=vtok_s[half * 2048:(half + 1) * 2048, 256 + g * 64:256 + (g + 1) * 64].rearrange(
                        "(kt p) d -> p kt d", p=128)), t_VW[gi], True)

        def stage_A(u, g, t):
            par = u % 2
            q0 = t * 512
            ya, t_ya = yacc[par], t_yacc[par]
            for hh in range(4):
                h = g * 4 + hh
                P.dma("sp", lambda e, hh=hh, h=h: e.dma_start(
                    out=QA[hh][par][0:64, :], in_=qT_s[h * 64:(h + 1) * 64, q0:q0 + 512]),
                    t_QA[hh][par], True)
                P.dma("sp", lambda e, hh=hh, h=h: e.dma_start(
                    out=QA[hh][par][127:128, :], in_=cd["vrow"][h:h + 1, :]), t_QA[hh][par], True)
            zt, t_z = ztile[par], t_zt[par]
            P.dma("sp", lambda e: e.dma_start(
                out=zt[:], in_=zs_s[t * 512:(t + 1) * 512, g * 256:(g + 1) * 256].rearrange(
                    "(s p) c -> p s c", p=128)), t_z, True)
            ncts = 2 if t >= 4 else 1
            for hh in range(4):
                h = g * 4 + hh
                Q, t_Q = QA[hh][par], t_QA[hh][par]
                op_, t_op = resA["Op"].next()
                for ct in range(ncts):
                    bcol = (h * 8 + t) * 2 + ct

                    def pv(pt, t_pt, op_=op_, t_op=t_op, ct=ct):
                        if ct == 0:
                            zero_bank(op_[:].rearrange("p a b -> p (a b)"), t_op, 512)
                            zero_bank(IMPp[:].rearrange("p a b -> p (a b)"), t_IMP, 256)
                        for qs in range(4):
                            P.op("pe", lambda e, qs=qs: e.matmul(
                                op_[:, qs, 0:65], lhsT=pt[:, qs * 128:(qs + 1) * 128], rhs=VC[:, g, ct, :],
                                start=False, stop=(ct == ncts - 1 and qs == 3)),
                                reads=[t_pt, t_VC], writes=[t_op])
                            P.op("pe", lambda e, qs=qs: e.matmul(
                                IMPp[:, qs, :], lhsT=pt[:, qs * 128:(qs + 1) * 128], rhs=ovt[:, ct, :],
                                start=False, stop=(ct == ncts - 1 and qs == 3)),
                                reads=[t_pt, t_c3], writes=[t_IMP])

                    def post(op_=op_, t_op=t_op, h=h, hh=hh):
                        sm, t_sm_ = finalize(resA, 0, h, hh, t, op_, t_op, ya, t_ya, True)
                        rb = sm[:, 4:8].unsqueeze(2).broadcast_to([128, 4, 64])
                        if hh == 0:
                            P.op("dve", lambda e: e.tensor_tensor(out=sc[:], in0=IMPp[:], in1=rb, op=ALU.mult),
                                 reads=[t_IMP, t_sm_], writes=[t_sc])
                        else:
                            to, t_to = resA["tmpo"].next()
                            P.op("dve", lambda e: e.tensor_tensor(out=to[:], in0=IMPp[:], in1=rb, op=ALU.mult),
                                 reads=[t_IMP, t_sm_], writes=[t_to])
                            P.op("dve", lambda e: e.tensor_tensor(out=sc[:], in0=sc[:], in1=to[:], op=ALU.add),
                                 reads=[t_to, t_sc], writes=[t_sc])

                    pipeA.push((0, h, t, Q, t_Q, KC[:, g, :], t_KC, ct * 128, 0, 512,
                               (0, cmask[:, ct, q0:q0 + 512], t_c3), biasc_tab[:, bcol:bcol + 1], t_c3),
                              pv, post if ct == ncts - 1 else None)
            pipeA.flush()
            for qs in range(4):
                tt = 4 * t + qs
                P.op("dve", lambda e, qs=qs, tt=tt: e.tensor_tensor(out=s1[:], in0=sc[:, qs, :],
                                                                    in1=sel_vnf[:, tt, :], op=ALU.mult),
                     reads=[t_sc, t_c3], writes=[t_s1])
                P.op("dve", lambda e, tt=tt: e.tensor_tensor(out=s1[:], in0=s1[:], in1=sel_addc[:, tt, :],
                                                             op=ALU.add),
                     reads=[t_s1, t_c3], writes=[t_s1])
                P.op("dve", lambda e: e.max(out=m8[:, 0:8], in_=s1[:]), reads=[t_s1], writes=[t_m8])
                P.op("dve", lambda e: e.tensor_scalar(out=s2[:], in0=s1[:], scalar1=m8[:, 7:8],
                                                      scalar2=None, op0=ALU.is_ge),
                     reads=[t_s1, t_m8], writes=[t_s2])
                P.op("dve", lambda e: e.scalar_tensor_tensor(out=s2[:], in0=s2[:], scalar=-3.0e9, in1=s1[:],
                                                             op0=ALU.mult, op1=ALU.add),
                     reads=[t_s1, t_s2], writes=[t_s2])
                P.op("dve", lambda e: e.max(out=m8[:, 8:16], in_=s2[:]), reads=[t_s2], writes=[t_m8])
                P.op("dve", lambda e: e.tensor_scalar(out=s2[:], in0=s1[:], scalar1=m8[:, 15:16],
                                                      scalar2=None, op0=ALU.is_ge),
                     reads=[t_s1, t_m8], writes=[t_s2])
                P.op("dve", lambda e, tt=tt: e.tensor_tensor(out=s2[:], in0=s2[:], in1=sel_valid[:, tt, :],
                                                             op=ALU.mult),
                     reads=[t_s2, t_c3], writes=[t_s2])
                P.op("dve", lambda e: e.tensor_scalar(out=negpad[:, 64:127], in0=s2[:, 0:63], scalar1=1.0,
                                                      scalar2=-NEG, op0=ALU.subtract, op1=ALU.mult),
                     reads=[t_s2], writes=[t_negpad])
                P.op("pe", lambda e, qs=qs: e.transpose(negTp[:, qs * 128:(qs + 1) * 128], negpad[:], ident_b[:]),
                     reads=[t_negpad, t_const], writes=[t_negT])
            for hh in range(4):
                h = g * 4 + hh
                P.op("dve", lambda e, hh=hh: e.tensor_copy(QA[hh][par][64:127, :], negTp[64:127, :]),
                     reads=[t_negT], writes=[t_QA[hh][par]])

        def stage_B(u, g, t):
            par = u % 2
            gi = g % 2
            ya, t_ya = yacc[par], t_yacc[par]
            zt, t_z = ztile[par], t_zt[par]
            for hh in range(4):
                h = g * 4 + hh
                Q, t_Q = QA[hh][par], t_QA[hh][par]
                for br, Kt, t_K, Vt, t_V in ((1, KS[gi], t_KS[gi], VS[gi], t_VS[gi]),
                                             (2, KW[gi], t_KW[gi], VW[gi], t_VW[gi])):
                    tiles = []
                    if br == 1:
                        for kt in range(0, 4 * t):
                            md = (4 * t - kt - 1) * 128 + 1
                            if SLOPES[h] * md >= ZERO_CUT:
                                continue
                            tiles.append((kt, 0, 512, None))
                    else:
                        for i in (3, 2, 1, 0):
                            kt = 4 * t - 4 + i
                            md = 385 - 128 * i if i >= 1 else 385
                            if kt >= 0 and SLOPES[h] * md < ZERO_CUT:
                                tiles.append((kt, 0, 128 * (i + 1), (128 * i, tri_b[:], t_c3)))
                    for i2 in range(4):
                        tiles.append((4 * t + i2, 128 * i2, 512 - 128 * i2, (128 * i2, tri_c[:], t_c3)))
                    last = {}
                    for ti, (kt, col0, ncol, mask) in enumerate(tiles):
                        for qs in range(col0 // 128, (col0 + ncol) // 128):
                            last[qs] = ti
                    op_, t_op = resB["Op"].next()
                    for ti, (kt, col0, ncol, mask) in enumerate(tiles):
                        bcol = h * 32 + (kt - 4 * t + 28)

                        def pv(pt, t_pt, op_=op_, t_op=t_op, ti=ti, kt=kt, col0=col0, ncol=ncol, Vt=Vt, t_V=t_V,
                               last=last, tiles=tiles):
                            if ti == 0:
                                zero_bank(op_[:].rearrange("p a b -> p (a b)"), t_op, 512)
                            for qs in range(col0 // 128, (col0 + ncol) // 128):
                                P.op("pe", lambda e, qs=qs: e.matmul(
                                    op_[:, qs, 0:65], lhsT=pt[:, qs * 128:(qs + 1) * 128], rhs=Vt[:, kt, :],
                                    start=False, stop=(ti == len(tiles) - 1 and qs == 3)),
                                    reads=[t_pt, t_V], writes=[t_op])

                        def post(op_=op_, t_op=t_op, br=br, h=h, hh=hh):
                            finalize(resB, br, h, hh, t, op_, t_op, ya, t_ya, False)

                        pipeB.push((br, h, t, Q, t_Q, Kt, t_K, kt * 128, col0, ncol, mask,
                                   bias_tab[:, bcol:bcol + 1], t_c3), pv,
                                  post if ti == len(tiles) - 1 else None)
            pipeB.flush()
            yb, t_yb = ybf[par], t_ybf[par]
            P.op("dve", lambda e: e.tensor_tensor(out=yb[:], in0=ya[:], in1=zt[:], op=ALU.mult),
                 reads=[t_ya, t_z], writes=[t_yb])
            P.dma("pool", lambda e: e.dma_start(
                out=y0_s[t * 512:(t + 1) * 512, g * 256:(g + 1) * 256].rearrange("(s p) c -> p s c", p=128),
                in_=yb[:]), t_yb, False)

        units = [(g, t) for g in range(4) for t in range(NQ)]
        for u in range(len(units) + 1):
            lists = []
            if u < len(units):
                g, t = units[u]
                if t == 0:
                    load_group(g)
                P.capture()
                stage_A(u, g, t)
                lists.append(P.end_capture())
            if u > 0:
                P.capture()
                stage_B(u - 1, *units[u - 1])
                lists.append(P.end_capture())
            P.replay_merged(lists)
        P.end_phase()
    es_w1.close()
    if stop_after == 3:
        return finish()


    es_g = ExitStack()
    ipre = sb(es_g, "ipre", [128, NT, 8], F32)
    logf = sb(es_g, "logf", [128, NT, 8], F32)
    es_h = ExitStack()
    hT = sb(es_h, "hT_l1", [128, 8, S], BF16)
    t_ipre, t_logf = P.tok("ipre"), P.tok("logf")
    with ExitStack() as es:
        R = norm_res(es)
        wof = sb(es, "wof", [128, 8, 512], F32)
        wo = sb(es, "wo", [128, 8, D], BF16)
        t_wof, t_wo = P.tok("wof", dma=True), P.tok("wo")
        for hf in range(2):
            P.dma("sp", lambda e, hf=hf: e.dma_start(
                out=wof[:], in_=woa_d[:, hf * 512:(hf + 1) * 512].rearrange("(kc p) n -> p kc n", p=128)),
                t_wof, True)
            for kc in range(8):
                P.op("pool", lambda e, hf=hf, kc=kc: e.tensor_copy(wo[:, kc, hf * 512:(hf + 1) * 512], wof[:, kc, :]),
                     reads=[t_wof], writes=[t_wo])
        yin = Rot([(sb(es, "yin%d" % i, [128, D], BF16), P.tok("yin%d" % i, dma=True)) for i in range(3)])
        xin = Rot([(sb(es, "xin4_%d" % i, [128, D], F32), P.tok("xin4_%d" % i, dma=True)) for i in range(3)])
        x1t = Rot([(sb(es, "x1t%d" % i, [128, D], F32), P.tok("x1t%d" % i, dma=True)) for i in range(3)])
        yT = Rot([(sb(es, "yT%d" % i, [128, 8, 128], BF16), P.tok("yT")) for i in range(2)])
        ytp = ps(es, "ytp", [128, D], BF16)
        t_ytp = P.tok("ytp")
        po = [ps(es, "po%d" % i, [128, 512], F32) for i in range(2)]
        t_po = [P.tok("po0"), P.tok("po1")]
        def stage_X(tt):
            yt, t_yt = yin.next()
            xt, t_xt = xin.next()
            x1, t_x1 = x1t.next()
            yTt, t_yT = yT.next()
            P.dma("sp", lambda e, yt=yt, tt=tt: e.dma_start(out=yt[:], in_=y0_s[tt * 128:(tt + 1) * 128, :]),
                  t_yt, True)
            P.dma("sp", lambda e, xt=xt, tt=tt: e.dma_start(out=xt[:], in_=x_d[tt * 128:(tt + 1) * 128, :]),
                  t_xt, True)
            for kc in range(8):
                P.op("pe", lambda e, yt=yt, kc=kc: e.transpose(
                    ytp[:, kc * 128:(kc + 1) * 128], yt[:, kc * 128:(kc + 1) * 128], ident_b[:]),
                    reads=[t_yt, t_const], writes=[t_ytp])
            P.op("act", lambda e, yTt=yTt: e.activation(
                out=yTt[:, 0:4, :], in_=ytp[:, 0:512].rearrange("p (a b) -> p a b", b=128), func=AF.Identity),
                reads=[t_ytp], writes=[t_yT])
            P.op("act", lambda e, yTt=yTt: e.activation(
                out=yTt[:, 4:8, :], in_=ytp[:, 512:1024].rearrange("p (a b) -> p a b", b=128), func=AF.Identity),
                reads=[t_ytp], writes=[t_yT])
            for hf in range(2):
                for kc in range(8):
                    P.op("pe", lambda e, yTt=yTt, hf=hf, kc=kc: e.matmul(
                        po[hf][:], lhsT=yTt[:, kc, :], rhs=wo[:, kc, hf * 512:(hf + 1) * 512],
                        start=(kc == 0), stop=(kc == 7)), reads=[t_yT, t_wo], writes=[t_po[hf]])
                P.op("dve", lambda e, x1=x1, hf=hf: e.tensor_tensor(
                    out=x1[:, hf * 512:(hf + 1) * 512], in0=po[hf][:], in1=gate_bc[0][:, hf * 512:(hf + 1) * 512],
                    op=ALU.mult), reads=[t_po[hf], t_gbc[0]], writes=[t_x1])
            P.op("dve", lambda e, x1=x1, xt=xt: e.tensor_tensor(out=x1[:], in0=x1[:], in1=xt[:], op=ALU.add),
                 reads=[t_xt, t_x1], writes=[t_x1])
            P.dma("pool", lambda e, x1=x1, tt=tt: e.dma_start(out=x1_s[tt * 128:(tt + 1) * 128, :], in_=x1[:]),
                  t_x1, False)
            return x1, t_x1

        x1s, xns = {}, {}
        for tt in range(NT + 2):
            lists = []
            if tt < NT:
                P.capture()
                x1s[tt] = stage_X(tt)
                lists.append(P.end_capture())
            if 0 <= tt - 1 < NT:
                P.capture()
                x1, t_x1 = x1s.pop(tt - 1)
                xns[tt - 1] = norm_a(x1[:], t_x1, R)
                lists.append(P.end_capture())
            if 0 <= tt - 2 < NT:
                P.capture()
                norm_b(1, tt - 2, *xns.pop(tt - 2), R)
                lists.append(P.end_capture())
            P.replay_merged(lists)
        P.end_phase()
    if stop_after == 4:
        return finish()

    with ExitStack() as es:
        frow = [sb(es, "frow5_%d" % i, [128, S], BF16) for i in range(2)]
        t_frow = [P.tok("frow5_%d" % i, dma=True) for i in range(2)]
        raw = Rot([(sb(es, "raw%d" % i, [128, 515], F32), P.tok("raw"), P.tok("rawc")) for i in range(3)])
        acc = Rot([(sb(es, "acc%d" % i, [128, 512], F32), P.tok("acc")) for i in range(3)])
        convw = sb(es, "convw", [128, 64], F32)
        convb = sb(es, "convb", [128, 16], F32)
        gb_bc = sb(es, "gb_bc", [128, 16], F32)
        t_cv = P.tok("cv", dma=True)
        P.dma("sp", lambda e: e.dma_start(out=convw[:], in_=convwT_d), t_cv, True)
        P.dma("sp", lambda e: e.dma_start(out=convb[:], in_=convbT_d), t_cv, True)
        P.dma("sp", lambda e: e.dma_start(out=gb_bc[:], in_=bass.AP(gateb_d.tensor, 0, [[0, 128], [1, 16]])), t_cv, True)
        tstb = Rot([(sb(es, "tstb5_%d" % i, [128, 4, 512], BF16), P.tok("tstb5_%d" % i, dma=True)) for i in range(2)])
        tsto = Rot([(sb(es, "tsto%d" % i, [128, 4, 256], F32), P.tok("tsto%d" % i, dma=True)) for i in range(2)])
        og1 = Rot([(sb(es, "og1_%d" % i, [128, 256], F32), P.tok("og1")) for i in range(2)])
        z1 = Rot([(sb(es, "z1_%d" % i, [128, 256], F32), P.tok("z1")) for i in range(2)])
        gtmp = Rot([(sb(es, "gtmp%d" % i, [128, 24], F32), P.tok("gtmp")) for i in range(2)])
        sgt = Rot([(sb(es, "sgt%d" % i, [128, 512], F32), P.tok("sgt")) for i in range(2)])
        st5 = {"prev": None}

        def fm_sink5(blk, qt, pa, t_pa):
            rw, t_rw, t_rwc = raw.next()
            ac, t_ac = acc.next()
            fr, tf = frow[blk % 2], t_frow[blk % 2]
            P.op("act", lambda e: e.activation(out=rw[:, 3:515], in_=pa[:], func=AF.Identity),
                 reads=[t_pa], writes=[t_rw])
            P.op("act", lambda e: e.activation(out=ac[:], in_=pa[:], func=AF.Identity,
                                               scale=convw[:, blk * 4 + 3:blk * 4 + 4]),
                 reads=[t_pa, t_cv], writes=[t_ac])
            if st5.get("pend"):
                st5.pop("pend")()
            if qt == 0:
                P.op("dve", lambda e: e.memset(rw[:, 0:3], 0.0), writes=[t_rwc])
            else:
                prw, t_prw = st5["prev"]
                P.op("dve", lambda e: e.tensor_copy(rw[:, 0:3], prw[:, 512:515]), reads=[t_prw], writes=[t_rwc])
            st5["prev"] = (rw, t_rw)
            for j in range(0, 3):
                P.op("dve", lambda e, j=j: e.scalar_tensor_tensor(
                    out=ac[:], in0=rw[:, j:j + 512], scalar=convw[:, blk * 4 + j:blk * 4 + j + 1], in1=ac[:],
                    op0=ALU.mult, op1=ALU.add), reads=[t_rw, t_rwc, t_cv, t_ac], writes=[t_ac])

            def pend():
                P.op("act", lambda e: e.activation(out=fr[:, qt * 512:(qt + 1) * 512], in_=ac[:], func=AF.Silu,
                                                   bias=convb[:, blk:blk + 1]),
                     reads=[t_ac, t_cv], writes=[tf])
                if qt == NQ - 1:
                    P.dma("pool", lambda e: e.dma_start(out=qkT1_s[blk], in_=fr[:]), tf, False)

            st5["pend"] = pend

        tm5 = {}

        def tm_sink5(meta, tt, pa, t_pa):
            if st5.get("pend"):
                st5.pop("pend")()
            if meta == "g":
                gt, t_gt = gtmp.next()
                P.op("dve", lambda e: e.tensor_tensor(out=ipre[:, tt, :], in0=pa[:, 0:8], in1=gb_bc[:, 0:8],
                                                      op=ALU.add), reads=[t_pa, t_cv], writes=[t_ipre])
                P.op("dve", lambda e: e.tensor_tensor(out=gt[:, 0:8], in0=pa[:, 8:16], in1=gb_bc[:, 8:16],
                                                      op=ALU.add), reads=[t_pa, t_cv], writes=[t_gt])
                P.op("act", lambda e: e.activation(out=gt[:, 8:16], in_=gt[:, 0:8], func=AF.Exp, scale=-1.0),
                     reads=[t_gt], writes=[t_gt])
                P.op("act", lambda e: e.activation(out=gt[:, 16:24], in_=gt[:, 8:16], func=AF.Ln,
                                                   bias=cvec[:, 1:2]), reads=[t_gt, t_cvec], writes=[t_gt])
                P.op("dve", lambda e: e.tensor_scalar(out=logf[:, tt, :], in0=gt[:, 16:24], scalar1=-1.0,
                                                      scalar2=None, op0=ALU.mult), reads=[t_gt], writes=[t_logf])
                return
            kind, i = meta
            if tt % 4 == 0:
                tm5["buf"] = tstb.next() if kind == "v" else tsto.next()
            st, t_s = tm5["buf"]
            if kind == "v":
                P.op("dve", lambda e: e.tensor_copy(st[:, tt % 4, :], pa[:]), reads=[t_pa], writes=[t_s])
                if tt % 4 == 3:
                    dst = vtok1_s[(tt - 3) * 128:(tt + 1) * 128, i * 512:(i + 1) * 512].rearrange(
                        "(t p) c -> p t c", p=128)
                    P.dma("pool", lambda e: e.dma_start(out=dst, in_=st[:]), t_s, False)
            else:
                o1, t_o1 = og1.next()
                zz, t_zz = z1.next()
                sg_, t_sg_ = sgt.next()
                P.op("act", lambda e: e.activation(out=sg_[:], in_=pa[:], func=AF.Sigmoid),
                     reads=[t_pa], writes=[t_sg_])
                P.op("dve", lambda e: e.tensor_tensor(out=zz[:], in0=pa[:, 256:512], in1=sg_[:, 256:512], op=ALU.mult),
                     reads=[t_pa, t_sg_], writes=[t_zz])
                P.op("dve", lambda e: e.tensor_tensor(out=st[:, tt % 4, :], in0=sg_[:, 0:256], in1=zz[:], op=ALU.mult),
                     reads=[t_sg_, t_zz], writes=[t_s])
                if tt % 4 == 3:
                    dst = ogz_s[(tt - 3) * 128:(tt + 1) * 128, i * 256:(i + 1) * 256].rearrange(
                        "(t p) c -> p t c", p=128)
                    P.dma("pool", lambda e: e.dma_start(out=dst, in_=st[:]), t_s, False)

        groups = ([("fm", i * 512, 512, i * 4) for i in range(4)]
                  + [("tm", 2048 + i * 512, 512, ("v", i)) for i in range(4)]
                  + [("tm", 4096 + i * 512, 512, ("ogz", i)) for i in range(8)]
                  + [("tm", 8192, 16, "g")])
        inproj(es, wb_d, groups, fm_sink5, tm_sink5)
        P.end_phase()
    es_h.close()
    if stop_after == 5:
        return finish()

    with ExitStack() as es:
        R = norm_res(es, full=False)
        t_c6 = P.tok("c6", dma=True)
        mask01 = sb(es, "mask01", [128, 128], F32)
        fing_bc = sb(es, "fing_bc", [128, D], F32)
        P.dma("sp", lambda e: e.dma_start(out=mask01[:], in_=cd["mask01"]), t_c6, True)
        P.dma("sp", lambda e: e.dma_start(out=fing_bc[:], in_=bass.AP(fing_d.tensor, 0, [[0, 128], [1, D]])), t_c6, True)
        hgT = sb(es, "hgT", [128, 16], F32)
        P.dma("sp", lambda e: e.dma_start(out=hgT[:], in_=headg_d), t_c6, True)
        wof = sb(es, "wof6", [128, 8, 256], F32)
        wo1 = sb(es, "wo1", [128, 16, D], BF16)
        t_wof, t_wo1 = P.tok("wof6", dma=True), P.tok("wo1")
        for kh in range(2):
            for hf in range(4):
                P.dma("sp", lambda e, kh=kh, hf=hf: e.dma_start(
                    out=wof[:], in_=wob_d[kh * 1024:(kh + 1) * 1024, hf * 256:(hf + 1) * 256].rearrange(
                        "(kc p) n -> p kc n", p=128)), t_wof, True)
                for kc in range(8):
                    P.op("act", lambda e, kh=kh, hf=hf, kc=kc: e.activation(
                        out=wo1[:, kh * 8 + kc, hf * 256:(hf + 1) * 256], in_=wof[:, kc, :], func=AF.Identity,
                        scale=hgT[:, kh * 8 + kc:kh * 8 + kc + 1]), reads=[t_wof, t_c6], writes=[t_wo1])
        Ct = sb(es, "Ct", [128, 8, 257], F32)
        Cbf = sb(es, "Cbf", [128, 8, 257], BF16)
        t_Ct = [P.tok("Ct_%d" % h) for h in range(8)]
        t_Cbf = [P.tok("Cbf_%d" % h) for h in range(8)]
        qkr = Rot([(sb(es, "qk%d" % i, [128, 16, 128], BF16), P.tok("qk%d" % i, dma=True)) for i in range(3)])
        Var = [(sb(es, "Va%d" % i, [128, 8, 257], BF16), P.tok("Va%d" % i, dma=True)) for i in range(3)]
        for va, t_va in Var:
            P.op("pool", lambda e, va=va: e.memset(va[:], 1.0), writes=[t_va])
        Var = Rot(Var)
        ogzr = Rot([(sb(es, "ogzt%d" % i, [128, 2048], F32), P.tok("ogzt%d" % i, dma=True)) for i in range(3)])
        x1r = Rot([(sb(es, "x1r%d" % i, [128, D], F32), P.tok("x1r%d" % i, dma=True)) for i in range(5)])
        outr = Rot([(sb(es, "outt%d" % i, [128, D], F32), P.tok("outt%d" % i, dma=True)) for i in range(3)])
        numr = Rot([(sb(es, "numsb%d" % i, [128, 8, 257], F32), P.tok("numsb")) for i in range(2)])
        junk6 = sb(es, "junk6", [128, 256], F32)
        t_junk6 = P.tok("junk6")
        yT6 = sb(es, "yT6", [128, 16, 128], BF16)
        x2 = sb(es, "x2", [128, D], F32)
        t_yT6, t_x2 = P.tok("yT6"), P.tok("x2")
        smr = Rot([(sb(es, "sm6_%d" % i, [128, 64], F32), P.tok("sm6")) for i in range(4)])
        sm2r = Rot([(sb(es, "sm6b_%d" % i, [128, 80], F32), P.tok("sm6b")) for i in range(2)])
        Sm8 = sb(es, "Sm8", [128, 8, 128], BF16)
        Kt8 = sb(es, "Kt8", [128, 8, 128], BF16)
        t_Sm8 = [P.tok("Sm8_%d" % h) for h in range(8)]
        t_Kt8 = [P.tok("Kt8_%d" % h) for h in range(8)]
        po = [ps(es, "po6_%d" % i, [128, 512], F32) for i in range(2)]
        t_po = [P.tok("po6_0"), P.tok("po6_1")]
        ptrk = ps(es, "ptrk", [128, D], BF16)
        ptry = ps(es, "ptry", [128, D], BF16)
        t_ptrk1 = P.tok("ptrk")
        t_ptrk = [t_ptrk1] * 8
        t_ptry = P.tok("ptry")
        pS2 = [ps(es, "pS2_%d" % i, [128, 4, 128], F32) for i in range(2)]
        t_pSb = [P.tok("pSb0"), P.tok("pSb1")]
        t_pS = [t_pSb[h // 4] for h in range(8)]
        pnd_ = ps(es, "pnd", [128, 512], F32)
        pnd2 = [pnd_, pnd_]
        pdC = ps(es, "pdC", [128, 512], F32)
        t_pnd_ = P.tok("pnd")
        t_pnd2, t_pdC = [t_pnd_, t_pnd_], P.tok("pdC")
        t_pbg = t_pdC
        pbg = pdC[:, 384:400]
        chunk = {}

        loaded = {}
        pending_stores = []

        def stage_L(n):
            qk, t_qk = qkr.next()
            va, t_va = Var.next()
            og, t_og = ogzr.next()
            x1, t_x1 = x1r.next()
            P.dma("sp", lambda e: e.dma_start(
                out=qk[:], in_=qkT1_s[:, :, n * 128:(n + 1) * 128].rearrange("b p t -> p b t")), t_qk, True)
            P.dma("sp", lambda e: e.dma_start(
                out=va[:, :, 0:256], in_=vtok1_s[n * 128:(n + 1) * 128, :].rearrange("p (h v) -> p h v", v=256)),
                t_va, True)
            P.dma("sp", lambda e: e.dma_start(out=x1[:], in_=x1_s[n * 128:(n + 1) * 128, :]), t_x1, True)
            P.dma("sp", lambda e: e.dma_start(out=og[:], in_=ogz_s[n * 128:(n + 1) * 128, :]), t_og, True)
            loaded[n] = (qk, t_qk, va, t_va, og, t_og, x1, t_x1)

        def stage_H(n):
            qk, t_qk, va, t_va, og, t_og, x1, t_x1 = loaded.pop(n)
            P.op("pe", lambda e: e.matmul(pdC[:, 384:392], lhsT=mask01[:], rhs=logf[:, n, :], start=True, stop=True),
                 reads=[t_c6, t_logf], writes=[t_pbg])
            P.op("pe", lambda e: e.matmul(pdC[:, 392:400], lhsT=ones_f[:], rhs=logf[:, n, :], start=True, stop=True),
                 reads=[t_const, t_logf], writes=[t_pbg])
            sm, t_sm = smr.next()
            P.op("act", lambda e: e.activation(out=sm[:, 32:48], in_=pdC[:, 384:400], func=AF.Identity),
                 reads=[t_pbg], writes=[t_sm])
            P.op("dve", lambda e: e.tensor_tensor(out=sm[:, 0:8], in0=ipre[:, n, :], in1=sm[:, 32:40],
                                                  op=ALU.subtract), reads=[t_ipre, t_sm], writes=[t_sm])
            P.op("act", lambda e: e.activation(out=sm[:, 8:16], in_=sm[:, 0:8], func=AF.Exp, bias=cvec[:, 2:3]),
                 reads=[t_sm, t_cvec], writes=[t_sm])
            P.op("act", lambda e: e.activation(out=sm[:, 16:32], in_=sm[:, 32:48], func=AF.Exp),
                 reads=[t_sm], writes=[t_sm])
            nums, t_nums = numr.next()
            last = (n == NT - 1)
            prev = chunk.get(n - 1)
            for h in range(8):
                P.op("pe", lambda e, h=h: e.matmul(pS2[h // 4][:, h % 4, :], lhsT=qk[:, 8 + h, :], rhs=qk[:, h, :],
                                                   start=True, stop=True), reads=[t_qk], writes=[t_pS[h]])
            if not last:
                for h in range(8):
                    P.op("pe", lambda e, h=h: e.transpose(ptrk[:, h * 128:(h + 1) * 128], qk[:, 8 + h, :], ident_b[:]),
                         reads=[t_qk, t_const], writes=[t_ptrk[h]])
            if prev is not None:
                psm, t_psm = prev
                for h in range(8):
                    P.op("act", lambda e, h=h: e.activation(
                        out=Cbf[:, h, :], in_=Ct[:, h, :], func=AF.Identity, scale=psm[:, 24 + h:25 + h]),
                        reads=[t_Ct[h], t_psm], writes=[t_Cbf[h]])
            for h in range(8):
                P.op("dve", lambda e, h=h: e.scalar_tensor_tensor(
                    out=Sm8[:, h, :], in0=pS2[h // 4][:, h % 4, :], scalar=sm[:, 8 + h:9 + h], in1=mask01[:],
                    op0=ALU.mult, op1=ALU.mult), reads=[t_pS[h], t_sm, t_c6], writes=[t_Sm8[h]])
                if not last:
                    P.op("act", lambda e, h=h: e.activation(
                        out=Kt8[:, h, :], in_=ptrk[:, h * 128:(h + 1) * 128], func=AF.Identity,
                        scale=sm[:, 8 + h:9 + h]), reads=[t_ptrk[h], t_sm], writes=[t_Kt8[h]])
            for h in range(8):
                pnd, t_pnd = pnd2[h % 2], t_pnd2[h % 2]
                P.op("pe", lambda e, h=h, pnd=pnd: e.matmul(pnd[:, 0:257], lhsT=Sm8[:, h, :], rhs=va[:, h, :],
                                                            start=True, stop=(n == 0)),
                     reads=[t_Sm8[h], t_va], writes=[t_pnd])
                if n > 0:
                    P.op("pe", lambda e, h=h, pnd=pnd: e.matmul(pnd[:, 0:257], lhsT=qk[:, h, :], rhs=Cbf[:, h, :],
                                                                start=False, stop=True),
                         reads=[t_qk, t_Cbf[h]], writes=[t_pnd])
                P.op("act", lambda e, h=h, pnd=pnd: e.activation(out=nums[:, h, :], in_=pnd[:, 0:257],
                                                                 func=AF.Identity),
                     reads=[t_pnd], writes=[t_nums])
                if not last:
                    P.op("pe", lambda e, h=h: e.matmul(pdC[:, 0:257], lhsT=Kt8[:, h, :], rhs=va[:, h, :],
                                                       start=True, stop=True),
                         reads=[t_Kt8[h], t_va], writes=[t_pdC])
                    if n == 0:
                        P.op("dve", lambda e, h=h: e.tensor_copy(Ct[:, h, :], pdC[:, 0:257]),
                             reads=[t_pdC], writes=[t_Ct[h]])
                    else:
                        psm, t_psm = prev
                        P.op("dve", lambda e, h=h: e.scalar_tensor_tensor(
                            out=Ct[:, h, :], in0=Ct[:, h, :], scalar=psm[:, 24 + h:25 + h], in1=pdC[:, 0:257],
                            op0=ALU.mult, op1=ALU.add), reads=[t_pdC, t_psm, t_Ct[h], t_Cbf[h]], writes=[t_Ct[h]])
            chunk[n] = (sm, t_sm)
            return (n, sm, t_sm, nums, t_nums, og, t_og, x1, t_x1)

        ybfr = Rot([(sb(es, "ybf6r%d" % i, [128, 2048], BF16), P.tok("ybf6r")) for i in range(2)])

        def stage_E1(ctx):
            n, sm, t_sm, nums, t_nums, og, t_og, x1, t_x1 = ctx
            s2_, t_s2_ = sm2r.next()
            for h in range(8):
                P.op("act", lambda e, h=h: e.activation(out=junk6[:], in_=nums[:, h, 0:256], func=AF.Square,
                                                        accum_out=s2_[:, h:h + 1]),
                     reads=[t_nums], writes=[t_junk6, t_s2_])
            P.op("dve", lambda e: e.tensor_tensor(out=s2_[:, 8:16], in0=nums[:, :, 256], in1=sm[:, 16:24], op=ALU.mult),
                 reads=[t_nums, t_sm], writes=[t_s2_])
            P.op("dve", lambda e: e.tensor_scalar(out=s2_[:, 16:24], in0=s2_[:, 8:16], scalar1=-1.0, scalar2=None,
                                                  op0=ALU.mult), reads=[t_s2_], writes=[t_s2_])
            P.op("dve", lambda e: e.tensor_tensor(out=s2_[:, 16:24], in0=s2_[:, 16:24], in1=s2_[:, 8:16], op=ALU.max),
                 reads=[t_s2_], writes=[t_s2_])
            P.op("dve", lambda e: e.tensor_scalar(out=s2_[:, 16:24], in0=s2_[:, 16:24], scalar1=1.0, scalar2=None,
                                                  op0=ALU.max), reads=[t_s2_], writes=[t_s2_])
            P.op("dve", lambda e: e.reciprocal(out=s2_[:, 24:32], in_=s2_[:, 16:24]), reads=[t_s2_], writes=[t_s2_])
            P.op("dve", lambda e: e.tensor_tensor(out=s2_[:, 32:40], in0=s2_[:, 24:32], in1=sm[:, 16:24], op=ALU.mult),
                 reads=[t_s2_, t_sm], writes=[t_s2_])
            P.op("dve", lambda e: e.tensor_tensor(out=s2_[:, 40:48], in0=s2_[:, 32:40], in1=s2_[:, 32:40], op=ALU.mult),
                 reads=[t_s2_], writes=[t_s2_])
            P.op("dve", lambda e: e.tensor_tensor(out=s2_[:, 40:48], in0=s2_[:, 40:48], in1=s2_[:, 0:8], op=ALU.mult),
                 reads=[t_s2_], writes=[t_s2_])
            P.op("act", lambda e: e.activation(out=s2_[:, 48:56], in_=s2_[:, 40:48], func=AF.Ln,
                                               scale=1.0 / 256, bias=cvec[:, 0:1]),
                 reads=[t_s2_, t_cvec], writes=[t_s2_])
            P.op("act", lambda e: e.activation(out=s2_[:, 56:64], in_=s2_[:, 48:56], func=AF.Exp, scale=-0.5),
                 reads=[t_s2_], writes=[t_s2_])
            P.op("dve", lambda e: e.tensor_tensor(out=s2_[:, 64:72], in0=s2_[:, 32:40], in1=s2_[:, 56:64], op=ALU.mult),
                 reads=[t_s2_], writes=[t_s2_])
            yb, t_yb = ybfr.next()
            for h in range(8):
                P.op("dve", lambda e, h=h: e.scalar_tensor_tensor(
                    out=yb[:, h * 256:(h + 1) * 256], in0=nums[:, h, 0:256], scalar=s2_[:, 64 + h:65 + h],
                    in1=og[:, h * 256:(h + 1) * 256], op0=ALU.mult, op1=ALU.mult),
                    reads=[t_nums, t_s2_, t_og], writes=[t_yb])
            return (n, x1, t_x1, yb, t_yb)

        def stage_E2(c2):
            n, x1, t_x1, yb, t_yb = c2
            for r8 in range(2):
                for kc in range(8):
                    P.op("pe", lambda e, r8=r8, kc=kc: e.transpose(
                        ptry[:, kc * 128:(kc + 1) * 128], yb[:, (r8 * 8 + kc) * 128:(r8 * 8 + kc + 1) * 128],
                        ident_b[:]), reads=[t_yb, t_const], writes=[t_ptry])
                P.op("act", lambda e, r8=r8: e.activation(
                    out=yT6[:, r8 * 8:(r8 + 1) * 8, :], in_=ptry[:].rearrange("p (a b) -> p a b", b=128),
                    func=AF.Identity), reads=[t_ptry], writes=[t_yT6])
            for hf in range(2):
                for kc in range(16):
                    P.op("pe", lambda e, hf=hf, kc=kc: e.matmul(
                        po[hf][:], lhsT=yT6[:, kc, :], rhs=wo1[:, kc, hf * 512:(hf + 1) * 512],
                        start=(kc == 0), stop=(kc == 15)), reads=[t_yT6, t_wo1], writes=[t_po[hf]])
                P.op("dve", lambda e, hf=hf: e.tensor_tensor(
                    out=x2[:, hf * 512:(hf + 1) * 512], in0=po[hf][:], in1=gate_bc[1][:, hf * 512:(hf + 1) * 512],
                    op=ALU.mult), reads=[t_po[hf], t_gbc[1]], writes=[t_x2])
            P.op("dve", lambda e: e.tensor_tensor(out=x2[:], in0=x2[:], in1=x1[:], op=ALU.add),
                 reads=[t_x1, t_x2], writes=[t_x2])
            st, t_st = rms_stats(x2[:], t_x2, R)
            ot, t_ot = outr.next()
            P.op("dve", lambda e: e.scalar_tensor_tensor(
                out=ot[:], in0=x2[:], scalar=st[:, 2:3], in1=fing_bc[:], op0=ALU.mult, op1=ALU.mult),
                reads=[t_x2, t_st, t_c6], writes=[t_ot])
            pending_stores.append((n, ot, t_ot))

        def flush_stores():
            while pending_stores:
                n_, ot, t_ot = pending_stores.pop(0)
                P.dma("sp", lambda e, n_=n_, ot=ot: e.dma_start(out=out_d[n_ * 128:(n_ + 1) * 128, :], in_=ot[:]),
                      t_ot, False)

        cH, cE = {}, {}
        stage_L(0)
        for n in range(NT + 2):
            if n + 1 < NT:
                stage_L(n + 1)
            flush_stores()
            lists = []
            if n < NT:
                P.capture()
                cH[n] = stage_H(n)
                lists.append(P.end_capture())
            if 0 <= n - 2 < NT:
                P.capture()
                stage_E2(cE.pop(n - 2))
                lists.append(P.end_capture())
            if 0 <= n - 1 < NT:
                P.capture()
                cE[n - 1] = stage_E1(cH.pop(n - 1))
                lists.append(P.end_capture())
            P.replay_merged(lists)
        flush_stores()
        P.end_phase()
    es_g.close()
    return nc


def prep_inputs(inputs):
    f = np.float32
    g = lambda k: np.asarray(inputs[k], dtype=f)
    sh = {}
    sh["ada_w"] = np.ascontiguousarray(g("ada_w"))
    sh["adabT"] = np.ascontiguousarray(g("ada_b").reshape(2, 24, 128).transpose(2, 0, 1).reshape(128, 48))
    sh["ngT"] = np.ascontiguousarray(g("norm_g").reshape(2, 8, 128).transpose(2, 0, 1).reshape(128, 16))
    sh["fing"] = np.ascontiguousarray(g("final_g").reshape(1, D))
    sh["wa"] = np.ascontiguousarray(g("a_w_in")[0][:, WA_PERM])
    sh["cmp_w1"] = np.ascontiguousarray(g("a_cmp_w1")[0])
    sh["peT"] = np.ascontiguousarray(g("a_cmp_pe")[0].transpose(2, 0, 1).reshape(64, 64))
    sh["b1T"] = np.ascontiguousarray(g("a_cmp_b1")[0].reshape(2, 2, 128).transpose(2, 0, 1).reshape(128, 4))
    sh["cmp_w2"] = np.ascontiguousarray(g("a_cmp_w2")[0])
    sh["b2T"] = np.ascontiguousarray(g("a_cmp_b2")[0].T)
    sh["a_w_out"] = np.ascontiguousarray(g("a_w_out")[0])
    sh["wb"] = np.ascontiguousarray(g("b_w_in")[0][:, WB_PERM])
    sh["convwT"] = np.ascontiguousarray(g("b_conv_w")[0].reshape(4, 16, 128).transpose(2, 1, 0).reshape(128, 64))
    sh["convbT"] = np.ascontiguousarray(g("b_conv_b")[0].reshape(16, 128).T)
    sh["gateb"] = np.ascontiguousarray(g("b_gate_b")[0].reshape(1, 16))
    sh["headgT"] = np.ascontiguousarray(g("b_head_g")[0].reshape(16, 128).T)
    sh["b_w_out"] = np.ascontiguousarray(g("b_w_out")[0])
    for k, v in make_consts().items():
        sh["k_" + k] = v
    per = []
    x = g("x")
    c = g("c")
    for b in range(8):
        m = dict(sh)
        m["x"] = np.ascontiguousarray(x[b])
        m["cT"] = np.ascontiguousarray(c[b].reshape(8, 128).T)
        per.append(m)
    return per


def kernel(**inputs):
    nc = build_program()
    in_maps = prep_inputs(inputs)
    res = run_bass_kernel_spmd(nc, in_maps, core_ids=list(range(8)))
    return np.stack([np.asarray(r["out"], dtype=np.float32) for r in res.results], axis=0)
```

```python
import math
from contextlib import ExitStack

import numpy as np
import ml_dtypes

import concourse.bass as bass
import concourse.mybir as mybir
from concourse.bass_utils import run_bass_kernel_spmd

F32 = mybir.dt.float32
BF16 = mybir.dt.bfloat16
AF = mybir.ActivationFunctionType
ALU = mybir.AluOpType
AX = mybir.AxisListType

S = 4096
D = 1024
NT = S // 128
NQ = S // 512
NEG = -30000.0
EPS = 1e-6
SLOPES = [2.0 ** (-8.0 * (i + 1) / 16) for i in range(16)]
LN_KSCALE = math.log(128 ** -0.5)
ZERO_CUT = 130.0

ENGS = ("pe", "act", "dve", "pool", "sp")
CENG = ("pe", "act", "dve", "pool")
N_DSEM = 90


class Tok:
    __slots__ = ("w", "r", "ds", "name")

    def __init__(self, name="", ds=None):
        self.w = None
        self.r = {}
        self.ds = ds
        self.name = name


class Prog:
    def __init__(self, nc, es):
        self.nc = nc
        self.q = {e: [] for e in ENGS}
        self.cnt = {e: 0 for e in CENG}
        self.seen = {e: {} for e in ENGS}
        self.sems = {}
        for e in CENG:
            self.sems[e] = es.enter_context(nc.semaphore("c_" + e))
        self.sems["bar"] = es.enter_context(nc.semaphore("c_bar"))
        self.bar = 0
        self.dsem_cnt = {}
        self.free = []
        for i in range(N_DSEM):
            k = "d%d" % i
            self.sems[k] = es.enter_context(nc.semaphore(k))
            self.dsem_cnt[k] = 0
            self.free.append(k)
        self.phase_ds = []

    def tok(self, name="", dma=False):
        ds = None
        if dma:
            ds = self.free.pop(0)
            self.phase_ds.append(ds)
        return Tok(name, ds)

    def _need(self, eng, ev):
        if ev is None:
            return
        key, val = ev
        if key == eng and eng == "pe":
            return
        if self.seen[eng].get(key, 0) >= val:
            return
        self.seen[eng][key] = val
        self.q[eng].append(("wait", key, val))

    def capture(self):
        self._cap = []

    def end_capture(self):
        lst, self._cap = self._cap, None
        return lst

    def replay_merged(self, lists):
        lists = [l for l in lists if l]
        pos = [0] * len(lists)
        while True:
            best, bf = None, None
            for i, l in enumerate(lists):
                if pos[i] < len(l):
                    f = pos[i] / len(l)
                    if bf is None or f < bf:
                        best, bf = i, f
            if best is None:
                break
            kind, args = lists[best][pos[best]]
            pos[best] += 1
            if kind == "op":
                self.op(*args)
            else:
                self.dma(*args)

    def op(self, eng, fn, reads=(), writes=()):
        if getattr(self, "_cap", None) is not None:
            self._cap.append(("op", (eng, fn, tuple(reads), tuple(writes))))
            return
        for t in reads:
            self._need(eng, t.w)
        for t in writes:
            self._need(eng, t.w)
            for k, v in t.r.items():
                self._need(eng, (k, v))
        self.cnt[eng] += 1
        n = self.cnt[eng]
        self.q[eng].append(("op", fn))
        for t in reads:
            if t.r.get(eng, 0) < n:
                t.r[eng] = n
        for t in writes:
            t.w = (eng, n)
            t.r = {}

    def dma(self, eng, fn, tok, load, reads=(), writes=()):
        assert tok.ds is not None, tok.name
        if getattr(self, "_cap", None) is not None:
            self._cap.append(("dma", (eng, fn, tok, load, tuple(reads), tuple(writes))))
            return
        self._need(eng, tok.w)
        if load:
            for k, v in tok.r.items():
                self._need(eng, (k, v))
        for t in reads:
            self._need(eng, t.w)
        for t in writes:
            self._need(eng, t.w)
            for k, v in t.r.items():
                self._need(eng, (k, v))
        self.dsem_cnt[tok.ds] += 16
        n = self.dsem_cnt[tok.ds]
        self.q[eng].append(("dma", fn, tok.ds))
        if load:
            tok.w = (tok.ds, n)
            tok.r = {}
        else:
            tok.r[tok.ds] = n
        for t in reads:
            t.r[tok.ds] = max(t.r.get(tok.ds, 0), n)
        for t in writes:
            t.w = (tok.ds, n)
            t.r = {}

    def barrier(self):
        for e in CENG:
            if self.cnt[e]:
                self._need("sp", (e, self.cnt[e]))
        for k, v in self.dsem_cnt.items():
            if v:
                self._need("sp", (k, v))
        self.bar += 1
        self.q["sp"].append(("inc", "bar", 1))
        for e in CENG:
            self.q[e].append(("wait", "bar", self.bar))
            for e2 in CENG:
                self.seen[e][e2] = max(self.seen[e].get(e2, 0), self.cnt[e2])
            for k, v in self.dsem_cnt.items():
                self.seen[e][k] = max(self.seen[e].get(k, 0), v)

    def end_phase(self):
        self.barrier()
        self.flush()
        self.free.extend(self.phase_ds)
        self.phase_ds = []

    def flush(self):
        nc = self.nc
        q = self.q
        sems = self.sems

        def emit(name, h):
            for it in q[name]:
                if it[0] == "wait":
                    h.wait_ge(sems[it[1]], it[2])
                elif it[0] == "op":
                    it[1](h).then_inc(sems[name], 1)
                elif it[0] == "dma":
                    it[1](h).then_inc(sems[it[2]], 16)
                elif it[0] == "inc":
                    h.sem_inc(sems[it[1]], it[2])

        with nc.Block() as block:
            @block.sync
            def _(h):
                emit("sp", h)

            @block.tensor
            def _(h):
                emit("pe", h)

            @block.scalar
            def _(h):
                emit("act", h)

            @block.vector
            def _(h):
                emit("dve", h)

            @block.gpsimd
            def _(h):
                emit("pool", h)
        self.q = {e: [] for e in ENGS}


class Rot:
    def __init__(self, items):
        self.items = items
        self.i = 0

    def next(self):
        it = self.items[self.i % len(self.items)]
        self.i += 1
        return it


def _bf(a):
    return np.ascontiguousarray(np.asarray(a, dtype=np.float32)).astype(ml_dtypes.bfloat16)


def make_consts():
    c = {}
    c["ident_f"] = np.eye(128, dtype=np.float32)
    c["ident_b"] = _bf(np.eye(128))
    c["ones_f"] = np.ones((128, 128), np.float32)
    key = np.arange(S)
    kaug_slc = np.zeros((64, S), np.float32)
    for j in range(63):
        kaug_slc[j] = (key // 64 == j)
    kaug_slc[63] = 1.0
    kaug_one = np.zeros((64, S), np.float32)
    kaug_one[63] = 1.0
    c["kaug_slc"] = _bf(kaug_slc)
    c["kaug_one"] = _bf(kaug_one)
    qq = np.arange(512, dtype=np.float64)
    c["vrow"] = _bf(np.stack([-8.0 * s * qq for s in SLOPES]))
    p = np.arange(128, dtype=np.float64)
    bt = np.zeros((128, 16, 32), np.float64)
    for h in range(16):
        for di in range(32):
            bt[:, h, di] = SLOPES[h] * (p + 128.0 * (di - 28))
    c["bias_tab"] = bt.reshape(128, 512).astype(np.float32)
    bc = np.zeros((128, 16, 8, 2), np.float64)
    for h in range(16):
        for t in range(8):
            for ct in range(2):
                bc[:, h, t, ct] = SLOPES[h] * (16.0 * (ct * 128 + p) + 31.0 - 512.0 * t)
    c["biasc_tab"] = bc.reshape(128, 256).astype(np.float32)
    cc = np.arange(256)
    tq = np.arange(S)
    ok = (16 * cc[:, None] + 31 <= tq[None, :]) & (cc[:, None] < 255)
    cm = np.where(ok, 0.0, NEG).astype(np.float32)
    c["cmask"] = _bf(cm.reshape(2, 128, S).transpose(1, 0, 2))
    kk = np.arange(128)[:, None]
    q2 = np.arange(128)[None, :]
    c["tri_causal"] = _bf(np.where(kk <= q2, 0.0, NEG))
    c["tri_band"] = _bf(np.where(kk > q2, 0.0, NEG))
    c["mask01"] = (kk <= q2).astype(np.float32)
    c0 = np.arange(256)[:, None] * 16
    s0 = np.arange(64)[None, :] * 64
    ov = np.clip(np.minimum(c0 + 32, s0 + 64) - np.maximum(c0, s0), 0, None) / 32.0
    ov[255] = 0.0
    c["ov"] = _bf(ov.reshape(2, 128, 64).transpose(1, 0, 2))
    blk = np.arange(64)[None, :]
    cur = (tq // 64)[:, None]
    forced = (blk == 0) | (blk == cur) | (blk == cur - 1)
    valid = blk <= cur
    vnf = (valid & ~forced).astype(np.float32)
    epsj = (64 - np.arange(64)).astype(np.float32)[None, :] * 1e-30
    fbig = (1e9 + 1e5 * np.arange(64)).astype(np.float32)[None, :]
    addc = np.where(forced, fbig, np.where(valid, epsj, -1.0)).astype(np.float32)

    def tm(a):
        return np.ascontiguousarray(a.reshape(32, 128, 64).transpose(1, 0, 2))

    c["sel_vnf"] = tm(vnf)
    c["sel_addc"] = tm(addc)
    c["sel_valid"] = tm(valid.astype(np.float32))
    return c


WA_PERM = np.concatenate([
    np.arange(0, 1024),
    np.arange(1024, 1536),
    np.arange(1536, 1792),
    np.arange(2048, 2304),
    np.arange(1792, 2048), np.arange(2304, 2560),
    np.arange(2608, 3632),
    np.arange(2560, 2608),
])
WB_PERM = np.concatenate(
    [np.arange(0, 2048), np.arange(2048, 4096)]
    + [np.concatenate([np.arange(4112 + i * 256, 4112 + (i + 1) * 256),
                       np.arange(6160 + i * 256, 6160 + (i + 1) * 256)]) for i in range(8)]
    + [np.arange(4096, 4112)])


def build_program(stop_after=99, dbg=False):
    nc = bass.Bass("TRN2", target_bir_lowering=False)
    consts = make_consts()

    def din(name, shape, dt=F32):
        return nc.dram_tensor(name, list(shape), dt, kind="ExternalInput").ap()

    def dscr(name, shape, dt):
        return nc.dram_tensor(name, list(shape), dt, kind=("ExternalOutput" if dbg else "Internal")).ap()

    x_d = din("x", [S, D])
    cT_d = din("cT", [128, 8])
    adaw_d = din("ada_w", [2, D, 3 * D])
    adabT_d = din("adabT", [128, 48])
    ngT_d = din("ngT", [128, 16])
    fing_d = din("fing", [1, D])
    wa_d = din("wa", [D, 3632])
    w1_d = din("cmp_w1", [2, 2048, 256])
    peT_d = din("peT", [64, 64])
    b1T_d = din("b1T", [128, 4])
    w2_d = din("cmp_w2", [2, 256, 64])
    b2T_d = din("b2T", [64, 2])
    woa_d = din("a_w_out", [D, D])
    wb_d = din("wb", [D, 8208])
    convwT_d = din("convwT", [128, 64])
    convbT_d = din("convbT", [128, 16])
    gateb_d = din("gateb", [1, 16])
    headg_d = din("headgT", [128, 16])
    wob_d = din("b_w_out", [2048, D])
    cd = {}
    for k, v in consts.items():
        cd[k] = din("k_" + k, v.shape, BF16 if v.dtype == ml_dtypes.bfloat16 else F32)
    out_d = nc.dram_tensor("out", [S, D], F32, kind="ExternalOutput").ap()

    qT_s = dscr("qT_s", [1024, S], BF16)
    kvT_s = dscr("kvT_s", [4, 256, S], BF16)
    vtok_s = dscr("vtok_s", [S, 512], BF16)
    zs_s = dscr("zs_s", [S, 1024], F32)
    y0_s = dscr("y0_s", [S, 1024], BF16)
    x1_s = dscr("x1_s", [S, D], F32)
    qkT1_s = dscr("qkT1_s", [16, 128, S], BF16)
    vtok1_s = dscr("vtok1_s", [S, 2048], BF16)
    ogz_s = dscr("ogz_s", [S, 2048], F32)

    es_all = ExitStack()
    P = Prog(nc, es_all)

    uid = {"n": 0}

    def sb(es, name, shape, dt):
        uid["n"] += 1
        return es.enter_context(nc.sbuf_tensor("s%d_%s" % (uid["n"], name), list(shape), dt))

    def ps(es, name, shape, dt=F32):
        uid["n"] += 1
        return es.enter_context(nc.psum_tensor("p%d_%s" % (uid["n"], name), list(shape), dt))

    def finish(dbg=None):
        P.end_phase()
        return nc

    ident_f = sb(es_all, "ident_f", [128, 128], F32)
    ident_b = sb(es_all, "ident_b", [128, 128], BF16)
    ones_f = sb(es_all, "ones_f", [128, 128], F32)
    cvec = sb(es_all, "cvec", [128, 4], F32)
    modT = sb(es_all, "modT", [128, 48], F32)
    gsc = sb(es_all, "gsc", [128, 16], F32)
    gate_bc = [sb(es_all, "gate_bc%d" % l, [128, D], F32) for l in range(2)]
    gate_sb = sb(es_all, "gate_sb", [128, NT, 48], F32)
    es_w1 = ExitStack()
    w1f = sb(es_w1, "w1f", [64, 32, 256], F32)
    t_w1f = Tok("w1f", P.free.pop(0))
    es_h = ExitStack()
    hT = sb(es_h, "hT", [128, 8, S], BF16)
    t_const = Tok("const", P.free.pop(0))
    t_cvec = P.tok("cvec")
    t_modT = P.tok("modT")
    t_gsc = P.tok("gsc")
    t_gbc = [P.tok("gbc0"), P.tok("gbc1")]
    t_gate = P.tok("gate_sb")
    t_hT = [P.tok("hT%d" % i) for i in range(NQ)]

    P.dma("sp", lambda e: e.dma_start(out=ident_f[:], in_=cd["ident_f"]), t_const, True)
    P.dma("sp", lambda e: e.dma_start(out=ident_b[:], in_=cd["ident_b"]), t_const, True)
    P.dma("sp", lambda e: e.dma_start(out=ones_f[:], in_=cd["ones_f"]), t_const, True)
    P.op("pool", lambda e: e.memset(cvec[:, 0:1], EPS), writes=[t_cvec])
    P.op("pool", lambda e: e.memset(cvec[:, 1:2], 1.0), writes=[t_cvec])
    P.op("pool", lambda e: e.memset(cvec[:, 2:3], LN_KSCALE), writes=[t_cvec])
    P.op("pool", lambda e: e.memset(cvec[:, 3:4], 0.0), writes=[t_cvec])

    with ExitStack() as es:
        cT = sb(es, "cT", [128, 8], F32)
        adab = sb(es, "adab", [128, 48], F32)
        ngT = sb(es, "ngT", [128, 16], F32)
        wst = [sb(es, "adaw%d" % i, [128, 8, 768], F32) for i in range(3)]
        dg = [sb(es, "dg%d" % i, [128, 128], F32) for i in range(2)]
        pmod = ps(es, "pmod", [128, 48], F32)
        pbc = [ps(es, "pbc%d" % i, [128, 512], F32) for i in range(2)]
        t_small = P.tok("small", dma=True)
        t_w = [P.tok("adaw%d" % i, dma=True) for i in range(3)]
        t_pmod = P.tok("pmod")
        t_dg = [P.tok("dg0"), P.tok("dg1")]
        t_pbc = [P.tok("pbc0"), P.tok("pbc1")]
        P.dma("sp", lambda e: e.dma_start(out=cT[:], in_=cT_d), t_small, True)
        P.dma("sp", lambda e: e.dma_start(out=adab[:], in_=adabT_d), t_small, True)
        P.dma("sp", lambda e: e.dma_start(out=ngT[:], in_=ngT_d), t_small, True)
        k = 0
        for l in range(2):
            for pc in range(4):
                w = wst[k % 3]
                tw = t_w[k % 3]
                k += 1
                src = adaw_d[l, :, pc * 768:(pc + 1) * 768].rearrange("(kc p) w -> p kc w", p=128)
                P.dma("sp", lambda e, w=w, src=src: e.dma_start(out=w[:], in_=src), tw, True)
                for jj in range(6):
                    col = l * 24 + pc * 6 + jj
                    for kc in range(8):
                        P.op("pe", lambda e, w=w, jj=jj, kc=kc, col=col: e.matmul(
                            pmod[:, col:col + 1], lhsT=w[:, kc, jj * 128:(jj + 1) * 128],
                            rhs=cT[:, kc:kc + 1], start=(kc == 0), stop=(kc == 7)),
                            reads=[tw, t_small], writes=[t_pmod])
        P.op("dve", lambda e: e.tensor_tensor(out=modT[:], in0=pmod[:], in1=adab[:], op=ALU.add),
             reads=[t_pmod, t_small], writes=[t_modT])
        for l in range(2):
            P.op("dve", lambda e, l=l: e.scalar_tensor_tensor(
                out=gsc[:, l * 8:(l + 1) * 8], in0=modT[:, l * 24 + 8:l * 24 + 16], scalar=1.0,
                in1=ngT[:, l * 8:(l + 1) * 8], op0=ALU.add, op1=ALU.mult),
                reads=[t_modT, t_small], writes=[t_gsc])
            for kc in range(8):
                d_, td = dg[kc % 2], t_dg[kc % 2]
                P.op("dve", lambda e, l=l, kc=kc, d_=d_: e.tensor_scalar(
                    out=d_[:], in0=ident_f[:], scalar1=modT[:, l * 24 + 16 + kc:l * 24 + 17 + kc],
                    scalar2=None, op0=ALU.mult), reads=[t_modT, t_const], writes=[td])
                P.op("pe", lambda e, kc=kc, d_=d_: e.matmul(
                    pbc[kc // 4][:, (kc % 4) * 128:(kc % 4 + 1) * 128], lhsT=ones_f[:], rhs=d_[:],
                    start=True, stop=True), reads=[td, t_const], writes=[t_pbc[kc // 4]])
            for hf in range(2):
                P.op("dve", lambda e, l=l, hf=hf: e.tensor_copy(
                    gate_bc[l][:, hf * 512:(hf + 1) * 512], pbc[hf][:]),
                    reads=[t_pbc[hf]], writes=[t_gbc[l]])
        P.end_phase()
    if stop_after == 0:
        return finish()

    nres = {"i": 0}

    def norm_res(es, full=True):
        R = {}
        k = nres["i"]
        nres["i"] += 1
        R["junk"] = Rot([(sb(es, "njunk%d_%d" % (k, i), [128, D], F32), P.tok("njunk")) for i in range(2 if full else 1)])
        R["stat"] = Rot([(sb(es, "nstat%d_%d" % (k, i), [128, 4], F32), P.tok("nstat")) for i in range(3)])
        if full:
            R["xn"] = Rot([(sb(es, "nxn%d_%d" % (k, i), [128, D], F32), P.tok("nxn")) for i in range(3)])
            R["ptr"] = Rot([(ps(es, "nptr%d_%d" % (k, i), [128, 512], F32), P.tok("nptr")) for i in range(2)])
        return R

    def rms_stats(xt_ap, t_xt, R):
        junk, t_junk = R["junk"].next()
        st, t_st = R["stat"].next()
        P.op("act", lambda e: e.activation(out=junk[:], in_=xt_ap, func=AF.Square, accum_out=st[:, 0:1]),
             reads=[t_xt], writes=[t_junk, t_st])
        P.op("act", lambda e: e.activation(out=st[:, 1:2], in_=st[:, 0:1], func=AF.Ln,
                                           scale=1.0 / D, bias=cvec[:, 0:1]),
             reads=[t_st, t_cvec], writes=[t_st])
        P.op("act", lambda e: e.activation(out=st[:, 2:3], in_=st[:, 1:2], func=AF.Exp, scale=-0.5),
             reads=[t_st], writes=[t_st])
        return st, t_st

    def norm_a(xt_ap, t_xt, R):
        st, t_st = rms_stats(xt_ap, t_xt, R)
        xn, t_xn = R["xn"].next()
        P.op("dve", lambda e: e.tensor_scalar(out=xn[:], in0=xt_ap, scalar1=st[:, 2:3], scalar2=None,
                                              op0=ALU.mult), reads=[t_xt, t_st], writes=[t_xn])
        return xn, t_xn

    def norm_b(l, tt, xn, t_xn, R):
        for hf in range(2):
            pt, t_pt = R["ptr"].next()
            for c4 in range(4):
                kc = hf * 4 + c4
                P.op("pe", lambda e, pt=pt, c4=c4, kc=kc: e.transpose(
                    pt[:, c4 * 128:(c4 + 1) * 128], xn[:, kc * 128:(kc + 1) * 128], ident_f[:]),
                    reads=[t_xn, t_const], writes=[t_pt])
            for c4 in range(4):
                kc = hf * 4 + c4
                P.op("act", lambda e, pt=pt, c4=c4, kc=kc: e.activation(
                    out=hT[:, kc, tt * 128:(tt + 1) * 128], in_=pt[:, c4 * 128:(c4 + 1) * 128],
                    func=AF.Identity, scale=gsc[:, l * 8 + kc:l * 8 + kc + 1],
                    bias=modT[:, l * 24 + kc:l * 24 + kc + 1]),
                    reads=[t_pt, t_gsc, t_modT], writes=[t_hT[tt // 4]])

    def norm_to_hT(l, tt, xt_ap, t_xt, R):
        xn, t_xn = norm_a(xt_ap, t_xt, R)
        norm_b(l, tt, xn, t_xn, R)

    def inproj(es, w_d, groups, fm_sink, tm_sink):
        wst = [sb(es, "wst%d" % i, [128, 8, 512], F32) for i in range(2)]
        wbf = [sb(es, "wbf%d" % i, [128, 8, 512], BF16) for i in range(2)]
        t_wst = [P.tok("wst%d" % i, dma=True) for i in range(2)]
        t_wbf = [P.tok("wbf%d" % i) for i in range(2)]
        pacc = Rot([(ps(es, "pacc%d" % i, [128, 512], F32), P.tok("pacc")) for i in range(4)])
        for gi, (kind, c0, width, meta) in enumerate(groups):
            w32, w16 = wst[gi % 2], wbf[gi % 2]
            tw32, tw16 = t_wst[gi % 2], t_wbf[gi % 2]
            src = w_d[:, c0:c0 + width].rearrange("(kc p) w -> p kc w", p=128)
            P.dma("sp", lambda e, w32=w32, src=src, width=width: e.dma_start(out=w32[:, :, 0:width], in_=src),
                  tw32, True)
            for kc in range(8):
                P.op("act", lambda e, w32=w32, w16=w16, kc=kc, width=width: e.activation(
                    out=w16[:, kc, 0:width], in_=w32[:, kc, 0:width], func=AF.Identity), reads=[tw32], writes=[tw16])
            if kind == "fm":
                for bi in range(width // 128):
                    for qt in range(NQ):
                        pa, t_pa = pacc.next()
                        for kc in range(8):
                            P.op("pe", lambda e, pa=pa, w16=w16, bi=bi, kc=kc, qt=qt: e.matmul(
                                pa[:], lhsT=w16[:, kc, bi * 128:(bi + 1) * 128],
                                rhs=hT[:, kc, qt * 512:(qt + 1) * 512], start=(kc == 0), stop=(kc == 7)),
                                reads=[tw16, t_hT[qt]], writes=[t_pa])
                        fm_sink(meta + bi, qt, pa, t_pa)
            else:
                for tt in range(NT):
                    pa, t_pa = pacc.next()
                    for kc in range(8):
                        P.op("pe", lambda e, pa=pa, w16=w16, kc=kc, tt=tt, width=width: e.matmul(
                            pa[:, 0:width], lhsT=hT[:, kc, tt * 128:(tt + 1) * 128],
                            rhs=w16[:, kc, 0:width], start=(kc == 0), stop=(kc == 7)),
                            reads=[tw16, t_hT[tt // 4]], writes=[t_pa])
                    tm_sink(meta, tt, pa, t_pa)

    with ExitStack() as es:
        R = norm_res(es)
        xin = Rot([(sb(es, "xin%d" % i, [128, D], F32), P.tok("xin%d" % i, dma=True)) for i in range(3)])
        xns = {}
        for tt in range(NT + 1):
            lists = []
            if tt < NT:
                xt, t_xt = xin.next()
                P.dma("sp", lambda e, xt=xt, tt=tt: e.dma_start(out=xt[:], in_=x_d[tt * 128:(tt + 1) * 128, :]),
                      t_xt, True)
                P.capture()
                xns[tt] = norm_a(xt[:], t_xt, R)
                lists.append(P.end_capture())
            if tt >= 1:
                P.capture()
                norm_b(0, tt - 1, *xns.pop(tt - 1), R)
                lists.append(P.end_capture())
            P.replay_merged(lists)
        P.end_phase()
    if stop_after == 1:
        return finish()

    with ExitStack() as es:
        frow = [sb(es, "frow%d" % i, [128, S], BF16) for i in range(2)]
        t_frow = [P.tok("frow%d" % i, dma=True) for i in range(2)]
        tstg_b = Rot([(sb(es, "tstb%d" % i, [128, 4, 512], BF16), P.tok("tstb%d" % i, dma=True)) for i in range(2)])
        tstg_f = Rot([(sb(es, "tstf%d" % i, [128, 4, 512], F32), P.tok("tstf%d" % i, dma=True)) for i in range(2)])
        FM_DEST = ([qT_s[i * 128:(i + 1) * 128, :] for i in range(8)]
                   + [kvT_s[j, i * 128:(i + 1) * 128, :] for j in range(4) for i in range(2)])

        def fm_sink(blk, qt, pa, t_pa):
            fr, tf = frow[blk % 2], t_frow[blk % 2]
            if qt % 2 == 0:
                P.op("act", lambda e: e.activation(out=fr[:, qt * 512:(qt + 1) * 512], in_=pa[:],
                                                   func=AF.Identity), reads=[t_pa], writes=[tf])
            else:
                P.op("dve", lambda e: e.tensor_copy(fr[:, qt * 512:(qt + 1) * 512], pa[:]),
                     reads=[t_pa], writes=[tf])
            if qt == NQ - 1:
                dst = FM_DEST[blk]
                P.dma("pool", lambda e: e.dma_start(out=dst, in_=fr[:]), tf, False)

        tm_cur = {}

        def tm_sink(meta, tt, pa, t_pa):
            if meta == "gates":
                P.op("act", lambda e: e.activation(out=gate_sb[:, tt, :], in_=pa[:, 0:48], func=AF.Sigmoid),
                     reads=[t_pa], writes=[t_gate])
                return
            if tt % 4 == 0:
                tm_cur["buf"] = tstg_b.next() if meta == "v" else tstg_f.next()
            st, t_s = tm_cur["buf"]
            if meta == "v":
                P.op("dve", lambda e: e.tensor_copy(st[:, tt % 4, :], pa[:]), reads=[t_pa], writes=[t_s])
                if tt % 4 == 3:
                    dst = vtok_s[(tt - 3) * 128:(tt + 1) * 128, :].rearrange("(t p) c -> p t c", p=128)
                    P.dma("pool", lambda e: e.dma_start(out=dst, in_=st[:]), t_s, False)
            else:
                zi = meta
                P.op("act", lambda e: e.activation(out=st[:, tt % 4, :], in_=pa[:], func=AF.Silu),
                     reads=[t_pa], writes=[t_s])
                if tt % 4 == 3:
                    dst = zs_s[(tt - 3) * 128:(tt + 1) * 128, zi * 512:(zi + 1) * 512].rearrange(
                        "(t p) c -> p t c", p=128)
                    P.dma("pool", lambda e: e.dma_start(out=dst, in_=st[:]), t_s, False)

        groups = [("fm", 0, 512, 0), ("fm", 512, 512, 4), ("fm", 1024, 512, 8), ("fm", 1536, 512, 12),
                  ("tm", 2048, 512, "v"), ("tm", 2560, 512, 0), ("tm", 3072, 512, 1), ("tm", 3584, 48, "gates")]
        inproj(es, wa_d, groups, fm_sink, tm_sink)
        P.dma("sp", lambda e: e.dma_start(out=w1f[:], in_=w1_d[0].rearrange("(l d) h -> d l h", d=64)), t_w1f, True)
        P.end_phase()
    es_h.close()
    if stop_after == 2:
        return finish()

    with ExitStack() as es:
        t_c3 = P.tok("c3", dma=True)
        cmask = sb(es, "cmask", [128, 2, S], BF16)
        tri_c = sb(es, "tri_c", [128, 128], BF16)
        tri_b = sb(es, "tri_b", [128, 128], BF16)
        bias_tab = sb(es, "bias_tab", [128, 512], F32)
        biasc_tab = sb(es, "biasc_tab", [128, 256], F32)
        ovt = sb(es, "ovt", [128, 2, 64], BF16)
        sel_vnf = sb(es, "sel_vnf", [128, 32, 64], F32)
        sel_addc = sb(es, "sel_addc", [128, 32, 64], F32)
        sel_valid = sb(es, "sel_valid", [128, 32, 64], F32)
        KC = sb(es, "KC", [128, 4, 256], BF16)
        VC = sb(es, "VC", [128, 4, 2, 65], BF16)
        t_KC = P.tok("KC", dma=True)
        t_VC = P.tok("VC")
        for g in range(4):
            P.dma("sp", lambda e, g=g: e.dma_start(out=KC[64:128, g, :], in_=cd["kaug_one"][:, 0:256]), t_KC, True)
        P.op("pool", lambda e: e.memset(VC[:], 1.0), writes=[t_VC])

        with ExitStack() as es2:
            w1b = [sb(es2, "w1b%d" % i, [64, 32, 256], BF16) for i in range(2)]
            kvs_ = sb(es2, "kvs", [64, 4, S], BF16)
            kvs = [kvs_, kvs_]
            kvd = sb(es2, "kvd", [64, 4, 16, 256], BF16)
            t_kvd = P.tok("kvd")
            pef = sb(es2, "pef", [64, 64], F32)
            peb = sb(es2, "peb", [64, 64], BF16)
            b1T = sb(es2, "b1T", [128, 4], F32)
            bias1 = sb(es2, "bias1", [128, 4], F32)
            w2f = sb(es2, "w2f", [128, 2, 2, 64], F32)
            w2b = sb(es2, "w2b", [128, 2, 2, 64], BF16)
            b2T = sb(es2, "b2T", [64, 2], F32)
            xh = sb(es2, "xh", [128, 256], F32)
            tt1 = sb(es2, "tt1", [128, 256], F32)
            sg = sb(es2, "sg", [128, 256], F32)
            gel = [sb(es2, "gel%d" % i, [128, 2, 256], BF16) for i in range(2)]
            vct = sb(es2, "vct", [64, 256], BF16)
            ph = [ps(es2, "ph%d" % i, [128, 256], F32) for i in range(2)]
            pb1 = ps(es2, "pb1", [128, 4], F32)
            pk = ps(es2, "pk", [64, 256], F32)
            pvt = ps(es2, "pvt", [128, 128], BF16)
            t_w1b = [P.tok("w1b0"), P.tok("w1b1")]
            t_kvs_ = P.tok("kvs", dma=True)
            t_kvs = [t_kvs_, t_kvs_]
            t_sm = P.tok("cmpsmall", dma=True)
            t_peb = P.tok("peb")
            t_bias1 = P.tok("bias1")
            t_w2b = P.tok("w2b")
            t_xh, t_tt1, t_sg = P.tok("xh"), P.tok("tt1"), P.tok("sg")
            t_gel = [P.tok("gel0"), P.tok("gel1")]
            t_vct = P.tok("vct")
            t_ph = [P.tok("ph0"), P.tok("ph1")]
            t_pb1, t_pk, t_pvt = P.tok("pb1"), P.tok("pk"), P.tok("pvt")
            P.dma("sp", lambda e: e.dma_start(out=pef[:], in_=peT_d), t_sm, True)
            P.dma("sp", lambda e: e.dma_start(out=b1T[:], in_=b1T_d), t_sm, True)
            P.dma("sp", lambda e: e.dma_start(out=b2T[:], in_=b2T_d), t_sm, True)
            for xx in range(2):
                P.dma("sp", lambda e, xx=xx: e.dma_start(
                    out=w2f[:, xx, :, :], in_=w2_d[xx].rearrange("(hh p) d -> p hh d", p=128)), t_sm, True)
            P.op("dve", lambda e: e.tensor_copy(peb[:], pef[:]), reads=[t_sm], writes=[t_peb])
            P.op("dve", lambda e: e.tensor_copy(w2b[:], w2f[:]), reads=[t_sm], writes=[t_w2b])
            P.op("pool", lambda e: e.memset(gel[0][:], 0.0), writes=[t_gel[0]])
            P.op("pool", lambda e: e.memset(gel[1][:], 0.0), writes=[t_gel[1]])

            def load_x(xx):
                if xx == 1:
                    P.dma("sp", lambda e: e.dma_start(
                        out=w1f[:], in_=w1_d[xx].rearrange("(l d) h -> d l h", d=64)), t_w1f, True)
                for l4 in range(4):
                    if l4 % 2 == 0:
                        P.op("act", lambda e, l4=l4: e.activation(
                            out=w1b[xx][:, l4 * 8:(l4 + 1) * 8, :], in_=w1f[:, l4 * 8:(l4 + 1) * 8, :],
                            func=AF.Identity), reads=[t_w1f], writes=[t_w1b[xx]])
                    else:
                        P.op("dve", lambda e, l4=l4: e.tensor_copy(
                            w1b[xx][:, l4 * 8:(l4 + 1) * 8, :], w1f[:, l4 * 8:(l4 + 1) * 8, :]),
                            reads=[t_w1f], writes=[t_w1b[xx]])
                P.dma("sp", lambda e: e.dma_start(
                    out=kvs[xx][:], in_=kvT_s[xx].rearrange("(g d) s -> d g s", d=64)), t_kvs[xx], True)
                for hh in range(2):
                    col = xx * 2 + hh
                    for l in range(32):
                        P.op("pe", lambda e, hh=hh, l=l, col=col: e.matmul(
                            pb1[:, col:col + 1], lhsT=w1b[xx][:, l, hh * 128:(hh + 1) * 128],
                            rhs=peb[:, xx * 32 + l:xx * 32 + l + 1], start=(l == 0), stop=(l == 31)),
                            reads=[t_w1b[xx], t_peb], writes=[t_pb1])
                P.op("dve", lambda e: e.tensor_tensor(out=bias1[:, 2 * xx:2 * xx + 2], in0=pb1[:, 2 * xx:2 * xx + 2],
                                                      in1=b1T[:, 2 * xx:2 * xx + 2], op=ALU.add),
                     reads=[t_pb1, t_sm], writes=[t_bias1])

            it = 0
            load_x(0)
            for dst, nm in ((cmask, "cmask"), (tri_c, "tri_causal"), (tri_b, "tri_band"), (bias_tab, "bias_tab"),
                            (biasc_tab, "biasc_tab"), (ovt, "ov"), (sel_vnf, "sel_vnf"), (sel_addc, "sel_addc"),
                            (sel_valid, "sel_valid")):
                P.dma("sp", lambda e, dst=dst, nm=nm: e.dma_start(out=dst[:], in_=cd[nm]), t_c3, True)

            for xx in range(2):
                if xx == 1:
                    load_x(1)
                for g in range(4):
                    P.op("dve", lambda e, g=g, xx=xx: e.tensor_copy(
                        kvd[:, g, :, :], kvs[xx][:, g, :].rearrange("d (c s) -> d s c", s=16)),
                        reads=[t_kvs[xx]], writes=[t_kvd])
                for g in range(4):
                    ge, t_ge = gel[(xx * 4 + g) % 2], t_gel[(xx * 4 + g) % 2]
                    for hh in range(2):
                        p_, t_p = ph[it % 2], t_ph[it % 2]
                        it += 1
                        for l in range(32):
                            P.op("pe", lambda e, p_=p_, xx=xx, g=g, hh=hh, l=l: e.matmul(
                                p_[:, 0:255], lhsT=w1b[xx][:, l, hh * 128:(hh + 1) * 128],
                                rhs=kvd[:, g, l % 16, l // 16:l // 16 + 255], start=(l == 0), stop=(l == 31)),
                                reads=[t_w1b[xx], t_kvd], writes=[t_p])
                        col = xx * 2 + hh
                        P.op("act", lambda e, p_=p_, col=col: e.activation(
                            out=xh[:, 0:255], in_=p_[:, 0:255], func=AF.Identity, bias=bias1[:, col:col + 1]),
                            reads=[t_p, t_bias1], writes=[t_xh])
                        P.op("dve", lambda e: e.tensor_tensor(out=tt1[:, 0:255], in0=xh[:, 0:255],
                                                              in1=xh[:, 0:255], op=ALU.mult),
                             reads=[t_xh], writes=[t_tt1])
                        P.op("dve", lambda e: e.tensor_scalar(out=tt1[:, 0:255], in0=tt1[:, 0:255],
                                                              scalar1=0.044715, scalar2=1.0,
                                                              op0=ALU.mult, op1=ALU.add),
                             reads=[t_tt1], writes=[t_tt1])
                        P.op("dve", lambda e: e.tensor_tensor(out=tt1[:, 0:255], in0=tt1[:, 0:255],
                                                              in1=xh[:, 0:255], op=ALU.mult),
                             reads=[t_tt1, t_xh], writes=[t_tt1])
                        P.op("act", lambda e: e.activation(out=sg[:, 0:255], in_=tt1[:, 0:255], func=AF.Sigmoid,
                                                           scale=1.5957691216057308),
                             reads=[t_tt1], writes=[t_sg])
                        P.op("dve", lambda e, ge=ge, hh=hh: e.tensor_tensor(
                            out=ge[:, hh, 0:255], in0=xh[:, 0:255], in1=sg[:, 0:255], op=ALU.mult),
                            reads=[t_xh, t_sg], writes=[t_ge])
                    for hh in range(2):
                        P.op("pe", lambda e, ge=ge, xx=xx, hh=hh: e.matmul(
                            pk[:], lhsT=w2b[:, xx, hh, :], rhs=ge[:, hh, :], start=(hh == 0), stop=(hh == 1)),
                            reads=[t_ge, t_w2b], writes=[t_pk])
                    if xx == 0:
                        P.op("act", lambda e, g=g: e.activation(
                            out=KC[0:64, g, :], in_=pk[:], func=AF.Identity, bias=b2T[:, 0:1]),
                            reads=[t_pk, t_sm], writes=[t_KC])
                    else:
                        P.op("act", lambda e: e.activation(
                            out=vct[:], in_=pk[:], func=AF.Identity, bias=b2T[:, 1:2]),
                            reads=[t_pk, t_sm], writes=[t_vct])
                        for ct in range(2):
                            P.op("pe", lambda e, ct=ct: e.transpose(
                                pvt[:, 0:64], vct[:, ct * 128:(ct + 1) * 128], ident_b[0:64, 0:64]),
                                reads=[t_vct, t_const], writes=[t_pvt])
                            P.op("dve", lambda e, g=g, ct=ct: e.tensor_copy(VC[:, g, ct, 0:64], pvt[:, 0:64]),
                                 reads=[t_pvt], writes=[t_VC])
            P.barrier()
            P.flush()
        if stop_after == 2.5:
            return finish()

        KS = [sb(es, "KS%d" % i, [128, S], BF16) for i in range(2)]
        KW = [sb(es, "KW%d" % i, [128, S], BF16) for i in range(2)]
        VS = [sb(es, "VS%d" % i, [128, 32, 65], BF16) for i in range(2)]
        VW = [sb(es, "VW%d" % i, [128, 32, 65], BF16) for i in range(2)]
        t_KS = [P.tok("KS%d" % i, dma=True) for i in range(2)]
        t_KW = [P.tok("KW%d" % i, dma=True) for i in range(2)]
        t_VS = [P.tok("VS%d" % i, dma=True) for i in range(2)]
        t_VW = [P.tok("VW%d" % i, dma=True) for i in range(2)]
        for i in range(2):
            P.op("pool", lambda e, i=i: e.memset(VS[i][:], 1.0), writes=[t_VS[i]])
            P.op("pool", lambda e, i=i: e.memset(VW[i][:], 1.0), writes=[t_VW[i]])
            P.dma("sp", lambda e, i=i: e.dma_start(out=KS[i][64:128, :], in_=cd["kaug_slc"]), t_KS[i], True)
            P.dma("sp", lambda e, i=i: e.dma_start(out=KW[i][64:128, :], in_=cd["kaug_one"]), t_KW[i], True)
        QA = [[sb(es, "QA%d_%d" % (hh, i), [128, 512], BF16) for i in range(2)] for hh in range(4)]
        t_QA = [[P.tok("QA%d_%d" % (hh, i), dma=True) for i in range(2)] for hh in range(4)]
        for hh in range(4):
            for i in range(2):
                P.op("pool", lambda e, hh=hh, i=i: e.memset(QA[hh][i][:], 0.0), writes=[t_QA[hh][i]])
        resA = {"PT": Rot([(sb(es, "PTA%d" % i, [128, 512], BF16), P.tok("PTA")) for i in range(2)]),
                "Sp": Rot([(ps(es, "SpA%d" % i, [128, 512], F32), P.tok("SpA")) for i in range(1)]),
                "Op": Rot([(ps(es, "OpA%d" % i, [128, 4, 128], F32), P.tok("OpA")) for i in range(1)])}
        resB = {"PT": Rot([(sb(es, "PTB%d" % i, [128, 512], BF16), P.tok("PTB")) for i in range(4)]),
                "Sp": Rot([(ps(es, "SpB%d" % i, [128, 512], F32), P.tok("SpB")) for i in range(2)]),
                "Op": Rot([(ps(es, "OpB%d" % i, [128, 4, 128], F32), P.tok("OpB")) for i in range(2)])}
        IMPp = ps(es, "IMPp", [128, 4, 64], F32)
        t_IMP = P.tok("IMP")
        negTp = ps(es, "negTp", [128, 512], BF16)
        t_negT = P.tok("negT")
        yacc = [sb(es, "yacc%d" % i, [128, 4, 256], F32) for i in range(2)]
        t_yacc = [P.tok("yacc0"), P.tok("yacc1")]
        ztile = [sb(es, "ztile%d" % i, [128, 4, 256], F32) for i in range(2)]
        t_zt = [P.tok("zt%d" % i, dma=True) for i in range(2)]
        ybf = [sb(es, "ybf%d" % i, [128, 4, 256], BF16) for i in range(2)]
        t_ybf = [P.tok("ybf%d" % i, dma=True) for i in range(2)]
        sc = sb(es, "sc", [128, 4, 64], F32)
        t_sc = P.tok("sc")
        for nm, r_ in (("A", resA), ("B", resB)):
            r_["small"] = Rot([(sb(es, "sm%s%d" % (nm, i), [128, 16], F32), P.tok("sm")) for i in range(3)])
            r_["tmpo"] = Rot([(sb(es, "tmpo%s%d" % (nm, i), [128, 4, 64], F32), P.tok("tmpo")) for i in range(2)])
        s1 = sb(es, "s1", [128, 64], F32)
        s2 = sb(es, "s2", [128, 64], F32)
        m8 = sb(es, "m8", [128, 16], F32)
        negpad = sb(es, "negpad", [128, 128], BF16)
        t_s1, t_s2, t_m8, t_negpad = P.tok("s1"), P.tok("s2"), P.tok("m8"), P.tok("negpad")
        P.op("pool", lambda e: e.memset(negpad[:], 0.0), writes=[t_negpad])
        zb = sb(es, "zb", [128, 512], BF16)
        t_zb = P.tok("zb")
        P.op("pool", lambda e: e.memset(zb[:], 0.0), writes=[t_zb])

        def zero_bank(ap2d, t_bank, n):
            P.op("pe", lambda e: e.matmul(ap2d, lhsT=zb[:, 0:128], rhs=zb[:, 0:n], start=True, stop=False),
                 reads=[t_zb], writes=[t_bank])

        def s_tile(res, branch, h, t, Q, t_Q, Ktile, t_K, kcol, col0, ncol, mask, bias_ap, t_bias):
            sp, t_sp = res["Sp"].next()
            c1 = col0 + ncol
            P.op("pe", lambda e: e.matmul(sp[:, col0:c1], lhsT=Ktile[:, kcol:kcol + 128], rhs=Q[:, col0:c1],
                                          start=True, stop=(mask is None)),
                 reads=[t_Q, t_K], writes=[t_sp])
            if mask is not None:
                mcol, mrhs, t_m = mask
                w = mrhs.shape[-1] if hasattr(mrhs, "shape") else 128
                P.op("pe", lambda e: e.matmul(sp[:, mcol:mcol + w], lhsT=ident_b[:], rhs=mrhs,
                                              start=False, stop=True),
                     reads=[t_const, t_m], writes=[t_sp])
            pt, t_pt = res["PT"].next()
            P.op("act", lambda e: e.activation(out=pt[:, col0:c1], in_=sp[:, col0:c1], func=AF.Exp,
                                               scale=0.125, bias=bias_ap),
                 reads=[t_sp, t_bias], writes=[t_pt])
            return pt, t_pt

        def finalize(res, br, h, hh, t, op_, t_op, ya, t_ya, first):
            sm, t_sm_ = res["small"].next()
            gcol = br * 16 + h
            P.op("dve", lambda e: e.tensor_scalar(out=sm[:, 0:4], in0=op_[:, :, 64], scalar1=1e-30,
                                                  scalar2=None, op0=ALU.max),
                 reads=[t_op], writes=[t_sm_])
            P.op("dve", lambda e: e.reciprocal(out=sm[:, 4:8], in_=sm[:, 0:4]), reads=[t_sm_], writes=[t_sm_])
            P.op("dve", lambda e: e.tensor_tensor(out=sm[:, 8:12], in0=sm[:, 4:8],
                                                  in1=gate_sb[:, 4 * t:4 * t + 4, gcol], op=ALU.mult),
                 reads=[t_sm_, t_gate], writes=[t_sm_])
            fb = sm[:, 8:12].unsqueeze(2).broadcast_to([128, 4, 64])
            if first:
                P.op("dve", lambda e: e.tensor_tensor(out=ya[:, :, hh * 64:(hh + 1) * 64], in0=op_[:, :, 0:64],
                                                      in1=fb, op=ALU.mult),
                     reads=[t_op, t_sm_], writes=[t_ya])
            else:
                to, t_to = res["tmpo"].next()
                P.op("dve", lambda e: e.tensor_tensor(out=to[:], in0=op_[:, :, 0:64], in1=fb, op=ALU.mult),
                     reads=[t_op, t_sm_], writes=[t_to])
                P.op("dve", lambda e: e.tensor_tensor(out=ya[:, :, hh * 64:(hh + 1) * 64],
                                                      in0=ya[:, :, hh * 64:(hh + 1) * 64], in1=to[:], op=ALU.add),
                     reads=[t_to, t_ya], writes=[t_ya])
            return sm, t_sm_

        class Pipe:
            def __init__(self, res, depth):
                self.pending = []
                self.depth = depth
                self.res = res

            def push(self, s_args, pv_fn, post_fn=None):
                pt, t_pt = s_tile(self.res, *s_args)
                self.pending.append((pt, t_pt, pv_fn, post_fn))
                while len(self.pending) > self.depth:
                    self._pop()

            def _pop(self):
                pt, t_pt, pv_fn, post_fn = self.pending.pop(0)
                pv_fn(pt, t_pt)
                if post_fn is not None:
                    post_fn()

            def flush(self):
                while self.pending:
                    self._pop()

        pipeA = Pipe(resA, 1)
        pipeB = Pipe(resB, 2)

        def load_group(g):
            gi = g % 2
            P.dma("sp", lambda e: e.dma_start(
                out=KS[gi][0:64, :], in_=kvT_s[2, g * 64:(g + 1) * 64, :]), t_KS[gi], True)
            P.dma("sp", lambda e: e.dma_start(
                out=KW[gi][0:64, :], in_=kvT_s[3, g * 64:(g + 1) * 64, :]), t_KW[gi], True)
            for half in range(2):
                P.dma("sp", lambda e, half=half: e.dma_start(
                    out=VS[gi][:, half * 16:(half + 1) * 16, 0:64],
                    in_=vtok_s[half * 2048:(half + 1) * 2048, g * 64:(g + 1) * 64].rearrange(
                        "(kt p) d -> p kt d", p=128)), t_VS[gi], True)
                P.dma("sp", lambda e, half=half: e.dma_start(
                    out=VW[gi][:, half * 16:(half + 1) * 16, 0:64],
                    in_=vtok_s[half * 2048:(half + 1) * 2048, 256 + g * 64:256 + (g + 1) * 64].rearrange(
                        "(kt p) d -> p kt d", p=128)), t_VW[gi], True)

        def stage_A(u, g, t):
            par = u % 2
            q0 = t * 512
            ya, t_ya = yacc[par], t_yacc[par]
            for hh in range(4):
                h = g * 4 + hh
                P.dma("sp", lambda e, hh=hh, h=h: e.dma_start(
                    out=QA[hh][par][0:64, :], in_=qT_s[h * 64:(h + 1) * 64, q0:q0 + 512]),
                    t_QA[hh][par], True)
                P.dma("sp", lambda e, hh=hh, h=h: e.dma_start(
                    out=QA[hh][par][127:128, :], in_=cd["vrow"][h:h + 1, :]), t_QA[hh][par], True)
            zt, t_z = ztile[par], t_zt[par]
            P.dma("sp", lambda e: e.dma_start(
                out=zt[:], in_=zs_s[t * 512:(t + 1) * 512, g * 256:(g + 1) * 256].rearrange(
                    "(s p) c -> p s c", p=128)), t_z, True)
            ncts = 2 if t >= 4 else 1
            for hh in range(4):
                h = g * 4 + hh
                Q, t_Q = QA[hh][par], t_QA[hh][par]
                op_, t_op = resA["Op"].next()
                for ct in range(ncts):
                    bcol = (h * 8 + t) * 2 + ct

                    def pv(pt, t_pt, op_=op_, t_op=t_op, ct=ct):
                        if ct == 0:
                            zero_bank(op_[:].rearrange("p a b -> p (a b)"), t_op, 512)
                            zero_bank(IMPp[:].rearrange("p a b -> p (a b)"), t_IMP, 256)
                        for qs in range(4):
                            P.op("pe", lambda e, qs=qs: e.matmul(
                                op_[:, qs, 0:65], lhsT=pt[:, qs * 128:(qs + 1) * 128], rhs=VC[:, g, ct, :],
                                start=False, stop=(ct == ncts - 1 and qs == 3)),
                                reads=[t_pt, t_VC], writes=[t_op])
                            P.op("pe", lambda e, qs=qs: e.matmul(
                                IMPp[:, qs, :], lhsT=pt[:, qs * 128:(qs + 1) * 128], rhs=ovt[:, ct, :],
                                start=False, stop=(ct == ncts - 1 and qs == 3)),
                                reads=[t_pt, t_c3], writes=[t_IMP])

                    def post(op_=op_, t_op=t_op, h=h, hh=hh):
                        sm, t_sm_ = finalize(resA, 0, h, hh, t, op_, t_op, ya, t_ya, True)
                        rb = sm[:, 4:8].unsqueeze(2).broadcast_to([128, 4, 64])
                        if hh == 0:
                            P.op("dve", lambda e: e.tensor_tensor(out=sc[:], in0=IMPp[:], in1=rb, op=ALU.mult),
                                 reads=[t_IMP, t_sm_], writes=[t_sc])
                        else:
                            to, t_to = resA["tmpo"].next()
                            P.op("dve", lambda e: e.tensor_tensor(out=to[:], in0=IMPp[:], in1=rb, op=ALU.mult),
                                 reads=[t_IMP, t_sm_], writes=[t_to])
                            P.op("dve", lambda e: e.tensor_tensor(out=sc[:], in0=sc[:], in1=to[:], op=ALU.add),
                                 reads=[t_to, t_sc], writes=[t_sc])

                    pipeA.push((0, h, t, Q, t_Q, KC[:, g, :], t_KC, ct * 128, 0, 512,
                               (0, cmask[:, ct, q0:q0 + 512], t_c3), biasc_tab[:, bcol:bcol + 1], t_c3),
                              pv, post if ct == ncts - 1 else None)
            pipeA.flush()
            for qs in range(4):
                tt = 4 * t + qs
                P.op("dve", lambda e, qs=qs, tt=tt: e.tensor_tensor(out=s1[:], in0=sc[:, qs, :],
                                                                    in1=sel_vnf[:, tt, :], op=ALU.mult),
                     reads=[t_sc, t_c3], writes=[t_s1])
                P.op("dve", lambda e, tt=tt: e.tensor_tensor(out=s1[:], in0=s1[:], in1=sel_addc[:, tt, :],
                                                             op=ALU.add),
                     reads=[t_s1, t_c3], writes=[t_s1])
                P.op("dve", lambda e: e.max(out=m8[:, 0:8], in_=s1[:]), reads=[t_s1], writes=[t_m8])
                P.op("dve", lambda e: e.tensor_scalar(out=s2[:], in0=s1[:], scalar1=m8[:, 7:8],
                                                      scalar2=None, op0=ALU.is_ge),
                     reads=[t_s1, t_m8], writes=[t_s2])
                P.op("dve", lambda e: e.scalar_tensor_tensor(out=s2[:], in0=s2[:], scalar=-3.0e9, in1=s1[:],
                                                             op0=ALU.mult, op1=ALU.add),
                     reads=[t_s1, t_s2], writes=[t_s2])
                P.op("dve", lambda e: e.max(out=m8[:, 8:16], in_=s2[:]), reads=[t_s2], writes=[t_m8])
                P.op("dve", lambda e: e.tensor_scalar(out=s2[:], in0=s1[:], scalar1=m8[:, 15:16],
                                                      scalar2=None, op0=ALU.is_ge),
                     reads=[t_s1, t_m8], writes=[t_s2])
                P.op("dve", lambda e, tt=tt: e.tensor_tensor(out=s2[:], in0=s2[:], in1=sel_valid[:, tt, :],
                                                             op=ALU.mult),
                     reads=[t_s2, t_c3], writes=[t_s2])
                P.op("dve", lambda e: e.tensor_scalar(out=negpad[:, 64:127], in0=s2[:, 0:63], scalar1=1.0,
                                                      scalar2=-NEG, op0=ALU.subtract, op1=ALU.mult),
                     reads=[t_s2], writes=[t_negpad])
                P.op("pe", lambda e, qs=qs: e.transpose(negTp[:, qs * 128:(qs + 1) * 128], negpad[:], ident_b[:]),
                     reads=[t_negpad, t_const], writes=[t_negT])
            for hh in range(4):
                h = g * 4 + hh
                P.op("dve", lambda e, hh=hh: e.tensor_copy(QA[hh][par][64:127, :], negTp[64:127, :]),
                     reads=[t_negT], writes=[t_QA[hh][par]])

        def stage_B(u, g, t):
            par = u % 2
            gi = g % 2
            ya, t_ya = yacc[par], t_yacc[par]
            zt, t_z = ztile[par], t_zt[par]
            for hh in range(4):
                h = g * 4 + hh
                Q, t_Q = QA[hh][par], t_QA[hh][par]
                for br, Kt, t_K, Vt, t_V in ((1, KS[gi], t_KS[gi], VS[gi], t_VS[gi]),
                                             (2, KW[gi], t_KW[gi], VW[gi], t_VW[gi])):
                    tiles = []
                    if br == 1:
                        for kt in range(0, 4 * t):
                            md = (4 * t - kt - 1) * 128 + 1
                            if SLOPES[h] * md >= ZERO_CUT:
                                continue
                            tiles.append((kt, 0, 512, None))
                    else:
                        for i in (3, 2, 1, 0):
                            kt = 4 * t - 4 + i
                            md = 385 - 128 * i if i >= 1 else 385
                            if kt >= 0 and SLOPES[h] * md < ZERO_CUT:
                                tiles.append((kt, 0, 128 * (i + 1), (128 * i, tri_b[:], t_c3)))
                    for i2 in range(4):
                        tiles.append((4 * t + i2, 128 * i2, 512 - 128 * i2, (128 * i2, tri_c[:], t_c3)))
                    last = {}
                    for ti, (kt, col0, ncol, mask) in enumerate(tiles):
                        for qs in range(col0 // 128, (col0 + ncol) // 128):
                            last[qs] = ti
                    op_, t_op = resB["Op"].next()
                    for ti, (kt, col0, ncol, mask) in enumerate(tiles):
                        bcol = h * 32 + (kt - 4 * t + 28)

                        def pv(pt, t_pt, op_=op_, t_op=t_op, ti=ti, kt=kt, col0=col0, ncol=ncol, Vt=Vt, t_V=t_V,
                               last=last, tiles=tiles):
                            if ti == 0:
                                zero_bank(op_[:].rearrange("p a b -> p (a b)"), t_op, 512)
                            for qs in range(col0 // 128, (col0 + ncol) // 128):
                                P.op("pe", lambda e, qs=qs: e.matmul(
                                    op_[:, qs, 0:65], lhsT=pt[:, qs * 128:(qs + 1) * 128], rhs=Vt[:, kt, :],
                                    start=False, stop=(ti == len(tiles) - 1 and qs == 3)),
                                    reads=[t_pt, t_V], writes=[t_op])

                        def post(op_=op_, t_op=t_op, br=br, h=h, hh=hh):
                            finalize(resB, br, h, hh, t, op_, t_op, ya, t_ya, False)

                        pipeB.push((br, h, t, Q, t_Q, Kt, t_K, kt * 128, col0, ncol, mask,
                                   bias_tab[:, bcol:bcol + 1], t_c3), pv,
                                  post if ti == len(tiles) - 1 else None)
            pipeB.flush()
            yb, t_yb = ybf[par], t_ybf[par]
            P.op("dve", lambda e: e.tensor_tensor(out=yb[:], in0=ya[:], in1=zt[:], op=ALU.mult),
                 reads=[t_ya, t_z], writes=[t_yb])
            P.dma("pool", lambda e: e.dma_start(
                out=y0_s[t * 512:(t + 1) * 512, g * 256:(g + 1) * 256].rearrange("(s p) c -> p s c", p=128),
                in_=yb[:]), t_yb, False)

        units = [(g, t) for g in range(4) for t in range(NQ)]
        for u in range(len(units) + 1):
            lists = []
            if u < len(units):
                g, t = units[u]
                if t == 0:
                    load_group(g)
                P.capture()
                stage_A(u, g, t)
                lists.append(P.end_capture())
            if u > 0:
                P.capture()
                stage_B(u - 1, *units[u - 1])
                lists.append(P.end_capture())
            P.replay_merged(lists)
        P.end_phase()
    es_w1.close()
    if stop_after == 3:
        return finish()


    es_g = ExitStack()
    ipre = sb(es_g, "ipre", [128, NT, 8], F32)
    logf = sb(es_g, "logf", [128, NT, 8], F32)
    hgT = sb(es_g, "hgT", [128, 16], F32)
    wof6 = sb(es_g, "wof6", [128, 8, 128], F32)
    wo1 = sb(es_g, "wo1", [128, 16, D], BF16)
    t_wof6, t_wo1 = Tok("wof6", P.free.pop(0)), P.tok("wo1")
    es_h = ExitStack()
    hT = sb(es_h, "hT_l1", [128, 8, S], BF16)
    t_ipre, t_logf = P.tok("ipre"), P.tok("logf")
    with ExitStack() as es:
        R = norm_res(es)
        wof = sb(es, "wof", [128, 8, 512], F32)
        wo = sb(es, "wo", [128, 8, D], BF16)
        t_wof, t_wo = P.tok("wof", dma=True), P.tok("wo")
        for hf in range(2):
            P.dma("sp", lambda e, hf=hf: e.dma_start(
                out=wof[:], in_=woa_d[:, hf * 512:(hf + 1) * 512].rearrange("(kc p) n -> p kc n", p=128)),
                t_wof, True)
            for kc in range(8):
                P.op("act", lambda e, hf=hf, kc=kc: e.activation(out=wo[:, kc, hf * 512:(hf + 1) * 512],
                                                                 in_=wof[:, kc, :], func=AF.Identity),
                     reads=[t_wof], writes=[t_wo])
        yin = Rot([(sb(es, "yin%d" % i, [128, D], BF16), P.tok("yin%d" % i, dma=True)) for i in range(3)])
        xin = Rot([(sb(es, "xin4_%d" % i, [128, D], F32), P.tok("xin4_%d" % i, dma=True)) for i in range(3)])
        x1t = Rot([(sb(es, "x1t%d" % i, [128, D], F32), P.tok("x1t%d" % i, dma=True)) for i in range(3)])
        yT = Rot([(sb(es, "yT%d" % i, [128, 8, 128], BF16), P.tok("yT")) for i in range(2)])
        ytp = ps(es, "ytp", [128, D], BF16)
        t_ytp = P.tok("ytp")
        po = [ps(es, "po%d" % i, [128, 512], F32) for i in range(2)]
        t_po = [P.tok("po0"), P.tok("po1")]
        def stage_X(tt):
            yt, t_yt = yin.next()
            xt, t_xt = xin.next()
            x1, t_x1 = x1t.next()
            yTt, t_yT = yT.next()
            P.dma("sp", lambda e, yt=yt, tt=tt: e.dma_start(out=yt[:], in_=y0_s[tt * 128:(tt + 1) * 128, :]),
                  t_yt, True)
            P.dma("sp", lambda e, xt=xt, tt=tt: e.dma_start(out=xt[:], in_=x_d[tt * 128:(tt + 1) * 128, :]),
                  t_xt, True)
            for kc in range(8):
                P.op("pe", lambda e, yt=yt, kc=kc: e.transpose(
                    ytp[:, kc * 128:(kc + 1) * 128], yt[:, kc * 128:(kc + 1) * 128], ident_b[:]),
                    reads=[t_yt, t_const], writes=[t_ytp])
            P.op("act", lambda e, yTt=yTt: e.activation(
                out=yTt[:, 0:4, :], in_=ytp[:, 0:512].rearrange("p (a b) -> p a b", b=128), func=AF.Identity),
                reads=[t_ytp], writes=[t_yT])
            P.op("act", lambda e, yTt=yTt: e.activation(
                out=yTt[:, 4:8, :], in_=ytp[:, 512:1024].rearrange("p (a b) -> p a b", b=128), func=AF.Identity),
                reads=[t_ytp], writes=[t_yT])
            for hf in range(2):
                for kc in range(8):
                    P.op("pe", lambda e, yTt=yTt, hf=hf, kc=kc: e.matmul(
                        po[hf][:], lhsT=yTt[:, kc, :], rhs=wo[:, kc, hf * 512:(hf + 1) * 512],
                        start=(kc == 0), stop=(kc == 7)), reads=[t_yT, t_wo], writes=[t_po[hf]])
                P.op("dve", lambda e, x1=x1, hf=hf: e.tensor_tensor(
                    out=x1[:, hf * 512:(hf + 1) * 512], in0=po[hf][:], in1=gate_bc[0][:, hf * 512:(hf + 1) * 512],
                    op=ALU.mult), reads=[t_po[hf], t_gbc[0]], writes=[t_x1])
            P.op("dve", lambda e, x1=x1, xt=xt: e.tensor_tensor(out=x1[:], in0=x1[:], in1=xt[:], op=ALU.add),
                 reads=[t_xt, t_x1], writes=[t_x1])
            P.dma("pool", lambda e, x1=x1, tt=tt: e.dma_start(out=x1_s[tt * 128:(tt + 1) * 128, :], in_=x1[:]),
                  t_x1, False)
            return x1, t_x1

        x1s, xns = {}, {}
        for tt in range(NT + 2):
            lists = []
            if tt < NT:
                P.capture()
                x1s[tt] = stage_X(tt)
                lists.append(P.end_capture())
            if 0 <= tt - 1 < NT:
                P.capture()
                x1, t_x1 = x1s.pop(tt - 1)
                xns[tt - 1] = norm_a(x1[:], t_x1, R)
                lists.append(P.end_capture())
            if 0 <= tt - 2 < NT:
                P.capture()
                norm_b(1, tt - 2, *xns.pop(tt - 2), R)
                lists.append(P.end_capture())
            P.replay_merged(lists)
        P.end_phase()
    if stop_after == 4:
        return finish()

    with ExitStack() as es:
        frow_ = sb(es, "frow5", [128, S], BF16)
        t_frow_ = P.tok("frow5", dma=True)
        frow = [frow_, frow_]
        t_frow = [t_frow_, t_frow_]
        raw = Rot([(sb(es, "raw%d" % i, [128, 515], F32), P.tok("raw"), P.tok("rawc")) for i in range(2)])
        acc = Rot([(sb(es, "acc%d" % i, [128, 512], F32), P.tok("acc")) for i in range(2)])
        convw = sb(es, "convw", [128, 64], F32)
        convb = sb(es, "convb", [128, 16], F32)
        gb_bc = sb(es, "gb_bc", [128, 16], F32)
        t_cv = P.tok("cv", dma=True)
        P.dma("sp", lambda e: e.dma_start(out=convw[:], in_=convwT_d), t_cv, True)
        P.dma("sp", lambda e: e.dma_start(out=convb[:], in_=convbT_d), t_cv, True)
        P.dma("sp", lambda e: e.dma_start(out=gb_bc[:], in_=bass.AP(gateb_d.tensor, 0, [[0, 128], [1, 16]])), t_cv, True)
        tstb = Rot([(sb(es, "tstb5_%d" % i, [128, 4, 512], BF16), P.tok("tstb5_%d" % i, dma=True)) for i in range(2)])
        tsto = Rot([(sb(es, "tsto%d" % i, [128, 4, 256], F32), P.tok("tsto%d" % i, dma=True)) for i in range(2)])
        z1 = Rot([(sb(es, "z1_%d" % i, [128, 256], F32), P.tok("z1")) for i in range(2)])
        gtmp = Rot([(sb(es, "gtmp%d" % i, [128, 24], F32), P.tok("gtmp")) for i in range(2)])
        sgt = Rot([(sb(es, "sgt%d" % i, [128, 512], F32), P.tok("sgt")) for i in range(2)])
        st5 = {"prev": None}

        def fm_sink5(blk, qt, pa, t_pa):
            rw, t_rw, t_rwc = raw.next()
            ac, t_ac = acc.next()
            fr, tf = frow[blk % 2], t_frow[blk % 2]
            P.op("act", lambda e: e.activation(out=rw[:, 3:515], in_=pa[:], func=AF.Identity),
                 reads=[t_pa], writes=[t_rw])
            P.op("act", lambda e: e.activation(out=ac[:], in_=pa[:], func=AF.Identity,
                                               scale=convw[:, blk * 4 + 3:blk * 4 + 4]),
                 reads=[t_pa, t_cv], writes=[t_ac])
            if st5.get("pend"):
                st5.pop("pend")()
            if qt == 0:
                P.op("dve", lambda e: e.memset(rw[:, 0:3], 0.0), writes=[t_rwc])
            else:
                prw, t_prw = st5["prev"]
                P.op("dve", lambda e: e.tensor_copy(rw[:, 0:3], prw[:, 512:515]), reads=[t_prw], writes=[t_rwc])
            st5["prev"] = (rw, t_rw)
            for j in range(0, 3):
                P.op("dve", lambda e, j=j: e.scalar_tensor_tensor(
                    out=ac[:], in0=rw[:, j:j + 512], scalar=convw[:, blk * 4 + j:blk * 4 + j + 1], in1=ac[:],
                    op0=ALU.mult, op1=ALU.add), reads=[t_rw, t_rwc, t_cv, t_ac], writes=[t_ac])

            def pend():
                P.op("act", lambda e: e.activation(out=fr[:, qt * 512:(qt + 1) * 512], in_=ac[:], func=AF.Silu,
                                                   bias=convb[:, blk:blk + 1]),
                     reads=[t_ac, t_cv], writes=[tf])
                if qt == NQ - 1:
                    P.dma("pool", lambda e: e.dma_start(out=qkT1_s[blk], in_=fr[:]), tf, False)

            st5["pend"] = pend

        tm5 = {}

        def tm_sink5(meta, tt, pa, t_pa):
            if st5.get("pend"):
                st5.pop("pend")()
            if meta == "g":
                gt, t_gt = gtmp.next()
                P.op("dve", lambda e: e.tensor_tensor(out=ipre[:, tt, :], in0=pa[:, 0:8], in1=gb_bc[:, 0:8],
                                                      op=ALU.add), reads=[t_pa, t_cv], writes=[t_ipre])
                P.op("dve", lambda e: e.tensor_tensor(out=gt[:, 0:8], in0=pa[:, 8:16], in1=gb_bc[:, 8:16],
                                                      op=ALU.add), reads=[t_pa, t_cv], writes=[t_gt])
                P.op("act", lambda e: e.activation(out=gt[:, 8:16], in_=gt[:, 0:8], func=AF.Exp, scale=-1.0),
                     reads=[t_gt], writes=[t_gt])
                P.op("act", lambda e: e.activation(out=gt[:, 16:24], in_=gt[:, 8:16], func=AF.Ln,
                                                   bias=cvec[:, 1:2]), reads=[t_gt, t_cvec], writes=[t_gt])
                P.op("dve", lambda e: e.tensor_scalar(out=logf[:, tt, :], in0=gt[:, 16:24], scalar1=-1.0,
                                                      scalar2=None, op0=ALU.mult), reads=[t_gt], writes=[t_logf])
                return
            kind, i = meta
            if tt % 4 == 0:
                tm5["buf"] = tstb.next() if kind == "v" else tsto.next()
            st, t_s = tm5["buf"]
            if kind == "v":
                P.op("dve", lambda e: e.tensor_copy(st[:, tt % 4, :], pa[:]), reads=[t_pa], writes=[t_s])
                if tt % 4 == 3:
                    dst = vtok1_s[(tt - 3) * 128:(tt + 1) * 128, i * 512:(i + 1) * 512].rearrange(
                        "(t p) c -> p t c", p=128)
                    P.dma("pool", lambda e: e.dma_start(out=dst, in_=st[:]), t_s, False)
            else:
                zz, t_zz = z1.next()
                sg_, t_sg_ = sgt.next()
                P.op("act", lambda e: e.activation(out=sg_[:], in_=pa[:], func=AF.Sigmoid),
                     reads=[t_pa], writes=[t_sg_])
                P.op("dve", lambda e: e.tensor_tensor(out=zz[:], in0=pa[:, 256:512], in1=sg_[:, 256:512], op=ALU.mult),
                     reads=[t_pa, t_sg_], writes=[t_zz])
                P.op("dve", lambda e: e.tensor_tensor(out=st[:, tt % 4, :], in0=sg_[:, 0:256], in1=zz[:], op=ALU.mult),
                     reads=[t_sg_, t_zz], writes=[t_s])
                if tt % 4 == 3:
                    dst = ogz_s[(tt - 3) * 128:(tt + 1) * 128, i * 256:(i + 1) * 256].rearrange(
                        "(t p) c -> p t c", p=128)
                    P.dma("pool", lambda e: e.dma_start(out=dst, in_=st[:]), t_s, False)

        groups = ([("fm", i * 512, 512, i * 4) for i in range(4)]
                  + [("tm", 2048 + i * 512, 512, ("v", i)) for i in range(4)]
                  + [("tm", 4096 + i * 512, 512, ("ogz", i)) for i in range(8)]
                  + [("tm", 8192, 16, "g")])
        inproj(es, wb_d, groups, fm_sink5, tm_sink5)
        P.dma("sp", lambda e: e.dma_start(out=hgT[:], in_=headg_d), t_wof6, True)
        for kh in range(2):
            for hf in range(8):
                P.dma("sp", lambda e, kh=kh, hf=hf: e.dma_start(
                    out=wof6[:], in_=wob_d[kh * 1024:(kh + 1) * 1024, hf * 128:(hf + 1) * 128].rearrange(
                        "(kc p) n -> p kc n", p=128)), t_wof6, True)
                for kc in range(8):
                    P.op("act", lambda e, kh=kh, hf=hf, kc=kc: e.activation(
                        out=wo1[:, kh * 8 + kc, hf * 128:(hf + 1) * 128], in_=wof6[:, kc, :], func=AF.Identity,
                        scale=hgT[:, kh * 8 + kc:kh * 8 + kc + 1]), reads=[t_wof6], writes=[t_wo1])
        P.end_phase()
    es_h.close()
    if stop_after == 5:
        return finish()

    with ExitStack() as es:
        R = norm_res(es, full=False)
        t_c6 = P.tok("c6", dma=True)
        mask01 = sb(es, "mask01", [128, 128], F32)
        fing_bc = sb(es, "fing_bc", [128, D], F32)
        P.dma("sp", lambda e: e.dma_start(out=mask01[:], in_=cd["mask01"]), t_c6, True)
        P.dma("sp", lambda e: e.dma_start(out=fing_bc[:], in_=bass.AP(fing_d.tensor, 0, [[0, 128], [1, D]])), t_c6, True)
        Ct = sb(es, "Ct", [128, 8, 257], F32)
        Cbf = sb(es, "Cbf", [128, 8, 257], BF16)
        t_Ct = [P.tok("Ct_%d" % h) for h in range(8)]
        t_Cbf = [P.tok("Cbf_%d" % h) for h in range(8)]
        qkr = Rot([(sb(es, "qk%d" % i, [128, 16, 128], BF16), P.tok("qk%d" % i, dma=True)) for i in range(3)])
        Var = [(sb(es, "Va%d" % i, [128, 8, 257], BF16), P.tok("Va%d" % i, dma=True)) for i in range(3)]
        for va, t_va in Var:
            P.op("pool", lambda e, va=va: e.memset(va[:], 1.0), writes=[t_va])
        Var = Rot(Var)
        ogzr = Rot([(sb(es, "ogzt%d" % i, [128, 2048], F32), P.tok("ogzt%d" % i, dma=True)) for i in range(3)])
        x1r = Rot([(sb(es, "x1r%d" % i, [128, D], F32), P.tok("x1r%d" % i, dma=True)) for i in range(5)])
        outr = Rot([(sb(es, "outt%d" % i, [128, D], F32), P.tok("outt%d" % i, dma=True)) for i in range(3)])
        numr = Rot([(sb(es, "numsb%d" % i, [128, 8, 257], F32), P.tok("numsb")) for i in range(2)])
        junk6 = sb(es, "junk6", [128, 256], F32)
        t_junk6 = P.tok("junk6")
        yT6 = sb(es, "yT6", [128, 16, 128], BF16)
        x2 = sb(es, "x2", [128, D], F32)
        t_yT6, t_x2 = P.tok("yT6"), P.tok("x2")
        smr = Rot([(sb(es, "sm6_%d" % i, [128, 64], F32), P.tok("sm6")) for i in range(4)])
        sm2r = Rot([(sb(es, "sm6b_%d" % i, [128, 80], F32), P.tok("sm6b")) for i in range(2)])
        Sm8 = sb(es, "Sm8", [128, 8, 128], BF16)
        Kt8 = sb(es, "Kt8", [128, 8, 128], BF16)
        t_Sm8 = [P.tok("Sm8_%d" % h) for h in range(8)]
        t_Kt8 = [P.tok("Kt8_%d" % h) for h in range(8)]
        po = [ps(es, "po6_%d" % i, [128, 512], F32) for i in range(2)]
        t_po = [P.tok("po6_0"), P.tok("po6_1")]
        ptrk = ps(es, "ptrk", [128, D], BF16)
        ptry = ps(es, "ptry", [128, D], BF16)
        t_ptrk1 = P.tok("ptrk")
        t_ptrk = [t_ptrk1] * 8
        t_ptry = P.tok("ptry")
        pS2 = [ps(es, "pS2_%d" % i, [128, 4, 128], F32) for i in range(2)]
        t_pSb = [P.tok("pSb0"), P.tok("pSb1")]
        t_pS = [t_pSb[h // 4] for h in range(8)]
        pnd_ = ps(es, "pnd", [128, 512], F32)
        pnd2 = [pnd_, pnd_]
        pdC = ps(es, "pdC", [128, 512], F32)
        t_pnd_ = P.tok("pnd")
        t_pnd2, t_pdC = [t_pnd_, t_pnd_], P.tok("pdC")
        t_pbg = t_pdC
        pbg = pdC[:, 384:400]
        chunk = {}

        loaded = {}
        pending_stores = []

        def stage_L(n):
            qk, t_qk = qkr.next()
            va, t_va = Var.next()
            og, t_og = ogzr.next()
            x1, t_x1 = x1r.next()
            P.dma("sp", lambda e: e.dma_start(
                out=qk[:], in_=qkT1_s[:, :, n * 128:(n + 1) * 128].rearrange("b p t -> p b t")), t_qk, True)
            P.dma("sp", lambda e: e.dma_start(
                out=va[:, :, 0:256], in_=vtok1_s[n * 128:(n + 1) * 128, :].rearrange("p (h v) -> p h v", v=256)),
                t_va, True)
            P.dma("sp", lambda e: e.dma_start(out=x1[:], in_=x1_s[n * 128:(n + 1) * 128, :]), t_x1, True)
            P.dma("sp", lambda e: e.dma_start(out=og[:], in_=ogz_s[n * 128:(n + 1) * 128, :]), t_og, True)
            loaded[n] = (qk, t_qk, va, t_va, og, t_og, x1, t_x1)

        def stage_H(n):
            qk, t_qk, va, t_va, og, t_og, x1, t_x1 = loaded.pop(n)
            P.op("pe", lambda e: e.matmul(pdC[:, 384:392], lhsT=mask01[:], rhs=logf[:, n, :], start=True, stop=True),
                 reads=[t_c6, t_logf], writes=[t_pbg])
            P.op("pe", lambda e: e.matmul(pdC[:, 392:400], lhsT=ones_f[:], rhs=logf[:, n, :], start=True, stop=True),
                 reads=[t_const, t_logf], writes=[t_pbg])
            sm, t_sm = smr.next()
            P.op("act", lambda e: e.activation(out=sm[:, 32:48], in_=pdC[:, 384:400], func=AF.Identity),
                 reads=[t_pbg], writes=[t_sm])
            P.op("dve", lambda e: e.tensor_tensor(out=sm[:, 0:8], in0=ipre[:, n, :], in1=sm[:, 32:40],
                                                  op=ALU.subtract), reads=[t_ipre, t_sm], writes=[t_sm])
            P.op("act", lambda e: e.activation(out=sm[:, 8:16], in_=sm[:, 0:8], func=AF.Exp, bias=cvec[:, 2:3]),
                 reads=[t_sm, t_cvec], writes=[t_sm])
            P.op("act", lambda e: e.activation(out=sm[:, 16:32], in_=sm[:, 32:48], func=AF.Exp),
                 reads=[t_sm], writes=[t_sm])
            nums, t_nums = numr.next()
            last = (n == NT - 1)
            prev = chunk.get(n - 1)
            for h in range(8):
                P.op("pe", lambda e, h=h: e.matmul(pS2[h // 4][:, h % 4, :], lhsT=qk[:, 8 + h, :], rhs=qk[:, h, :],
                                                   start=True, stop=True), reads=[t_qk], writes=[t_pS[h]])
            if not last:
                for h in range(8):
                    P.op("pe", lambda e, h=h: e.transpose(ptrk[:, h * 128:(h + 1) * 128], qk[:, 8 + h, :], ident_b[:]),
                         reads=[t_qk, t_const], writes=[t_ptrk[h]])
            if prev is not None:
                psm, t_psm = prev
                for h in range(8):
                    P.op("act", lambda e, h=h: e.activation(
                        out=Cbf[:, h, :], in_=Ct[:, h, :], func=AF.Identity, scale=psm[:, 24 + h:25 + h]),
                        reads=[t_Ct[h], t_psm], writes=[t_Cbf[h]])
            for h in range(8):
                P.op("dve", lambda e, h=h: e.scalar_tensor_tensor(
                    out=Sm8[:, h, :], in0=pS2[h // 4][:, h % 4, :], scalar=sm[:, 8 + h:9 + h], in1=mask01[:],
                    op0=ALU.mult, op1=ALU.mult), reads=[t_pS[h], t_sm, t_c6], writes=[t_Sm8[h]])
                if not last:
                    P.op("act", lambda e, h=h: e.activation(
                        out=Kt8[:, h, :], in_=ptrk[:, h * 128:(h + 1) * 128], func=AF.Identity,
                        scale=sm[:, 8 + h:9 + h]), reads=[t_ptrk[h], t_sm], writes=[t_Kt8[h]])
            for h in range(8):
                pnd, t_pnd = pnd2[h % 2], t_pnd2[h % 2]
                P.op("pe", lambda e, h=h, pnd=pnd: e.matmul(pnd[:, 0:257], lhsT=Sm8[:, h, :], rhs=va[:, h, :],
                                                            start=True, stop=(n == 0)),
                     reads=[t_Sm8[h], t_va], writes=[t_pnd])
                if n > 0:
                    P.op("pe", lambda e, h=h, pnd=pnd: e.matmul(pnd[:, 0:257], lhsT=qk[:, h, :], rhs=Cbf[:, h, :],
                                                                start=False, stop=True),
                         reads=[t_qk, t_Cbf[h]], writes=[t_pnd])
                P.op("act", lambda e, h=h, pnd=pnd: e.activation(out=nums[:, h, :], in_=pnd[:, 0:257],
                                                                 func=AF.Identity),
                     reads=[t_pnd], writes=[t_nums])
                if not last:
                    P.op("pe", lambda e, h=h: e.matmul(pdC[:, 0:257], lhsT=Kt8[:, h, :], rhs=va[:, h, :],
                                                       start=True, stop=True),
                         reads=[t_Kt8[h], t_va], writes=[t_pdC])
                    if n == 0:
                        P.op("dve", lambda e, h=h: e.tensor_copy(Ct[:, h, :], pdC[:, 0:257]),
                             reads=[t_pdC], writes=[t_Ct[h]])
                    else:
                        psm, t_psm = prev
                        P.op("dve", lambda e, h=h: e.scalar_tensor_tensor(
                            out=Ct[:, h, :], in0=Ct[:, h, :], scalar=psm[:, 24 + h:25 + h], in1=pdC[:, 0:257],
                            op0=ALU.mult, op1=ALU.add), reads=[t_pdC, t_psm, t_Ct[h], t_Cbf[h]], writes=[t_Ct[h]])
            chunk[n] = (sm, t_sm)
            return (n, sm, t_sm, nums, t_nums, og, t_og, x1, t_x1)

        ybfr = Rot([(sb(es, "ybf6r%d" % i, [128, 2048], BF16), P.tok("ybf6r")) for i in range(2)])

        def stage_E1(ctx):
            n, sm, t_sm, nums, t_nums, og, t_og, x1, t_x1 = ctx
            s2_, t_s2_ = sm2r.next()
            for h in range(8):
                P.op("act", lambda e, h=h: e.activation(out=junk6[:], in_=nums[:, h, 0:256], func=AF.Square,
                                                        accum_out=s2_[:, h:h + 1]),
                     reads=[t_nums], writes=[t_junk6, t_s2_])
            P.op("dve", lambda e: e.tensor_tensor(out=s2_[:, 8:16], in0=nums[:, :, 256], in1=sm[:, 16:24], op=ALU.mult),
                 reads=[t_nums, t_sm], writes=[t_s2_])
            P.op("dve", lambda e: e.tensor_scalar(out=s2_[:, 16:24], in0=s2_[:, 8:16], scalar1=-1.0, scalar2=None,
                                                  op0=ALU.mult), reads=[t_s2_], writes=[t_s2_])
            P.op("dve", lambda e: e.tensor_tensor(out=s2_[:, 16:24], in0=s2_[:, 16:24], in1=s2_[:, 8:16], op=ALU.max),
                 reads=[t_s2_], writes=[t_s2_])
            P.op("dve", lambda e: e.tensor_scalar(out=s2_[:, 16:24], in0=s2_[:, 16:24], scalar1=1.0, scalar2=None,
                                                  op0=ALU.max), reads=[t_s2_], writes=[t_s2_])
            P.op("dve", lambda e: e.reciprocal(out=s2_[:, 24:32], in_=s2_[:, 16:24]), reads=[t_s2_], writes=[t_s2_])
            P.op("dve", lambda e: e.tensor_tensor(out=s2_[:, 32:40], in0=s2_[:, 24:32], in1=sm[:, 16:24], op=ALU.mult),
                 reads=[t_s2_, t_sm], writes=[t_s2_])
            P.op("dve", lambda e: e.tensor_tensor(out=s2_[:, 40:48], in0=s2_[:, 32:40], in1=s2_[:, 32:40], op=ALU.mult),
                 reads=[t_s2_], writes=[t_s2_])
            P.op("dve", lambda e: e.tensor_tensor(out=s2_[:, 40:48], in0=s2_[:, 40:48], in1=s2_[:, 0:8], op=ALU.mult),
                 reads=[t_s2_], writes=[t_s2_])
            P.op("act", lambda e: e.activation(out=s2_[:, 48:56], in_=s2_[:, 40:48], func=AF.Ln,
                                               scale=1.0 / 256, bias=cvec[:, 0:1]),
                 reads=[t_s2_, t_cvec], writes=[t_s2_])
            P.op("act", lambda e: e.activation(out=s2_[:, 56:64], in_=s2_[:, 48:56], func=AF.Exp, scale=-0.5),
                 reads=[t_s2_], writes=[t_s2_])
            P.op("dve", lambda e: e.tensor_tensor(out=s2_[:, 64:72], in0=s2_[:, 32:40], in1=s2_[:, 56:64], op=ALU.mult),
                 reads=[t_s2_], writes=[t_s2_])
            yb, t_yb = ybfr.next()
            for h in range(8):
                P.op("dve", lambda e, h=h: e.scalar_tensor_tensor(
                    out=yb[:, h * 256:(h + 1) * 256], in0=nums[:, h, 0:256], scalar=s2_[:, 64 + h:65 + h],
                    in1=og[:, h * 256:(h + 1) * 256], op0=ALU.mult, op1=ALU.mult),
                    reads=[t_nums, t_s2_, t_og], writes=[t_yb])
            return (n, x1, t_x1, yb, t_yb)

        def stage_E2(c2):
            n, x1, t_x1, yb, t_yb = c2
            for r8 in range(2):
                for kc in range(8):
                    P.op("pe", lambda e, r8=r8, kc=kc: e.transpose(
                        ptry[:, kc * 128:(kc + 1) * 128], yb[:, (r8 * 8 + kc) * 128:(r8 * 8 + kc + 1) * 128],
                        ident_b[:]), reads=[t_yb, t_const], writes=[t_ptry])
                P.op("act", lambda e, r8=r8: e.activation(
                    out=yT6[:, r8 * 8:(r8 + 1) * 8, :], in_=ptry[:].rearrange("p (a b) -> p a b", b=128),
                    func=AF.Identity), reads=[t_ptry], writes=[t_yT6])
            for hf in range(2):
                for kc in range(16):
                    P.op("pe", lambda e, hf=hf, kc=kc: e.matmul(
                        po[hf][:], lhsT=yT6[:, kc, :], rhs=wo1[:, kc, hf * 512:(hf + 1) * 512],
                        start=(kc == 0), stop=(kc == 15)), reads=[t_yT6, t_wo1], writes=[t_po[hf]])
                P.op("dve", lambda e, hf=hf: e.tensor_tensor(
                    out=x2[:, hf * 512:(hf + 1) * 512], in0=po[hf][:], in1=gate_bc[1][:, hf * 512:(hf + 1) * 512],
                    op=ALU.mult), reads=[t_po[hf], t_gbc[1]], writes=[t_x2])
            P.op("dve", lambda e: e.tensor_tensor(out=x2[:], in0=x2[:], in1=x1[:], op=ALU.add),
                 reads=[t_x1, t_x2], writes=[t_x2])
            st, t_st = rms_stats(x2[:], t_x2, R)
            ot, t_ot = outr.next()
            P.op("dve", lambda e: e.scalar_tensor_tensor(
                out=ot[:], in0=x2[:], scalar=st[:, 2:3], in1=fing_bc[:], op0=ALU.mult, op1=ALU.mult),
                reads=[t_x2, t_st, t_c6], writes=[t_ot])
            pending_stores.append((n, ot, t_ot))

        def flush_stores():
            while pending_stores:
                n_, ot, t_ot = pending_stores.pop(0)
                P.dma("sp", lambda e, n_=n_, ot=ot: e.dma_start(out=out_d[n_ * 128:(n_ + 1) * 128, :], in_=ot[:]),
                      t_ot, False)

        cH, cE = {}, {}
        stage_L(0)
        for n in range(NT + 2):
            if n + 1 < NT:
                stage_L(n + 1)
            flush_stores()
            lists = []
            if n < NT:
                P.capture()
                cH[n] = stage_H(n)
                lists.append(P.end_capture())
            if 0 <= n - 2 < NT:
                P.capture()
                stage_E2(cE.pop(n - 2))
                lists.append(P.end_capture())
            if 0 <= n - 1 < NT:
                P.capture()
                cE[n - 1] = stage_E1(cH.pop(n - 1))
                lists.append(P.end_capture())
            P.replay_merged(lists)
        flush_stores()
        P.end_phase()
    es_g.close()
    return nc


def prep_inputs(inputs):
    f = np.float32
    g = lambda k: np.asarray(inputs[k], dtype=f)
    sh = {}
    sh["ada_w"] = np.ascontiguousarray(g("ada_w"))
    sh["adabT"] = np.ascontiguousarray(g("ada_b").reshape(2, 24, 128).transpose(2, 0, 1).reshape(128, 48))
    sh["ngT"] = np.ascontiguousarray(g("norm_g").reshape(2, 8, 128).transpose(2, 0, 1).reshape(128, 16))
    sh["fing"] = np.ascontiguousarray(g("final_g").reshape(1, D))
    sh["wa"] = np.ascontiguousarray(g("a_w_in")[0][:, WA_PERM])
    sh["cmp_w1"] = np.ascontiguousarray(g("a_cmp_w1")[0])
    sh["peT"] = np.ascontiguousarray(g("a_cmp_pe")[0].transpose(2, 0, 1).reshape(64, 64))
    sh["b1T"] = np.ascontiguousarray(g("a_cmp_b1")[0].reshape(2, 2, 128).transpose(2, 0, 1).reshape(128, 4))
    sh["cmp_w2"] = np.ascontiguousarray(g("a_cmp_w2")[0])
    sh["b2T"] = np.ascontiguousarray(g("a_cmp_b2")[0].T)
    sh["a_w_out"] = np.ascontiguousarray(g("a_w_out")[0])
    sh["wb"] = np.ascontiguousarray(g("b_w_in")[0][:, WB_PERM])
    sh["convwT"] = np.ascontiguousarray(g("b_conv_w")[0].reshape(4, 16, 128).transpose(2, 1, 0).reshape(128, 64))
    sh["convbT"] = np.ascontiguousarray(g("b_conv_b")[0].reshape(16, 128).T)
    sh["gateb"] = np.ascontiguousarray(g("b_gate_b")[0].reshape(1, 16))
    sh["headgT"] = np.ascontiguousarray(g("b_head_g")[0].reshape(16, 128).T)
    sh["b_w_out"] = np.ascontiguousarray(g("b_w_out")[0])
    for k, v in make_consts().items():
        sh["k_" + k] = v
    per = []
    x = g("x")
    c = g("c")
    for b in range(8):
        m = dict(sh)
        m["x"] = np.ascontiguousarray(x[b])
        m["cT"] = np.ascontiguousarray(c[b].reshape(8, 128).T)
        per.append(m)
    return per


def kernel(**inputs):
    nc = build_program()
    in_maps = prep_inputs(inputs)
    res = run_bass_kernel_spmd(nc, in_maps, core_ids=list(range(8)))
    return np.stack([np.asarray(r["out"], dtype=np.float32) for r in res.results], axis=0)
```

```python
import math
from contextlib import ExitStack

import numpy as np
import ml_dtypes

import concourse.bass as bass
import concourse.mybir as mybir
from concourse.bass_utils import run_bass_kernel_spmd

F32 = mybir.dt.float32
BF16 = mybir.dt.bfloat16
AF = mybir.ActivationFunctionType
ALU = mybir.AluOpType
AX = mybir.AxisListType

S = 4096
D = 1024
NT = S // 128
NQ = S // 512
NEG = -30000.0
EPS = 1e-6
SLOPES = [2.0 ** (-8.0 * (i + 1) / 16) for i in range(16)]
LN_KSCALE = math.log(128 ** -0.5)
ZERO_CUT = 130.0

ENGS = ("pe", "act", "dve", "pool", "sp")
CENG = ("pe", "act", "dve", "pool")
N_DSEM = 90


class Tok:
    __slots__ = ("w", "r", "ds", "name")

    def __init__(self, name="", ds=None):
        self.w = None
        self.r = {}
        self.ds = ds
        self.name = name


class Prog:
    def __init__(self, nc, es):
        self.nc = nc
        self.q = {e: [] for e in ENGS}
        self.cnt = {e: 0 for e in CENG}
        self.seen = {e: {} for e in ENGS}
        self.sems = {}
        for e in CENG:
            self.sems[e] = es.enter_context(nc.semaphore("c_" + e))
        self.sems["bar"] = es.enter_context(nc.semaphore("c_bar"))
        self.bar = 0
        self.dsem_cnt = {}
        self.free = []
        for i in range(N_DSEM):
            k = "d%d" % i
            self.sems[k] = es.enter_context(nc.semaphore(k))
            self.dsem_cnt[k] = 0
            self.free.append(k)
        self.phase_ds = []

    def tok(self, name="", dma=False):
        ds = None
        if dma:
            ds = self.free.pop(0)
            self.phase_ds.append(ds)
        return Tok(name, ds)

    def _need(self, eng, ev):
        if ev is None:
            return
        key, val = ev
        if key == eng and eng == "pe":
            return
        if self.seen[eng].get(key, 0) >= val:
            return
        self.seen[eng][key] = val
        self.q[eng].append(("wait", key, val))

    def capture(self):
        self._cap = []

    def end_capture(self):
        lst, self._cap = self._cap, None
        return lst

    def replay_merged(self, lists):
        lists = [l for l in lists if l]
        pos = [0] * len(lists)
        while True:
            best, bf = None, None
            for i, l in enumerate(lists):
                if pos[i] < len(l):
                    f = pos[i] / len(l)
                    if bf is None or f < bf:
                        best, bf = i, f
            if best is None:
                break
            kind, args = lists[best][pos[best]]
            pos[best] += 1
            if kind == "op":
                self.op(*args)
            else:
                self.dma(*args)

    def op(self, eng, fn, reads=(), writes=()):
        if getattr(self, "_cap", None) is not None:
            self._cap.append(("op", (eng, fn, tuple(reads), tuple(writes))))
            return
        for t in reads:
            self._need(eng, t.w)
        for t in writes:
            self._need(eng, t.w)
            for k, v in t.r.items():
                self._need(eng, (k, v))
        self.cnt[eng] += 1
        n = self.cnt[eng]
        self.q[eng].append(("op", fn))
        for t in reads:
            if t.r.get(eng, 0) < n:
                t.r[eng] = n
        for t in writes:
            t.w = (eng, n)
            t.r = {}

    def dma(self, eng, fn, tok, load, reads=(), writes=()):
        assert tok.ds is not None, tok.name
        if getattr(self, "_cap", None) is not None:
            self._cap.append(("dma", (eng, fn, tok, load, tuple(reads), tuple(writes))))
            return
        self._need(eng, tok.w)
        if load:
            for k, v in tok.r.items():
                self._need(eng, (k, v))
        for t in reads:
            self._need(eng, t.w)
        for t in writes:
            self._need(eng, t.w)
            for k, v in t.r.items():
                self._need(eng, (k, v))
        self.dsem_cnt[tok.ds] += 16
        n = self.dsem_cnt[tok.ds]
        self.q[eng].append(("dma", fn, tok.ds))
        if load:
            tok.w = (tok.ds, n)
            tok.r = {}
        else:
            tok.r[tok.ds] = n
        for t in reads:
            t.r[tok.ds] = max(t.r.get(tok.ds, 0), n)
        for t in writes:
            t.w = (tok.ds, n)
            t.r = {}

    def barrier(self):
        for e in CENG:
            if self.cnt[e]:
                self._need("sp", (e, self.cnt[e]))
        for k, v in self.dsem_cnt.items():
            if v:
                self._need("sp", (k, v))
        self.bar += 1
        self.q["sp"].append(("inc", "bar", 1))
        for e in CENG:
            self.q[e].append(("wait", "bar", self.bar))
            for e2 in CENG:
                self.seen[e][e2] = max(self.seen[e].get(e2, 0), self.cnt[e2])
            for k, v in self.dsem_cnt.items():
                self.seen[e][k] = max(self.seen[e].get(k, 0), v)

    def end_phase(self):
        self.barrier()
        self.flush()
        self.free.extend(self.phase_ds)
        self.phase_ds = []

    def flush(self):
        nc = self.nc
        q = self.q
        sems = self.sems

        def emit(name, h):
            for it in q[name]:
                if it[0] == "wait":
                    h.wait_ge(sems[it[1]], it[2])
                elif it[0] == "op":
                    it[1](h).then_inc(sems[name], 1)
                elif it[0] == "dma":
                    it[1](h).then_inc(sems[it[2]], 16)
                elif it[0] == "inc":
                    h.sem_inc(sems[it[1]], it[2])

        with nc.Block() as block:
            @block.sync
            def _(h):
                emit("sp", h)

            @block.tensor
            def _(h):
                emit("pe", h)

            @block.scalar
            def _(h):
                emit("act", h)

            @block.vector
            def _(h):
                emit("dve", h)

            @block.gpsimd
            def _(h):
                emit("pool", h)
        self.q = {e: [] for e in ENGS}


class Rot:
    def __init__(self, items):
        self.items = items
        self.i = 0

    def next(self):
        it = self.items[self.i % len(self.items)]
        self.i += 1
        return it


def _bf(a):
    return np.ascontiguousarray(np.asarray(a, dtype=np.float32)).astype(ml_dtypes.bfloat16)


def make_consts():
    c = {}
    c["ident_f"] = np.eye(128, dtype=np.float32)
    c["ident_b"] = _bf(np.eye(128))
    c["ones_f"] = np.ones((128, 128), np.float32)
    key = np.arange(S)
    kaug_slc = np.zeros((64, S), np.float32)
    for j in range(63):
        kaug_slc[j] = (key // 64 == j)
    kaug_slc[63] = 1.0
    kaug_one = np.zeros((64, S), np.float32)
    kaug_one[63] = 1.0
    c["kaug_slc"] = _bf(kaug_slc)
    c["kaug_one"] = _bf(kaug_one)
    qq = np.arange(512, dtype=np.float64)
    c["vrow"] = _bf(np.stack([-8.0 * s * qq for s in SLOPES]))
    p = np.arange(128, dtype=np.float64)
    bt = np.zeros((128, 16, 32), np.float64)
    for h in range(16):
        for di in range(32):
            bt[:, h, di] = SLOPES[h] * (p + 128.0 * (di - 28))
    c["bias_tab"] = bt.reshape(128, 512).astype(np.float32)
    bc = np.zeros((128, 16, 8, 2), np.float64)
    for h in range(16):
        for t in range(8):
            for ct in range(2):
                bc[:, h, t, ct] = SLOPES[h] * (16.0 * (ct * 128 + p) + 31.0 - 512.0 * t)
    c["biasc_tab"] = bc.reshape(128, 256).astype(np.float32)
    cc = np.arange(256)
    tq = np.arange(S)
    ok = (16 * cc[:, None] + 31 <= tq[None, :]) & (cc[:, None] < 255)
    cm = np.where(ok, 0.0, NEG).astype(np.float32)
    c["cmask"] = _bf(cm.reshape(2, 128, S).transpose(1, 0, 2))
    kk = np.arange(128)[:, None]
    q2 = np.arange(128)[None, :]
    c["tri_causal"] = _bf(np.where(kk <= q2, 0.0, NEG))
    c["tri_band"] = _bf(np.where(kk > q2, 0.0, NEG))
    c["mask01"] = (kk <= q2).astype(np.float32)
    c0 = np.arange(256)[:, None] * 16
    s0 = np.arange(64)[None, :] * 64
    ov = np.clip(np.minimum(c0 + 32, s0 + 64) - np.maximum(c0, s0), 0, None) / 32.0
    ov[255] = 0.0
    c["ov"] = _bf(ov.reshape(2, 128, 64).transpose(1, 0, 2))
    blk = np.arange(64)[None, :]
    cur = (tq // 64)[:, None]
    forced = (blk == 0) | (blk == cur) | (blk == cur - 1)
    valid = blk <= cur
    vnf = (valid & ~forced).astype(np.float32)
    epsj = (64 - np.arange(64)).astype(np.float32)[None, :] * 1e-30
    fbig = (1e9 + 1e5 * np.arange(64)).astype(np.float32)[None, :]
    addc = np.where(forced, fbig, np.where(valid, epsj, -1.0)).astype(np.float32)

    def tm(a):
        return np.ascontiguousarray(a.reshape(32, 128, 64).transpose(1, 0, 2))

    c["sel_vnf"] = tm(vnf)
    c["sel_addc"] = tm(addc)
    c["sel_valid"] = tm(valid.astype(np.float32))
    return c


WA_PERM = np.concatenate([
    np.arange(0, 1024),
    np.arange(1024, 1536),
    np.arange(1536, 1792),
    np.arange(2048, 2304),
    np.arange(1792, 2048), np.arange(2304, 2560),
    np.arange(2608, 3632),
    np.arange(2560, 2608),
])
WB_PERM = np.concatenate(
    [np.arange(0, 2048), np.arange(2048, 4096)]
    + [np.concatenate([np.arange(4112 + i * 256, 4112 + (i + 1) * 256),
                       np.arange(6160 + i * 256, 6160 + (i + 1) * 256)]) for i in range(8)]
    + [np.arange(4096, 4112)])


def build_program(stop_after=99, dbg=False):
    nc = bass.Bass("TRN2", target_bir_lowering=False)
    consts = make_consts()

    def din(name, shape, dt=F32):
        return nc.dram_tensor(name, list(shape), dt, kind="ExternalInput").ap()

    def dscr(name, shape, dt):
        return nc.dram_tensor(name, list(shape), dt, kind=("ExternalOutput" if dbg else "Internal")).ap()

    x_d = din("x", [S, D])
    cT_d = din("cT", [128, 8])
    adaw_d = din("ada_w", [2, D, 3 * D])
    adabT_d = din("adabT", [128, 48])
    ngT_d = din("ngT", [128, 16])
    fing_d = din("fing", [1, D])
    wa_d = din("wa", [D, 3632])
    w1_d = din("cmp_w1", [2, 2048, 256])
    peT_d = din("peT", [64, 64])
    b1T_d = din("b1T", [128, 4])
    w2_d = din("cmp_w2", [2, 256, 64])
    b2T_d = din("b2T", [64, 2])
    woa_d = din("a_w_out", [D, D])
    wb_d = din("wb", [D, 8208])
    convwT_d = din("convwT", [128, 64])
    convbT_d = din("convbT", [128, 16])
    gateb_d = din("gateb", [1, 16])
    headg_d = din("headg", [1, 2048])
    wob_d = din("b_w_out", [2048, D])
    cd = {}
    for k, v in consts.items():
        cd[k] = din("k_" + k, v.shape, BF16 if v.dtype == ml_dtypes.bfloat16 else F32)
    out_d = nc.dram_tensor("out", [S, D], F32, kind="ExternalOutput").ap()

    qT_s = dscr("qT_s", [1024, S], BF16)
    kvT_s = dscr("kvT_s", [4, 256, S], BF16)
    vtok_s = dscr("vtok_s", [S, 512], BF16)
    zs_s = dscr("zs_s", [S, 1024], F32)
    y0_s = dscr("y0_s", [S, 1024], BF16)
    x1_s = dscr("x1_s", [S, D], F32)
    qkT1_s = dscr("qkT1_s", [16, 128, S], BF16)
    vtok1_s = dscr("vtok1_s", [S, 2048], BF16)
    ogz_s = dscr("ogz_s", [S, 2048], F32)

    es_all = ExitStack()
    P = Prog(nc, es_all)

    uid = {"n": 0}

    def sb(es, name, shape, dt):
        uid["n"] += 1
        return es.enter_context(nc.sbuf_tensor("s%d_%s" % (uid["n"], name), list(shape), dt))

    def ps(es, name, shape, dt=F32):
        uid["n"] += 1
        return es.enter_context(nc.psum_tensor("p%d_%s" % (uid["n"], name), list(shape), dt))

    def finish(dbg=None):
        P.end_phase()
        return nc

    ident_f = sb(es_all, "ident_f", [128, 128], F32)
    ident_b = sb(es_all, "ident_b", [128, 128], BF16)
    ones_f = sb(es_all, "ones_f", [128, 128], F32)
    cvec = sb(es_all, "cvec", [128, 4], F32)
    modT = sb(es_all, "modT", [128, 48], F32)
    gsc = sb(es_all, "gsc", [128, 16], F32)
    gate_bc = [sb(es_all, "gate_bc%d" % l, [128, D], F32) for l in range(2)]
    gate_sb = sb(es_all, "gate_sb", [128, NT, 48], F32)
    es_w1 = ExitStack()
    w1f = sb(es_w1, "w1f", [64, 32, 256], F32)
    t_w1f = Tok("w1f", P.free.pop(0))
    es_h = ExitStack()
    hT = sb(es_h, "hT", [128, 8, S], BF16)
    t_const = Tok("const", P.free.pop(0))
    t_cvec = P.tok("cvec")
    t_modT = P.tok("modT")
    t_gsc = P.tok("gsc")
    t_gbc = [P.tok("gbc0"), P.tok("gbc1")]
    t_gate = P.tok("gate_sb")
    t_hT = [P.tok("hT%d" % i) for i in range(NQ)]

    P.dma("sp", lambda e: e.dma_start(out=ident_f[:], in_=cd["ident_f"]), t_const, True)
    P.dma("sp", lambda e: e.dma_start(out=ident_b[:], in_=cd["ident_b"]), t_const, True)
    P.dma("sp", lambda e: e.dma_start(out=ones_f[:], in_=cd["ones_f"]), t_const, True)
    P.op("pool", lambda e: e.memset(cvec[:, 0:1], EPS), writes=[t_cvec])
    P.op("pool", lambda e: e.memset(cvec[:, 1:2], 1.0), writes=[t_cvec])
    P.op("pool", lambda e: e.memset(cvec[:, 2:3], LN_KSCALE), writes=[t_cvec])
    P.op("pool", lambda e: e.memset(cvec[:, 3:4], 0.0), writes=[t_cvec])

    with ExitStack() as es:
        cT = sb(es, "cT", [128, 8], F32)
        adab = sb(es, "adab", [128, 48], F32)
        ngT = sb(es, "ngT", [128, 16], F32)
        wst = [sb(es, "adaw%d" % i, [128, 8, 768], F32) for i in range(3)]
        dg = [sb(es, "dg%d" % i, [128, 128], F32) for i in range(2)]
        pmod = ps(es, "pmod", [128, 48], F32)
        pbc = [ps(es, "pbc%d" % i, [128, 512], F32) for i in range(2)]
        t_small = P.tok("small", dma=True)
        t_w = [P.tok("adaw%d" % i, dma=True) for i in range(3)]
        t_pmod = P.tok("pmod")
        t_dg = [P.tok("dg0"), P.tok("dg1")]
        t_pbc = [P.tok("pbc0"), P.tok("pbc1")]
        P.dma("sp", lambda e: e.dma_start(out=cT[:], in_=cT_d), t_small, True)
        P.dma("sp", lambda e: e.dma_start(out=adab[:], in_=adabT_d), t_small, True)
        P.dma("sp", lambda e: e.dma_start(out=ngT[:], in_=ngT_d), t_small, True)
        k = 0
        for l in range(2):
            for pc in range(4):
                w = wst[k % 3]
                tw = t_w[k % 3]
                k += 1
                src = adaw_d[l, :, pc * 768:(pc + 1) * 768].rearrange("(kc p) w -> p kc w", p=128)
                P.dma("sp", lambda e, w=w, src=src: e.dma_start(out=w[:], in_=src), tw, True)
                for jj in range(6):
                    col = l * 24 + pc * 6 + jj
                    for kc in range(8):
                        P.op("pe", lambda e, w=w, jj=jj, kc=kc, col=col: e.matmul(
                            pmod[:, col:col + 1], lhsT=w[:, kc, jj * 128:(jj + 1) * 128],
                            rhs=cT[:, kc:kc + 1], start=(kc == 0), stop=(kc == 7)),
                            reads=[tw, t_small], writes=[t_pmod])
        P.op("dve", lambda e: e.tensor_tensor(out=modT[:], in0=pmod[:], in1=adab[:], op=ALU.add),
             reads=[t_pmod, t_small], writes=[t_modT])
        for l in range(2):
            P.op("dve", lambda e, l=l: e.scalar_tensor_tensor(
                out=gsc[:, l * 8:(l + 1) * 8], in0=modT[:, l * 24 + 8:l * 24 + 16], scalar=1.0,
                in1=ngT[:, l * 8:(l + 1) * 8], op0=ALU.add, op1=ALU.mult),
                reads=[t_modT, t_small], writes=[t_gsc])
            for kc in range(8):
                d_, td = dg[kc % 2], t_dg[kc % 2]
                P.op("dve", lambda e, l=l, kc=kc, d_=d_: e.tensor_scalar(
                    out=d_[:], in0=ident_f[:], scalar1=modT[:, l * 24 + 16 + kc:l * 24 + 17 + kc],
                    scalar2=None, op0=ALU.mult), reads=[t_modT, t_const], writes=[td])
                P.op("pe", lambda e, kc=kc, d_=d_: e.matmul(
                    pbc[kc // 4][:, (kc % 4) * 128:(kc % 4 + 1) * 128], lhsT=ones_f[:], rhs=d_[:],
                    start=True, stop=True), reads=[td, t_const], writes=[t_pbc[kc // 4]])
            for hf in range(2):
                P.op("dve", lambda e, l=l, hf=hf: e.tensor_copy(
                    gate_bc[l][:, hf * 512:(hf + 1) * 512], pbc[hf][:]),
                    reads=[t_pbc[hf]], writes=[t_gbc[l]])
        P.end_phase()
    if stop_after == 0:
        return finish()

    nres = {"i": 0}

    def norm_res(es, full=True):
        R = {}
        k = nres["i"]
        nres["i"] += 1
        R["junk"] = Rot([(sb(es, "njunk%d_%d" % (k, i), [128, D], F32), P.tok("njunk")) for i in range(2 if full else 1)])
        R["stat"] = Rot([(sb(es, "nstat%d_%d" % (k, i), [128, 4], F32), P.tok("nstat")) for i in range(3)])
        if full:
            R["xn"] = Rot([(sb(es, "nxn%d_%d" % (k, i), [128, D], F32), P.tok("nxn")) for i in range(3)])
            R["ptr"] = Rot([(ps(es, "nptr%d_%d" % (k, i), [128, 512], F32), P.tok("nptr")) for i in range(2)])
        return R

    def rms_stats(xt_ap, t_xt, R):
        junk, t_junk = R["junk"].next()
        st, t_st = R["stat"].next()
        P.op("act", lambda e: e.activation(out=junk[:], in_=xt_ap, func=AF.Square, accum_out=st[:, 0:1]),
             reads=[t_xt], writes=[t_junk, t_st])
        P.op("act", lambda e: e.activation(out=st[:, 1:2], in_=st[:, 0:1], func=AF.Ln,
                                           scale=1.0 / D, bias=cvec[:, 0:1]),
             reads=[t_st, t_cvec], writes=[t_st])
        P.op("act", lambda e: e.activation(out=st[:, 2:3], in_=st[:, 1:2], func=AF.Exp, scale=-0.5),
             reads=[t_st], writes=[t_st])
        return st, t_st

    def norm_a(xt_ap, t_xt, R):
        st, t_st = rms_stats(xt_ap, t_xt, R)
        xn, t_xn = R["xn"].next()
        P.op("dve", lambda e: e.tensor_scalar(out=xn[:], in0=xt_ap, scalar1=st[:, 2:3], scalar2=None,
                                              op0=ALU.mult), reads=[t_xt, t_st], writes=[t_xn])
        return xn, t_xn

    def norm_b(l, tt, xn, t_xn, R):
        for hf in range(2):
            pt, t_pt = R["ptr"].next()
            for c4 in range(4):
                kc = hf * 4 + c4
                P.op("pe", lambda e, pt=pt, c4=c4, kc=kc: e.transpose(
                    pt[:, c4 * 128:(c4 + 1) * 128], xn[:, kc * 128:(kc + 1) * 128], ident_f[:]),
                    reads=[t_xn, t_const], writes=[t_pt])
            for c4 in range(4):
                kc = hf * 4 + c4
                P.op("act", lambda e, pt=pt, c4=c4, kc=kc: e.activation(
                    out=hT[:, kc, tt * 128:(tt + 1) * 128], in_=pt[:, c4 * 128:(c4 + 1) * 128],
                    func=AF.Identity, scale=gsc[:, l * 8 + kc:l * 8 + kc + 1],
                    bias=modT[:, l * 24 + kc:l * 24 + kc + 1]),
                    reads=[t_pt, t_gsc, t_modT], writes=[t_hT[tt // 4]])

    def norm_to_hT(l, tt, xt_ap, t_xt, R):
        xn, t_xn = norm_a(xt_ap, t_xt, R)
        norm_b(l, tt, xn, t_xn, R)

    def inproj(es, w_d, groups, fm_sink, tm_sink):
        wst = [sb(es, "wst%d" % i, [128, 8, 512], F32) for i in range(2)]
        wbf = [sb(es, "wbf%d" % i, [128, 8, 512], BF16) for i in range(2)]
        t_wst = [P.tok("wst%d" % i, dma=True) for i in range(2)]
        t_wbf = [P.tok("wbf%d" % i) for i in range(2)]
        pacc = Rot([(ps(es, "pacc%d" % i, [128, 512], F32), P.tok("pacc")) for i in range(4)])
        for gi, (kind, c0, width, meta) in enumerate(groups):
            w32, w16 = wst[gi % 2], wbf[gi % 2]
            tw32, tw16 = t_wst[gi % 2], t_wbf[gi % 2]
            src = w_d[:, c0:c0 + width].rearrange("(kc p) w -> p kc w", p=128)
            P.dma("sp", lambda e, w32=w32, src=src, width=width: e.dma_start(out=w32[:, :, 0:width], in_=src),
                  tw32, True)
            for kc in range(8):
                P.op("act", lambda e, w32=w32, w16=w16, kc=kc, width=width: e.activation(
                    out=w16[:, kc, 0:width], in_=w32[:, kc, 0:width], func=AF.Identity), reads=[tw32], writes=[tw16])
            if kind == "fm":
                for bi in range(width // 128):
                    for qt in range(NQ):
                        pa, t_pa = pacc.next()
                        for kc in range(8):
                            P.op("pe", lambda e, pa=pa, w16=w16, bi=bi, kc=kc, qt=qt: e.matmul(
                                pa[:], lhsT=w16[:, kc, bi * 128:(bi + 1) * 128],
                                rhs=hT[:, kc, qt * 512:(qt + 1) * 512], start=(kc == 0), stop=(kc == 7)),
                                reads=[tw16, t_hT[qt]], writes=[t_pa])
                        fm_sink(meta + bi, qt, pa, t_pa)
            else:
                for tt in range(NT):
                    pa, t_pa = pacc.next()
                    for kc in range(8):
                        P.op("pe", lambda e, pa=pa, w16=w16, kc=kc, tt=tt, width=width: e.matmul(
                            pa[:, 0:width], lhsT=hT[:, kc, tt * 128:(tt + 1) * 128],
                            rhs=w16[:, kc, 0:width], start=(kc == 0), stop=(kc == 7)),
                            reads=[tw16, t_hT[tt // 4]], writes=[t_pa])
                    tm_sink(meta, tt, pa, t_pa)

    with ExitStack() as es:
        R = norm_res(es)
        xin = Rot([(sb(es, "xin%d" % i, [128, D], F32), P.tok("xin%d" % i, dma=True)) for i in range(3)])
        xns = {}
        for tt in range(NT + 1):
            lists = []
            if tt < NT:
                xt, t_xt = xin.next()
                P.dma("sp", lambda e, xt=xt, tt=tt: e.dma_start(out=xt[:], in_=x_d[tt * 128:(tt + 1) * 128, :]),
                      t_xt, True)
                P.capture()
                xns[tt] = norm_a(xt[:], t_xt, R)
                lists.append(P.end_capture())
            if tt >= 1:
                P.capture()
                norm_b(0, tt - 1, *xns.pop(tt - 1), R)
                lists.append(P.end_capture())
            P.replay_merged(lists)
        P.end_phase()
    if stop_after == 1:
        return finish()

    with ExitStack() as es:
        frow = [sb(es, "frow%d" % i, [128, S], BF16) for i in range(2)]
        t_frow = [P.tok("frow%d" % i, dma=True) for i in range(2)]
        tstg_b = Rot([(sb(es, "tstb%d" % i, [128, 4, 512], BF16), P.tok("tstb%d" % i, dma=True)) for i in range(2)])
        tstg_f = Rot([(sb(es, "tstf%d" % i, [128, 4, 512], F32), P.tok("tstf%d" % i, dma=True)) for i in range(2)])
        FM_DEST = ([qT_s[i * 128:(i + 1) * 128, :] for i in range(8)]
                   + [kvT_s[j, i * 128:(i + 1) * 128, :] for j in range(4) for i in range(2)])

        def fm_sink(blk, qt, pa, t_pa):
            fr, tf = frow[blk % 2], t_frow[blk % 2]
            if qt % 2 == 0:
                P.op("act", lambda e: e.activation(out=fr[:, qt * 512:(qt + 1) * 512], in_=pa[:],
                                                   func=AF.Identity), reads=[t_pa], writes=[tf])
            else:
                P.op("dve", lambda e: e.tensor_copy(fr[:, qt * 512:(qt + 1) * 512], pa[:]),
                     reads=[t_pa], writes=[tf])
            if qt == NQ - 1:
                dst = FM_DEST[blk]
                P.dma("pool", lambda e: e.dma_start(out=dst, in_=fr[:]), tf, False)

        tm_cur = {}

        def tm_sink(meta, tt, pa, t_pa):
            if meta == "gates":
                P.op("act", lambda e: e.activation(out=gate_sb[:, tt, :], in_=pa[:, 0:48], func=AF.Sigmoid),
                     reads=[t_pa], writes=[t_gate])
                return
            if tt % 4 == 0:
                tm_cur["buf"] = tstg_b.next() if meta == "v" else tstg_f.next()
            st, t_s = tm_cur["buf"]
            if meta == "v":
                P.op("dve", lambda e: e.tensor_copy(st[:, tt % 4, :], pa[:]), reads=[t_pa], writes=[t_s])
                if tt % 4 == 3:
                    dst = vtok_s[(tt - 3) * 128:(tt + 1) * 128, :].rearrange("(t p) c -> p t c", p=128)
                    P.dma("pool", lambda e: e.dma_start(out=dst, in_=st[:]), t_s, False)
            else:
                zi = meta
                P.op("act", lambda e: e.activation(out=st[:, tt % 4, :], in_=pa[:], func=AF.Silu),
                     reads=[t_pa], writes=[t_s])
                if tt % 4 == 3:
                    dst = zs_s[(tt - 3) * 128:(tt + 1) * 128, zi * 512:(zi + 1) * 512].rearrange(
                        "(t p) c -> p t c", p=128)
                    P.dma("pool", lambda e: e.dma_start(out=dst, in_=st[:]), t_s, False)

        groups = [("fm", 0, 512, 0), ("fm", 512, 512, 4), ("fm", 1024, 512, 8), ("fm", 1536, 512, 12),
                  ("tm", 2048, 512, "v"), ("tm", 2560, 512, 0), ("tm", 3072, 512, 1), ("tm", 3584, 48, "gates")]
        inproj(es, wa_d, groups, fm_sink, tm_sink)
        P.dma("sp", lambda e: e.dma_start(out=w1f[:], in_=w1_d[0].rearrange("(l d) h -> d l h", d=64)), t_w1f, True)
        P.end_phase()
    es_h.close()
    if stop_after == 2:
        return finish()

    with ExitStack() as es:
        t_c3 = P.tok("c3", dma=True)
        cmask = sb(es, "cmask", [128, 2, S], BF16)
        tri_c = sb(es, "tri_c", [128, 128], BF16)
        tri_b = sb(es, "tri_b", [128, 128], BF16)
        bias_tab = sb(es, "bias_tab", [128, 512], F32)
        biasc_tab = sb(es, "biasc_tab", [128, 256], F32)
        ovt = sb(es, "ovt", [128, 2, 64], BF16)
        sel_vnf = sb(es, "sel_vnf", [128, 32, 64], F32)
        sel_addc = sb(es, "sel_addc", [128, 32, 64], F32)
        sel_valid = sb(es, "sel_valid", [128, 32, 64], F32)
        KC = sb(es, "KC", [128, 4, 256], BF16)
        VC = sb(es, "VC", [128, 4, 2, 65], BF16)
        t_KC = P.tok("KC", dma=True)
        t_VC = P.tok("VC")
        for g in range(4):
            P.dma("sp", lambda e, g=g: e.dma_start(out=KC[64:128, g, :], in_=cd["kaug_one"][:, 0:256]), t_KC, True)
        P.op("pool", lambda e: e.memset(VC[:], 1.0), writes=[t_VC])

        with ExitStack() as es2:
            w1b = [sb(es2, "w1b%d" % i, [64, 32, 256], BF16) for i in range(2)]
            kvs_ = sb(es2, "kvs", [64, 4, S], BF16)
            kvs = [kvs_, kvs_]
            kvd = sb(es2, "kvd", [64, 4, 16, 256], BF16)
            t_kvd = P.tok("kvd")
            pef = sb(es2, "pef", [64, 64], F32)
            peb = sb(es2, "peb", [64, 64], BF16)
            b1T = sb(es2, "b1T", [128, 4], F32)
            bias1 = sb(es2, "bias1", [128, 4], F32)
            w2f = sb(es2, "w2f", [128, 2, 2, 64], F32)
            w2b = sb(es2, "w2b", [128, 2, 2, 64], BF16)
            b2T = sb(es2, "b2T", [64, 2], F32)
            xh = sb(es2, "xh", [128, 256], F32)
            tt1 = sb(es2, "tt1", [128, 256], F32)
            sg = sb(es2, "sg", [128, 256], F32)
            gel = [sb(es2, "gel%d" % i, [128, 2, 256], BF16) for i in range(2)]
            vct = sb(es2, "vct", [64, 256], BF16)
            ph = [ps(es2, "ph%d" % i, [128, 256], F32) for i in range(2)]
            pb1 = ps(es2, "pb1", [128, 4], F32)
            pk = ps(es2, "pk", [64, 256], F32)
            pvt = ps(es2, "pvt", [128, 128], BF16)
            t_w1b = [P.tok("w1b0"), P.tok("w1b1")]
            t_kvs_ = P.tok("kvs", dma=True)
            t_kvs = [t_kvs_, t_kvs_]
            t_sm = P.tok("cmpsmall", dma=True)
            t_peb = P.tok("peb")
            t_bias1 = P.tok("bias1")
            t_w2b = P.tok("w2b")
            t_xh, t_tt1, t_sg = P.tok("xh"), P.tok("tt1"), P.tok("sg")
            t_gel = [P.tok("gel0"), P.tok("gel1")]
            t_vct = P.tok("vct")
            t_ph = [P.tok("ph0"), P.tok("ph1")]
            t_pb1, t_pk, t_pvt = P.tok("pb1"), P.tok("pk"), P.tok("pvt")
            P.dma("sp", lambda e: e.dma_start(out=pef[:], in_=peT_d), t_sm, True)
            P.dma("sp", lambda e: e.dma_start(out=b1T[:], in_=b1T_d), t_sm, True)
            P.dma("sp", lambda e: e.dma_start(out=b2T[:], in_=b2T_d), t_sm, True)
            for xx in range(2):
                P.dma("sp", lambda e, xx=xx: e.dma_start(
                    out=w2f[:, xx, :, :], in_=w2_d[xx].rearrange("(hh p) d -> p hh d", p=128)), t_sm, True)
            P.op("dve", lambda e: e.tensor_copy(peb[:], pef[:]), reads=[t_sm], writes=[t_peb])
            P.op("dve", lambda e: e.tensor_copy(w2b[:], w2f[:]), reads=[t_sm], writes=[t_w2b])
            P.op("pool", lambda e: e.memset(gel[0][:], 0.0), writes=[t_gel[0]])
            P.op("pool", lambda e: e.memset(gel[1][:], 0.0), writes=[t_gel[1]])

            def load_x(xx):
                if xx == 1:
                    P.dma("sp", lambda e: e.dma_start(
                        out=w1f[:], in_=w1_d[xx].rearrange("(l d) h -> d l h", d=64)), t_w1f, True)
                for l4 in range(4):
                    if l4 % 2 == 0:
                        P.op("act", lambda e, l4=l4: e.activation(
                            out=w1b[xx][:, l4 * 8:(l4 + 1) * 8, :], in_=w1f[:, l4 * 8:(l4 + 1) * 8, :],
                            func=AF.Identity), reads=[t_w1f], writes=[t_w1b[xx]])
                    else:
                        P.op("dve", lambda e, l4=l4: e.tensor_copy(
                            w1b[xx][:, l4 * 8:(l4 + 1) * 8, :], w1f[:, l4 * 8:(l4 + 1) * 8, :]),
                            reads=[t_w1f], writes=[t_w1b[xx]])
                P.dma("sp", lambda e: e.dma_start(
                    out=kvs[xx][:], in_=kvT_s[xx].rearrange("(g d) s -> d g s", d=64)), t_kvs[xx], True)
                for hh in range(2):
                    col = xx * 2 + hh
                    for l in range(32):
                        P.op("pe", lambda e, hh=hh, l=l, col=col: e.matmul(
                            pb1[:, col:col + 1], lhsT=w1b[xx][:, l, hh * 128:(hh + 1) * 128],
                            rhs=peb[:, xx * 32 + l:xx * 32 + l + 1], start=(l == 0), stop=(l == 31)),
                            reads=[t_w1b[xx], t_peb], writes=[t_pb1])
                P.op("dve", lambda e: e.tensor_tensor(out=bias1[:, 2 * xx:2 * xx + 2], in0=pb1[:, 2 * xx:2 * xx + 2],
                                                      in1=b1T[:, 2 * xx:2 * xx + 2], op=ALU.add),
                     reads=[t_pb1, t_sm], writes=[t_bias1])

            it = 0
            load_x(0)
            for dst, nm in ((cmask, "cmask"), (tri_c, "tri_causal"), (tri_b, "tri_band"), (bias_tab, "bias_tab"),
                            (biasc_tab, "biasc_tab"), (ovt, "ov"), (sel_vnf, "sel_vnf"), (sel_addc, "sel_addc"),
                            (sel_valid, "sel_valid")):
                P.dma("sp", lambda e, dst=dst, nm=nm: e.dma_start(out=dst[:], in_=cd[nm]), t_c3, True)

            for xx in range(2):
                if xx == 1:
                    load_x(1)
                for g in range(4):
                    P.op("dve", lambda e, g=g, xx=xx: e.tensor_copy(
                        kvd[:, g, :, :], kvs[xx][:, g, :].rearrange("d (c s) -> d s c", s=16)),
                        reads=[t_kvs[xx]], writes=[t_kvd])
                for g in range(4):
                    ge, t_ge = gel[(xx * 4 + g) % 2], t_gel[(xx * 4 + g) % 2]
                    for hh in range(2):
                        p_, t_p = ph[it % 2], t_ph[it % 2]
                        it += 1
                        for l in range(32):
                            P.op("pe", lambda e, p_=p_, xx=xx, g=g, hh=hh, l=l: e.matmul(
                                p_[:, 0:255], lhsT=w1b[xx][:, l, hh * 128:(hh + 1) * 128],
                                rhs=kvd[:, g, l % 16, l // 16:l // 16 + 255], start=(l == 0), stop=(l == 31)),
                                reads=[t_w1b[xx], t_kvd], writes=[t_p])
                        col = xx * 2 + hh
                        P.op("act", lambda e, p_=p_, col=col: e.activation(
                            out=xh[:, 0:255], in_=p_[:, 0:255], func=AF.Identity, bias=bias1[:, col:col + 1]),
                            reads=[t_p, t_bias1], writes=[t_xh])
                        P.op("dve", lambda e: e.tensor_tensor(out=tt1[:, 0:255], in0=xh[:, 0:255],
                                                              in1=xh[:, 0:255], op=ALU.mult),
                             reads=[t_xh], writes=[t_tt1])
                        P.op("dve", lambda e: e.tensor_scalar(out=tt1[:, 0:255], in0=tt1[:, 0:255],
                                                              scalar1=0.044715, scalar2=1.0,
                                                              op0=ALU.mult, op1=ALU.add),
                             reads=[t_tt1], writes=[t_tt1])
                        P.op("dve", lambda e: e.tensor_tensor(out=tt1[:, 0:255], in0=tt1[:, 0:255],
                                                              in1=xh[:, 0:255], op=ALU.mult),
                             reads=[t_tt1, t_xh], writes=[t_tt1])
                        P.op("act", lambda e: e.activation(out=sg[:, 0:255], in_=tt1[:, 0:255], func=AF.Sigmoid,
                                                           scale=1.5957691216057308),
                             reads=[t_tt1], writes=[t_sg])
                        P.op("dve", lambda e, ge=ge, hh=hh: e.tensor_tensor(
                            out=ge[:, hh, 0:255], in0=xh[:, 0:255], in1=sg[:, 0:255], op=ALU.mult),
                            reads=[t_xh, t_sg], writes=[t_ge])
                    for hh in range(2):
                        P.op("pe", lambda e, ge=ge, xx=xx, hh=hh: e.matmul(
                            pk[:], lhsT=w2b[:, xx, hh, :], rhs=ge[:, hh, :], start=(hh == 0), stop=(hh == 1)),
                            reads=[t_ge, t_w2b], writes=[t_pk])
                    if xx == 0:
                        P.op("act", lambda e, g=g: e.activation(
                            out=KC[0:64, g, :], in_=pk[:], func=AF.Identity, bias=b2T[:, 0:1]),
                            reads=[t_pk, t_sm], writes=[t_KC])
                    else:
                        P.op("act", lambda e: e.activation(
                            out=vct[:], in_=pk[:], func=AF.Identity, bias=b2T[:, 1:2]),
                            reads=[t_pk, t_sm], writes=[t_vct])
                        for ct in range(2):
                            P.op("pe", lambda e, ct=ct: e.transpose(
                                pvt[:, 0:64], vct[:, ct * 128:(ct + 1) * 128], ident_b[0:64, 0:64]),
                                reads=[t_vct, t_const], writes=[t_pvt])
                            P.op("dve", lambda e, g=g, ct=ct: e.tensor_copy(VC[:, g, ct, 0:64], pvt[:, 0:64]),
                                 reads=[t_pvt], writes=[t_VC])
            P.barrier()
            P.flush()
        if stop_after == 2.5:
            return finish()

        KS = [sb(es, "KS%d" % i, [128, S], BF16) for i in range(2)]
        KW = [sb(es, "KW%d" % i, [128, S], BF16) for i in range(2)]
        VS = [sb(es, "VS%d" % i, [128, 32, 65], BF16) for i in range(2)]
        VW = [sb(es, "VW%d" % i, [128, 32, 65], BF16) for i in range(2)]
        t_KS = [P.tok("KS%d" % i, dma=True) for i in range(2)]
        t_KW = [P.tok("KW%d" % i, dma=True) for i in range(2)]
        t_VS = [P.tok("VS%d" % i, dma=True) for i in range(2)]
        t_VW = [P.tok("VW%d" % i, dma=True) for i in range(2)]
        for i in range(2):
            P.op("pool", lambda e, i=i: e.memset(VS[i][:], 1.0), writes=[t_VS[i]])
            P.op("pool", lambda e, i=i: e.memset(VW[i][:], 1.0), writes=[t_VW[i]])
            P.dma("sp", lambda e, i=i: e.dma_start(out=KS[i][64:128, :], in_=cd["kaug_slc"]), t_KS[i], True)
            P.dma("sp", lambda e, i=i: e.dma_start(out=KW[i][64:128, :], in_=cd["kaug_one"]), t_KW[i], True)
        QA = [[sb(es, "QA%d_%d" % (hh, i), [128, 512], BF16) for i in range(2)] for hh in range(4)]
        t_QA = [[P.tok("QA%d_%d" % (hh, i), dma=True) for i in range(2)] for hh in range(4)]
        for hh in range(4):
            for i in range(2):
                P.op("pool", lambda e, hh=hh, i=i: e.memset(QA[hh][i][:], 0.0), writes=[t_QA[hh][i]])
        resA = {"PT": Rot([(sb(es, "PTA%d" % i, [128, 512], BF16), P.tok("PTA")) for i in range(2)]),
                "Sp": Rot([(ps(es, "SpA%d" % i, [128, 512], F32), P.tok("SpA")) for i in range(1)]),
                "Op": Rot([(ps(es, "OpA%d" % i, [128, 4, 128], F32), P.tok("OpA")) for i in range(1)])}
        resB = {"PT": Rot([(sb(es, "PTB%d" % i, [128, 512], BF16), P.tok("PTB")) for i in range(4)]),
                "Sp": Rot([(ps(es, "SpB%d" % i, [128, 512], F32), P.tok("SpB")) for i in range(2)]),
                "Op": Rot([(ps(es, "OpB%d" % i, [128, 4, 128], F32), P.tok("OpB")) for i in range(2)])}
        IMPp = ps(es, "IMPp", [128, 4, 64], F32)
        t_IMP = P.tok("IMP")
        negTp = ps(es, "negTp", [128, 512], BF16)
        t_negT = P.tok("negT")
        yacc = [sb(es, "yacc%d" % i, [128, 4, 256], F32) for i in range(2)]
        t_yacc = [P.tok("yacc0"), P.tok("yacc1")]
        ztile = [sb(es, "ztile%d" % i, [128, 4, 256], F32) for i in range(2)]
        t_zt = [P.tok("zt%d" % i, dma=True) for i in range(2)]
        ybf = [sb(es, "ybf%d" % i, [128, 4, 256], BF16) for i in range(2)]
        t_ybf = [P.tok("ybf%d" % i, dma=True) for i in range(2)]
        sc = sb(es, "sc", [128, 4, 64], F32)
        t_sc = P.tok("sc")
        for nm, r_ in (("A", resA), ("B", resB)):
            r_["small"] = Rot([(sb(es, "sm%s%d" % (nm, i), [128, 16], F32), P.tok("sm")) for i in range(3)])
            r_["tmpo"] = Rot([(sb(es, "tmpo%s%d" % (nm, i), [128, 4, 64], F32), P.tok("tmpo")) for i in range(2)])
        s1 = sb(es, "s1", [128, 64], F32)
        s2 = sb(es, "s2", [128, 64], F32)
        m8 = sb(es, "m8", [128, 16], F32)
        negpad = sb(es, "negpad", [128, 128], BF16)
        t_s1, t_s2, t_m8, t_negpad = P.tok("s1"), P.tok("s2"), P.tok("m8"), P.tok("negpad")
        P.op("pool", lambda e: e.memset(negpad[:], 0.0), writes=[t_negpad])
        zb = sb(es, "zb", [128, 512], BF16)
        t_zb = P.tok("zb")
        P.op("pool", lambda e: e.memset(zb[:], 0.0), writes=[t_zb])

        def zero_bank(ap2d, t_bank, n):
            P.op("pe", lambda e: e.matmul(ap2d, lhsT=zb[:, 0:128], rhs=zb[:, 0:n], start=True, stop=False),
                 reads=[t_zb], writes=[t_bank])

        def s_tile(res, branch, h, t, Q, t_Q, Ktile, t_K, kcol, col0, ncol, mask, bias_ap, t_bias):
            sp, t_sp = res["Sp"].next()
            c1 = col0 + ncol
            P.op("pe", lambda e: e.matmul(sp[:, col0:c1], lhsT=Ktile[:, kcol:kcol + 128], rhs=Q[:, col0:c1],
                                          start=True, stop=(mask is None)),
                 reads=[t_Q, t_K], writes=[t_sp])
            if mask is not None:
                mcol, mrhs, t_m = mask
                w = mrhs.shape[-1] if hasattr(mrhs, "shape") else 128
                P.op("pe", lambda e: e.matmul(sp[:, mcol:mcol + w], lhsT=ident_b[:], rhs=mrhs,
                                              start=False, stop=True),
                     reads=[t_const, t_m], writes=[t_sp])
            pt, t_pt = res["PT"].next()
            P.op("act", lambda e: e.activation(out=pt[:, col0:c1], in_=sp[:, col0:c1], func=AF.Exp,
                                               scale=0.125, bias=bias_ap),
                 reads=[t_sp, t_bias], writes=[t_pt])
            return pt, t_pt

        def finalize(res, br, h, hh, t, op_, t_op, ya, t_ya, first):
            sm, t_sm_ = res["small"].next()
            gcol = br * 16 + h
            P.op("dve", lambda e: e.tensor_scalar(out=sm[:, 0:4], in0=op_[:, :, 64], scalar1=1e-30,
                                                  scalar2=None, op0=ALU.max),
                 reads=[t_op], writes=[t_sm_])
            P.op("dve", lambda e: e.reciprocal(out=sm[:, 4:8], in_=sm[:, 0:4]), reads=[t_sm_], writes=[t_sm_])
            P.op("dve", lambda e: e.tensor_tensor(out=sm[:, 8:12], in0=sm[:, 4:8],
                                                  in1=gate_sb[:, 4 * t:4 * t + 4, gcol], op=ALU.mult),
                 reads=[t_sm_, t_gate], writes=[t_sm_])
            fb = sm[:, 8:12].unsqueeze(2).broadcast_to([128, 4, 64])
            if first:
                P.op("dve", lambda e: e.tensor_tensor(out=ya[:, :, hh * 64:(hh + 1) * 64], in0=op_[:, :, 0:64],
                                                      in1=fb, op=ALU.mult),
                     reads=[t_op, t_sm_], writes=[t_ya])
            else:
                to, t_to = res["tmpo"].next()
                P.op("dve", lambda e: e.tensor_tensor(out=to[:], in0=op_[:, :, 0:64], in1=fb, op=ALU.mult),
                     reads=[t_op, t_sm_], writes=[t_to])
                P.op("dve", lambda e: e.tensor_tensor(out=ya[:, :, hh * 64:(hh + 1) * 64],
                                                      in0=ya[:, :, hh * 64:(hh + 1) * 64], in1=to[:], op=ALU.add),
                     reads=[t_to, t_ya], writes=[t_ya])
            return sm, t_sm_

        class Pipe:
            def __init__(self, res, depth):
                self.pending = []
                self.depth = depth
                self.res = res

            def push(self, s_args, pv_fn, post_fn=None):
                pt, t_pt = s_tile(self.res, *s_args)
                self.pending.append((pt, t_pt, pv_fn, post_fn))
                while len(self.pending) > self.depth:
                    self._pop()

            def _pop(self):
                pt, t_pt, pv_fn, post_fn = self.pending.pop(0)
                pv_fn(pt, t_pt)
                if post_fn is not None:
                    post_fn()

            def flush(self):
                while self.pending:
                    self._pop()

        pipeA = Pipe(resA, 1)
        pipeB = Pipe(resB, 2)

        def load_group(g):
            gi = g % 2
            P.dma("sp", lambda e: e.dma_start(
                out=KS[gi][0:64, :], in_=kvT_s[2, g * 64:(g + 1) * 64, :]), t_KS[gi], True)
            P.dma("sp", lambda e: e.dma_start(
                out=KW[gi][0:64, :], in_=kvT_s[3, g * 64:(g + 1) * 64, :]), t_KW[gi], True)
            for half in range(2):
                P.dma("sp", lambda e, half=half: e.dma_start(
                    out=VS[gi][:, half * 16:(half + 1) * 16, 0:64],
                    in_=vtok_s[half * 2048:(half + 1) * 2048, g * 64:(g + 1) * 64].rearrange(
                        "(kt p) d -> p kt d", p=128)), t_VS[gi], True)
                P.dma("sp", lambda e, half=half: e.dma_start(
                    out=VW[gi][:, half * 16:(half + 1) * 16, 0:64],
                    in_=vtok_s[half * 2048:(half + 1) * 2048, 256 + g * 64:256 + (g + 1) * 64].rearrange(
                        "(kt p) d -> p kt d", p=128)), t_VW[gi], True)

        def stage_A(u, g, t):
            par = u % 2
            q0 = t * 512
            ya, t_ya = yacc[par], t_yacc[par]
            for hh in range(4):
                h = g * 4 + hh
                P.dma("sp", lambda e, hh=hh, h=h: e.dma_start(
                    out=QA[hh][par][0:64, :], in_=qT_s[h * 64:(h + 1) * 64, q0:q0 + 512]),
                    t_QA[hh][par], True)
                P.dma("sp", lambda e, hh=hh, h=h: e.dma_start(
                    out=QA[hh][par][127:128, :], in_=cd["vrow"][h:h + 1, :]), t_QA[hh][par], True)
            zt, t_z = ztile[par], t_zt[par]
            P.dma("sp", lambda e: e.dma_start(
                out=zt[:], in_=zs_s[t * 512:(t + 1) * 512, g * 256:(g + 1) * 256].rearrange(
                    "(s p) c -> p s c", p=128)), t_z, True)
            ncts = 2 if t >= 4 else 1
            for hh in range(4):
                h = g * 4 + hh
                Q, t_Q = QA[hh][par], t_QA[hh][par]
                op_, t_op = resA["Op"].next()
                for ct in range(ncts):
                    bcol = (h * 8 + t) * 2 + ct

                    def pv(pt, t_pt, op_=op_, t_op=t_op, ct=ct):
                        if ct == 0:
                            zero_bank(op_[:].rearrange("p a b -> p (a b)"), t_op, 512)
                            zero_bank(IMPp[:].rearrange("p a b -> p (a b)"), t_IMP, 256)
                        for qs in range(4):
                            P.op("pe", lambda e, qs=qs: e.matmul(
                                op_[:, qs, 0:65], lhsT=pt[:, qs * 128:(qs + 1) * 128], rhs=VC[:, g, ct, :],
                                start=False, stop=(ct == ncts - 1 and qs == 3)),
                                reads=[t_pt, t_VC], writes=[t_op])
                            P.op("pe", lambda e, qs=qs: e.matmul(
                                IMPp[:, qs, :], lhsT=pt[:, qs * 128:(qs + 1) * 128], rhs=ovt[:, ct, :],
                                start=False, stop=(ct == ncts - 1 and qs == 3)),
                                reads=[t_pt, t_c3], writes=[t_IMP])

                    def post(op_=op_, t_op=t_op, h=h, hh=hh):
                        sm, t_sm_ = finalize(resA, 0, h, hh, t, op_, t_op, ya, t_ya, True)
                        rb = sm[:, 4:8].unsqueeze(2).broadcast_to([128, 4, 64])
                        if hh == 0:
                            P.op("dve", lambda e: e.tensor_tensor(out=sc[:], in0=IMPp[:], in1=rb, op=ALU.mult),
                                 reads=[t_IMP, t_sm_], writes=[t_sc])
                        else:
                            to, t_to = resA["tmpo"].next()
                            P.op("dve", lambda e: e.tensor_tensor(out=to[:], in0=IMPp[:], in1=rb, op=ALU.mult),
                                 reads=[t_IMP, t_sm_], writes=[t_to])
                            P.op("dve", lambda e: e.tensor_tensor(out=sc[:], in0=sc[:], in1=to[:], op=ALU.add),
                                 reads=[t_to, t_sc], writes=[t_sc])

                    pipeA.push((0, h, t, Q, t_Q, KC[:, g, :], t_KC, ct * 128, 0, 512,
                               (0, cmask[:, ct, q0:q0 + 512], t_c3), biasc_tab[:, bcol:bcol + 1], t_c3),
                              pv, post if ct == ncts - 1 else None)
            pipeA.flush()
            for qs in range(4):
                tt = 4 * t + qs
                P.op("dve", lambda e, qs=qs, tt=tt: e.tensor_tensor(out=s1[:], in0=sc[:, qs, :],
                                                                    in1=sel_vnf[:, tt, :], op=ALU.mult),
                     reads=[t_sc, t_c3], writes=[t_s1])
                P.op("dve", lambda e, tt=tt: e.tensor_tensor(out=s1[:], in0=s1[:], in1=sel_addc[:, tt, :],
                                                             op=ALU.add),
                     reads=[t_s1, t_c3], writes=[t_s1])
                P.op("dve", lambda e: e.max(out=m8[:, 0:8], in_=s1[:]), reads=[t_s1], writes=[t_m8])
                P.op("dve", lambda e: e.tensor_scalar(out=s2[:], in0=s1[:], scalar1=m8[:, 7:8],
                                                      scalar2=None, op0=ALU.is_ge),
                     reads=[t_s1, t_m8], writes=[t_s2])
                P.op("dve", lambda e: e.scalar_tensor_tensor(out=s2[:], in0=s2[:], scalar=-3.0e9, in1=s1[:],
                                                             op0=ALU.mult, op1=ALU.add),
                     reads=[t_s1, t_s2], writes=[t_s2])
                P.op("dve", lambda e: e.max(out=m8[:, 8:16], in_=s2[:]), reads=[t_s2], writes=[t_m8])
                P.op("dve", lambda e: e.tensor_scalar(out=s2[:], in0=s1[:], scalar1=m8[:, 15:16],
                                                      scalar2=None, op0=ALU.is_ge),
                     reads=[t_s1, t_m8], writes=[t_s2])
                P.op("dve", lambda e, tt=tt: e.tensor_tensor(out=s2[:], in0=s2[:], in1=sel_valid[:, tt, :],
                                                             op=ALU.mult),
                     reads=[t_s2, t_c3], writes=[t_s2])
                P.op("dve", lambda e: e.tensor_scalar(out=negpad[:, 64:127], in0=s2[:, 0:63], scalar1=1.0,
                                                      scalar2=-NEG, op0=ALU.subtract, op1=ALU.mult),
                     reads=[t_s2], writes=[t_negpad])
                P.op("pe", lambda e, qs=qs: e.transpose(negTp[:, qs * 128:(qs + 1) * 128], negpad[:], ident_b[:]),
                     reads=[t_negpad, t_const], writes=[t_negT])
            for hh in range(4):
                h = g * 4 + hh
                P.op("dve", lambda e, hh=hh: e.tensor_copy(QA[hh][par][64:127, :], negTp[64:127, :]),
                     reads=[t_negT], writes=[t_QA[hh][par]])

        def stage_B(u, g, t):
            par = u % 2
            gi = g % 2
            ya, t_ya = yacc[par], t_yacc[par]
            zt, t_z = ztile[par], t_zt[par]
            for hh in range(4):
                h = g * 4 + hh
                Q, t_Q = QA[hh][par], t_QA[hh][par]
                for br, Kt, t_K, Vt, t_V in ((1, KS[gi], t_KS[gi], VS[gi], t_VS[gi]),
                                             (2, KW[gi], t_KW[gi], VW[gi], t_VW[gi])):
                    tiles = []
                    if br == 1:
                        for kt in range(0, 4 * t):
                            md = (4 * t - kt - 1) * 128 + 1
                            if SLOPES[h] * md >= ZERO_CUT:
                                continue
                            tiles.append((kt, 0, 512, None))
                    else:
                        for i in (3, 2, 1, 0):
                            kt = 4 * t - 4 + i
                            md = 385 - 128 * i if i >= 1 else 385
                            if kt >= 0 and SLOPES[h] * md < ZERO_CUT:
                                tiles.append((kt, 0, 128 * (i + 1), (128 * i, tri_b[:], t_c3)))
                    for i2 in range(4):
                        tiles.append((4 * t + i2, 128 * i2, 512 - 128 * i2, (128 * i2, tri_c[:], t_c3)))
                    last = {}
                    for ti, (kt, col0, ncol, mask) in enumerate(tiles):
                        for qs in range(col0 // 128, (col0 + ncol) // 128):
                            last[qs] = ti
                    op_, t_op = resB["Op"].next()
                    for ti, (kt, col0, ncol, mask) in enumerate(tiles):
                        bcol = h * 32 + (kt - 4 * t + 28)

                        def pv(pt, t_pt, op_=op_, t_op=t_op, ti=ti, kt=kt, col0=col0, ncol=ncol, Vt=Vt, t_V=t_V,
                               last=last, tiles=tiles):
                            if ti == 0:
                                zero_bank(op_[:].rearrange("p a b -> p (a b)"), t_op, 512)
                            for qs in range(col0 // 128, (col0 + ncol) // 128):
                                P.op("pe", lambda e, qs=qs: e.matmul(
                                    op_[:, qs, 0:65], lhsT=pt[:, qs * 128:(qs + 1) * 128], rhs=Vt[:, kt, :],
                                    start=False, stop=(ti == len(tiles) - 1 and qs == 3)),
                                    reads=[t_pt, t_V], writes=[t_op])

                        def post(op_=op_, t_op=t_op, br=br, h=h, hh=hh):
                            finalize(resB, br, h, hh, t, op_, t_op, ya, t_ya, False)

                        pipeB.push((br, h, t, Q, t_Q, Kt, t_K, kt * 128, col0, ncol, mask,
                                   bias_tab[:, bcol:bcol + 1], t_c3), pv,
                                  post if ti == len(tiles) - 1 else None)
            pipeB.flush()
            yb, t_yb = ybf[par], t_ybf[par]
            P.op("dve", lambda e: e.tensor_tensor(out=yb[:], in0=ya[:], in1=zt[:], op=ALU.mult),
                 reads=[t_ya, t_z], writes=[t_yb])
            P.dma("pool", lambda e: e.dma_start(
                out=y0_s[t * 512:(t + 1) * 512, g * 256:(g + 1) * 256].rearrange("(s p) c -> p s c", p=128),
                in_=yb[:]), t_yb, False)

        units = [(g, t) for g in range(4) for t in range(NQ)]
        for u in range(len(units) + 1):
            lists = []
            if u < len(units):
                g, t = units[u]
                if t == 0:
                    load_group(g)
                P.capture()
                stage_A(u, g, t)
                lists.append(P.end_capture())
            if u > 0:
                P.capture()
                stage_B(u - 1, *units[u - 1])
                lists.append(P.end_capture())
            P.replay_merged(lists)
        P.end_phase()
    es_w1.close()
    if stop_after == 3:
        return finish()


    es_g = ExitStack()
    ipre = sb(es_g, "ipre", [128, NT, 8], F32)
    logf = sb(es_g, "logf", [128, NT, 8], F32)
    wo1 = sb(es_g, "wo1", [128, 16, D], BF16)
    t_wo1 = Tok("wo1", P.free.pop(0))
    es_h = ExitStack()
    hT = sb(es_h, "hT_l1", [128, 8, S], BF16)
    t_ipre, t_logf = P.tok("ipre"), P.tok("logf")
    with ExitStack() as es:
        R = norm_res(es)
        wo = sb(es, "wo", [128, 8, D], BF16)
        t_wo = P.tok("wo", dma=True)
        P.dma("pool", lambda e: e.dma_start(out=wo[:], in_=woa_d.rearrange("(kc p) n -> p kc n", p=128)), t_wo, True)
        yin = Rot([(sb(es, "yin%d" % i, [128, D], BF16), P.tok("yin%d" % i, dma=True)) for i in range(3)])
        xin = Rot([(sb(es, "xin4_%d" % i, [128, D], F32), P.tok("xin4_%d" % i, dma=True)) for i in range(3)])
        x1t = Rot([(sb(es, "x1t%d" % i, [128, D], F32), P.tok("x1t%d" % i, dma=True)) for i in range(3)])
        yT = Rot([(sb(es, "yT%d" % i, [128, 8, 128], BF16), P.tok("yT")) for i in range(2)])
        ytp = ps(es, "ytp", [128, D], BF16)
        t_ytp = P.tok("ytp")
        po = [ps(es, "po%d" % i, [128, 512], F32) for i in range(2)]
        t_po = [P.tok("po0"), P.tok("po1")]
        def stage_X(tt):
            yt, t_yt = yin.next()
            xt, t_xt = xin.next()
            x1, t_x1 = x1t.next()
            yTt, t_yT = yT.next()
            P.dma("sp", lambda e, yt=yt, tt=tt: e.dma_start(out=yt[:], in_=y0_s[tt * 128:(tt + 1) * 128, :]),
                  t_yt, True)
            P.dma("sp", lambda e, xt=xt, tt=tt: e.dma_start(out=xt[:], in_=x_d[tt * 128:(tt + 1) * 128, :]),
                  t_xt, True)
            for kc in range(8):
                P.op("pe", lambda e, yt=yt, kc=kc: e.transpose(
                    ytp[:, kc * 128:(kc + 1) * 128], yt[:, kc * 128:(kc + 1) * 128], ident_b[:]),
                    reads=[t_yt, t_const], writes=[t_ytp])
            P.op("act", lambda e, yTt=yTt: e.activation(
                out=yTt[:, 0:4, :], in_=ytp[:, 0:512].rearrange("p (a b) -> p a b", b=128), func=AF.Identity),
                reads=[t_ytp], writes=[t_yT])
            P.op("act", lambda e, yTt=yTt: e.activation(
                out=yTt[:, 4:8, :], in_=ytp[:, 512:1024].rearrange("p (a b) -> p a b", b=128), func=AF.Identity),
                reads=[t_ytp], writes=[t_yT])
            for hf in range(2):
                for kc in range(8):
                    P.op("pe", lambda e, yTt=yTt, hf=hf, kc=kc: e.matmul(
                        po[hf][:], lhsT=yTt[:, kc, :], rhs=wo[:, kc, hf * 512:(hf + 1) * 512],
                        start=(kc == 0), stop=(kc == 7)), reads=[t_yT, t_wo], writes=[t_po[hf]])
                P.op("dve", lambda e, x1=x1, hf=hf: e.tensor_tensor(
                    out=x1[:, hf * 512:(hf + 1) * 512], in0=po[hf][:], in1=gate_bc[0][:, hf * 512:(hf + 1) * 512],
                    op=ALU.mult), reads=[t_po[hf], t_gbc[0]], writes=[t_x1])
            P.op("dve", lambda e, x1=x1, xt=xt: e.tensor_tensor(out=x1[:], in0=x1[:], in1=xt[:], op=ALU.add),
                 reads=[t_xt, t_x1], writes=[t_x1])
            P.dma("pool", lambda e, x1=x1, tt=tt: e.dma_start(out=x1_s[tt * 128:(tt + 1) * 128, :], in_=x1[:]),
                  t_x1, False)
            return x1, t_x1

        x1s, xns = {}, {}
        for tt in range(NT + 2):
            lists = []
            if tt < NT:
                P.capture()
                x1s[tt] = stage_X(tt)
                lists.append(P.end_capture())
            if 0 <= tt - 1 < NT:
                P.capture()
                x1, t_x1 = x1s.pop(tt - 1)
                xns[tt - 1] = norm_a(x1[:], t_x1, R)
                lists.append(P.end_capture())
            if 0 <= tt - 2 < NT:
                P.capture()
                norm_b(1, tt - 2, *xns.pop(tt - 2), R)
                lists.append(P.end_capture())
            P.replay_merged(lists)
        P.end_phase()
    if stop_after == 4:
        return finish()

    with ExitStack() as es:
        frow_ = sb(es, "frow5", [128, S], BF16)
        t_frow_ = P.tok("frow5", dma=True)
        frow = [frow_, frow_]
        t_frow = [t_frow_, t_frow_]
        raw = Rot([(sb(es, "raw%d" % i, [128, 515], F32), P.tok("raw"), P.tok("rawc")) for i in range(2)])
        acc = Rot([(sb(es, "acc%d" % i, [128, 512], F32), P.tok("acc")) for i in range(2)])
        convw = sb(es, "convw", [128, 64], F32)
        convb = sb(es, "convb", [128, 16], F32)
        gb_bc = sb(es, "gb_bc", [128, 16], F32)
        t_cv = P.tok("cv", dma=True)
        P.dma("sp", lambda e: e.dma_start(out=convw[:], in_=convwT_d), t_cv, True)
        P.dma("sp", lambda e: e.dma_start(out=convb[:], in_=convbT_d), t_cv, True)
        P.dma("sp", lambda e: e.dma_start(out=gb_bc[:], in_=bass.AP(gateb_d.tensor, 0, [[0, 128], [1, 16]])), t_cv, True)
        tstb = Rot([(sb(es, "tstb5_%d" % i, [128, 4, 512], BF16), P.tok("tstb5_%d" % i, dma=True)) for i in range(2)])
        tsto = Rot([(sb(es, "tsto%d" % i, [128, 4, 256], F32), P.tok("tsto%d" % i, dma=True)) for i in range(2)])
        z1 = Rot([(sb(es, "z1_%d" % i, [128, 256], F32), P.tok("z1")) for i in range(1)])
        gtmp = Rot([(sb(es, "gtmp%d" % i, [128, 24], F32), P.tok("gtmp")) for i in range(2)])
        for kh in range(4):
            P.dma("pool", lambda e, kh=kh: e.dma_start(
                out=wo1[:, kh * 4:(kh + 1) * 4, :], in_=wob_d[kh * 512:(kh + 1) * 512, :].rearrange(
                    "(kc p) n -> p kc n", p=128)), t_wo1, True)
        hg_bc5 = sb(es, "hg_bc5", [128, 2048], F32)
        P.dma("sp", lambda e: e.dma_start(out=hg_bc5[:], in_=bass.AP(headg_d.tensor, 0, [[0, 128], [1, 2048]])),
              t_cv, True)
        sgt = Rot([(sb(es, "sgt%d" % i, [128, 512], F32), P.tok("sgt")) for i in range(2)])
        st5 = {"prev": None}

        def fm_sink5(blk, qt, pa, t_pa):
            rw, t_rw, t_rwc = raw.next()
            ac, t_ac = acc.next()
            fr, tf = frow[blk % 2], t_frow[blk % 2]
            P.op("act", lambda e: e.activation(out=rw[:, 3:515], in_=pa[:], func=AF.Identity),
                 reads=[t_pa], writes=[t_rw])
            P.op("act", lambda e: e.activation(out=ac[:], in_=pa[:], func=AF.Identity,
                                               scale=convw[:, blk * 4 + 3:blk * 4 + 4]),
                 reads=[t_pa, t_cv], writes=[t_ac])
            if st5.get("pend"):
                st5.pop("pend")()
            if qt == 0:
                P.op("dve", lambda e: e.memset(rw[:, 0:3], 0.0), writes=[t_rwc])
            else:
                prw, t_prw = st5["prev"]
                P.op("dve", lambda e: e.tensor_copy(rw[:, 0:3], prw[:, 512:515]), reads=[t_prw], writes=[t_rwc])
            st5["prev"] = (rw, t_rw)
            for j in range(0, 3):
                P.op("dve", lambda e, j=j: e.scalar_tensor_tensor(
                    out=ac[:], in0=rw[:, j:j + 512], scalar=convw[:, blk * 4 + j:blk * 4 + j + 1], in1=ac[:],
                    op0=ALU.mult, op1=ALU.add), reads=[t_rw, t_rwc, t_cv, t_ac], writes=[t_ac])

            def pend():
                P.op("act", lambda e: e.activation(out=fr[:, qt * 512:(qt + 1) * 512], in_=ac[:], func=AF.Silu,
                                                   bias=convb[:, blk:blk + 1]),
                     reads=[t_ac, t_cv], writes=[tf])
                if qt == NQ - 1:
                    P.dma("pool", lambda e: e.dma_start(out=qkT1_s[blk], in_=fr[:]), tf, False)

            st5["pend"] = pend

        tm5 = {}

        def tm_sink5(meta, tt, pa, t_pa):
            if st5.get("pend"):
                st5.pop("pend")()
            if meta == "g":
                gt, t_gt = gtmp.next()
                P.op("dve", lambda e: e.tensor_tensor(out=ipre[:, tt, :], in0=pa[:, 0:8], in1=gb_bc[:, 0:8],
                                                      op=ALU.add), reads=[t_pa, t_cv], writes=[t_ipre])
                P.op("dve", lambda e: e.tensor_tensor(out=gt[:, 0:8], in0=pa[:, 8:16], in1=gb_bc[:, 8:16],
                                                      op=ALU.add), reads=[t_pa, t_cv], writes=[t_gt])
                P.op("act", lambda e: e.activation(out=gt[:, 8:16], in_=gt[:, 0:8], func=AF.Exp, scale=-1.0),
                     reads=[t_gt], writes=[t_gt])
                P.op("act", lambda e: e.activation(out=gt[:, 16:24], in_=gt[:, 8:16], func=AF.Ln,
                                                   bias=cvec[:, 1:2]), reads=[t_gt, t_cvec], writes=[t_gt])
                P.op("dve", lambda e: e.tensor_scalar(out=logf[:, tt, :], in0=gt[:, 16:24], scalar1=-1.0,
                                                      scalar2=None, op0=ALU.mult), reads=[t_gt], writes=[t_logf])
                return
            kind, i = meta
            if tt % 4 == 0:
                tm5["buf"] = tstb.next() if kind == "v" else tsto.next()
            st, t_s = tm5["buf"]
            if kind == "v":
                P.op("dve", lambda e: e.tensor_copy(st[:, tt % 4, :], pa[:]), reads=[t_pa], writes=[t_s])
                if tt % 4 == 3:
                    dst = vtok1_s[(tt - 3) * 128:(tt + 1) * 128, i * 512:(i + 1) * 512].rearrange(
                        "(t p) c -> p t c", p=128)
                    P.dma("pool", lambda e: e.dma_start(out=dst, in_=st[:]), t_s, False)
            else:
                zz, t_zz = z1.next()
                sg_, t_sg_ = sgt.next()
                P.op("act", lambda e: e.activation(out=sg_[:], in_=pa[:], func=AF.Sigmoid),
                     reads=[t_pa], writes=[t_sg_])
                P.op("dve", lambda e: e.tensor_tensor(out=zz[:], in0=pa[:, 256:512], in1=sg_[:, 256:512], op=ALU.mult),
                     reads=[t_pa, t_sg_], writes=[t_zz])
                P.op("dve", lambda e: e.tensor_tensor(out=st[:, tt % 4, :], in0=sg_[:, 0:256], in1=zz[:], op=ALU.mult),
                     reads=[t_sg_, t_zz], writes=[t_s])
                P.op("dve", lambda e: e.tensor_tensor(out=st[:, tt % 4, :], in0=st[:, tt % 4, :],
                                                      in1=hg_bc5[:, i * 256:(i + 1) * 256], op=ALU.mult),
                     reads=[t_cv], writes=[t_s])
                if tt % 4 == 3:
                    dst = ogz_s[(tt - 3) * 128:(tt + 1) * 128, i * 256:(i + 1) * 256].rearrange(
                        "(t p) c -> p t c", p=128)
                    P.dma("pool", lambda e: e.dma_start(out=dst, in_=st[:]), t_s, False)

        groups = ([("fm", i * 512, 512, i * 4) for i in range(4)]
                  + [("tm", 2048 + i * 512, 512, ("v", i)) for i in range(4)]
                  + [("tm", 4096 + i * 512, 512, ("ogz", i)) for i in range(8)]
                  + [("tm", 8192, 16, "g")])
        inproj(es, wb_d, groups, fm_sink5, tm_sink5)
        P.end_phase()
    es_h.close()
    if stop_after == 5:
        return finish()

    with ExitStack() as es:
        R = norm_res(es, full=False)
        t_c6 = P.tok("c6", dma=True)
        mask01 = sb(es, "mask01", [128, 128], F32)
        fing_bc = sb(es, "fing_bc", [128, D], F32)
        P.dma("sp", lambda e: e.dma_start(out=mask01[:], in_=cd["mask01"]), t_c6, True)
        P.dma("sp", lambda e: e.dma_start(out=fing_bc[:], in_=bass.AP(fing_d.tensor, 0, [[0, 128], [1, D]])), t_c6, True)
        Ct = sb(es, "Ct", [128, 8, 257], F32)
        Cbf = sb(es, "Cbf", [128, 8, 257], BF16)
        t_Ct = [P.tok("Ct_%d" % h) for h in range(8)]
        t_Cbf = [P.tok("Cbf_%d" % h) for h in range(8)]
        qkr = Rot([(sb(es, "qk%d" % i, [128, 16, 128], BF16), P.tok("qk%d" % i, dma=True)) for i in range(3)])
        Var = [(sb(es, "Va%d" % i, [128, 8, 257], BF16), P.tok("Va%d" % i, dma=True)) for i in range(3)]
        for va, t_va in Var:
            P.op("pool", lambda e, va=va: e.memset(va[:], 1.0), writes=[t_va])
        Var = Rot(Var)
        ogzr = Rot([(sb(es, "ogzt%d" % i, [128, 2048], F32), P.tok("ogzt%d" % i, dma=True)) for i in range(3)])
        x1r = Rot([(sb(es, "x1r%d" % i, [128, D], F32), P.tok("x1r%d" % i, dma=True)) for i in range(5)])
        outr = Rot([(sb(es, "outt%d" % i, [128, D], F32), P.tok("outt%d" % i, dma=True)) for i in range(3)])
        numr = Rot([(sb(es, "numsb%d" % i, [128, 8, 257], F32), P.tok("numsb")) for i in range(2)])
        junk6 = sb(es, "junk6", [128, 256], F32)
        t_junk6 = P.tok("junk6")
        yT6 = sb(es, "yT6", [128, 16, 128], BF16)
        x2 = sb(es, "x2", [128, D], F32)
        t_yT6, t_x2 = P.tok("yT6"), P.tok("x2")
        smr = Rot([(sb(es, "sm6_%d" % i, [128, 64], F32), P.tok("sm6")) for i in range(4)])
        sm2r = Rot([(sb(es, "sm6b_%d" % i, [128, 80], F32), P.tok("sm6b")) for i in range(2)])
        Sm8 = sb(es, "Sm8", [128, 8, 128], BF16)
        Kt8 = sb(es, "Kt8", [128, 8, 128], BF16)
        t_Sm8 = [P.tok("Sm8_%d" % h) for h in range(8)]
        t_Kt8 = [P.tok("Kt8_%d" % h) for h in range(8)]
        po = [ps(es, "po6_%d" % i, [128, 512], F32) for i in range(2)]
        t_po = [P.tok("po6_0"), P.tok("po6_1")]
        ptrk = ps(es, "ptrk", [128, D], BF16)
        ptry = ps(es, "ptry", [128, D], BF16)
        t_ptrk1 = P.tok("ptrk")
        t_ptrk = [t_ptrk1] * 8
        t_ptry = P.tok("ptry")
        pS2 = [ps(es, "pS2_%d" % i, [128, 4, 128], F32) for i in range(2)]
        t_pSb = [P.tok("pSb0"), P.tok("pSb1")]
        t_pS = [t_pSb[h // 4] for h in range(8)]
        pnd_ = ps(es, "pnd", [128, 512], F32)
        pnd2 = [pnd_, pnd_]
        pdC = ps(es, "pdC", [128, 512], F32)
        t_pnd_ = P.tok("pnd")
        t_pnd2, t_pdC = [t_pnd_, t_pnd_], P.tok("pdC")
        t_pbg = t_pdC
        pbg = pdC[:, 384:400]
        chunk = {}

        loaded = {}
        pending_stores = []

        def stage_L(n):
            qk, t_qk = qkr.next()
            va, t_va = Var.next()
            og, t_og = ogzr.next()
            x1, t_x1 = x1r.next()
            P.dma("sp", lambda e: e.dma_start(
                out=qk[:], in_=qkT1_s[:, :, n * 128:(n + 1) * 128].rearrange("b p t -> p b t")), t_qk, True)
            P.dma("sp", lambda e: e.dma_start(
                out=va[:, :, 0:256], in_=vtok1_s[n * 128:(n + 1) * 128, :].rearrange("p (h v) -> p h v", v=256)),
                t_va, True)
            P.dma("sp", lambda e: e.dma_start(out=x1[:], in_=x1_s[n * 128:(n + 1) * 128, :]), t_x1, True)
            P.dma("sp", lambda e: e.dma_start(out=og[:], in_=ogz_s[n * 128:(n + 1) * 128, :]), t_og, True)
            loaded[n] = (qk, t_qk, va, t_va, og, t_og, x1, t_x1)

        def stage_H(n):
            qk, t_qk, va, t_va, og, t_og, x1, t_x1 = loaded.pop(n)
            P.op("pe", lambda e: e.matmul(pdC[:, 384:392], lhsT=mask01[:], rhs=logf[:, n, :], start=True, stop=True),
                 reads=[t_c6, t_logf], writes=[t_pbg])
            P.op("pe", lambda e: e.matmul(pdC[:, 392:400], lhsT=ones_f[:], rhs=logf[:, n, :], start=True, stop=True),
                 reads=[t_const, t_logf], writes=[t_pbg])
            sm, t_sm = smr.next()
            P.op("act", lambda e: e.activation(out=sm[:, 32:48], in_=pdC[:, 384:400], func=AF.Identity),
                 reads=[t_pbg], writes=[t_sm])
            P.op("dve", lambda e: e.tensor_tensor(out=sm[:, 0:8], in0=ipre[:, n, :], in1=sm[:, 32:40],
                                                  op=ALU.subtract), reads=[t_ipre, t_sm], writes=[t_sm])
            P.op("act", lambda e: e.activation(out=sm[:, 8:16], in_=sm[:, 0:8], func=AF.Exp, bias=cvec[:, 2:3]),
                 reads=[t_sm, t_cvec], writes=[t_sm])
            P.op("act", lambda e: e.activation(out=sm[:, 16:32], in_=sm[:, 32:48], func=AF.Exp),
                 reads=[t_sm], writes=[t_sm])
            nums, t_nums = numr.next()
            last = (n == NT - 1)
            prev = chunk.get(n - 1)
            for h in range(8):
                P.op("pe", lambda e, h=h: e.matmul(pS2[h // 4][:, h % 4, :], lhsT=qk[:, 8 + h, :], rhs=qk[:, h, :],
                                                   start=True, stop=True), reads=[t_qk], writes=[t_pS[h]])
            if not last:
                for h in range(8):
                    P.op("pe", lambda e, h=h: e.transpose(ptrk[:, h * 128:(h + 1) * 128], qk[:, 8 + h, :], ident_b[:]),
                         reads=[t_qk, t_const], writes=[t_ptrk[h]])
            if prev is not None:
                psm, t_psm = prev
                for h in range(8):
                    P.op("act", lambda e, h=h: e.activation(
                        out=Cbf[:, h, :], in_=Ct[:, h, :], func=AF.Identity, scale=psm[:, 24 + h:25 + h]),
                        reads=[t_Ct[h], t_psm], writes=[t_Cbf[h]])
            for h in range(8):
                P.op("dve", lambda e, h=h: e.scalar_tensor_tensor(
                    out=Sm8[:, h, :], in0=pS2[h // 4][:, h % 4, :], scalar=sm[:, 8 + h:9 + h], in1=mask01[:],
                    op0=ALU.mult, op1=ALU.mult), reads=[t_pS[h], t_sm, t_c6], writes=[t_Sm8[h]])
                if not last:
                    P.op("act", lambda e, h=h: e.activation(
                        out=Kt8[:, h, :], in_=ptrk[:, h * 128:(h + 1) * 128], func=AF.Identity,
                        scale=sm[:, 8 + h:9 + h]), reads=[t_ptrk[h], t_sm], writes=[t_Kt8[h]])
            for h in range(8):
                pnd, t_pnd = pnd2[h % 2], t_pnd2[h % 2]
                P.op("pe", lambda e, h=h, pnd=pnd: e.matmul(pnd[:, 0:257], lhsT=Sm8[:, h, :], rhs=va[:, h, :],
                                                            start=True, stop=(n == 0)),
                     reads=[t_Sm8[h], t_va], writes=[t_pnd])
                if n > 0:
                    P.op("pe", lambda e, h=h, pnd=pnd: e.matmul(pnd[:, 0:257], lhsT=qk[:, h, :], rhs=Cbf[:, h, :],
                                                                start=False, stop=True),
                         reads=[t_qk, t_Cbf[h]], writes=[t_pnd])
                P.op("act", lambda e, h=h, pnd=pnd: e.activation(out=nums[:, h, :], in_=pnd[:, 0:257],
                                                                 func=AF.Identity),
                     reads=[t_pnd], writes=[t_nums])
                if not last:
                    P.op("pe", lambda e, h=h: e.matmul(pdC[:, 0:257], lhsT=Kt8[:, h, :], rhs=va[:, h, :],
                                                       start=True, stop=True),
                         reads=[t_Kt8[h], t_va], writes=[t_pdC])
                    if n == 0:
                        P.op("dve", lambda e, h=h: e.tensor_copy(Ct[:, h, :], pdC[:, 0:257]),
                             reads=[t_pdC], writes=[t_Ct[h]])
                    else:
                        psm, t_psm = prev
                        P.op("dve", lambda e, h=h: e.scalar_tensor_tensor(
                            out=Ct[:, h, :], in0=Ct[:, h, :], scalar=psm[:, 24 + h:25 + h], in1=pdC[:, 0:257],
                            op0=ALU.mult, op1=ALU.add), reads=[t_pdC, t_psm, t_Ct[h], t_Cbf[h]], writes=[t_Ct[h]])
            chunk[n] = (sm, t_sm)
            return (n, sm, t_sm, nums, t_nums, og, t_og, x1, t_x1)

        ybfr = Rot([(sb(es, "ybf6r%d" % i, [128, 2048], BF16), P.tok("ybf6r")) for i in range(2)])

        def stage_E1(ctx):
            n, sm, t_sm, nums, t_nums, og, t_og, x1, t_x1 = ctx
            s2_, t_s2_ = sm2r.next()
            for h in range(8):
                P.op("act", lambda e, h=h: e.activation(out=junk6[:], in_=nums[:, h, 0:256], func=AF.Square,
                                                        accum_out=s2_[:, h:h + 1]),
                     reads=[t_nums], writes=[t_junk6, t_s2_])
            P.op("dve", lambda e: e.tensor_tensor(out=s2_[:, 8:16], in0=nums[:, :, 256], in1=sm[:, 16:24], op=ALU.mult),
                 reads=[t_nums, t_sm], writes=[t_s2_])
            P.op("dve", lambda e: e.tensor_scalar(out=s2_[:, 16:24], in0=s2_[:, 8:16], scalar1=-1.0, scalar2=None,
                                                  op0=ALU.mult), reads=[t_s2_], writes=[t_s2_])
            P.op("dve", lambda e: e.tensor_tensor(out=s2_[:, 16:24], in0=s2_[:, 16:24], in1=s2_[:, 8:16], op=ALU.max),
                 reads=[t_s2_], writes=[t_s2_])
            P.op("dve", lambda e: e.tensor_scalar(out=s2_[:, 16:24], in0=s2_[:, 16:24], scalar1=1.0, scalar2=None,
                                                  op0=ALU.max), reads=[t_s2_], writes=[t_s2_])
            P.op("dve", lambda e: e.reciprocal(out=s2_[:, 24:32], in_=s2_[:, 16:24]), reads=[t_s2_], writes=[t_s2_])
            P.op("dve", lambda e: e.tensor_tensor(out=s2_[:, 32:40], in0=s2_[:, 24:32], in1=sm[:, 16:24], op=ALU.mult),
                 reads=[t_s2_, t_sm], writes=[t_s2_])
            P.op("dve", lambda e: e.tensor_tensor(out=s2_[:, 40:48], in0=s2_[:, 32:40], in1=s2_[:, 32:40], op=ALU.mult),
                 reads=[t_s2_], writes=[t_s2_])
            P.op("dve", lambda e: e.tensor_tensor(out=s2_[:, 40:48], in0=s2_[:, 40:48], in1=s2_[:, 0:8], op=ALU.mult),
                 reads=[t_s2_], writes=[t_s2_])
            P.op("act", lambda e: e.activation(out=s2_[:, 48:56], in_=s2_[:, 40:48], func=AF.Ln,
                                               scale=1.0 / 256, bias=cvec[:, 0:1]),
                 reads=[t_s2_, t_cvec], writes=[t_s2_])
            P.op("act", lambda e: e.activation(out=s2_[:, 56:64], in_=s2_[:, 48:56], func=AF.Exp, scale=-0.5),
                 reads=[t_s2_], writes=[t_s2_])
            P.op("dve", lambda e: e.tensor_tensor(out=s2_[:, 64:72], in0=s2_[:, 32:40], in1=s2_[:, 56:64], op=ALU.mult),
                 reads=[t_s2_], writes=[t_s2_])
            yb, t_yb = ybfr.next()
            for h in range(8):
                P.op("dve", lambda e, h=h: e.scalar_tensor_tensor(
                    out=yb[:, h * 256:(h + 1) * 256], in0=nums[:, h, 0:256], scalar=s2_[:, 64 + h:65 + h],
                    in1=og[:, h * 256:(h + 1) * 256], op0=ALU.mult, op1=ALU.mult),
                    reads=[t_nums, t_s2_, t_og], writes=[t_yb])
            return (n, x1, t_x1, yb, t_yb)

        def stage_E2(c2):
            n, x1, t_x1, yb, t_yb = c2
            for r8 in range(2):
                for kc in range(8):
                    P.op("pe", lambda e, r8=r8, kc=kc: e.transpose(
                        ptry[:, kc * 128:(kc + 1) * 128], yb[:, (r8 * 8 + kc) * 128:(r8 * 8 + kc + 1) * 128],
                        ident_b[:]), reads=[t_yb, t_const], writes=[t_ptry])
                P.op("act", lambda e, r8=r8: e.activation(
                    out=yT6[:, r8 * 8:(r8 + 1) * 8, :], in_=ptry[:].rearrange("p (a b) -> p a b", b=128),
                    func=AF.Identity), reads=[t_ptry], writes=[t_yT6])
            for hf in range(2):
                for kc in range(16):
                    P.op("pe", lambda e, hf=hf, kc=kc: e.matmul(
                        po[hf][:], lhsT=yT6[:, kc, :], rhs=wo1[:, kc, hf * 512:(hf + 1) * 512],
                        start=(kc == 0), stop=(kc == 15)), reads=[t_yT6, t_wo1], writes=[t_po[hf]])
                P.op("dve", lambda e, hf=hf: e.tensor_tensor(
                    out=x2[:, hf * 512:(hf + 1) * 512], in0=po[hf][:], in1=gate_bc[1][:, hf * 512:(hf + 1) * 512],
                    op=ALU.mult), reads=[t_po[hf], t_gbc[1]], writes=[t_x2])
            P.op("dve", lambda e: e.tensor_tensor(out=x2[:], in0=x2[:], in1=x1[:], op=ALU.add),
                 reads=[t_x1, t_x2], writes=[t_x2])
            st, t_st = rms_stats(x2[:], t_x2, R)
            ot, t_ot = outr.next()
            P.op("dve", lambda e: e.scalar_tensor_tensor(
                out=ot[:], in0=x2[:], scalar=st[:, 2:3], in1=fing_bc[:], op0=ALU.mult, op1=ALU.mult),
                reads=[t_x2, t_st, t_c6], writes=[t_ot])
            pending_stores.append((n, ot, t_ot))

        def flush_stores():
            while pending_stores:
                n_, ot, t_ot = pending_stores.pop(0)
                P.dma("sp", lambda e, n_=n_, ot=ot: e.dma_start(out=out_d[n_ * 128:(n_ + 1) * 128, :], in_=ot[:]),
                      t_ot, False)

        cH, cE = {}, {}
        stage_L(0)
        for n in range(NT + 2):
            if n + 1 < NT:
                stage_L(n + 1)
            flush_stores()
            lists = []
            if n < NT:
                P.capture()
                cH[n] = stage_H(n)
                lists.append(P.end_capture())
            if 0 <= n - 2 < NT:
                P.capture()
                stage_E2(cE.pop(n - 2))
                lists.append(P.end_capture())
            if 0 <= n - 1 < NT:
                P.capture()
                cE[n - 1] = stage_E1(cH.pop(n - 1))
                lists.append(P.end_capture())
            P.replay_merged(lists)
        flush_stores()
        P.end_phase()
    es_g.close()
    return nc


def prep_inputs(inputs):
    f = np.float32
    g = lambda k: np.asarray(inputs[k], dtype=f)
    sh = {}
    sh["ada_w"] = np.ascontiguousarray(g("ada_w"))
    sh["adabT"] = np.ascontiguousarray(g("ada_b").reshape(2, 24, 128).transpose(2, 0, 1).reshape(128, 48))
    sh["ngT"] = np.ascontiguousarray(g("norm_g").reshape(2, 8, 128).transpose(2, 0, 1).reshape(128, 16))
    sh["fing"] = np.ascontiguousarray(g("final_g").reshape(1, D))
    sh["wa"] = np.ascontiguousarray(g("a_w_in")[0][:, WA_PERM])
    sh["cmp_w1"] = np.ascontiguousarray(g("a_cmp_w1")[0])
    sh["peT"] = np.ascontiguousarray(g("a_cmp_pe")[0].transpose(2, 0, 1).reshape(64, 64))
    sh["b1T"] = np.ascontiguousarray(g("a_cmp_b1")[0].reshape(2, 2, 128).transpose(2, 0, 1).reshape(128, 4))
    sh["cmp_w2"] = np.ascontiguousarray(g("a_cmp_w2")[0])
    sh["b2T"] = np.ascontiguousarray(g("a_cmp_b2")[0].T)
    sh["a_w_out"] = np.ascontiguousarray(g("a_w_out")[0])
    sh["wb"] = np.ascontiguousarray(g("b_w_in")[0][:, WB_PERM])
    sh["convwT"] = np.ascontiguousarray(g("b_conv_w")[0].reshape(4, 16, 128).transpose(2, 1, 0).reshape(128, 64))
    sh["convbT"] = np.ascontiguousarray(g("b_conv_b")[0].reshape(16, 128).T)
    sh["gateb"] = np.ascontiguousarray(g("b_gate_b")[0].reshape(1, 16))
    sh["headg"] = np.ascontiguousarray(g("b_head_g")[0].reshape(1, 2048))
    sh["b_w_out"] = np.ascontiguousarray(g("b_w_out")[0])
    for k, v in make_consts().items():
        sh["k_" + k] = v
    per = []
    x = g("x")
    c = g("c")
    for b in range(8):
        m = dict(sh)
        m["x"] = np.ascontiguousarray(x[b])
        m["cT"] = np.ascontiguousarray(c[b].reshape(8, 128).T)
        per.append(m)
    return per


def kernel(**inputs):
    nc = build_program()
    in_maps = prep_inputs(inputs)
    res = run_bass_kernel_spmd(nc, in_maps, core_ids=list(range(8)))
    return np.stack([np.asarray(r["out"], dtype=np.float32) for r in res.results], axis=0)
```

```python
import math
from contextlib import ExitStack

import numpy as np
import ml_dtypes

import concourse.bass as bass
import concourse.mybir as mybir
from concourse.bass_utils import run_bass_kernel_spmd

F32 = mybir.dt.float32
BF16 = mybir.dt.bfloat16
AF = mybir.ActivationFunctionType
ALU = mybir.AluOpType
AX = mybir.AxisListType

S = 4096
D = 1024
NT = S // 128
NQ = S // 512
NEG = -30000.0
EPS = 1e-6
SLOPES = [2.0 ** (-8.0 * (i + 1) / 16) for i in range(16)]
LN_KSCALE = math.log(128 ** -0.5)
ZERO_CUT = 130.0

ENGS = ("pe", "act", "dve", "pool", "sp")
CENG = ("pe", "act", "dve", "pool")
N_DSEM = 90


class Tok:
    __slots__ = ("w", "r", "ds", "name")

    def __init__(self, name="", ds=None):
        self.w = None
        self.r = {}
        self.ds = ds
        self.name = name


class Prog:
    def __init__(self, nc, es):
        self.nc = nc
        self.q = {e: [] for e in ENGS}
        self.cnt = {e: 0 for e in CENG}
        self.seen = {e: {} for e in ENGS}
        self.sems = {}
        for e in CENG:
            self.sems[e] = es.enter_context(nc.semaphore("c_" + e))
        self.sems["bar"] = es.enter_context(nc.semaphore("c_bar"))
        self.bar = 0
        self.dsem_cnt = {}
        self.free = []
        for i in range(N_DSEM):
            k = "d%d" % i
            self.sems[k] = es.enter_context(nc.semaphore(k))
            self.dsem_cnt[k] = 0
            self.free.append(k)
        self.phase_ds = []

    def tok(self, name="", dma=False):
        ds = None
        if dma:
            ds = self.free.pop(0)
            self.phase_ds.append(ds)
        return Tok(name, ds)

    def _need(self, eng, ev):
        if ev is None:
            return
        key, val = ev
        if key == eng and eng == "pe":
            return
        if self.seen[eng].get(key, 0) >= val:
            return
        self.seen[eng][key] = val
        self.q[eng].append(("wait", key, val))

    def capture(self):
        self._cap = []

    def end_capture(self):
        lst, self._cap = self._cap, None
        return lst

    def replay_merged(self, lists):
        lists = [l for l in lists if l]
        pos = [0] * len(lists)
        while True:
            best, bf = None, None
            for i, l in enumerate(lists):
                if pos[i] < len(l):
                    f = pos[i] / len(l)
                    if bf is None or f < bf:
                        best, bf = i, f
            if best is None:
                break
            kind, args = lists[best][pos[best]]
            pos[best] += 1
            if kind == "op":
                self.op(*args)
            else:
                self.dma(*args)

    def op(self, eng, fn, reads=(), writes=()):
        if getattr(self, "_cap", None) is not None:
            self._cap.append(("op", (eng, fn, tuple(reads), tuple(writes))))
            return
        for t in reads:
            self._need(eng, t.w)
        for t in writes:
            self._need(eng, t.w)
            for k, v in t.r.items():
                self._need(eng, (k, v))
        self.cnt[eng] += 1
        n = self.cnt[eng]
        self.q[eng].append(("op", fn))
        for t in reads:
            if t.r.get(eng, 0) < n:
                t.r[eng] = n
        for t in writes:
            t.w = (eng, n)
            t.r = {}

    def dma(self, eng, fn, tok, load, reads=(), writes=()):
        assert tok.ds is not None, tok.name
        if getattr(self, "_cap", None) is not None:
            self._cap.append(("dma", (eng, fn, tok, load, tuple(reads), tuple(writes))))
            return
        self._need(eng, tok.w)
        if load:
            for k, v in tok.r.items():
                self._need(eng, (k, v))
        for t in reads:
            self._need(eng, t.w)
        for t in writes:
            self._need(eng, t.w)
            for k, v in t.r.items():
                self._need(eng, (k, v))
        self.dsem_cnt[tok.ds] += 16
        n = self.dsem_cnt[tok.ds]
        self.q[eng].append(("dma", fn, tok.ds))
        if load:
            tok.w = (tok.ds, n)
            tok.r = {}
        else:
            tok.r[tok.ds] = n
        for t in reads:
            t.r[tok.ds] = max(t.r.get(tok.ds, 0), n)
        for t in writes:
            t.w = (tok.ds, n)
            t.r = {}

    def barrier(self):
        for e in CENG:
            if self.cnt[e]:
                self._need("sp", (e, self.cnt[e]))
        for k, v in self.dsem_cnt.items():
            if v:
                self._need("sp", (k, v))
        self.bar += 1
        self.q["sp"].append(("inc", "bar", 1))
        for e in CENG:
            self.q[e].append(("wait", "bar", self.bar))
            for e2 in CENG:
                self.seen[e][e2] = max(self.seen[e].get(e2, 0), self.cnt[e2])
            for k, v in self.dsem_cnt.items():
                self.seen[e][k] = max(self.seen[e].get(k, 0), v)

    def end_phase(self):
        self.barrier()
        self.flush()
        self.free.extend(self.phase_ds)
        self.phase_ds = []

    def flush(self):
        nc = self.nc
        q = self.q
        sems = self.sems

        def emit(name, h):
            for it in q[name]:
                if it[0] == "wait":
                    h.wait_ge(sems[it[1]], it[2])
                elif it[0] == "op":
                    it[1](h).then_inc(sems[name], 1)
                elif it[0] == "dma":
                    it[1](h).then_inc(sems[it[2]], 16)
                elif it[0] == "inc":
                    h.sem_inc(sems[it[1]], it[2])

        with nc.Block() as block:
            @block.sync
            def _(h):
                emit("sp", h)

            @block.tensor
            def _(h):
                emit("pe", h)

            @block.scalar
            def _(h):
                emit("act", h)

            @block.vector
            def _(h):
                emit("dve", h)

            @block.gpsimd
            def _(h):
                emit("pool", h)
        self.q = {e: [] for e in ENGS}


class Rot:
    def __init__(self, items):
        self.items = items
        self.i = 0

    def next(self):
        it = self.items[self.i % len(self.items)]
        self.i += 1
        return it


def _bf(a):
    return np.ascontiguousarray(np.asarray(a, dtype=np.float32)).astype(ml_dtypes.bfloat16)


def make_consts():
    c = {}
    c["ident_f"] = np.eye(128, dtype=np.float32)
    c["ident_b"] = _bf(np.eye(128))
    c["ones_f"] = np.ones((128, 128), np.float32)
    key = np.arange(S)
    kaug_slc = np.zeros((64, S), np.float32)
    for j in range(63):
        kaug_slc[j] = (key // 64 == j)
    kaug_slc[63] = 1.0
    kaug_one = np.zeros((64, S), np.float32)
    kaug_one[63] = 1.0
    c["kaug_slc"] = _bf(kaug_slc)
    c["kaug_one"] = _bf(kaug_one)
    qq = np.arange(512, dtype=np.float64)
    c["vrow"] = _bf(np.stack([-8.0 * s * qq for s in SLOPES]))
    p = np.arange(128, dtype=np.float64)
    bt = np.zeros((128, 16, 32), np.float64)
    for h in range(16):
        for di in range(32):
            bt[:, h, di] = SLOPES[h] * (p + 128.0 * (di - 28))
    c["bias_tab"] = bt.reshape(128, 512).astype(np.float32)
    bc = np.zeros((128, 16, 8, 2), np.float64)
    for h in range(16):
        for t in range(8):
            for ct in range(2):
                bc[:, h, t, ct] = SLOPES[h] * (16.0 * (ct * 128 + p) + 31.0 - 512.0 * t)
    c["biasc_tab"] = bc.reshape(128, 256).astype(np.float32)
    cc = np.arange(256)
    tq = np.arange(S)
    ok = (16 * cc[:, None] + 31 <= tq[None, :]) & (cc[:, None] < 255)
    cm = np.where(ok, 0.0, NEG).astype(np.float32)
    c["cmask"] = _bf(cm.reshape(2, 128, S).transpose(1, 0, 2))
    kk = np.arange(128)[:, None]
    q2 = np.arange(128)[None, :]
    c["tri_causal"] = _bf(np.where(kk <= q2, 0.0, NEG))
    c["tri_band"] = _bf(np.where(kk > q2, 0.0, NEG))
    c["mask01"] = (kk <= q2).astype(np.float32)
    c0 = np.arange(256)[:, None] * 16
    s0 = np.arange(64)[None, :] * 64
    ov = np.clip(np.minimum(c0 + 32, s0 + 64) - np.maximum(c0, s0), 0, None) / 32.0
    ov[255] = 0.0
    c["ov"] = _bf(ov.reshape(2, 128, 64).transpose(1, 0, 2))
    blk = np.arange(64)[None, :]
    cur = (tq // 64)[:, None]
    forced = (blk == 0) | (blk == cur) | (blk == cur - 1)
    valid = blk <= cur
    vnf = (valid & ~forced).astype(np.float32)
    epsj = (64 - np.arange(64)).astype(np.float32)[None, :] * 1e-30
    fbig = (1e9 + 1e5 * np.arange(64)).astype(np.float32)[None, :]
    addc = np.where(forced, fbig, np.where(valid, epsj, -1.0)).astype(np.float32)

    def tm(a):
        return np.ascontiguousarray(a.reshape(32, 128, 64).transpose(1, 0, 2))

    c["sel_vnf"] = tm(vnf)
    c["sel_addc"] = tm(addc)
    c["sel_valid"] = tm(valid.astype(np.float32))
    return c


WA_PERM = np.concatenate([
    np.arange(0, 1024),
    np.arange(1024, 1536),
    np.arange(1536, 1792),
    np.arange(2048, 2304),
    np.arange(1792, 2048), np.arange(2304, 2560),
    np.arange(2608, 3632),
    np.arange(2560, 2608),
])
WB_PERM = np.concatenate(
    [np.arange(0, 2048), np.arange(2048, 4096)]
    + [np.concatenate([np.arange(4112 + i * 256, 4112 + (i + 1) * 256),
                       np.arange(6160 + i * 256, 6160 + (i + 1) * 256)]) for i in range(8)]
    + [np.arange(4096, 4112)])


def build_program(stop_after=99, dbg=False):
    nc = bass.Bass("TRN2", target_bir_lowering=False)
    consts = make_consts()

    def din(name, shape, dt=F32):
        return nc.dram_tensor(name, list(shape), dt, kind="ExternalInput").ap()

    def dscr(name, shape, dt):
        return nc.dram_tensor(name, list(shape), dt, kind=("ExternalOutput" if dbg else "Internal")).ap()

    x_d = din("x", [S, D])
    cT_d = din("cT", [128, 8])
    adaw_d = din("ada_w", [2, D, 3 * D])
    adabT_d = din("adabT", [128, 48])
    ngT_d = din("ngT", [128, 16])
    fing_d = din("fing", [1, D])
    wa_d = din("wa", [D, 3632])
    w1_d = din("cmp_w1", [2, 2048, 256])
    peT_d = din("peT", [64, 64])
    b1T_d = din("b1T", [128, 4])
    w2_d = din("cmp_w2", [2, 256, 64])
    b2T_d = din("b2T", [64, 2])
    woa_d = din("a_w_out", [D, D])
    wb_d = din("wb", [D, 8208])
    convwT_d = din("convwT", [128, 64])
    convbT_d = din("convbT", [128, 16])
    gateb_d = din("gateb", [1, 16])
    headg_d = din("headg", [1, 2048])
    wob_d = din("b_w_out", [2048, D])
    cd = {}
    for k, v in consts.items():
        cd[k] = din("k_" + k, v.shape, BF16 if v.dtype == ml_dtypes.bfloat16 else F32)
    out_d = nc.dram_tensor("out", [S, D], F32, kind="ExternalOutput").ap()

    qT_s = dscr("qT_s", [1024, S], BF16)
    kvT_s = dscr("kvT_s", [4, 256, S], BF16)
    vtok_s = dscr("vtok_s", [S, 512], BF16)
    zs_s = dscr("zs_s", [S, 1024], F32)
    y0_s = dscr("y0_s", [S, 1024], BF16)
    x1_s = dscr("x1_s", [S, D], F32)
    qkT1_s = dscr("qkT1_s", [16, 128, S], BF16)
    vtok1_s = dscr("vtok1_s", [S, 2048], BF16)
    ogz_s = dscr("ogz_s", [S, 2048], F32)

    es_all = ExitStack()
    P = Prog(nc, es_all)

    uid = {"n": 0}

    def sb(es, name, shape, dt):
        uid["n"] += 1
        return es.enter_context(nc.sbuf_tensor("s%d_%s" % (uid["n"], name), list(shape), dt))

    def ps(es, name, shape, dt=F32):
        uid["n"] += 1
        return es.enter_context(nc.psum_tensor("p%d_%s" % (uid["n"], name), list(shape), dt))

    def finish(dbg=None):
        P.end_phase()
        return nc

    ident_f = sb(es_all, "ident_f", [128, 128], F32)
    ident_b = sb(es_all, "ident_b", [128, 128], BF16)
    ones_f = sb(es_all, "ones_f", [128, 128], F32)
    cvec = sb(es_all, "cvec", [128, 4], F32)
    modT = sb(es_all, "modT", [128, 48], F32)
    gsc = sb(es_all, "gsc", [128, 16], F32)
    gate_bc = [sb(es_all, "gate_bc%d" % l, [128, D], F32) for l in range(2)]
    gate_sb = sb(es_all, "gate_sb", [128, NT, 48], F32)
    es_w1 = ExitStack()
    w1f = sb(es_w1, "w1f", [64, 32, 256], F32)
    t_w1f = Tok("w1f", P.free.pop(0))
    es_h = ExitStack()
    hT = sb(es_h, "hT", [128, 8, S], BF16)
    t_const = Tok("const", P.free.pop(0))
    t_cvec = P.tok("cvec")
    t_modT = P.tok("modT")
    t_gsc = P.tok("gsc")
    t_gbc = [P.tok("gbc0"), P.tok("gbc1")]
    t_gate = P.tok("gate_sb")
    t_hT = [P.tok("hT%d" % i) for i in range(NQ)]

    P.dma("sp", lambda e: e.dma_start(out=ident_f[:], in_=cd["ident_f"]), t_const, True)
    P.dma("sp", lambda e: e.dma_start(out=ident_b[:], in_=cd["ident_b"]), t_const, True)
    P.dma("sp", lambda e: e.dma_start(out=ones_f[:], in_=cd["ones_f"]), t_const, True)
    P.op("pool", lambda e: e.memset(cvec[:, 0:1], EPS), writes=[t_cvec])
    P.op("pool", lambda e: e.memset(cvec[:, 1:2], 1.0), writes=[t_cvec])
    P.op("pool", lambda e: e.memset(cvec[:, 2:3], LN_KSCALE), writes=[t_cvec])
    P.op("pool", lambda e: e.memset(cvec[:, 3:4], 0.0), writes=[t_cvec])

    with ExitStack() as es:
        cT = sb(es, "cT", [128, 8], F32)
        adab = sb(es, "adab", [128, 48], F32)
        ngT = sb(es, "ngT", [128, 16], F32)
        wst = [sb(es, "adaw%d" % i, [128, 8, 768], F32) for i in range(3)]
        dg = [sb(es, "dg%d" % i, [128, 128], F32) for i in range(2)]
        pmod = ps(es, "pmod", [128, 48], F32)
        pbc = [ps(es, "pbc%d" % i, [128, 512], F32) for i in range(2)]
        t_small = P.tok("small", dma=True)
        t_w = [P.tok("adaw%d" % i, dma=True) for i in range(3)]
        t_pmod = P.tok("pmod")
        t_dg = [P.tok("dg0"), P.tok("dg1")]
        t_pbc = [P.tok("pbc0"), P.tok("pbc1")]
        P.dma("sp", lambda e: e.dma_start(out=cT[:], in_=cT_d), t_small, True)
        P.dma("sp", lambda e: e.dma_start(out=adab[:], in_=adabT_d), t_small, True)
        P.dma("sp", lambda e: e.dma_start(out=ngT[:], in_=ngT_d), t_small, True)
        k = 0
        for l in range(2):
            for pc in range(4):
                w = wst[k % 3]
                tw = t_w[k % 3]
                k += 1
                src = adaw_d[l, :, pc * 768:(pc + 1) * 768].rearrange("(kc p) w -> p kc w", p=128)
                P.dma("sp", lambda e, w=w, src=src: e.dma_start(out=w[:], in_=src), tw, True)
                for jj in range(6):
                    col = l * 24 + pc * 6 + jj
                    for kc in range(8):
                        P.op("pe", lambda e, w=w, jj=jj, kc=kc, col=col: e.matmul(
                            pmod[:, col:col + 1], lhsT=w[:, kc, jj * 128:(jj + 1) * 128],
                            rhs=cT[:, kc:kc + 1], start=(kc == 0), stop=(kc == 7)),
                            reads=[tw, t_small], writes=[t_pmod])
        P.op("dve", lambda e: e.tensor_tensor(out=modT[:], in0=pmod[:], in1=adab[:], op=ALU.add),
             reads=[t_pmod, t_small], writes=[t_modT])
        for l in range(2):
            P.op("dve", lambda e, l=l: e.scalar_tensor_tensor(
                out=gsc[:, l * 8:(l + 1) * 8], in0=modT[:, l * 24 + 8:l * 24 + 16], scalar=1.0,
                in1=ngT[:, l * 8:(l + 1) * 8], op0=ALU.add, op1=ALU.mult),
                reads=[t_modT, t_small], writes=[t_gsc])
            for kc in range(8):
                d_, td = dg[kc % 2], t_dg[kc % 2]
                P.op("dve", lambda e, l=l, kc=kc, d_=d_: e.tensor_scalar(
                    out=d_[:], in0=ident_f[:], scalar1=modT[:, l * 24 + 16 + kc:l * 24 + 17 + kc],
                    scalar2=None, op0=ALU.mult), reads=[t_modT, t_const], writes=[td])
                P.op("pe", lambda e, kc=kc, d_=d_: e.matmul(
                    pbc[kc // 4][:, (kc % 4) * 128:(kc % 4 + 1) * 128], lhsT=ones_f[:], rhs=d_[:],
                    start=True, stop=True), reads=[td, t_const], writes=[t_pbc[kc // 4]])
            for hf in range(2):
                P.op("dve", lambda e, l=l, hf=hf: e.tensor_copy(
                    gate_bc[l][:, hf * 512:(hf + 1) * 512], pbc[hf][:]),
                    reads=[t_pbc[hf]], writes=[t_gbc[l]])
        P.end_phase()
    if stop_after == 0:
        return finish()

    nres = {"i": 0}

    def norm_res(es, full=True):
        R = {}
        k = nres["i"]
        nres["i"] += 1
        R["junk"] = Rot([(sb(es, "njunk%d_%d" % (k, i), [128, D], F32), P.tok("njunk")) for i in range(2 if full else 1)])
        R["stat"] = Rot([(sb(es, "nstat%d_%d" % (k, i), [128, 4], F32), P.tok("nstat")) for i in range(3)])
        if full:
            R["xn"] = Rot([(sb(es, "nxn%d_%d" % (k, i), [128, D], F32), P.tok("nxn")) for i in range(3)])
            R["ptr"] = Rot([(ps(es, "nptr%d_%d" % (k, i), [128, 512], F32), P.tok("nptr")) for i in range(2)])
        return R

    def rms_stats(xt_ap, t_xt, R):
        junk, t_junk = R["junk"].next()
        st, t_st = R["stat"].next()
        P.op("act", lambda e: e.activation(out=junk[:], in_=xt_ap, func=AF.Square, accum_out=st[:, 0:1]),
             reads=[t_xt], writes=[t_junk, t_st])
        P.op("act", lambda e: e.activation(out=st[:, 1:2], in_=st[:, 0:1], func=AF.Ln,
                                           scale=1.0 / D, bias=cvec[:, 0:1]),
             reads=[t_st, t_cvec], writes=[t_st])
        P.op("act", lambda e: e.activation(out=st[:, 2:3], in_=st[:, 1:2], func=AF.Exp, scale=-0.5),
             reads=[t_st], writes=[t_st])
        return st, t_st

    def norm_a(xt_ap, t_xt, R):
        st, t_st = rms_stats(xt_ap, t_xt, R)
        xn, t_xn = R["xn"].next()
        P.op("dve", lambda e: e.tensor_scalar(out=xn[:], in0=xt_ap, scalar1=st[:, 2:3], scalar2=None,
                                              op0=ALU.mult), reads=[t_xt, t_st], writes=[t_xn])
        return xn, t_xn

    def norm_b(l, tt, xn, t_xn, R):
        for hf in range(2):
            pt, t_pt = R["ptr"].next()
            for c4 in range(4):
                kc = hf * 4 + c4
                P.op("pe", lambda e, pt=pt, c4=c4, kc=kc: e.transpose(
                    pt[:, c4 * 128:(c4 + 1) * 128], xn[:, kc * 128:(kc + 1) * 128], ident_f[:]),
                    reads=[t_xn, t_const], writes=[t_pt])
            for c4 in range(4):
                kc = hf * 4 + c4
                P.op("act", lambda e, pt=pt, c4=c4, kc=kc: e.activation(
                    out=hT[:, kc, tt * 128:(tt + 1) * 128], in_=pt[:, c4 * 128:(c4 + 1) * 128],
                    func=AF.Identity, scale=gsc[:, l * 8 + kc:l * 8 + kc + 1],
                    bias=modT[:, l * 24 + kc:l * 24 + kc + 1]),
                    reads=[t_pt, t_gsc, t_modT], writes=[t_hT[tt // 4]])

    def norm_to_hT(l, tt, xt_ap, t_xt, R):
        xn, t_xn = norm_a(xt_ap, t_xt, R)
        norm_b(l, tt, xn, t_xn, R)

    def inproj(es, w_d, groups, fm_sink, tm_sink):
        wbf = [sb(es, "wbf%d" % i, [128, 8, 512], BF16) for i in range(2)]
        t_wbf = [P.tok("wbf%d" % i, dma=True) for i in range(2)]
        pacc = Rot([(ps(es, "pacc%d" % i, [128, 512], F32), P.tok("pacc")) for i in range(4)])
        for gi, (kind, c0, width, meta) in enumerate(groups):
            w16 = wbf[gi % 2]
            tw16 = t_wbf[gi % 2]
            src = w_d[:, c0:c0 + width].rearrange("(kc p) w -> p kc w", p=128)
            P.dma("pool", lambda e, w16=w16, src=src, width=width: e.dma_start(out=w16[:, :, 0:width], in_=src),
                  tw16, True)
            if kind == "fm":
                for bi in range(width // 128):
                    for qt in range(NQ):
                        pa, t_pa = pacc.next()
                        for kc in range(8):
                            P.op("pe", lambda e, pa=pa, w16=w16, bi=bi, kc=kc, qt=qt: e.matmul(
                                pa[:], lhsT=w16[:, kc, bi * 128:(bi + 1) * 128],
                                rhs=hT[:, kc, qt * 512:(qt + 1) * 512], start=(kc == 0), stop=(kc == 7)),
                                reads=[tw16, t_hT[qt]], writes=[t_pa])
                        fm_sink(meta + bi, qt, pa, t_pa)
            else:
                for tt in range(NT):
                    pa, t_pa = pacc.next()
                    for kc in range(8):
                        P.op("pe", lambda e, pa=pa, w16=w16, kc=kc, tt=tt, width=width: e.matmul(
                            pa[:, 0:width], lhsT=hT[:, kc, tt * 128:(tt + 1) * 128],
                            rhs=w16[:, kc, 0:width], start=(kc == 0), stop=(kc == 7)),
                            reads=[tw16, t_hT[tt // 4]], writes=[t_pa])
                    tm_sink(meta, tt, pa, t_pa)

    with ExitStack() as es:
        R = norm_res(es)
        xin = Rot([(sb(es, "xin%d" % i, [128, D], F32), P.tok("xin%d" % i, dma=True)) for i in range(3)])
        xns = {}
        for tt in range(NT + 1):
            lists = []
            if tt < NT:
                xt, t_xt = xin.next()
                P.dma("sp", lambda e, xt=xt, tt=tt: e.dma_start(out=xt[:], in_=x_d[tt * 128:(tt + 1) * 128, :]),
                      t_xt, True)
                P.capture()
                xns[tt] = norm_a(xt[:], t_xt, R)
                lists.append(P.end_capture())
            if tt >= 1:
                P.capture()
                norm_b(0, tt - 1, *xns.pop(tt - 1), R)
                lists.append(P.end_capture())
            P.replay_merged(lists)
        P.end_phase()
    if stop_after == 1:
        return finish()

    with ExitStack() as es:
        frow = [sb(es, "frow%d" % i, [128, S], BF16) for i in range(2)]
        t_frow = [P.tok("frow%d" % i, dma=True) for i in range(2)]
        tstg_b = Rot([(sb(es, "tstb%d" % i, [128, 4, 512], BF16), P.tok("tstb%d" % i, dma=True)) for i in range(2)])
        tstg_f = Rot([(sb(es, "tstf%d" % i, [128, 4, 512], F32), P.tok("tstf%d" % i, dma=True)) for i in range(2)])
        FM_DEST = ([qT_s[i * 128:(i + 1) * 128, :] for i in range(8)]
                   + [kvT_s[j, i * 128:(i + 1) * 128, :] for j in range(4) for i in range(2)])

        def fm_sink(blk, qt, pa, t_pa):
            fr, tf = frow[blk % 2], t_frow[blk % 2]
            if qt % 2 == 0:
                P.op("act", lambda e: e.activation(out=fr[:, qt * 512:(qt + 1) * 512], in_=pa[:],
                                                   func=AF.Identity), reads=[t_pa], writes=[tf])
            else:
                P.op("dve", lambda e: e.tensor_copy(fr[:, qt * 512:(qt + 1) * 512], pa[:]),
                     reads=[t_pa], writes=[tf])
            if qt == NQ - 1:
                dst = FM_DEST[blk]
                P.dma("sp", lambda e: e.dma_start(out=dst, in_=fr[:]), tf, False)

        tm_cur = {}

        def tm_sink(meta, tt, pa, t_pa):
            if meta == "gates":
                P.op("act", lambda e: e.activation(out=gate_sb[:, tt, :], in_=pa[:, 0:48], func=AF.Sigmoid),
                     reads=[t_pa], writes=[t_gate])
                return
            if tt % 4 == 0:
                tm_cur["buf"] = tstg_b.next() if meta == "v" else tstg_f.next()
            st, t_s = tm_cur["buf"]
            if meta == "v":
                P.op("dve", lambda e: e.tensor_copy(st[:, tt % 4, :], pa[:]), reads=[t_pa], writes=[t_s])
                if tt % 4 == 3:
                    dst = vtok_s[(tt - 3) * 128:(tt + 1) * 128, :].rearrange("(t p) c -> p t c", p=128)
                    P.dma("sp", lambda e: e.dma_start(out=dst, in_=st[:]), t_s, False)
            else:
                zi = meta
                P.op("act", lambda e: e.activation(out=st[:, tt % 4, :], in_=pa[:], func=AF.Silu),
                     reads=[t_pa], writes=[t_s])
                if tt % 4 == 3:
                    dst = zs_s[(tt - 3) * 128:(tt + 1) * 128, zi * 512:(zi + 1) * 512].rearrange(
                        "(t p) c -> p t c", p=128)
                    P.dma("sp", lambda e: e.dma_start(out=dst, in_=st[:]), t_s, False)

        groups = [("fm", 0, 512, 0), ("fm", 512, 512, 4), ("fm", 1024, 512, 8), ("fm", 1536, 512, 12),
                  ("tm", 2048, 512, "v"), ("tm", 2560, 512, 0), ("tm", 3072, 512, 1), ("tm", 3584, 48, "gates")]
        inproj(es, wa_d, groups, fm_sink, tm_sink)
        P.dma("sp", lambda e: e.dma_start(out=w1f[:], in_=w1_d[0].rearrange("(l d) h -> d l h", d=64)), t_w1f, True)
        P.end_phase()
    es_h.close()
    if stop_after == 2:
        return finish()

    with ExitStack() as es:
        t_c3 = P.tok("c3", dma=True)
        cmask = sb(es, "cmask", [128, 2, S], BF16)
        tri_c = sb(es, "tri_c", [128, 128], BF16)
        tri_b = sb(es, "tri_b", [128, 128], BF16)
        bias_tab = sb(es, "bias_tab", [128, 512], F32)
        biasc_tab = sb(es, "biasc_tab", [128, 256], F32)
        ovt = sb(es, "ovt", [128, 2, 64], BF16)
        sel_vnf = sb(es, "sel_vnf", [128, 32, 64], F32)
        sel_addc = sb(es, "sel_addc", [128, 32, 64], F32)
        sel_valid = sb(es, "sel_valid", [128, 32, 64], F32)
        KC = sb(es, "KC", [128, 4, 256], BF16)
        VC = sb(es, "VC", [128, 4, 2, 65], BF16)
        t_KC = P.tok("KC", dma=True)
        t_VC = P.tok("VC")
        for g in range(4):
            P.dma("sp", lambda e, g=g: e.dma_start(out=KC[64:128, g, :], in_=cd["kaug_one"][:, 0:256]), t_KC, True)
        P.op("pool", lambda e: e.memset(VC[:], 1.0), writes=[t_VC])

        with ExitStack() as es2:
            w1b = [sb(es2, "w1b%d" % i, [64, 32, 256], BF16) for i in range(2)]
            kvs_ = sb(es2, "kvs", [64, 4, S], BF16)
            kvs = [kvs_, kvs_]
            kvd = sb(es2, "kvd", [64, 4, 16, 256], BF16)
            t_kvd = P.tok("kvd")
            pef = sb(es2, "pef", [64, 64], F32)
            peb = sb(es2, "peb", [64, 64], BF16)
            b1T = sb(es2, "b1T", [128, 4], F32)
            bias1 = sb(es2, "bias1", [128, 4], F32)
            w2f = sb(es2, "w2f", [128, 2, 2, 64], F32)
            w2b = sb(es2, "w2b", [128, 2, 2, 64], BF16)
            b2T = sb(es2, "b2T", [64, 2], F32)
            xh = sb(es2, "xh", [128, 256], F32)
            tt1 = sb(es2, "tt1", [128, 256], F32)
            sg = sb(es2, "sg", [128, 256], F32)
            gel = [sb(es2, "gel%d" % i, [128, 2, 256], BF16) for i in range(2)]
            vct = sb(es2, "vct", [64, 256], BF16)
            ph = [ps(es2, "ph%d" % i, [128, 256], F32) for i in range(2)]
            pb1 = ps(es2, "pb1", [128, 4], F32)
            pk = ps(es2, "pk", [64, 256], F32)
            pvt = ps(es2, "pvt", [128, 128], BF16)
            t_w1b = [P.tok("w1b0"), P.tok("w1b1")]
            t_kvs_ = P.tok("kvs", dma=True)
            t_kvs = [t_kvs_, t_kvs_]
            t_sm = P.tok("cmpsmall", dma=True)
            t_peb = P.tok("peb")
            t_bias1 = P.tok("bias1")
            t_w2b = P.tok("w2b")
            t_xh, t_tt1, t_sg = P.tok("xh"), P.tok("tt1"), P.tok("sg")
            t_gel = [P.tok("gel0"), P.tok("gel1")]
            t_vct = P.tok("vct")
            t_ph = [P.tok("ph0"), P.tok("ph1")]
            t_pb1, t_pk, t_pvt = P.tok("pb1"), P.tok("pk"), P.tok("pvt")
            P.dma("sp", lambda e: e.dma_start(out=pef[:], in_=peT_d), t_sm, True)
            P.dma("sp", lambda e: e.dma_start(out=b1T[:], in_=b1T_d), t_sm, True)
            P.dma("sp", lambda e: e.dma_start(out=b2T[:], in_=b2T_d), t_sm, True)
            for xx in range(2):
                P.dma("sp", lambda e, xx=xx: e.dma_start(
                    out=w2f[:, xx, :, :], in_=w2_d[xx].rearrange("(hh p) d -> p hh d", p=128)), t_sm, True)
            P.op("dve", lambda e: e.tensor_copy(peb[:], pef[:]), reads=[t_sm], writes=[t_peb])
            P.op("dve", lambda e: e.tensor_copy(w2b[:], w2f[:]), reads=[t_sm], writes=[t_w2b])
            P.op("pool", lambda e: e.memset(gel[0][:], 0.0), writes=[t_gel[0]])
            P.op("pool", lambda e: e.memset(gel[1][:], 0.0), writes=[t_gel[1]])

            def load_x(xx):
                if xx == 1:
                    P.dma("sp", lambda e: e.dma_start(
                        out=w1f[:], in_=w1_d[xx].rearrange("(l d) h -> d l h", d=64)), t_w1f, True)
                for l4 in range(4):
                    if l4 % 2 == 0:
                        P.op("act", lambda e, l4=l4: e.activation(
                            out=w1b[xx][:, l4 * 8:(l4 + 1) * 8, :], in_=w1f[:, l4 * 8:(l4 + 1) * 8, :],
                            func=AF.Identity), reads=[t_w1f], writes=[t_w1b[xx]])
                    else:
                        P.op("dve", lambda e, l4=l4: e.tensor_copy(
                            w1b[xx][:, l4 * 8:(l4 + 1) * 8, :], w1f[:, l4 * 8:(l4 + 1) * 8, :]),
                            reads=[t_w1f], writes=[t_w1b[xx]])
                P.dma("sp", lambda e: e.dma_start(
                    out=kvs[xx][:], in_=kvT_s[xx].rearrange("(g d) s -> d g s", d=64)), t_kvs[xx], True)
                for hh in range(2):
                    col = xx * 2 + hh
                    for l in range(32):
                        P.op("pe", lambda e, hh=hh, l=l, col=col: e.matmul(
                            pb1[:, col:col + 1], lhsT=w1b[xx][:, l, hh * 128:(hh + 1) * 128],
                            rhs=peb[:, xx * 32 + l:xx * 32 + l + 1], start=(l == 0), stop=(l == 31)),
                            reads=[t_w1b[xx], t_peb], writes=[t_pb1])
                P.op("dve", lambda e: e.tensor_tensor(out=bias1[:, 2 * xx:2 * xx + 2], in0=pb1[:, 2 * xx:2 * xx + 2],
                                                      in1=b1T[:, 2 * xx:2 * xx + 2], op=ALU.add),
                     reads=[t_pb1, t_sm], writes=[t_bias1])

            it = 0
            load_x(0)
            for dst, nm in ((cmask, "cmask"), (tri_c, "tri_causal"), (tri_b, "tri_band"), (bias_tab, "bias_tab"),
                            (biasc_tab, "biasc_tab"), (ovt, "ov"), (sel_vnf, "sel_vnf"), (sel_addc, "sel_addc"),
                            (sel_valid, "sel_valid")):
                P.dma("sp", lambda e, dst=dst, nm=nm: e.dma_start(out=dst[:], in_=cd[nm]), t_c3, True)

            for xx in range(2):
                if xx == 1:
                    load_x(1)
                for g in range(4):
                    P.op("dve", lambda e, g=g, xx=xx: e.tensor_copy(
                        kvd[:, g, :, :], kvs[xx][:, g, :].rearrange("d (c s) -> d s c", s=16)),
                        reads=[t_kvs[xx]], writes=[t_kvd])
                for g in range(4):
                    ge, t_ge = gel[(xx * 4 + g) % 2], t_gel[(xx * 4 + g) % 2]
                    for hh in range(2):
                        p_, t_p = ph[it % 2], t_ph[it % 2]
                        it += 1
                        for l in range(32):
                            P.op("pe", lambda e, p_=p_, xx=xx, g=g, hh=hh, l=l: e.matmul(
                                p_[:, 0:255], lhsT=w1b[xx][:, l, hh * 128:(hh + 1) * 128],
                                rhs=kvd[:, g, l % 16, l // 16:l // 16 + 255], start=(l == 0), stop=(l == 31)),
                                reads=[t_w1b[xx], t_kvd], writes=[t_p])
                        col = xx * 2 + hh
                        P.op("act", lambda e, p_=p_, col=col: e.activation(
                            out=xh[:, 0:255], in_=p_[:, 0:255], func=AF.Identity, bias=bias1[:, col:col + 1]),
                            reads=[t_p, t_bias1], writes=[t_xh])
                        P.op("dve", lambda e: e.tensor_tensor(out=tt1[:, 0:255], in0=xh[:, 0:255],
                                                              in1=xh[:, 0:255], op=ALU.mult),
                             reads=[t_xh], writes=[t_tt1])
                        P.op("dve", lambda e: e.tensor_scalar(out=tt1[:, 0:255], in0=tt1[:, 0:255],
                                                              scalar1=0.044715, scalar2=1.0,
                                                              op0=ALU.mult, op1=ALU.add),
                             reads=[t_tt1], writes=[t_tt1])
                        P.op("dve", lambda e: e.tensor_tensor(out=tt1[:, 0:255], in0=tt1[:, 0:255],
                                                              in1=xh[:, 0:255], op=ALU.mult),
                             reads=[t_tt1, t_xh], writes=[t_tt1])
                        P.op("act", lambda e: e.activation(out=sg[:, 0:255], in_=tt1[:, 0:255], func=AF.Sigmoid,
                                                           scale=1.5957691216057308),
                             reads=[t_tt1], writes=[t_sg])
                        P.op("dve", lambda e, ge=ge, hh=hh: e.tensor_tensor(
                            out=ge[:, hh, 0:255], in0=xh[:, 0:255], in1=sg[:, 0:255], op=ALU.mult),
                            reads=[t_xh, t_sg], writes=[t_ge])
                    for hh in range(2):
                        P.op("pe", lambda e, ge=ge, xx=xx, hh=hh: e.matmul(
                            pk[:], lhsT=w2b[:, xx, hh, :], rhs=ge[:, hh, :], start=(hh == 0), stop=(hh == 1)),
                            reads=[t_ge, t_w2b], writes=[t_pk])
                    if xx == 0:
                        P.op("act", lambda e, g=g: e.activation(
                            out=KC[0:64, g, :], in_=pk[:], func=AF.Identity, bias=b2T[:, 0:1]),
                            reads=[t_pk, t_sm], writes=[t_KC])
                    else:
                        P.op("act", lambda e: e.activation(
                            out=vct[:], in_=pk[:], func=AF.Identity, bias=b2T[:, 1:2]),
                            reads=[t_pk, t_sm], writes=[t_vct])
                        for ct in range(2):
                            P.op("pe", lambda e, ct=ct: e.transpose(
                                pvt[:, 0:64], vct[:, ct * 128:(ct + 1) * 128], ident_b[0:64, 0:64]),
                                reads=[t_vct, t_const], writes=[t_pvt])
                            P.op("dve", lambda e, g=g, ct=ct: e.tensor_copy(VC[:, g, ct, 0:64], pvt[:, 0:64]),
                                 reads=[t_pvt], writes=[t_VC])
            P.barrier()
            P.flush()
        if stop_after == 2.5:
            return finish()

        KS = [sb(es, "KS%d" % i, [128, S], BF16) for i in range(2)]
        KW = [sb(es, "KW%d" % i, [128, S], BF16) for i in range(2)]
        VS = [sb(es, "VS%d" % i, [128, 32, 65], BF16) for i in range(2)]
        VW = [sb(es, "VW%d" % i, [128, 32, 65], BF16) for i in range(2)]
        t_KS = [P.tok("KS%d" % i, dma=True) for i in range(2)]
        t_KW = [P.tok("KW%d" % i, dma=True) for i in range(2)]
        t_VS = [P.tok("VS%d" % i, dma=True) for i in range(2)]
        t_VW = [P.tok("VW%d" % i, dma=True) for i in range(2)]
        for i in range(2):
            P.op("pool", lambda e, i=i: e.memset(VS[i][:], 1.0), writes=[t_VS[i]])
            P.op("pool", lambda e, i=i: e.memset(VW[i][:], 1.0), writes=[t_VW[i]])
            P.dma("sp", lambda e, i=i: e.dma_start(out=KS[i][64:128, :], in_=cd["kaug_slc"]), t_KS[i], True)
            P.dma("sp", lambda e, i=i: e.dma_start(out=KW[i][64:128, :], in_=cd["kaug_one"]), t_KW[i], True)
        QA = [[sb(es, "QA%d_%d" % (hh, i), [128, 512], BF16) for i in range(2)] for hh in range(4)]
        t_QA = [[P.tok("QA%d_%d" % (hh, i), dma=True) for i in range(2)] for hh in range(4)]
        for hh in range(4):
            for i in range(2):
                P.op("pool", lambda e, hh=hh, i=i: e.memset(QA[hh][i][:], 0.0), writes=[t_QA[hh][i]])
        resA = {"PT": Rot([(sb(es, "PTA%d" % i, [128, 512], BF16), P.tok("PTA")) for i in range(2)]),
                "Sp": Rot([(ps(es, "SpA%d" % i, [128, 512], F32), P.tok("SpA")) for i in range(1)]),
                "Op": Rot([(ps(es, "OpA%d" % i, [128, 4, 128], F32), P.tok("OpA")) for i in range(1)])}
        resB = {"PT": Rot([(sb(es, "PTB%d" % i, [128, 512], BF16), P.tok("PTB")) for i in range(4)]),
                "Sp": Rot([(ps(es, "SpB%d" % i, [128, 512], F32), P.tok("SpB")) for i in range(2)]),
                "Op": Rot([(ps(es, "OpB%d" % i, [128, 4, 128], F32), P.tok("OpB")) for i in range(2)])}
        IMPp = ps(es, "IMPp", [128, 4, 64], F32)
        t_IMP = P.tok("IMP")
        negTp = ps(es, "negTp", [128, 512], BF16)
        t_negT = P.tok("negT")
        yacc = [sb(es, "yacc%d" % i, [128, 4, 256], F32) for i in range(2)]
        t_yacc = [P.tok("yacc0"), P.tok("yacc1")]
        ztile = [sb(es, "ztile%d" % i, [128, 4, 256], F32) for i in range(2)]
        t_zt = [P.tok("zt%d" % i, dma=True) for i in range(2)]
        ybf = [sb(es, "ybf%d" % i, [128, 4, 256], BF16) for i in range(2)]
        t_ybf = [P.tok("ybf%d" % i, dma=True) for i in range(2)]
        sc = sb(es, "sc", [128, 4, 64], F32)
        t_sc = P.tok("sc")
        for nm, r_ in (("A", resA), ("B", resB)):
            r_["small"] = Rot([(sb(es, "sm%s%d" % (nm, i), [128, 16], F32), P.tok("sm")) for i in range(3)])
            r_["tmpo"] = Rot([(sb(es, "tmpo%s%d" % (nm, i), [128, 4, 64], F32), P.tok("tmpo")) for i in range(2)])
        s1 = sb(es, "s1", [128, 64], F32)
        s2 = sb(es, "s2", [128, 64], F32)
        m8 = sb(es, "m8", [128, 16], F32)
        negpad = sb(es, "negpad", [128, 128], BF16)
        t_s1, t_s2, t_m8, t_negpad = P.tok("s1"), P.tok("s2"), P.tok("m8"), P.tok("negpad")
        P.op("pool", lambda e: e.memset(negpad[:], 0.0), writes=[t_negpad])
        zb = sb(es, "zb", [128, 512], BF16)
        t_zb = P.tok("zb")
        P.op("pool", lambda e: e.memset(zb[:], 0.0), writes=[t_zb])

        def zero_bank(ap2d, t_bank, n):
            P.op("pe", lambda e: e.matmul(ap2d, lhsT=zb[:, 0:128], rhs=zb[:, 0:n], start=True, stop=False),
                 reads=[t_zb], writes=[t_bank])

        def s_tile(res, branch, h, t, Q, t_Q, Ktile, t_K, kcol, col0, ncol, mask, bias_ap, t_bias):
            sp, t_sp = res["Sp"].next()
            c1 = col0 + ncol
            P.op("pe", lambda e: e.matmul(sp[:, col0:c1], lhsT=Ktile[:, kcol:kcol + 128], rhs=Q[:, col0:c1],
                                          start=True, stop=(mask is None)),
                 reads=[t_Q, t_K], writes=[t_sp])
            if mask is not None:
                mcol, mrhs, t_m = mask
                w = mrhs.shape[-1] if hasattr(mrhs, "shape") else 128
                P.op("pe", lambda e: e.matmul(sp[:, mcol:mcol + w], lhsT=ident_b[:], rhs=mrhs,
                                              start=False, stop=True),
                     reads=[t_const, t_m], writes=[t_sp])
            pt, t_pt = res["PT"].next()
            P.op("act", lambda e: e.activation(out=pt[:, col0:c1], in_=sp[:, col0:c1], func=AF.Exp,
                                               scale=0.125, bias=bias_ap),
                 reads=[t_sp, t_bias], writes=[t_pt])
            return pt, t_pt

        def finalize(res, br, h, hh, t, op_, t_op, ya, t_ya, first):
            sm, t_sm_ = res["small"].next()
            gcol = br * 16 + h
            P.op("dve", lambda e: e.tensor_scalar(out=sm[:, 0:4], in0=op_[:, :, 64], scalar1=1e-30,
                                                  scalar2=None, op0=ALU.max),
                 reads=[t_op], writes=[t_sm_])
            P.op("dve", lambda e: e.reciprocal(out=sm[:, 4:8], in_=sm[:, 0:4]), reads=[t_sm_], writes=[t_sm_])
            P.op("dve", lambda e: e.tensor_tensor(out=sm[:, 8:12], in0=sm[:, 4:8],
                                                  in1=gate_sb[:, 4 * t:4 * t + 4, gcol], op=ALU.mult),
                 reads=[t_sm_, t_gate], writes=[t_sm_])
            fb = sm[:, 8:12].unsqueeze(2).broadcast_to([128, 4, 64])
            if first:
                P.op("dve", lambda e: e.tensor_tensor(out=ya[:, :, hh * 64:(hh + 1) * 64], in0=op_[:, :, 0:64],
                                                      in1=fb, op=ALU.mult),
                     reads=[t_op, t_sm_], writes=[t_ya])
            else:
                to, t_to = res["tmpo"].next()
                P.op("dve", lambda e: e.tensor_tensor(out=to[:], in0=op_[:, :, 0:64], in1=fb, op=ALU.mult),
                     reads=[t_op, t_sm_], writes=[t_to])
                P.op("dve", lambda e: e.tensor_tensor(out=ya[:, :, hh * 64:(hh + 1) * 64],
                                                      in0=ya[:, :, hh * 64:(hh + 1) * 64], in1=to[:], op=ALU.add),
                     reads=[t_to, t_ya], writes=[t_ya])
            return sm, t_sm_

        class Pipe:
            def __init__(self, res, depth):
                self.pending = []
                self.depth = depth
                self.res = res

            def push(self, s_args, pv_fn, post_fn=None):
                pt, t_pt = s_tile(self.res, *s_args)
                self.pending.append((pt, t_pt, pv_fn, post_fn))
                while len(self.pending) > self.depth:
                    self._pop()

            def _pop(self):
                pt, t_pt, pv_fn, post_fn = self.pending.pop(0)
                pv_fn(pt, t_pt)
                if post_fn is not None:
                    post_fn()

            def flush(self):
                while self.pending:
                    self._pop()

        pipeA = Pipe(resA, 1)
        pipeB = Pipe(resB, 2)

        def load_group(g):
            gi = g % 2
            P.dma("sp", lambda e: e.dma_start(
                out=KS[gi][0:64, :], in_=kvT_s[2, g * 64:(g + 1) * 64, :]), t_KS[gi], True)
            P.dma("sp", lambda e: e.dma_start(
                out=KW[gi][0:64, :], in_=kvT_s[3, g * 64:(g + 1) * 64, :]), t_KW[gi], True)
            for half in range(2):
                P.dma("sp", lambda e, half=half: e.dma_start(
                    out=VS[gi][:, half * 16:(half + 1) * 16, 0:64],
                    in_=vtok_s[half * 2048:(half + 1) * 2048, g * 64:(g + 1) * 64].rearrange(
                        "(kt p) d -> p kt d", p=128)), t_VS[gi], True)
                P.dma("sp", lambda e, half=half: e.dma_start(
                    out=VW[gi][:, half * 16:(half + 1) * 16, 0:64],
                    in_=vtok_s[half * 2048:(half + 1) * 2048, 256 + g * 64:256 + (g + 1) * 64].rearrange(
                        "(kt p) d -> p kt d", p=128)), t_VW[gi], True)

        def stage_A(u, g, t):
            par = u % 2
            q0 = t * 512
            ya, t_ya = yacc[par], t_yacc[par]
            for hh in range(4):
                h = g * 4 + hh
                P.dma("sp", lambda e, hh=hh, h=h: e.dma_start(
                    out=QA[hh][par][0:64, :], in_=qT_s[h * 64:(h + 1) * 64, q0:q0 + 512]),
                    t_QA[hh][par], True)
                P.dma("sp", lambda e, hh=hh, h=h: e.dma_start(
                    out=QA[hh][par][127:128, :], in_=cd["vrow"][h:h + 1, :]), t_QA[hh][par], True)
            zt, t_z = ztile[par], t_zt[par]
            P.dma("sp", lambda e: e.dma_start(
                out=zt[:], in_=zs_s[t * 512:(t + 1) * 512, g * 256:(g + 1) * 256].rearrange(
                    "(s p) c -> p s c", p=128)), t_z, True)
            ncts = 2 if t >= 4 else 1
            for hh in range(4):
                h = g * 4 + hh
                Q, t_Q = QA[hh][par], t_QA[hh][par]
                op_, t_op = resA["Op"].next()
                for ct in range(ncts):
                    bcol = (h * 8 + t) * 2 + ct

                    def pv(pt, t_pt, op_=op_, t_op=t_op, ct=ct):
                        if ct == 0:
                            zero_bank(op_[:].rearrange("p a b -> p (a b)"), t_op, 512)
                            zero_bank(IMPp[:].rearrange("p a b -> p (a b)"), t_IMP, 256)
                        for qs in range(4):
                            P.op("pe", lambda e, qs=qs: e.matmul(
                                op_[:, qs, 0:65], lhsT=pt[:, qs * 128:(qs + 1) * 128], rhs=VC[:, g, ct, :],
                                start=False, stop=(ct == ncts - 1 and qs == 3)),
                                reads=[t_pt, t_VC], writes=[t_op])
                            P.op("pe", lambda e, qs=qs: e.matmul(
                                IMPp[:, qs, :], lhsT=pt[:, qs * 128:(qs + 1) * 128], rhs=ovt[:, ct, :],
                                start=False, stop=(ct == ncts - 1 and qs == 3)),
                                reads=[t_pt, t_c3], writes=[t_IMP])

                    def post(op_=op_, t_op=t_op, h=h, hh=hh):
                        sm, t_sm_ = finalize(resA, 0, h, hh, t, op_, t_op, ya, t_ya, True)
                        rb = sm[:, 4:8].unsqueeze(2).broadcast_to([128, 4, 64])
                        if hh == 0:
                            P.op("dve", lambda e: e.tensor_tensor(out=sc[:], in0=IMPp[:], in1=rb, op=ALU.mult),
                                 reads=[t_IMP, t_sm_], writes=[t_sc])
                        else:
                            to, t_to = resA["tmpo"].next()
                            P.op("dve", lambda e: e.tensor_tensor(out=to[:], in0=IMPp[:], in1=rb, op=ALU.mult),
                                 reads=[t_IMP, t_sm_], writes=[t_to])
                            P.op("dve", lambda e: e.tensor_tensor(out=sc[:], in0=sc[:], in1=to[:], op=ALU.add),
                                 reads=[t_to, t_sc], writes=[t_sc])

                    pipeA.push((0, h, t, Q, t_Q, KC[:, g, :], t_KC, ct * 128, 0, 512,
                               (0, cmask[:, ct, q0:q0 + 512], t_c3), biasc_tab[:, bcol:bcol + 1], t_c3),
                              pv, post if ct == ncts - 1 else None)
            pipeA.flush()
            for qs in range(4):
                tt = 4 * t + qs
                P.op("dve", lambda e, qs=qs, tt=tt: e.tensor_tensor(out=s1[:], in0=sc[:, qs, :],
                                                                    in1=sel_vnf[:, tt, :], op=ALU.mult),
                     reads=[t_sc, t_c3], writes=[t_s1])
                P.op("dve", lambda e, tt=tt: e.tensor_tensor(out=s1[:], in0=s1[:], in1=sel_addc[:, tt, :],
                                                             op=ALU.add),
                     reads=[t_s1, t_c3], writes=[t_s1])
                P.op("dve", lambda e: e.max(out=m8[:, 0:8], in_=s1[:]), reads=[t_s1], writes=[t_m8])
                P.op("dve", lambda e: e.tensor_scalar(out=s2[:], in0=s1[:], scalar1=m8[:, 7:8],
                                                      scalar2=None, op0=ALU.is_ge),
                     reads=[t_s1, t_m8], writes=[t_s2])
                P.op("dve", lambda e: e.scalar_tensor_tensor(out=s2[:], in0=s2[:], scalar=-3.0e9, in1=s1[:],
                                                             op0=ALU.mult, op1=ALU.add),
                     reads=[t_s1, t_s2], writes=[t_s2])
                P.op("dve", lambda e: e.max(out=m8[:, 8:16], in_=s2[:]), reads=[t_s2], writes=[t_m8])
                P.op("dve", lambda e: e.tensor_scalar(out=s2[:], in0=s1[:], scalar1=m8[:, 15:16],
                                                      scalar2=None, op0=ALU.is_ge),
                     reads=[t_s1, t_m8], writes=[t_s2])
                P.op("dve", lambda e, tt=tt: e.tensor_tensor(out=s2[:], in0=s2[:], in1=sel_valid[:, tt, :],
                                                             op=ALU.mult),
                     reads=[t_s2, t_c3], writes=[t_s2])
                P.op("dve", lambda e: e.tensor_scalar(out=negpad[:, 64:127], in0=s2[:, 0:63], scalar1=1.0,
                                                      scalar2=-NEG, op0=ALU.subtract, op1=ALU.mult),
                     reads=[t_s2], writes=[t_negpad])
                P.op("pe", lambda e, qs=qs: e.transpose(negTp[:, qs * 128:(qs + 1) * 128], negpad[:], ident_b[:]),
                     reads=[t_negpad, t_const], writes=[t_negT])
            for hh in range(4):
                h = g * 4 + hh
                P.op("dve", lambda e, hh=hh: e.tensor_copy(QA[hh][par][64:127, :], negTp[64:127, :]),
                     reads=[t_negT], writes=[t_QA[hh][par]])

        def stage_B(u, g, t):
            par = u % 2
            gi = g % 2
            ya, t_ya = yacc[par], t_yacc[par]
            zt, t_z = ztile[par], t_zt[par]
            for hh in range(4):
                h = g * 4 + hh
                Q, t_Q = QA[hh][par], t_QA[hh][par]
                for br, Kt, t_K, Vt, t_V in ((1, KS[gi], t_KS[gi], VS[gi], t_VS[gi]),
                                             (2, KW[gi], t_KW[gi], VW[gi], t_VW[gi])):
                    tiles = []
                    if br == 1:
                        for kt in range(0, 4 * t):
                            md = (4 * t - kt - 1) * 128 + 1
                            if SLOPES[h] * md >= ZERO_CUT:
                                continue
                            tiles.append((kt, 0, 512, None))
                    else:
                        for i in (3, 2, 1, 0):
                            kt = 4 * t - 4 + i
                            md = 385 - 128 * i if i >= 1 else 385
                            if kt >= 0 and SLOPES[h] * md < ZERO_CUT:
                                tiles.append((kt, 0, 128 * (i + 1), (128 * i, tri_b[:], t_c3)))
                    for i2 in range(4):
                        tiles.append((4 * t + i2, 128 * i2, 512 - 128 * i2, (128 * i2, tri_c[:], t_c3)))
                    last = {}
                    for ti, (kt, col0, ncol, mask) in enumerate(tiles):
                        for qs in range(col0 // 128, (col0 + ncol) // 128):
                            last[qs] = ti
                    op_, t_op = resB["Op"].next()
                    for ti, (kt, col0, ncol, mask) in enumerate(tiles):
                        bcol = h * 32 + (kt - 4 * t + 28)

                        def pv(pt, t_pt, op_=op_, t_op=t_op, ti=ti, kt=kt, col0=col0, ncol=ncol, Vt=Vt, t_V=t_V,
                               last=last, tiles=tiles):
                            if ti == 0:
                                zero_bank(op_[:].rearrange("p a b -> p (a b)"), t_op, 512)
                            for qs in range(col0 // 128, (col0 + ncol) // 128):
                                P.op("pe", lambda e, qs=qs: e.matmul(
                                    op_[:, qs, 0:65], lhsT=pt[:, qs * 128:(qs + 1) * 128], rhs=Vt[:, kt, :],
                                    start=False, stop=(ti == len(tiles) - 1 and qs == 3)),
                                    reads=[t_pt, t_V], writes=[t_op])

                        def post(op_=op_, t_op=t_op, br=br, h=h, hh=hh):
                            finalize(resB, br, h, hh, t, op_, t_op, ya, t_ya, False)

                        pipeB.push((br, h, t, Q, t_Q, Kt, t_K, kt * 128, col0, ncol, mask,
                                   bias_tab[:, bcol:bcol + 1], t_c3), pv,
                                  post if ti == len(tiles) - 1 else None)
            pipeB.flush()
            yb, t_yb = ybf[par], t_ybf[par]
            P.op("dve", lambda e: e.tensor_tensor(out=yb[:], in0=ya[:], in1=zt[:], op=ALU.mult),
                 reads=[t_ya, t_z], writes=[t_yb])
            P.dma("pool", lambda e: e.dma_start(
                out=y0_s[t * 512:(t + 1) * 512, g * 256:(g + 1) * 256].rearrange("(s p) c -> p s c", p=128),
                in_=yb[:]), t_yb, False)

        units = [(g, t) for g in range(4) for t in range(NQ)]
        for u in range(len(units) + 1):
            lists = []
            if u < len(units):
                g, t = units[u]
                if t == 0:
                    load_group(g)
                P.capture()
                stage_A(u, g, t)
                lists.append(P.end_capture())
            if u > 0:
                P.capture()
                stage_B(u - 1, *units[u - 1])
                lists.append(P.end_capture())
            P.replay_merged(lists)
        P.end_phase()
    es_w1.close()
    if stop_after == 3:
        return finish()


    es_g = ExitStack()
    ipre = sb(es_g, "ipre", [128, NT, 8], F32)
    logf = sb(es_g, "logf", [128, NT, 8], F32)
    wo1 = sb(es_g, "wo1", [128, 16, D], BF16)
    t_wo1 = Tok("wo1", P.free.pop(0))
    es_h = ExitStack()
    hT = sb(es_h, "hT_l1", [128, 8, S], BF16)
    t_ipre, t_logf = P.tok("ipre"), P.tok("logf")
    with ExitStack() as es:
        R = norm_res(es)
        wo = sb(es, "wo", [128, 8, D], BF16)
        t_wo = P.tok("wo", dma=True)
        P.dma("pool", lambda e: e.dma_start(out=wo[:], in_=woa_d.rearrange("(kc p) n -> p kc n", p=128)), t_wo, True)
        yin = Rot([(sb(es, "yin%d" % i, [128, D], BF16), P.tok("yin%d" % i, dma=True)) for i in range(3)])
        xin = Rot([(sb(es, "xin4_%d" % i, [128, D], F32), P.tok("xin4_%d" % i, dma=True)) for i in range(3)])
        x1t = Rot([(sb(es, "x1t%d" % i, [128, D], F32), P.tok("x1t%d" % i, dma=True)) for i in range(3)])
        yT = Rot([(sb(es, "yT%d" % i, [128, 8, 128], BF16), P.tok("yT")) for i in range(2)])
        ytp = ps(es, "ytp", [128, D], BF16)
        t_ytp = P.tok("ytp")
        po = [ps(es, "po%d" % i, [128, 512], F32) for i in range(2)]
        t_po = [P.tok("po0"), P.tok("po1")]
        def stage_X(tt):
            yt, t_yt = yin.next()
            xt, t_xt = xin.next()
            x1, t_x1 = x1t.next()
            yTt, t_yT = yT.next()
            P.dma("sp", lambda e, yt=yt, tt=tt: e.dma_start(out=yt[:], in_=y0_s[tt * 128:(tt + 1) * 128, :]),
                  t_yt, True)
            P.dma("sp", lambda e, xt=xt, tt=tt: e.dma_start(out=xt[:], in_=x_d[tt * 128:(tt + 1) * 128, :]),
                  t_xt, True)
            for kc in range(8):
                P.op("pe", lambda e, yt=yt, kc=kc: e.transpose(
                    ytp[:, kc * 128:(kc + 1) * 128], yt[:, kc * 128:(kc + 1) * 128], ident_b[:]),
                    reads=[t_yt, t_const], writes=[t_ytp])
            P.op("act", lambda e, yTt=yTt: e.activation(
                out=yTt[:, 0:4, :], in_=ytp[:, 0:512].rearrange("p (a b) -> p a b", b=128), func=AF.Identity),
                reads=[t_ytp], writes=[t_yT])
            P.op("act", lambda e, yTt=yTt: e.activation(
                out=yTt[:, 4:8, :], in_=ytp[:, 512:1024].rearrange("p (a b) -> p a b", b=128), func=AF.Identity),
                reads=[t_ytp], writes=[t_yT])
            for hf in range(2):
                for kc in range(8):
                    P.op("pe", lambda e, yTt=yTt, hf=hf, kc=kc: e.matmul(
                        po[hf][:], lhsT=yTt[:, kc, :], rhs=wo[:, kc, hf * 512:(hf + 1) * 512],
                        start=(kc == 0), stop=(kc == 7)), reads=[t_yT, t_wo], writes=[t_po[hf]])
                P.op("dve", lambda e, x1=x1, hf=hf: e.tensor_tensor(
                    out=x1[:, hf * 512:(hf + 1) * 512], in0=po[hf][:], in1=gate_bc[0][:, hf * 512:(hf + 1) * 512],
                    op=ALU.mult), reads=[t_po[hf], t_gbc[0]], writes=[t_x1])
            P.op("dve", lambda e, x1=x1, xt=xt: e.tensor_tensor(out=x1[:], in0=x1[:], in1=xt[:], op=ALU.add),
                 reads=[t_xt, t_x1], writes=[t_x1])
            P.dma("pool", lambda e, x1=x1, tt=tt: e.dma_start(out=x1_s[tt * 128:(tt + 1) * 128, :], in_=x1[:]),
                  t_x1, False)
            return x1, t_x1

        x1s, xns = {}, {}
        for tt in range(NT + 2):
            lists = []
            if tt < NT:
                P.capture()
                x1s[tt] = stage_X(tt)
                lists.append(P.end_capture())
            if 0 <= tt - 1 < NT:
                P.capture()
                x1, t_x1 = x1s.pop(tt - 1)
                xns[tt - 1] = norm_a(x1[:], t_x1, R)
                lists.append(P.end_capture())
            if 0 <= tt - 2 < NT:
                P.capture()
                norm_b(1, tt - 2, *xns.pop(tt - 2), R)
                lists.append(P.end_capture())
            P.replay_merged(lists)
        P.end_phase()
    if stop_after == 4:
        return finish()

    with ExitStack() as es:
        frow = [sb(es, "frow5_%d" % i, [128, S], BF16) for i in range(2)]
        t_frow = [P.tok("frow5_%d" % i, dma=True) for i in range(2)]
        raw = Rot([(sb(es, "raw%d" % i, [128, 515], F32), P.tok("raw"), P.tok("rawc")) for i in range(3)])
        acc = Rot([(sb(es, "acc%d" % i, [128, 512], F32), P.tok("acc")) for i in range(3)])
        convw = sb(es, "convw", [128, 64], F32)
        convb = sb(es, "convb", [128, 16], F32)
        gb_bc = sb(es, "gb_bc", [128, 16], F32)
        t_cv = P.tok("cv", dma=True)
        P.dma("sp", lambda e: e.dma_start(out=convw[:], in_=convwT_d), t_cv, True)
        P.dma("sp", lambda e: e.dma_start(out=convb[:], in_=convbT_d), t_cv, True)
        P.dma("sp", lambda e: e.dma_start(out=gb_bc[:], in_=bass.AP(gateb_d.tensor, 0, [[0, 128], [1, 16]])), t_cv, True)
        tstb = Rot([(sb(es, "tstb5_%d" % i, [128, 4, 512], BF16), P.tok("tstb5_%d" % i, dma=True)) for i in range(2)])
        tsto = Rot([(sb(es, "tsto%d" % i, [128, 4, 256], F32), P.tok("tsto%d" % i, dma=True)) for i in range(2)])
        z1 = Rot([(sb(es, "z1_%d" % i, [128, 256], F32), P.tok("z1")) for i in range(2)])
        gtmp = Rot([(sb(es, "gtmp%d" % i, [128, 24], F32), P.tok("gtmp")) for i in range(2)])
        for kh in range(4):
            P.dma("pool", lambda e, kh=kh: e.dma_start(
                out=wo1[:, kh * 4:(kh + 1) * 4, :], in_=wob_d[kh * 512:(kh + 1) * 512, :].rearrange(
                    "(kc p) n -> p kc n", p=128)), t_wo1, True)
        hg_bc5 = sb(es, "hg_bc5", [128, 2048], F32)
        P.dma("sp", lambda e: e.dma_start(out=hg_bc5[:], in_=bass.AP(headg_d.tensor, 0, [[0, 128], [1, 2048]])),
              t_cv, True)
        sgt = Rot([(sb(es, "sgt%d" % i, [128, 512], F32), P.tok("sgt")) for i in range(2)])
        st5 = {"prev": None}

        def fm_sink5(blk, qt, pa, t_pa):
            rw, t_rw, t_rwc = raw.next()
            ac, t_ac = acc.next()
            fr, tf = frow[blk % 2], t_frow[blk % 2]
            P.op("act", lambda e: e.activation(out=rw[:, 3:515], in_=pa[:], func=AF.Identity),
                 reads=[t_pa], writes=[t_rw])
            P.op("act", lambda e: e.activation(out=ac[:], in_=pa[:], func=AF.Identity,
                                               scale=convw[:, blk * 4 + 3:blk * 4 + 4]),
                 reads=[t_pa, t_cv], writes=[t_ac])
            if st5.get("pend"):
                st5.pop("pend")()
            if qt == 0:
                P.op("dve", lambda e: e.memset(rw[:, 0:3], 0.0), writes=[t_rwc])
            else:
                prw, t_prw = st5["prev"]
                P.op("dve", lambda e: e.tensor_copy(rw[:, 0:3], prw[:, 512:515]), reads=[t_prw], writes=[t_rwc])
            st5["prev"] = (rw, t_rw)
            for j in range(0, 3):
                P.op("dve", lambda e, j=j: e.scalar_tensor_tensor(
                    out=ac[:], in0=rw[:, j:j + 512], scalar=convw[:, blk * 4 + j:blk * 4 + j + 1], in1=ac[:],
                    op0=ALU.mult, op1=ALU.add), reads=[t_rw, t_rwc, t_cv, t_ac], writes=[t_ac])

            def pend():
                P.op("act", lambda e: e.activation(out=fr[:, qt * 512:(qt + 1) * 512], in_=ac[:], func=AF.Silu,
                                                   bias=convb[:, blk:blk + 1]),
                     reads=[t_ac, t_cv], writes=[tf])
                if qt == NQ - 1:
                    P.dma("sp", lambda e: e.dma_start(out=qkT1_s[blk], in_=fr[:]), tf, False)

            st5["pend"] = pend

        tm5 = {}

        def tm_sink5(meta, tt, pa, t_pa):
            if st5.get("pend"):
                st5.pop("pend")()
            if meta == "g":
                gt, t_gt = gtmp.next()
                P.op("dve", lambda e: e.tensor_tensor(out=ipre[:, tt, :], in0=pa[:, 0:8], in1=gb_bc[:, 0:8],
                                                      op=ALU.add), reads=[t_pa, t_cv], writes=[t_ipre])
                P.op("dve", lambda e: e.tensor_tensor(out=gt[:, 0:8], in0=pa[:, 8:16], in1=gb_bc[:, 8:16],
                                                      op=ALU.add), reads=[t_pa, t_cv], writes=[t_gt])
                P.op("act", lambda e: e.activation(out=gt[:, 8:16], in_=gt[:, 0:8], func=AF.Exp, scale=-1.0),
                     reads=[t_gt], writes=[t_gt])
                P.op("act", lambda e: e.activation(out=gt[:, 16:24], in_=gt[:, 8:16], func=AF.Ln,
                                                   bias=cvec[:, 1:2]), reads=[t_gt, t_cvec], writes=[t_gt])
                P.op("dve", lambda e: e.tensor_scalar(out=logf[:, tt, :], in0=gt[:, 16:24], scalar1=-1.0,
                                                      scalar2=None, op0=ALU.mult), reads=[t_gt], writes=[t_logf])
                return
            kind, i = meta
            if tt % 4 == 0:
                tm5["buf"] = tstb.next() if kind == "v" else tsto.next()
            st, t_s = tm5["buf"]
            if kind == "v":
                P.op("dve", lambda e: e.tensor_copy(st[:, tt % 4, :], pa[:]), reads=[t_pa], writes=[t_s])
                if tt % 4 == 3:
                    dst = vtok1_s[(tt - 3) * 128:(tt + 1) * 128, i * 512:(i + 1) * 512].rearrange(
                        "(t p) c -> p t c", p=128)
                    P.dma("sp", lambda e: e.dma_start(out=dst, in_=st[:]), t_s, False)
            else:
                zz, t_zz = z1.next()
                sg_, t_sg_ = sgt.next()
                P.op("act", lambda e: e.activation(out=sg_[:], in_=pa[:], func=AF.Sigmoid),
                     reads=[t_pa], writes=[t_sg_])
                P.op("dve", lambda e: e.tensor_tensor(out=zz[:], in0=pa[:, 256:512], in1=sg_[:, 256:512], op=ALU.mult),
                     reads=[t_pa, t_sg_], writes=[t_zz])
                P.op("dve", lambda e: e.tensor_tensor(out=st[:, tt % 4, :], in0=sg_[:, 0:256], in1=zz[:], op=ALU.mult),
                     reads=[t_sg_, t_zz], writes=[t_s])
                P.op("dve", lambda e: e.tensor_tensor(out=st[:, tt % 4, :], in0=st[:, tt % 4, :],
                                                      in1=hg_bc5[:, i * 256:(i + 1) * 256], op=ALU.mult),
                     reads=[t_cv], writes=[t_s])
                if tt % 4 == 3:
                    dst = ogz_s[(tt - 3) * 128:(tt + 1) * 128, i * 256:(i + 1) * 256].rearrange(
                        "(t p) c -> p t c", p=128)
                    P.dma("sp", lambda e: e.dma_start(out=dst, in_=st[:]), t_s, False)

        groups = ([("fm", i * 512, 512, i * 4) for i in range(4)]
                  + [("tm", 2048 + i * 512, 512, ("v", i)) for i in range(4)]
                  + [("tm", 4096 + i * 512, 512, ("ogz", i)) for i in range(8)]
                  + [("tm", 8192, 16, "g")])
        inproj(es, wb_d, groups, fm_sink5, tm_sink5)
        P.end_phase()
    es_h.close()
    if stop_after == 5:
        return finish()

    with ExitStack() as es:
        R = norm_res(es, full=False)
        t_c6 = P.tok("c6", dma=True)
        mask01 = sb(es, "mask01", [128, 128], F32)
        fing_bc = sb(es, "fing_bc", [128, D], F32)
        P.dma("sp", lambda e: e.dma_start(out=mask01[:], in_=cd["mask01"]), t_c6, True)
        P.dma("sp", lambda e: e.dma_start(out=fing_bc[:], in_=bass.AP(fing_d.tensor, 0, [[0, 128], [1, D]])), t_c6, True)
        Ct = sb(es, "Ct", [128, 8, 257], F32)
        Cbf = sb(es, "Cbf", [128, 8, 257], BF16)
        t_Ct = [P.tok("Ct_%d" % h) for h in range(8)]
        t_Cbf = [P.tok("Cbf_%d" % h) for h in range(8)]
        qkr = Rot([(sb(es, "qk%d" % i, [128, 16, 128], BF16), P.tok("qk%d" % i, dma=True)) for i in range(3)])
        Var = [(sb(es, "Va%d" % i, [128, 8, 257], BF16), P.tok("Va%d" % i, dma=True)) for i in range(3)]
        for va, t_va in Var:
            P.op("pool", lambda e, va=va: e.memset(va[:], 1.0), writes=[t_va])
        Var = Rot(Var)
        ogzr = Rot([(sb(es, "ogzt%d" % i, [128, 2048], F32), P.tok("ogzt%d" % i, dma=True)) for i in range(3)])
        x1r = Rot([(sb(es, "x1r%d" % i, [128, D], F32), P.tok("x1r%d" % i, dma=True)) for i in range(5)])
        outr = Rot([(sb(es, "outt%d" % i, [128, D], F32), P.tok("outt%d" % i, dma=True)) for i in range(3)])
        numr = Rot([(sb(es, "numsb%d" % i, [128, 8, 257], F32), P.tok("numsb")) for i in range(2)])
        junk6 = sb(es, "junk6", [128, 256], F32)
        t_junk6 = P.tok("junk6")
        yT6 = sb(es, "yT6", [128, 16, 128], BF16)
        x2 = sb(es, "x2", [128, D], F32)
        t_yT6, t_x2 = P.tok("yT6"), P.tok("x2")
        smr = Rot([(sb(es, "sm6_%d" % i, [128, 64], F32), P.tok("sm6")) for i in range(4)])
        sm2r = Rot([(sb(es, "sm6b_%d" % i, [128, 80], F32), P.tok("sm6b")) for i in range(2)])
        Sm8 = sb(es, "Sm8", [128, 8, 128], BF16)
        Kt8 = sb(es, "Kt8", [128, 8, 128], BF16)
        t_Sm8 = [P.tok("Sm8_%d" % h) for h in range(8)]
        t_Kt8 = [P.tok("Kt8_%d" % h) for h in range(8)]
        po = [ps(es, "po6_%d" % i, [128, 512], F32) for i in range(2)]
        t_po = [P.tok("po6_0"), P.tok("po6_1")]
        ptrk = ps(es, "ptrk", [128, D], BF16)
        ptry = ps(es, "ptry", [128, D], BF16)
        t_ptrk1 = P.tok("ptrk")
        t_ptrk = [t_ptrk1] * 8
        t_ptry = P.tok("ptry")
        pS2 = [ps(es, "pS2_%d" % i, [128, 4, 128], F32) for i in range(2)]
        t_pSb = [P.tok("pSb0"), P.tok("pSb1")]
        t_pS = [t_pSb[h // 4] for h in range(8)]
        pnd_ = ps(es, "pnd", [128, 512], F32)
        pnd2 = [pnd_, pnd_]
        pdC = ps(es, "pdC", [128, 512], F32)
        t_pnd_ = P.tok("pnd")
        t_pnd2, t_pdC = [t_pnd_, t_pnd_], P.tok("pdC")
        t_pbg = t_pdC
        pbg = pdC[:, 384:400]
        chunk = {}

        loaded = {}
        pending_stores = []

        def stage_L(n):
            qk, t_qk = qkr.next()
            va, t_va = Var.next()
            og, t_og = ogzr.next()
            x1, t_x1 = x1r.next()
            P.dma("sp", lambda e: e.dma_start(
                out=qk[:], in_=qkT1_s[:, :, n * 128:(n + 1) * 128].rearrange("b p t -> p b t")), t_qk, True)
            P.dma("sp", lambda e: e.dma_start(
                out=va[:, :, 0:256], in_=vtok1_s[n * 128:(n + 1) * 128, :].rearrange("p (h v) -> p h v", v=256)),
                t_va, True)
            P.dma("sp", lambda e: e.dma_start(out=x1[:], in_=x1_s[n * 128:(n + 1) * 128, :]), t_x1, True)
            P.dma("sp", lambda e: e.dma_start(out=og[:], in_=ogz_s[n * 128:(n + 1) * 128, :]), t_og, True)
            loaded[n] = (qk, t_qk, va, t_va, og, t_og, x1, t_x1)

        def stage_H(n):
            qk, t_qk, va, t_va, og, t_og, x1, t_x1 = loaded.pop(n)
            P.op("pe", lambda e: e.matmul(pdC[:, 384:392], lhsT=mask01[:], rhs=logf[:, n, :], start=True, stop=True),
                 reads=[t_c6, t_logf], writes=[t_pbg])
            P.op("pe", lambda e: e.matmul(pdC[:, 392:400], lhsT=ones_f[:], rhs=logf[:, n, :], start=True, stop=True),
                 reads=[t_const, t_logf], writes=[t_pbg])
            sm, t_sm = smr.next()
            P.op("act", lambda e: e.activation(out=sm[:, 32:48], in_=pdC[:, 384:400], func=AF.Identity),
                 reads=[t_pbg], writes=[t_sm])
            P.op("dve", lambda e: e.tensor_tensor(out=sm[:, 0:8], in0=ipre[:, n, :], in1=sm[:, 32:40],
                                                  op=ALU.subtract), reads=[t_ipre, t_sm], writes=[t_sm])
            P.op("act", lambda e: e.activation(out=sm[:, 8:16], in_=sm[:, 0:8], func=AF.Exp, bias=cvec[:, 2:3]),
                 reads=[t_sm, t_cvec], writes=[t_sm])
            P.op("act", lambda e: e.activation(out=sm[:, 16:32], in_=sm[:, 32:48], func=AF.Exp),
                 reads=[t_sm], writes=[t_sm])
            nums, t_nums = numr.next()
            last = (n == NT - 1)
            prev = chunk.get(n - 1)
            for h in range(8):
                P.op("pe", lambda e, h=h: e.matmul(pS2[h // 4][:, h % 4, :], lhsT=qk[:, 8 + h, :], rhs=qk[:, h, :],
                                                   start=True, stop=True), reads=[t_qk], writes=[t_pS[h]])
            if not last:
                for h in range(8):
                    P.op("pe", lambda e, h=h: e.transpose(ptrk[:, h * 128:(h + 1) * 128], qk[:, 8 + h, :], ident_b[:]),
                         reads=[t_qk, t_const], writes=[t_ptrk[h]])
            if prev is not None:
                psm, t_psm = prev
                for h in range(8):
                    P.op("act", lambda e, h=h: e.activation(
                        out=Cbf[:, h, :], in_=Ct[:, h, :], func=AF.Identity, scale=psm[:, 24 + h:25 + h]),
                        reads=[t_Ct[h], t_psm], writes=[t_Cbf[h]])
            for h in range(8):
                P.op("dve", lambda e, h=h: e.scalar_tensor_tensor(
                    out=Sm8[:, h, :], in0=pS2[h // 4][:, h % 4, :], scalar=sm[:, 8 + h:9 + h], in1=mask01[:],
                    op0=ALU.mult, op1=ALU.mult), reads=[t_pS[h], t_sm, t_c6], writes=[t_Sm8[h]])
                if not last:
                    P.op("act", lambda e, h=h: e.activation(
                        out=Kt8[:, h, :], in_=ptrk[:, h * 128:(h + 1) * 128], func=AF.Identity,
                        scale=sm[:, 8 + h:9 + h]), reads=[t_ptrk[h], t_sm], writes=[t_Kt8[h]])
            for h in range(8):
                pnd, t_pnd = pnd2[h % 2], t_pnd2[h % 2]
                P.op("pe", lambda e, h=h, pnd=pnd: e.matmul(pnd[:, 0:257], lhsT=Sm8[:, h, :], rhs=va[:, h, :],
                                                            start=True, stop=(n == 0)),
                     reads=[t_Sm8[h], t_va], writes=[t_pnd])
                if n > 0:
                    P.op("pe", lambda e, h=h, pnd=pnd: e.matmul(pnd[:, 0:257], lhsT=qk[:, h, :], rhs=Cbf[:, h, :],
                                                                start=False, stop=True),
                         reads=[t_qk, t_Cbf[h]], writes=[t_pnd])
                P.op("act", lambda e, h=h, pnd=pnd: e.activation(out=nums[:, h, :], in_=pnd[:, 0:257],
                                                                 func=AF.Identity),
                     reads=[t_pnd], writes=[t_nums])
                if not last:
                    P.op("pe", lambda e, h=h: e.matmul(pdC[:, 0:257], lhsT=Kt8[:, h, :], rhs=va[:, h, :],
                                                       start=True, stop=True),
                         reads=[t_Kt8[h], t_va], writes=[t_pdC])
                    if n == 0:
                        P.op("dve", lambda e, h=h: e.tensor_copy(Ct[:, h, :], pdC[:, 0:257]),
                             reads=[t_pdC], writes=[t_Ct[h]])
                    else:
                        psm, t_psm = prev
                        P.op("dve", lambda e, h=h: e.scalar_tensor_tensor(
                            out=Ct[:, h, :], in0=Ct[:, h, :], scalar=psm[:, 24 + h:25 + h], in1=pdC[:, 0:257],
                            op0=ALU.mult, op1=ALU.add), reads=[t_pdC, t_psm, t_Ct[h], t_Cbf[h]], writes=[t_Ct[h]])
            chunk[n] = (sm, t_sm)
            return (n, sm, t_sm, nums, t_nums, og, t_og, x1, t_x1)

        ybfr = Rot([(sb(es, "ybf6r%d" % i, [128, 2048], BF16), P.tok("ybf6r")) for i in range(2)])

        def stage_E1(ctx):
            n, sm, t_sm, nums, t_nums, og, t_og, x1, t_x1 = ctx
            s2_, t_s2_ = sm2r.next()
            for h in range(8):
                P.op("act", lambda e, h=h: e.activation(out=junk6[:], in_=nums[:, h, 0:256], func=AF.Square,
                                                        accum_out=s2_[:, h:h + 1]),
                     reads=[t_nums], writes=[t_junk6, t_s2_])
            P.op("dve", lambda e: e.tensor_tensor(out=s2_[:, 8:16], in0=nums[:, :, 256], in1=sm[:, 16:24], op=ALU.mult),
                 reads=[t_nums, t_sm], writes=[t_s2_])
            P.op("dve", lambda e: e.tensor_scalar(out=s2_[:, 16:24], in0=s2_[:, 8:16], scalar1=-1.0, scalar2=None,
                                                  op0=ALU.mult), reads=[t_s2_], writes=[t_s2_])
            P.op("dve", lambda e: e.tensor_tensor(out=s2_[:, 16:24], in0=s2_[:, 16:24], in1=s2_[:, 8:16], op=ALU.max),
                 reads=[t_s2_], writes=[t_s2_])
            P.op("dve", lambda e: e.tensor_scalar(out=s2_[:, 16:24], in0=s2_[:, 16:24], scalar1=1.0, scalar2=None,
                                                  op0=ALU.max), reads=[t_s2_], writes=[t_s2_])
            P.op("dve", lambda e: e.reciprocal(out=s2_[:, 24:32], in_=s2_[:, 16:24]), reads=[t_s2_], writes=[t_s2_])
            P.op("dve", lambda e: e.tensor_tensor(out=s2_[:, 32:40], in0=s2_[:, 24:32], in1=sm[:, 16:24], op=ALU.mult),
                 reads=[t_s2_, t_sm], writes=[t_s2_])
            P.op("dve", lambda e: e.tensor_tensor(out=s2_[:, 40:48], in0=s2_[:, 32:40], in1=s2_[:, 32:40], op=ALU.mult),
                 reads=[t_s2_], writes=[t_s2_])
            P.op("dve", lambda e: e.tensor_tensor(out=s2_[:, 40:48], in0=s2_[:, 40:48], in1=s2_[:, 0:8], op=ALU.mult),
                 reads=[t_s2_], writes=[t_s2_])
            P.op("act", lambda e: e.activation(out=s2_[:, 48:56], in_=s2_[:, 40:48], func=AF.Ln,
                                               scale=1.0 / 256, bias=cvec[:, 0:1]),
                 reads=[t_s2_, t_cvec], writes=[t_s2_])
            P.op("act", lambda e: e.activation(out=s2_[:, 56:64], in_=s2_[:, 48:56], func=AF.Exp, scale=-0.5),
                 reads=[t_s2_], writes=[t_s2_])
            P.op("dve", lambda e: e.tensor_tensor(out=s2_[:, 64:72], in0=s2_[:, 32:40], in1=s2_[:, 56:64], op=ALU.mult),
                 reads=[t_s2_], writes=[t_s2_])
            yb, t_yb = ybfr.next()
            for h in range(8):
                P.op("dve", lambda e, h=h: e.scalar_tensor_tensor(
                    out=yb[:, h * 256:(h + 1) * 256], in0=nums[:, h, 0:256], scalar=s2_[:, 64 + h:65 + h],
                    in1=og[:, h * 256:(h + 1) * 256], op0=ALU.mult, op1=ALU.mult),
                    reads=[t_nums, t_s2_, t_og], writes=[t_yb])
            return (n, x1, t_x1, yb, t_yb)

        def stage_E2(c2):
            n, x1, t_x1, yb, t_yb = c2
            for r8 in range(2):
                for kc in range(8):
                    P.op("pe", lambda e, r8=r8, kc=kc: e.transpose(
                        ptry[:, kc * 128:(kc + 1) * 128], yb[:, (r8 * 8 + kc) * 128:(r8 * 8 + kc + 1) * 128],
                        ident_b[:]), reads=[t_yb, t_const], writes=[t_ptry])
                P.op("act", lambda e, r8=r8: e.activation(
                    out=yT6[:, r8 * 8:(r8 + 1) * 8, :], in_=ptry[:].rearrange("p (a b) -> p a b", b=128),
                    func=AF.Identity), reads=[t_ptry], writes=[t_yT6])
            for hf in range(2):
                for kc in range(16):
                    P.op("pe", lambda e, hf=hf, kc=kc: e.matmul(
                        po[hf][:], lhsT=yT6[:, kc, :], rhs=wo1[:, kc, hf * 512:(hf + 1) * 512],
                        start=(kc == 0), stop=(kc == 15)), reads=[t_yT6, t_wo1], writes=[t_po[hf]])
                P.op("dve", lambda e, hf=hf: e.tensor_tensor(
                    out=x2[:, hf * 512:(hf + 1) * 512], in0=po[hf][:], in1=gate_bc[1][:, hf * 512:(hf + 1) * 512],
                    op=ALU.mult), reads=[t_po[hf], t_gbc[1]], writes=[t_x2])
            P.op("dve", lambda e: e.tensor_tensor(out=x2[:], in0=x2[:], in1=x1[:], op=ALU.add),
                 reads=[t_x1, t_x2], writes=[t_x2])
            st, t_st = rms_stats(x2[:], t_x2, R)
            ot, t_ot = outr.next()
            P.op("dve", lambda e: e.scalar_tensor_tensor(
                out=ot[:], in0=x2[:], scalar=st[:, 2:3], in1=fing_bc[:], op0=ALU.mult, op1=ALU.mult),
                reads=[t_x2, t_st, t_c6], writes=[t_ot])
            pending_stores.append((n, ot, t_ot))

        def flush_stores():
            while pending_stores:
                n_, ot, t_ot = pending_stores.pop(0)
                P.dma("sp", lambda e, n_=n_, ot=ot: e.dma_start(out=out_d[n_ * 128:(n_ + 1) * 128, :], in_=ot[:]),
                      t_ot, False)

        cH, cE = {}, {}
        stage_L(0)
        for n in range(NT + 2):
            if n + 1 < NT:
                stage_L(n + 1)
            flush_stores()
            lists = []
            if n < NT:
                P.capture()
                cH[n] = stage_H(n)
                lists.append(P.end_capture())
            if 0 <= n - 2 < NT:
                P.capture()
                stage_E2(cE.pop(n - 2))
                lists.append(P.end_capture())
            if 0 <= n - 1 < NT:
                P.capture()
                cE[n - 1] = stage_E1(cH.pop(n - 1))
                lists.append(P.end_capture())
            P.replay_merged(lists)
        flush_stores()
        P.end_phase()
    es_g.close()
    return nc


def prep_inputs(inputs):
    f = np.float32
    g = lambda k: np.asarray(inputs[k], dtype=f)
    sh = {}
    sh["ada_w"] = np.ascontiguousarray(g("ada_w"))
    sh["adabT"] = np.ascontiguousarray(g("ada_b").reshape(2, 24, 128).transpose(2, 0, 1).reshape(128, 48))
    sh["ngT"] = np.ascontiguousarray(g("norm_g").reshape(2, 8, 128).transpose(2, 0, 1).reshape(128, 16))
    sh["fing"] = np.ascontiguousarray(g("final_g").reshape(1, D))
    sh["wa"] = np.ascontiguousarray(g("a_w_in")[0][:, WA_PERM])
    sh["cmp_w1"] = np.ascontiguousarray(g("a_cmp_w1")[0])
    sh["peT"] = np.ascontiguousarray(g("a_cmp_pe")[0].transpose(2, 0, 1).reshape(64, 64))
    sh["b1T"] = np.ascontiguousarray(g("a_cmp_b1")[0].reshape(2, 2, 128).transpose(2, 0, 1).reshape(128, 4))
    sh["cmp_w2"] = np.ascontiguousarray(g("a_cmp_w2")[0])
    sh["b2T"] = np.ascontiguousarray(g("a_cmp_b2")[0].T)
    sh["a_w_out"] = np.ascontiguousarray(g("a_w_out")[0])
    sh["wb"] = np.ascontiguousarray(g("b_w_in")[0][:, WB_PERM])
    sh["convwT"] = np.ascontiguousarray(g("b_conv_w")[0].reshape(4, 16, 128).transpose(2, 1, 0).reshape(128, 64))
    sh["convbT"] = np.ascontiguousarray(g("b_conv_b")[0].reshape(16, 128).T)
    sh["gateb"] = np.ascontiguousarray(g("b_gate_b")[0].reshape(1, 16))
    sh["headg"] = np.ascontiguousarray(g("b_head_g")[0].reshape(1, 2048))
    sh["b_w_out"] = np.ascontiguousarray(g("b_w_out")[0])
    for k, v in make_consts().items():
        sh["k_" + k] = v
    per = []
    x = g("x")
    c = g("c")
    for b in range(8):
        m = dict(sh)
        m["x"] = np.ascontiguousarray(x[b])
        m["cT"] = np.ascontiguousarray(c[b].reshape(8, 128).T)
        per.append(m)
    return per


def kernel(**inputs):
    nc = build_program()
    in_maps = prep_inputs(inputs)
    res = run_bass_kernel_spmd(nc, in_maps, core_ids=list(range(8)))
    return np.stack([np.asarray(r["out"], dtype=np.float32) for r in res.results], axis=0)
```

```python
import math
from contextlib import ExitStack

import numpy as np
import ml_dtypes

import concourse.bass as bass
import concourse.mybir as mybir
from concourse.bass_utils import run_bass_kernel_spmd

F32 = mybir.dt.float32
BF16 = mybir.dt.bfloat16
AF = mybir.ActivationFunctionType
ALU = mybir.AluOpType
AX = mybir.AxisListType

S = 4096
D = 1024
NT = S // 128
NQ = S // 512
NEG = -30000.0
EPS = 1e-6
SLOPES = [2.0 ** (-8.0 * (i + 1) / 16) for i in range(16)]
LN_KSCALE = math.log(128 ** -0.5)
ZERO_CUT = 130.0

ENGS = ("pe", "act", "dve", "pool", "sp")
CENG = ("pe", "act", "dve", "pool")
N_DSEM = 90


class Tok:
    __slots__ = ("w", "r", "ds", "name")

    def __init__(self, name="", ds=None):
        self.w = None
        self.r = {}
        self.ds = ds
        self.name = name


class Prog:
    def __init__(self, nc, es):
        self.nc = nc
        self.q = {e: [] for e in ENGS}
        self.cnt = {e: 0 for e in CENG}
        self.seen = {e: {} for e in ENGS}
        self.sems = {}
        for e in CENG:
            self.sems[e] = es.enter_context(nc.semaphore("c_" + e))
        self.sems["bar"] = es.enter_context(nc.semaphore("c_bar"))
        self.bar = 0
        self.dsem_cnt = {}
        self.free = []
        for i in range(N_DSEM):
            k = "d%d" % i
            self.sems[k] = es.enter_context(nc.semaphore(k))
            self.dsem_cnt[k] = 0
            self.free.append(k)
        self.phase_ds = []

    def tok(self, name="", dma=False):
        ds = None
        if dma:
            ds = self.free.pop(0)
            self.phase_ds.append(ds)
        return Tok(name, ds)

    def _need(self, eng, ev):
        if ev is None:
            return
        key, val = ev
        if key == eng and eng == "pe":
            return
        if self.seen[eng].get(key, 0) >= val:
            return
        self.seen[eng][key] = val
        self.q[eng].append(("wait", key, val))

    def capture(self):
        self._cap = []

    def end_capture(self):
        lst, self._cap = self._cap, None
        return lst

    def replay_merged(self, lists):
        lists = [l for l in lists if l]
        pos = [0] * len(lists)
        while True:
            best, bf = None, None
            for i, l in enumerate(lists):
                if pos[i] < len(l):
                    f = pos[i] / len(l)
                    if bf is None or f < bf:
                        best, bf = i, f
            if best is None:
                break
            kind, args = lists[best][pos[best]]
            pos[best] += 1
            if kind == "op":
                self.op(*args)
            else:
                self.dma(*args)

    def op(self, eng, fn, reads=(), writes=()):
        if getattr(self, "_cap", None) is not None:
            self._cap.append(("op", (eng, fn, tuple(reads), tuple(writes))))
            return
        for t in reads:
            self._need(eng, t.w)
        for t in writes:
            self._need(eng, t.w)
            for k, v in t.r.items():
                self._need(eng, (k, v))
        self.cnt[eng] += 1
        n = self.cnt[eng]
        self.q[eng].append(("op", fn))
        for t in reads:
            if t.r.get(eng, 0) < n:
                t.r[eng] = n
        for t in writes:
            t.w = (eng, n)
            t.r = {}

    def dma(self, eng, fn, tok, load, reads=(), writes=()):
        assert tok.ds is not None, tok.name
        if getattr(self, "_cap", None) is not None:
            self._cap.append(("dma", (eng, fn, tok, load, tuple(reads), tuple(writes))))
            return
        self._need(eng, tok.w)
        if load:
            for k, v in tok.r.items():
                self._need(eng, (k, v))
        for t in reads:
            self._need(eng, t.w)
        for t in writes:
            self._need(eng, t.w)
            for k, v in t.r.items():
                self._need(eng, (k, v))
        self.dsem_cnt[tok.ds] += 16
        n = self.dsem_cnt[tok.ds]
        self.q[eng].append(("dma", fn, tok.ds))
        if load:
            tok.w = (tok.ds, n)
            tok.r = {}
        else:
            tok.r[tok.ds] = n
        for t in reads:
            t.r[tok.ds] = max(t.r.get(tok.ds, 0), n)
        for t in writes:
            t.w = (tok.ds, n)
            t.r = {}

    def barrier(self):
        for e in CENG:
            if self.cnt[e]:
                self._need("sp", (e, self.cnt[e]))
        for k, v in self.dsem_cnt.items():
            if v:
                self._need("sp", (k, v))
        self.bar += 1
        self.q["sp"].append(("inc", "bar", 1))
        for e in CENG:
            self.q[e].append(("wait", "bar", self.bar))
            for e2 in CENG:
                self.seen[e][e2] = max(self.seen[e].get(e2, 0), self.cnt[e2])
            for k, v in self.dsem_cnt.items():
                self.seen[e][k] = max(self.seen[e].get(k, 0), v)

    def end_phase(self):
        self.barrier()
        self.flush()
        self.free.extend(self.phase_ds)
        self.phase_ds = []

    def flush(self):
        nc = self.nc
        q = self.q
        sems = self.sems

        def emit(name, h):
            for it in q[name]:
                if it[0] == "wait":
                    h.wait_ge(sems[it[1]], it[2])
                elif it[0] == "op":
                    it[1](h).then_inc(sems[name], 1)
                elif it[0] == "dma":
                    it[1](h).then_inc(sems[it[2]], 16)
                elif it[0] == "inc":
                    h.sem_inc(sems[it[1]], it[2])

        with nc.Block() as block:
            @block.sync
            def _(h):
                emit("sp", h)

            @block.tensor
            def _(h):
                emit("pe", h)

            @block.scalar
            def _(h):
                emit("act", h)

            @block.vector
            def _(h):
                emit("dve", h)

            @block.gpsimd
            def _(h):
                emit("pool", h)
        self.q = {e: [] for e in ENGS}


class Rot:
    def __init__(self, items):
        self.items = items
        self.i = 0

    def next(self):
        it = self.items[self.i % len(self.items)]
        self.i += 1
        return it


def _bf(a):
    return np.ascontiguousarray(np.asarray(a, dtype=np.float32)).astype(ml_dtypes.bfloat16)


def make_consts():
    c = {}
    c["ident_f"] = np.eye(128, dtype=np.float32)
    c["ident_b"] = _bf(np.eye(128))
    c["ones_f"] = np.ones((128, 128), np.float32)
    key = np.arange(S)
    kaug_slc = np.zeros((64, S), np.float32)
    for j in range(63):
        kaug_slc[j] = (key // 64 == j)
    kaug_slc[63] = 1.0
    kaug_one = np.zeros((64, S), np.float32)
    kaug_one[63] = 1.0
    c["kaug_slc"] = _bf(kaug_slc)
    c["kaug_one"] = _bf(kaug_one)
    qq = np.arange(512, dtype=np.float64)
    c["vrow"] = _bf(np.stack([-8.0 * s * qq for s in SLOPES]))
    p = np.arange(128, dtype=np.float64)
    bt = np.zeros((128, 16, 32), np.float64)
    for h in range(16):
        for di in range(32):
            bt[:, h, di] = SLOPES[h] * (p + 128.0 * (di - 28))
    c["bias_tab"] = bt.reshape(128, 512).astype(np.float32)
    bc = np.zeros((128, 16, 8, 2), np.float64)
    for h in range(16):
        for t in range(8):
            for ct in range(2):
                bc[:, h, t, ct] = SLOPES[h] * (16.0 * (ct * 128 + p) + 31.0 - 512.0 * t)
    c["biasc_tab"] = bc.reshape(128, 256).astype(np.float32)
    cc = np.arange(256)
    tq = np.arange(S)
    ok = (16 * cc[:, None] + 31 <= tq[None, :]) & (cc[:, None] < 255)
    cm = np.where(ok, 0.0, NEG).astype(np.float32)
    c["cmask"] = _bf(cm.reshape(2, 128, S).transpose(1, 0, 2))
    kk = np.arange(128)[:, None]
    q2 = np.arange(128)[None, :]
    c["tri_causal"] = _bf(np.where(kk <= q2, 0.0, NEG))
    c["tri_band"] = _bf(np.where(kk > q2, 0.0, NEG))
    c["mask01"] = (kk <= q2).astype(np.float32)
    c0 = np.arange(256)[:, None] * 16
    s0 = np.arange(64)[None, :] * 64
    ov = np.clip(np.minimum(c0 + 32, s0 + 64) - np.maximum(c0, s0), 0, None) / 32.0
    ov[255] = 0.0
    c["ov"] = _bf(ov.reshape(2, 128, 64).transpose(1, 0, 2))
    blk = np.arange(64)[None, :]
    cur = (tq // 64)[:, None]
    forced = (blk == 0) | (blk == cur) | (blk == cur - 1)
    valid = blk <= cur
    vnf = (valid & ~forced).astype(np.float32)
    epsj = (64 - np.arange(64)).astype(np.float32)[None, :] * 1e-30
    fbig = (1e9 + 1e5 * np.arange(64)).astype(np.float32)[None, :]
    addc = np.where(forced, fbig, np.where(valid, epsj, -1.0)).astype(np.float32)

    def tm(a):
        return np.ascontiguousarray(a.reshape(32, 128, 64).transpose(1, 0, 2))

    c["sel_vnf"] = tm(vnf)
    c["sel_addc"] = tm(addc)
    c["sel_valid"] = tm(valid.astype(np.float32))
    return c


WA_PERM = np.concatenate([
    np.arange(0, 1024),
    np.arange(1024, 1536),
    np.arange(1536, 1792),
    np.arange(2048, 2304),
    np.arange(1792, 2048), np.arange(2304, 2560),
    np.arange(2608, 3632),
    np.arange(2560, 2608),
])
WB_PERM = np.concatenate(
    [np.arange(0, 2048), np.arange(2048, 4096)]
    + [np.concatenate([np.arange(4112 + i * 256, 4112 + (i + 1) * 256),
                       np.arange(6160 + i * 256, 6160 + (i + 1) * 256)]) for i in range(8)]
    + [np.arange(4096, 4112)])


def build_program(stop_after=99, dbg=False):
    nc = bass.Bass("TRN2", target_bir_lowering=False)
    consts = make_consts()

    def din(name, shape, dt=F32):
        return nc.dram_tensor(name, list(shape), dt, kind="ExternalInput").ap()

    def dscr(name, shape, dt):
        return nc.dram_tensor(name, list(shape), dt, kind=("ExternalOutput" if dbg else "Internal")).ap()

    x_d = din("x", [S, D])
    cT_d = din("cT", [128, 8])
    adaw_d = din("ada_w", [2, D, 3 * D])
    adabT_d = din("adabT", [128, 48])
    ngT_d = din("ngT", [128, 16])
    fing_d = din("fing", [1, D])
    wa_d = din("wa", [D, 3632])
    w1_d = din("cmp_w1", [2, 2048, 256])
    peT_d = din("peT", [64, 64])
    b1T_d = din("b1T", [128, 4])
    w2_d = din("cmp_w2", [2, 256, 64])
    b2T_d = din("b2T", [64, 2])
    woa_d = din("a_w_out", [D, D])
    wb_d = din("wb", [D, 8208])
    convwT_d = din("convwT", [128, 64])
    convbT_d = din("convbT", [128, 16])
    gateb_d = din("gateb", [1, 16])
    headg_d = din("headg", [1, 2048])
    wob_d = din("b_w_out", [2048, D])
    cd = {}
    for k, v in consts.items():
        cd[k] = din("k_" + k, v.shape, BF16 if v.dtype == ml_dtypes.bfloat16 else F32)
    out_d = nc.dram_tensor("out", [S, D], F32, kind="ExternalOutput").ap()

    qT_s = dscr("qT_s", [1024, S], BF16)
    kvT_s = dscr("kvT_s", [4, 256, S], BF16)
    vtok_s = dscr("vtok_s", [S, 512], BF16)
    zs_s = dscr("zs_s", [S, 1024], F32)
    y0_s = dscr("y0_s", [S, 1024], BF16)
    x1_s = dscr("x1_s", [S, D], F32)
    qkT1_s = dscr("qkT1_s", [16, 128, S], BF16)
    vtok1_s = dscr("vtok1_s", [S, 2048], BF16)
    ogz_s = dscr("ogz_s", [S, 2048], F32)

    es_all = ExitStack()
    P = Prog(nc, es_all)

    uid = {"n": 0}

    def sb(es, name, shape, dt):
        uid["n"] += 1
        return es.enter_context(nc.sbuf_tensor("s%d_%s" % (uid["n"], name), list(shape), dt))

    def ps(es, name, shape, dt=F32):
        uid["n"] += 1
        return es.enter_context(nc.psum_tensor("p%d_%s" % (uid["n"], name), list(shape), dt))

    def finish(dbg=None):
        P.end_phase()
        return nc

    ident_f = sb(es_all, "ident_f", [128, 128], F32)
    ident_b = sb(es_all, "ident_b", [128, 128], BF16)
    ones_f = sb(es_all, "ones_f", [128, 128], F32)
    cvec = sb(es_all, "cvec", [128, 4], F32)
    modT = sb(es_all, "modT", [128, 48], F32)
    gsc = sb(es_all, "gsc", [128, 16], F32)
    gate_bc = [sb(es_all, "gate_bc%d" % l, [128, D], F32) for l in range(2)]
    gate_sb = sb(es_all, "gate_sb", [128, NT, 48], F32)
    es_w1 = ExitStack()
    w1f = sb(es_w1, "w1f", [64, 32, 256], F32)
    t_w1f = Tok("w1f", P.free.pop(0))
    es_h = ExitStack()
    hT = sb(es_h, "hT", [128, 8, S], BF16)
    t_const = Tok("const", P.free.pop(0))
    t_cvec = P.tok("cvec")
    t_modT = P.tok("modT")
    t_gsc = P.tok("gsc")
    t_gbc = [P.tok("gbc0"), P.tok("gbc1")]
    t_gate = P.tok("gate_sb")
    t_hT = [P.tok("hT%d" % i) for i in range(NQ)]

    P.dma("sp", lambda e: e.dma_start(out=ident_f[:], in_=cd["ident_f"]), t_const, True)
    P.dma("sp", lambda e: e.dma_start(out=ident_b[:], in_=cd["ident_b"]), t_const, True)
    P.dma("sp", lambda e: e.dma_start(out=ones_f[:], in_=cd["ones_f"]), t_const, True)
    P.op("pool", lambda e: e.memset(cvec[:, 0:1], EPS), writes=[t_cvec])
    P.op("pool", lambda e: e.memset(cvec[:, 1:2], 1.0), writes=[t_cvec])
    P.op("pool", lambda e: e.memset(cvec[:, 2:3], LN_KSCALE), writes=[t_cvec])
    P.op("pool", lambda e: e.memset(cvec[:, 3:4], 0.0), writes=[t_cvec])

    with ExitStack() as es:
        cT = sb(es, "cT", [128, 8], F32)
        adab = sb(es, "adab", [128, 48], F32)
        ngT = sb(es, "ngT", [128, 16], F32)
        wst = [sb(es, "adaw%d" % i, [128, 8, 768], BF16) for i in range(3)]
        cTb = sb(es, "cTb", [128, 8], BF16)
        t_cTb = P.tok("cTb")
        dg = [sb(es, "dg%d" % i, [128, 128], F32) for i in range(2)]
        pmod = ps(es, "pmod", [128, 48], F32)
        pbc = [ps(es, "pbc%d" % i, [128, 512], F32) for i in range(2)]
        t_small = P.tok("small", dma=True)
        t_w = [P.tok("adaw%d" % i, dma=True) for i in range(3)]
        t_pmod = P.tok("pmod")
        t_dg = [P.tok("dg0"), P.tok("dg1")]
        t_pbc = [P.tok("pbc0"), P.tok("pbc1")]
        P.dma("sp", lambda e: e.dma_start(out=cT[:], in_=cT_d), t_small, True)
        P.dma("sp", lambda e: e.dma_start(out=adab[:], in_=adabT_d), t_small, True)
        P.dma("sp", lambda e: e.dma_start(out=ngT[:], in_=ngT_d), t_small, True)
        P.op("dve", lambda e: e.tensor_copy(cTb[:], cT[:]), reads=[t_small], writes=[t_cTb])
        k = 0
        for l in range(2):
            for pc in range(4):
                w = wst[k % 3]
                tw = t_w[k % 3]
                k += 1
                src = adaw_d[l, :, pc * 768:(pc + 1) * 768].rearrange("(kc p) w -> p kc w", p=128)
                P.dma("pool", lambda e, w=w, src=src: e.dma_start(out=w[:], in_=src), tw, True)
                for jj in range(6):
                    col = l * 24 + pc * 6 + jj
                    for kc in range(8):
                        P.op("pe", lambda e, w=w, jj=jj, kc=kc, col=col: e.matmul(
                            pmod[:, col:col + 1], lhsT=w[:, kc, jj * 128:(jj + 1) * 128],
                            rhs=cTb[:, kc:kc + 1], start=(kc == 0), stop=(kc == 7)),
                            reads=[tw, t_cTb], writes=[t_pmod])
        P.op("dve", lambda e: e.tensor_tensor(out=modT[:], in0=pmod[:], in1=adab[:], op=ALU.add),
             reads=[t_pmod, t_small], writes=[t_modT])
        for l in range(2):
            P.op("dve", lambda e, l=l: e.scalar_tensor_tensor(
                out=gsc[:, l * 8:(l + 1) * 8], in0=modT[:, l * 24 + 8:l * 24 + 16], scalar=1.0,
                in1=ngT[:, l * 8:(l + 1) * 8], op0=ALU.add, op1=ALU.mult),
                reads=[t_modT, t_small], writes=[t_gsc])
            for kc in range(8):
                d_, td = dg[kc % 2], t_dg[kc % 2]
                P.op("dve", lambda e, l=l, kc=kc, d_=d_: e.tensor_scalar(
                    out=d_[:], in0=ident_f[:], scalar1=modT[:, l * 24 + 16 + kc:l * 24 + 17 + kc],
                    scalar2=None, op0=ALU.mult), reads=[t_modT, t_const], writes=[td])
                P.op("pe", lambda e, kc=kc, d_=d_: e.matmul(
                    pbc[kc // 4][:, (kc % 4) * 128:(kc % 4 + 1) * 128], lhsT=ones_f[:], rhs=d_[:],
                    start=True, stop=True), reads=[td, t_const], writes=[t_pbc[kc // 4]])
            for hf in range(2):
                P.op("dve", lambda e, l=l, hf=hf: e.tensor_copy(
                    gate_bc[l][:, hf * 512:(hf + 1) * 512], pbc[hf][:]),
                    reads=[t_pbc[hf]], writes=[t_gbc[l]])
        P.end_phase()
    if stop_after == 0:
        return finish()

    nres = {"i": 0}

    def norm_res(es, full=True):
        R = {}
        k = nres["i"]
        nres["i"] += 1
        R["junk"] = Rot([(sb(es, "njunk%d_%d" % (k, i), [128, D], F32), P.tok("njunk")) for i in range(2 if full else 1)])
        R["stat"] = Rot([(sb(es, "nstat%d_%d" % (k, i), [128, 4], F32), P.tok("nstat")) for i in range(3)])
        if full:
            R["xn"] = Rot([(sb(es, "nxn%d_%d" % (k, i), [128, D], F32), P.tok("nxn")) for i in range(3)])
            R["ptr"] = Rot([(ps(es, "nptr%d_%d" % (k, i), [128, 512], F32), P.tok("nptr")) for i in range(2)])
        return R

    def rms_stats(xt_ap, t_xt, R):
        junk, t_junk = R["junk"].next()
        st, t_st = R["stat"].next()
        P.op("act", lambda e: e.activation(out=junk[:], in_=xt_ap, func=AF.Square, accum_out=st[:, 0:1]),
             reads=[t_xt], writes=[t_junk, t_st])
        P.op("act", lambda e: e.activation(out=st[:, 1:2], in_=st[:, 0:1], func=AF.Ln,
                                           scale=1.0 / D, bias=cvec[:, 0:1]),
             reads=[t_st, t_cvec], writes=[t_st])
        P.op("act", lambda e: e.activation(out=st[:, 2:3], in_=st[:, 1:2], func=AF.Exp, scale=-0.5),
             reads=[t_st], writes=[t_st])
        return st, t_st

    def norm_a(xt_ap, t_xt, R):
        st, t_st = rms_stats(xt_ap, t_xt, R)
        xn, t_xn = R["xn"].next()
        P.op("dve", lambda e: e.tensor_scalar(out=xn[:], in0=xt_ap, scalar1=st[:, 2:3], scalar2=None,
                                              op0=ALU.mult), reads=[t_xt, t_st], writes=[t_xn])
        return xn, t_xn

    def norm_b(l, tt, xn, t_xn, R):
        for hf in range(2):
            pt, t_pt = R["ptr"].next()
            for c4 in range(4):
                kc = hf * 4 + c4
                P.op("pe", lambda e, pt=pt, c4=c4, kc=kc: e.transpose(
                    pt[:, c4 * 128:(c4 + 1) * 128], xn[:, kc * 128:(kc + 1) * 128], ident_f[:]),
                    reads=[t_xn, t_const], writes=[t_pt])
            for c4 in range(4):
                kc = hf * 4 + c4
                P.op("act", lambda e, pt=pt, c4=c4, kc=kc: e.activation(
                    out=hT[:, kc, tt * 128:(tt + 1) * 128], in_=pt[:, c4 * 128:(c4 + 1) * 128],
                    func=AF.Identity, scale=gsc[:, l * 8 + kc:l * 8 + kc + 1],
                    bias=modT[:, l * 24 + kc:l * 24 + kc + 1]),
                    reads=[t_pt, t_gsc, t_modT], writes=[t_hT[tt // 4]])

    def norm_to_hT(l, tt, xt_ap, t_xt, R):
        xn, t_xn = norm_a(xt_ap, t_xt, R)
        norm_b(l, tt, xn, t_xn, R)

    def inproj(es, w_d, groups, fm_sink, tm_sink):
        wbf = [sb(es, "wbf%d" % i, [128, 8, 512], BF16) for i in range(2)]
        t_wbf = [P.tok("wbf%d" % i, dma=True) for i in range(2)]
        pacc = Rot([(ps(es, "pacc%d" % i, [128, 512], F32), P.tok("pacc")) for i in range(4)])
        for gi, (kind, c0, width, meta) in enumerate(groups):
            w16 = wbf[gi % 2]
            tw16 = t_wbf[gi % 2]
            src = w_d[:, c0:c0 + width].rearrange("(kc p) w -> p kc w", p=128)
            P.dma("pool", lambda e, w16=w16, src=src, width=width: e.dma_start(out=w16[:, :, 0:width], in_=src),
                  tw16, True)
            if kind == "fm":
                for bi in range(width // 128):
                    for qt in range(NQ):
                        pa, t_pa = pacc.next()
                        for kc in range(8):
                            P.op("pe", lambda e, pa=pa, w16=w16, bi=bi, kc=kc, qt=qt: e.matmul(
                                pa[:], lhsT=w16[:, kc, bi * 128:(bi + 1) * 128],
                                rhs=hT[:, kc, qt * 512:(qt + 1) * 512], start=(kc == 0), stop=(kc == 7)),
                                reads=[tw16, t_hT[qt]], writes=[t_pa])
                        fm_sink(meta + bi, qt, pa, t_pa)
            else:
                for tt in range(NT):
                    pa, t_pa = pacc.next()
                    for kc in range(8):
                        P.op("pe", lambda e, pa=pa, w16=w16, kc=kc, tt=tt, width=width: e.matmul(
                            pa[:, 0:width], lhsT=hT[:, kc, tt * 128:(tt + 1) * 128],
                            rhs=w16[:, kc, 0:width], start=(kc == 0), stop=(kc == 7)),
                            reads=[tw16, t_hT[tt // 4]], writes=[t_pa])
                    tm_sink(meta, tt, pa, t_pa)

    with ExitStack() as es:
        R = norm_res(es)
        xin = Rot([(sb(es, "xin%d" % i, [128, D], F32), P.tok("xin%d" % i, dma=True)) for i in range(3)])
        xns = {}
        for tt in range(NT + 1):
            lists = []
            if tt < NT:
                xt, t_xt = xin.next()
                P.dma("sp", lambda e, xt=xt, tt=tt: e.dma_start(out=xt[:], in_=x_d[tt * 128:(tt + 1) * 128, :]),
                      t_xt, True)
                P.capture()
                xns[tt] = norm_a(xt[:], t_xt, R)
                lists.append(P.end_capture())
            if tt >= 1:
                P.capture()
                norm_b(0, tt - 1, *xns.pop(tt - 1), R)
                lists.append(P.end_capture())
            P.replay_merged(lists)
        P.end_phase()
    if stop_after == 1:
        return finish()

    with ExitStack() as es:
        frow = [sb(es, "frow%d" % i, [128, S], BF16) for i in range(2)]
        t_frow = [P.tok("frow%d" % i, dma=True) for i in range(2)]
        tstg_b = Rot([(sb(es, "tstb%d" % i, [128, 4, 512], BF16), P.tok("tstb%d" % i, dma=True)) for i in range(2)])
        tstg_f = Rot([(sb(es, "tstf%d" % i, [128, 4, 512], F32), P.tok("tstf%d" % i, dma=True)) for i in range(2)])
        FM_DEST = ([qT_s[i * 128:(i + 1) * 128, :] for i in range(8)]
                   + [kvT_s[j, i * 128:(i + 1) * 128, :] for j in range(4) for i in range(2)])

        def fm_sink(blk, qt, pa, t_pa):
            fr, tf = frow[blk % 2], t_frow[blk % 2]
            if qt % 2 == 0:
                P.op("act", lambda e: e.activation(out=fr[:, qt * 512:(qt + 1) * 512], in_=pa[:],
                                                   func=AF.Identity), reads=[t_pa], writes=[tf])
            else:
                P.op("dve", lambda e: e.tensor_copy(fr[:, qt * 512:(qt + 1) * 512], pa[:]),
                     reads=[t_pa], writes=[tf])
            if qt == NQ - 1:
                dst = FM_DEST[blk]
                P.dma("sp", lambda e: e.dma_start(out=dst, in_=fr[:]), tf, False)

        tm_cur = {}

        def tm_sink(meta, tt, pa, t_pa):
            if meta == "gates":
                P.op("act", lambda e: e.activation(out=gate_sb[:, tt, :], in_=pa[:, 0:48], func=AF.Sigmoid),
                     reads=[t_pa], writes=[t_gate])
                return
            if tt % 4 == 0:
                tm_cur["buf"] = tstg_b.next() if meta == "v" else tstg_f.next()
            st, t_s = tm_cur["buf"]
            if meta == "v":
                P.op("dve", lambda e: e.tensor_copy(st[:, tt % 4, :], pa[:]), reads=[t_pa], writes=[t_s])
                if tt % 4 == 3:
                    dst = vtok_s[(tt - 3) * 128:(tt + 1) * 128, :].rearrange("(t p) c -> p t c", p=128)
                    P.dma("sp", lambda e: e.dma_start(out=dst, in_=st[:]), t_s, False)
            else:
                zi = meta
                P.op("act", lambda e: e.activation(out=st[:, tt % 4, :], in_=pa[:], func=AF.Silu),
                     reads=[t_pa], writes=[t_s])
                if tt % 4 == 3:
                    dst = zs_s[(tt - 3) * 128:(tt + 1) * 128, zi * 512:(zi + 1) * 512].rearrange(
                        "(t p) c -> p t c", p=128)
                    P.dma("sp", lambda e: e.dma_start(out=dst, in_=st[:]), t_s, False)

        groups = [("fm", 0, 512, 0), ("fm", 512, 512, 4), ("fm", 1024, 512, 8), ("fm", 1536, 512, 12),
                  ("tm", 2048, 512, "v"), ("tm", 2560, 512, 0), ("tm", 3072, 512, 1), ("tm", 3584, 48, "gates")]
        inproj(es, wa_d, groups, fm_sink, tm_sink)
        P.dma("sp", lambda e: e.dma_start(out=w1f[:], in_=w1_d[0].rearrange("(l d) h -> d l h", d=64)), t_w1f, True)
        P.end_phase()
    es_h.close()
    if stop_after == 2:
        return finish()

    with ExitStack() as es:
        t_c3 = P.tok("c3", dma=True)
        cmask = sb(es, "cmask", [128, 2, S], BF16)
        tri_c = sb(es, "tri_c", [128, 128], BF16)
        tri_b = sb(es, "tri_b", [128, 128], BF16)
        bias_tab = sb(es, "bias_tab", [128, 512], F32)
        biasc_tab = sb(es, "biasc_tab", [128, 256], F32)
        ovt = sb(es, "ovt", [128, 2, 64], BF16)
        sel_vnf = sb(es, "sel_vnf", [128, 32, 64], F32)
        sel_addc = sb(es, "sel_addc", [128, 32, 64], F32)
        sel_valid = sb(es, "sel_valid", [128, 32, 64], F32)
        KC = sb(es, "KC", [128, 4, 256], BF16)
        VC = sb(es, "VC", [128, 4, 2, 65], BF16)
        t_KC = P.tok("KC", dma=True)
        t_VC = P.tok("VC")
        for g in range(4):
            P.dma("sp", lambda e, g=g: e.dma_start(out=KC[64:128, g, :], in_=cd["kaug_one"][:, 0:256]), t_KC, True)
        P.op("pool", lambda e: e.memset(VC[:], 1.0), writes=[t_VC])

        with ExitStack() as es2:
            w1b = [sb(es2, "w1b%d" % i, [64, 32, 256], BF16) for i in range(2)]
            kvs_ = sb(es2, "kvs", [64, 4, S], BF16)
            kvs = [kvs_, kvs_]
            kvd = sb(es2, "kvd", [64, 4, 16, 256], BF16)
            t_kvd = P.tok("kvd")
            pef = sb(es2, "pef", [64, 64], F32)
            peb = sb(es2, "peb", [64, 64], BF16)
            b1T = sb(es2, "b1T", [128, 4], F32)
            bias1 = sb(es2, "bias1", [128, 4], F32)
            w2f = sb(es2, "w2f", [128, 2, 2, 64], F32)
            w2b = sb(es2, "w2b", [128, 2, 2, 64], BF16)
            b2T = sb(es2, "b2T", [64, 2], F32)
            xh = sb(es2, "xh", [128, 256], F32)
            tt1 = sb(es2, "tt1", [128, 256], F32)
            sg = sb(es2, "sg", [128, 256], F32)
            gel = [sb(es2, "gel%d" % i, [128, 2, 256], BF16) for i in range(2)]
            vct = sb(es2, "vct", [64, 256], BF16)
            ph = [ps(es2, "ph%d" % i, [128, 256], F32) for i in range(2)]
            pb1 = ps(es2, "pb1", [128, 4], F32)
            pk = ps(es2, "pk", [64, 256], F32)
            pvt = ps(es2, "pvt", [128, 128], BF16)
            t_w1b = [P.tok("w1b0"), P.tok("w1b1")]
            t_kvs_ = P.tok("kvs", dma=True)
            t_kvs = [t_kvs_, t_kvs_]
            t_sm = P.tok("cmpsmall", dma=True)
            t_peb = P.tok("peb")
            t_bias1 = P.tok("bias1")
            t_w2b = P.tok("w2b")
            t_xh, t_tt1, t_sg = P.tok("xh"), P.tok("tt1"), P.tok("sg")
            t_gel = [P.tok("gel0"), P.tok("gel1")]
            t_vct = P.tok("vct")
            t_ph = [P.tok("ph0"), P.tok("ph1")]
            t_pb1, t_pk, t_pvt = P.tok("pb1"), P.tok("pk"), P.tok("pvt")
            P.dma("sp", lambda e: e.dma_start(out=pef[:], in_=peT_d), t_sm, True)
            P.dma("sp", lambda e: e.dma_start(out=b1T[:], in_=b1T_d), t_sm, True)
            P.dma("sp", lambda e: e.dma_start(out=b2T[:], in_=b2T_d), t_sm, True)
            for xx in range(2):
                P.dma("sp", lambda e, xx=xx: e.dma_start(
                    out=w2f[:, xx, :, :], in_=w2_d[xx].rearrange("(hh p) d -> p hh d", p=128)), t_sm, True)
            P.op("dve", lambda e: e.tensor_copy(peb[:], pef[:]), reads=[t_sm], writes=[t_peb])
            P.op("dve", lambda e: e.tensor_copy(w2b[:], w2f[:]), reads=[t_sm], writes=[t_w2b])
            P.op("pool", lambda e: e.memset(gel[0][:], 0.0), writes=[t_gel[0]])
            P.op("pool", lambda e: e.memset(gel[1][:], 0.0), writes=[t_gel[1]])

            def load_x(xx):
                if xx == 1:
                    P.dma("sp", lambda e: e.dma_start(
                        out=w1f[:], in_=w1_d[xx].rearrange("(l d) h -> d l h", d=64)), t_w1f, True)
                for l4 in range(4):
                    if l4 % 2 == 0:
                        P.op("act", lambda e, l4=l4: e.activation(
                            out=w1b[xx][:, l4 * 8:(l4 + 1) * 8, :], in_=w1f[:, l4 * 8:(l4 + 1) * 8, :],
                            func=AF.Identity), reads=[t_w1f], writes=[t_w1b[xx]])
                    else:
                        P.op("dve", lambda e, l4=l4: e.tensor_copy(
                            w1b[xx][:, l4 * 8:(l4 + 1) * 8, :], w1f[:, l4 * 8:(l4 + 1) * 8, :]),
                            reads=[t_w1f], writes=[t_w1b[xx]])
                P.dma("sp", lambda e: e.dma_start(
                    out=kvs[xx][:], in_=kvT_s[xx].rearrange("(g d) s -> d g s", d=64)), t_kvs[xx], True)
                for hh in range(2):
                    col = xx * 2 + hh
                    for l in range(32):
                        P.op("pe", lambda e, hh=hh, l=l, col=col: e.matmul(
                            pb1[:, col:col + 1], lhsT=w1b[xx][:, l, hh * 128:(hh + 1) * 128],
                            rhs=peb[:, xx * 32 + l:xx * 32 + l + 1], start=(l == 0), stop=(l == 31)),
                            reads=[t_w1b[xx], t_peb], writes=[t_pb1])
                P.op("dve", lambda e: e.tensor_tensor(out=bias1[:, 2 * xx:2 * xx + 2], in0=pb1[:, 2 * xx:2 * xx + 2],
                                                      in1=b1T[:, 2 * xx:2 * xx + 2], op=ALU.add),
                     reads=[t_pb1, t_sm], writes=[t_bias1])

            it = 0
            load_x(0)
            for dst, nm in ((cmask, "cmask"), (tri_c, "tri_causal"), (tri_b, "tri_band"), (bias_tab, "bias_tab"),
                            (biasc_tab, "biasc_tab"), (ovt, "ov"), (sel_vnf, "sel_vnf"), (sel_addc, "sel_addc"),
                            (sel_valid, "sel_valid")):
                P.dma("sp", lambda e, dst=dst, nm=nm: e.dma_start(out=dst[:], in_=cd[nm]), t_c3, True)

            for xx in range(2):
                if xx == 1:
                    load_x(1)
                for g in range(4):
                    P.op("dve", lambda e, g=g, xx=xx: e.tensor_copy(
                        kvd[:, g, :, :], kvs[xx][:, g, :].rearrange("d (c s) -> d s c", s=16)),
                        reads=[t_kvs[xx]], writes=[t_kvd])
                for g in range(4):
                    ge, t_ge = gel[(xx * 4 + g) % 2], t_gel[(xx * 4 + g) % 2]
                    for hh in range(2):
                        p_, t_p = ph[it % 2], t_ph[it % 2]
                        it += 1
                        for l in range(32):
                            P.op("pe", lambda e, p_=p_, xx=xx, g=g, hh=hh, l=l: e.matmul(
                                p_[:, 0:255], lhsT=w1b[xx][:, l, hh * 128:(hh + 1) * 128],
                                rhs=kvd[:, g, l % 16, l // 16:l // 16 + 255], start=(l == 0), stop=(l == 31)),
                                reads=[t_w1b[xx], t_kvd], writes=[t_p])
                        col = xx * 2 + hh
                        P.op("act", lambda e, p_=p_, col=col: e.activation(
                            out=xh[:, 0:255], in_=p_[:, 0:255], func=AF.Identity, bias=bias1[:, col:col + 1]),
                            reads=[t_p, t_bias1], writes=[t_xh])
                        P.op("dve", lambda e: e.tensor_tensor(out=tt1[:, 0:255], in0=xh[:, 0:255],
                                                              in1=xh[:, 0:255], op=ALU.mult),
                             reads=[t_xh], writes=[t_tt1])
                        P.op("dve", lambda e: e.tensor_scalar(out=tt1[:, 0:255], in0=tt1[:, 0:255],
                                                              scalar1=0.044715, scalar2=1.0,
                                                              op0=ALU.mult, op1=ALU.add),
                             reads=[t_tt1], writes=[t_tt1])
                        P.op("dve", lambda e: e.tensor_tensor(out=tt1[:, 0:255], in0=tt1[:, 0:255],
                                                              in1=xh[:, 0:255], op=ALU.mult),
                             reads=[t_tt1, t_xh], writes=[t_tt1])
                        P.op("act", lambda e: e.activation(out=sg[:, 0:255], in_=tt1[:, 0:255], func=AF.Sigmoid,
                                                           scale=1.5957691216057308),
                             reads=[t_tt1], writes=[t_sg])
                        P.op("dve", lambda e, ge=ge, hh=hh: e.tensor_tensor(
                            out=ge[:, hh, 0:255], in0=xh[:, 0:255], in1=sg[:, 0:255], op=ALU.mult),
                            reads=[t_xh, t_sg], writes=[t_ge])
                    for hh in range(2):
                        P.op("pe", lambda e, ge=ge, xx=xx, hh=hh: e.matmul(
                            pk[:], lhsT=w2b[:, xx, hh, :], rhs=ge[:, hh, :], start=(hh == 0), stop=(hh == 1)),
                            reads=[t_ge, t_w2b], writes=[t_pk])
                    if xx == 0:
                        P.op("act", lambda e, g=g: e.activation(
                            out=KC[0:64, g, :], in_=pk[:], func=AF.Identity, bias=b2T[:, 0:1]),
                            reads=[t_pk, t_sm], writes=[t_KC])
                    else:
                        P.op("act", lambda e: e.activation(
                            out=vct[:], in_=pk[:], func=AF.Identity, bias=b2T[:, 1:2]),
                            reads=[t_pk, t_sm], writes=[t_vct])
                        for ct in range(2):
                            P.op("pe", lambda e, ct=ct: e.transpose(
                                pvt[:, 0:64], vct[:, ct * 128:(ct + 1) * 128], ident_b[0:64, 0:64]),
                                reads=[t_vct, t_const], writes=[t_pvt])
                            P.op("dve", lambda e, g=g, ct=ct: e.tensor_copy(VC[:, g, ct, 0:64], pvt[:, 0:64]),
                                 reads=[t_pvt], writes=[t_VC])
            P.barrier()
            P.flush()
        if stop_after == 2.5:
            return finish()

        KS = [sb(es, "KS%d" % i, [128, S], BF16) for i in range(2)]
        KW = [sb(es, "KW%d" % i, [128, S], BF16) for i in range(2)]
        VS = [sb(es, "VS%d" % i, [128, 32, 65], BF16) for i in range(2)]
        VW = [sb(es, "VW%d" % i, [128, 32, 65], BF16) for i in range(2)]
        t_KS = [P.tok("KS%d" % i, dma=True) for i in range(2)]
        t_KW = [P.tok("KW%d" % i, dma=True) for i in range(2)]
        t_VS = [P.tok("VS%d" % i, dma=True) for i in range(2)]
        t_VW = [P.tok("VW%d" % i, dma=True) for i in range(2)]
        for i in range(2):
            P.op("pool", lambda e, i=i: e.memset(VS[i][:], 1.0), writes=[t_VS[i]])
            P.op("pool", lambda e, i=i: e.memset(VW[i][:], 1.0), writes=[t_VW[i]])
            P.dma("sp", lambda e, i=i: e.dma_start(out=KS[i][64:128, :], in_=cd["kaug_slc"]), t_KS[i], True)
            P.dma("sp", lambda e, i=i: e.dma_start(out=KW[i][64:128, :], in_=cd["kaug_one"]), t_KW[i], True)
        QA = [[sb(es, "QA%d_%d" % (hh, i), [128, 512], BF16) for i in range(2)] for hh in range(4)]
        t_QA = [[P.tok("QA%d_%d" % (hh, i), dma=True) for i in range(2)] for hh in range(4)]
        for hh in range(4):
            for i in range(2):
                P.op("pool", lambda e, hh=hh, i=i: e.memset(QA[hh][i][:], 0.0), writes=[t_QA[hh][i]])
        resA = {"PT": Rot([(sb(es, "PTA%d" % i, [128, 512], BF16), P.tok("PTA")) for i in range(2)]),
                "Sp": Rot([(ps(es, "SpA%d" % i, [128, 512], F32), P.tok("SpA")) for i in range(1)]),
                "Op": Rot([(ps(es, "OpA%d" % i, [128, 4, 128], F32), P.tok("OpA")) for i in range(1)])}
        resB = {"PT": Rot([(sb(es, "PTB%d" % i, [128, 512], BF16), P.tok("PTB")) for i in range(4)]),
                "Sp": Rot([(ps(es, "SpB%d" % i, [128, 512], F32), P.tok("SpB")) for i in range(2)]),
                "Op": Rot([(ps(es, "OpB%d" % i, [128, 4, 128], F32), P.tok("OpB")) for i in range(2)])}
        IMPp = ps(es, "IMPp", [128, 4, 64], F32)
        t_IMP = P.tok("IMP")
        negTp = ps(es, "negTp", [128, 512], BF16)
        t_negT = P.tok("negT")
        yacc = [sb(es, "yacc%d" % i, [128, 4, 256], F32) for i in range(2)]
        t_yacc = [P.tok("yacc0"), P.tok("yacc1")]
        ztile = [sb(es, "ztile%d" % i, [128, 4, 256], F32) for i in range(2)]
        t_zt = [P.tok("zt%d" % i, dma=True) for i in range(2)]
        ybf = [sb(es, "ybf%d" % i, [128, 4, 256], BF16) for i in range(2)]
        t_ybf = [P.tok("ybf%d" % i, dma=True) for i in range(2)]
        sc = sb(es, "sc", [128, 4, 64], F32)
        t_sc = P.tok("sc")
        for nm, r_ in (("A", resA), ("B", resB)):
            r_["small"] = Rot([(sb(es, "sm%s%d" % (nm, i), [128, 16], F32), P.tok("sm")) for i in range(3)])
            r_["tmpo"] = Rot([(sb(es, "tmpo%s%d" % (nm, i), [128, 4, 64], F32), P.tok("tmpo")) for i in range(2)])
        s1 = sb(es, "s1", [128, 64], F32)
        s2 = sb(es, "s2", [128, 64], F32)
        m8 = sb(es, "m8", [128, 16], F32)
        negpad = sb(es, "negpad", [128, 128], BF16)
        t_s1, t_s2, t_m8, t_negpad = P.tok("s1"), P.tok("s2"), P.tok("m8"), P.tok("negpad")
        P.op("pool", lambda e: e.memset(negpad[:], 0.0), writes=[t_negpad])
        zb = sb(es, "zb", [128, 512], BF16)
        t_zb = P.tok("zb")
        P.op("pool", lambda e: e.memset(zb[:], 0.0), writes=[t_zb])

        def zero_bank(ap2d, t_bank, n):
            P.op("pe", lambda e: e.matmul(ap2d, lhsT=zb[:, 0:128], rhs=zb[:, 0:n], start=True, stop=False),
                 reads=[t_zb], writes=[t_bank])

        def s_tile(res, branch, h, t, Q, t_Q, Ktile, t_K, kcol, col0, ncol, mask, bias_ap, t_bias):
            sp, t_sp = res["Sp"].next()
            c1 = col0 + ncol
            P.op("pe", lambda e: e.matmul(sp[:, col0:c1], lhsT=Ktile[:, kcol:kcol + 128], rhs=Q[:, col0:c1],
                                          start=True, stop=(mask is None)),
                 reads=[t_Q, t_K], writes=[t_sp])
            if mask is not None:
                mcol, mrhs, t_m = mask
                w = mrhs.shape[-1] if hasattr(mrhs, "shape") else 128
                P.op("pe", lambda e: e.matmul(sp[:, mcol:mcol + w], lhsT=ident_b[:], rhs=mrhs,
                                              start=False, stop=True),
                     reads=[t_const, t_m], writes=[t_sp])
            pt, t_pt = res["PT"].next()
            P.op("act", lambda e: e.activation(out=pt[:, col0:c1], in_=sp[:, col0:c1], func=AF.Exp,
                                               scale=0.125, bias=bias_ap),
                 reads=[t_sp, t_bias], writes=[t_pt])
            return pt, t_pt

        def finalize(res, br, h, hh, t, op_, t_op, ya, t_ya, first):
            sm, t_sm_ = res["small"].next()
            gcol = br * 16 + h
            P.op("dve", lambda e: e.tensor_scalar(out=sm[:, 0:4], in0=op_[:, :, 64], scalar1=1e-30,
                                                  scalar2=None, op0=ALU.max),
                 reads=[t_op], writes=[t_sm_])
            P.op("dve", lambda e: e.reciprocal(out=sm[:, 4:8], in_=sm[:, 0:4]), reads=[t_sm_], writes=[t_sm_])
            P.op("dve", lambda e: e.tensor_tensor(out=sm[:, 8:12], in0=sm[:, 4:8],
                                                  in1=gate_sb[:, 4 * t:4 * t + 4, gcol], op=ALU.mult),
                 reads=[t_sm_, t_gate], writes=[t_sm_])
            fb = sm[:, 8:12].unsqueeze(2).broadcast_to([128, 4, 64])
            if first:
                P.op("dve", lambda e: e.tensor_tensor(out=ya[:, :, hh * 64:(hh + 1) * 64], in0=op_[:, :, 0:64],
                                                      in1=fb, op=ALU.mult),
                     reads=[t_op, t_sm_], writes=[t_ya])
            else:
                to, t_to = res["tmpo"].next()
                P.op("dve", lambda e: e.tensor_tensor(out=to[:], in0=op_[:, :, 0:64], in1=fb, op=ALU.mult),
                     reads=[t_op, t_sm_], writes=[t_to])
                P.op("dve", lambda e: e.tensor_tensor(out=ya[:, :, hh * 64:(hh + 1) * 64],
                                                      in0=ya[:, :, hh * 64:(hh + 1) * 64], in1=to[:], op=ALU.add),
                     reads=[t_to, t_ya], writes=[t_ya])
            return sm, t_sm_

        class Pipe:
            def __init__(self, res, depth):
                self.pending = []
                self.depth = depth
                self.res = res

            def push(self, s_args, pv_fn, post_fn=None):
                pt, t_pt = s_tile(self.res, *s_args)
                self.pending.append((pt, t_pt, pv_fn, post_fn))
                while len(self.pending) > self.depth:
                    self._pop()

            def _pop(self):
                pt, t_pt, pv_fn, post_fn = self.pending.pop(0)
                pv_fn(pt, t_pt)
                if post_fn is not None:
                    post_fn()

            def flush(self):
                while self.pending:
                    self._pop()

        pipeA = Pipe(resA, 1)
        pipeB = Pipe(resB, 2)

        def load_group(g):
            gi = g % 2
            P.dma("sp", lambda e: e.dma_start(
                out=KS[gi][0:64, :], in_=kvT_s[2, g * 64:(g + 1) * 64, :]), t_KS[gi], True)
            P.dma("sp", lambda e: e.dma_start(
                out=KW[gi][0:64, :], in_=kvT_s[3, g * 64:(g + 1) * 64, :]), t_KW[gi], True)
            for half in range(2):
                P.dma("sp", lambda e, half=half: e.dma_start(
                    out=VS[gi][:, half * 16:(half + 1) * 16, 0:64],
                    in_=vtok_s[half * 2048:(half + 1) * 2048, g * 64:(g + 1) * 64].rearrange(
                        "(kt p) d -> p kt d", p=128)), t_VS[gi], True)
                P.dma("sp", lambda e, half=half: e.dma_start(
                    out=VW[gi][:, half * 16:(half + 1) * 16, 0:64],
                    in_=vtok_s[half * 2048:(half + 1) * 2048, 256 + g * 64:256 + (g + 1) * 64].rearrange(
                        "(kt p) d -> p kt d", p=128)), t_VW[gi], True)

        def stage_A(u, g, t):
            par = u % 2
            q0 = t * 512
            ya, t_ya = yacc[par], t_yacc[par]
            for hh in range(4):
                h = g * 4 + hh
                P.dma("sp", lambda e, hh=hh, h=h: e.dma_start(
                    out=QA[hh][par][0:64, :], in_=qT_s[h * 64:(h + 1) * 64, q0:q0 + 512]),
                    t_QA[hh][par], True)
                P.dma("sp", lambda e, hh=hh, h=h: e.dma_start(
                    out=QA[hh][par][127:128, :], in_=cd["vrow"][h:h + 1, :]), t_QA[hh][par], True)
            zt, t_z = ztile[par], t_zt[par]
            P.dma("sp", lambda e: e.dma_start(
                out=zt[:], in_=zs_s[t * 512:(t + 1) * 512, g * 256:(g + 1) * 256].rearrange(
                    "(s p) c -> p s c", p=128)), t_z, True)
            ncts = 2 if t >= 4 else 1
            for hh in range(4):
                h = g * 4 + hh
                Q, t_Q = QA[hh][par], t_QA[hh][par]
                op_, t_op = resA["Op"].next()
                for ct in range(ncts):
                    bcol = (h * 8 + t) * 2 + ct

                    def pv(pt, t_pt, op_=op_, t_op=t_op, ct=ct):
                        if ct == 0:
                            zero_bank(op_[:].rearrange("p a b -> p (a b)"), t_op, 512)
                            zero_bank(IMPp[:].rearrange("p a b -> p (a b)"), t_IMP, 256)
                        for qs in range(4):
                            P.op("pe", lambda e, qs=qs: e.matmul(
                                op_[:, qs, 0:65], lhsT=pt[:, qs * 128:(qs + 1) * 128], rhs=VC[:, g, ct, :],
                                start=False, stop=(ct == ncts - 1 and qs == 3)),
                                reads=[t_pt, t_VC], writes=[t_op])
                            P.op("pe", lambda e, qs=qs: e.matmul(
                                IMPp[:, qs, :], lhsT=pt[:, qs * 128:(qs + 1) * 128], rhs=ovt[:, ct, :],
                                start=False, stop=(ct == ncts - 1 and qs == 3)),
                                reads=[t_pt, t_c3], writes=[t_IMP])

                    def post(op_=op_, t_op=t_op, h=h, hh=hh):
                        sm, t_sm_ = finalize(resA, 0, h, hh, t, op_, t_op, ya, t_ya, True)
                        rb = sm[:, 4:8].unsqueeze(2).broadcast_to([128, 4, 64])
                        if hh == 0:
                            P.op("dve", lambda e: e.tensor_tensor(out=sc[:], in0=IMPp[:], in1=rb, op=ALU.mult),
                                 reads=[t_IMP, t_sm_], writes=[t_sc])
                        else:
                            to, t_to = resA["tmpo"].next()
                            P.op("dve", lambda e: e.tensor_tensor(out=to[:], in0=IMPp[:], in1=rb, op=ALU.mult),
                                 reads=[t_IMP, t_sm_], writes=[t_to])
                            P.op("dve", lambda e: e.tensor_tensor(out=sc[:], in0=sc[:], in1=to[:], op=ALU.add),
                                 reads=[t_to, t_sc], writes=[t_sc])

                    pipeA.push((0, h, t, Q, t_Q, KC[:, g, :], t_KC, ct * 128, 0, 512,
                               (0, cmask[:, ct, q0:q0 + 512], t_c3), biasc_tab[:, bcol:bcol + 1], t_c3),
                              pv, post if ct == ncts - 1 else None)
            pipeA.flush()
            for qs in range(4):
                tt = 4 * t + qs
                P.op("dve", lambda e, qs=qs, tt=tt: e.tensor_tensor(out=s1[:], in0=sc[:, qs, :],
                                                                    in1=sel_vnf[:, tt, :], op=ALU.mult),
                     reads=[t_sc, t_c3], writes=[t_s1])
                P.op("dve", lambda e, tt=tt: e.tensor_tensor(out=s1[:], in0=s1[:], in1=sel_addc[:, tt, :],
                                                             op=ALU.add),
                     reads=[t_s1, t_c3], writes=[t_s1])
                P.op("dve", lambda e: e.max(out=m8[:, 0:8], in_=s1[:]), reads=[t_s1], writes=[t_m8])
                P.op("dve", lambda e: e.tensor_scalar(out=s2[:], in0=s1[:], scalar1=m8[:, 7:8],
                                                      scalar2=None, op0=ALU.is_ge),
                     reads=[t_s1, t_m8], writes=[t_s2])
                P.op("dve", lambda e: e.scalar_tensor_tensor(out=s2[:], in0=s2[:], scalar=-3.0e9, in1=s1[:],
                                                             op0=ALU.mult, op1=ALU.add),
                     reads=[t_s1, t_s2], writes=[t_s2])
                P.op("dve", lambda e: e.max(out=m8[:, 8:16], in_=s2[:]), reads=[t_s2], writes=[t_m8])
                P.op("dve", lambda e: e.tensor_scalar(out=s2[:], in0=s1[:], scalar1=m8[:, 15:16],
                                                      scalar2=None, op0=ALU.is_ge),
                     reads=[t_s1, t_m8], writes=[t_s2])
                P.op("dve", lambda e, tt=tt: e.tensor_tensor(out=s2[:], in0=s2[:], in1=sel_valid[:, tt, :],
                                                             op=ALU.mult),
                     reads=[t_s2, t_c3], writes=[t_s2])
                P.op("dve", lambda e: e.tensor_scalar(out=negpad[:, 64:127], in0=s2[:, 0:63], scalar1=1.0,
                                                      scalar2=-NEG, op0=ALU.subtract, op1=ALU.mult),
                     reads=[t_s2], writes=[t_negpad])
                P.op("pe", lambda e, qs=qs: e.transpose(negTp[:, qs * 128:(qs + 1) * 128], negpad[:], ident_b[:]),
                     reads=[t_negpad, t_const], writes=[t_negT])
            for hh in range(4):
                h = g * 4 + hh
                P.op("dve", lambda e, hh=hh: e.tensor_copy(QA[hh][par][64:127, :], negTp[64:127, :]),
                     reads=[t_negT], writes=[t_QA[hh][par]])

        def stage_B(u, g, t):
            par = u % 2
            gi = g % 2
            ya, t_ya = yacc[par], t_yacc[par]
            zt, t_z = ztile[par], t_zt[par]
            for hh in range(4):
                h = g * 4 + hh
                Q, t_Q = QA[hh][par], t_QA[hh][par]
                for br, Kt, t_K, Vt, t_V in ((1, KS[gi], t_KS[gi], VS[gi], t_VS[gi]),
                                             (2, KW[gi], t_KW[gi], VW[gi], t_VW[gi])):
                    tiles = []
                    if br == 1:
                        for kt in range(0, 4 * t):
                            md = (4 * t - kt - 1) * 128 + 1
                            if SLOPES[h] * md >= ZERO_CUT:
                                continue
                            tiles.append((kt, 0, 512, None))
                    else:
                        for i in (3, 2, 1, 0):
                            kt = 4 * t - 4 + i
                            md = 385 - 128 * i if i >= 1 else 385
                            if kt >= 0 and SLOPES[h] * md < ZERO_CUT:
                                tiles.append((kt, 0, 128 * (i + 1), (128 * i, tri_b[:], t_c3)))
                    for i2 in range(4):
                        tiles.append((4 * t + i2, 128 * i2, 512 - 128 * i2, (128 * i2, tri_c[:], t_c3)))
                    last = {}
                    for ti, (kt, col0, ncol, mask) in enumerate(tiles):
                        for qs in range(col0 // 128, (col0 + ncol) // 128):
                            last[qs] = ti
                    op_, t_op = resB["Op"].next()
                    for ti, (kt, col0, ncol, mask) in enumerate(tiles):
                        bcol = h * 32 + (kt - 4 * t + 28)

                        def pv(pt, t_pt, op_=op_, t_op=t_op, ti=ti, kt=kt, col0=col0, ncol=ncol, Vt=Vt, t_V=t_V,
                               last=last, tiles=tiles):
                            if ti == 0:
                                zero_bank(op_[:].rearrange("p a b -> p (a b)"), t_op, 512)
                            for qs in range(col0 // 128, (col0 + ncol) // 128):
                                P.op("pe", lambda e, qs=qs: e.matmul(
                                    op_[:, qs, 0:65], lhsT=pt[:, qs * 128:(qs + 1) * 128], rhs=Vt[:, kt, :],
                                    start=False, stop=(ti == len(tiles) - 1 and qs == 3)),
                                    reads=[t_pt, t_V], writes=[t_op])

                        def post(op_=op_, t_op=t_op, br=br, h=h, hh=hh):
                            finalize(resB, br, h, hh, t, op_, t_op, ya, t_ya, False)

                        pipeB.push((br, h, t, Q, t_Q, Kt, t_K, kt * 128, col0, ncol, mask,
                                   bias_tab[:, bcol:bcol + 1], t_c3), pv,
                                  post if ti == len(tiles) - 1 else None)
            pipeB.flush()
            yb, t_yb = ybf[par], t_ybf[par]
            P.op("dve", lambda e: e.tensor_tensor(out=yb[:], in0=ya[:], in1=zt[:], op=ALU.mult),
                 reads=[t_ya, t_z], writes=[t_yb])
            P.dma("pool", lambda e: e.dma_start(
                out=y0_s[t * 512:(t + 1) * 512, g * 256:(g + 1) * 256].rearrange("(s p) c -> p s c", p=128),
                in_=yb[:]), t_yb, False)

        units = [(g, t) for g in range(4) for t in range(NQ)]
        for u in range(len(units) + 1):
            lists = []
            if u < len(units):
                g, t = units[u]
                if t == 0:
                    load_group(g)
                P.capture()
                stage_A(u, g, t)
                lists.append(P.end_capture())
            if u > 0:
                P.capture()
                stage_B(u - 1, *units[u - 1])
                lists.append(P.end_capture())
            P.replay_merged(lists)
        P.end_phase()
    es_w1.close()
    if stop_after == 3:
        return finish()


    es_g = ExitStack()
    ipre = sb(es_g, "ipre", [128, NT, 8], F32)
    logf = sb(es_g, "logf", [128, NT, 8], F32)
    wo1 = sb(es_g, "wo1", [128, 16, D], BF16)
    t_wo1 = Tok("wo1", P.free.pop(0))
    es_h = ExitStack()
    hT = sb(es_h, "hT_l1", [128, 8, S], BF16)
    t_ipre, t_logf = P.tok("ipre"), P.tok("logf")
    with ExitStack() as es:
        R = norm_res(es)
        wo = sb(es, "wo", [128, 8, D], BF16)
        t_wo = P.tok("wo", dma=True)
        P.dma("pool", lambda e: e.dma_start(out=wo[:], in_=woa_d.rearrange("(kc p) n -> p kc n", p=128)), t_wo, True)
        yin = Rot([(sb(es, "yin%d" % i, [128, D], BF16), P.tok("yin%d" % i, dma=True)) for i in range(3)])
        xin = Rot([(sb(es, "xin4_%d" % i, [128, D], F32), P.tok("xin4_%d" % i, dma=True)) for i in range(3)])
        x1t = Rot([(sb(es, "x1t%d" % i, [128, D], F32), P.tok("x1t%d" % i, dma=True)) for i in range(3)])
        yT = Rot([(sb(es, "yT%d" % i, [128, 8, 128], BF16), P.tok("yT")) for i in range(2)])
        ytp = ps(es, "ytp", [128, D], BF16)
        t_ytp = P.tok("ytp")
        po = [ps(es, "po%d" % i, [128, 512], F32) for i in range(2)]
        t_po = [P.tok("po0"), P.tok("po1")]
        def stage_X(tt):
            yt, t_yt = yin.next()
            xt, t_xt = xin.next()
            x1, t_x1 = x1t.next()
            yTt, t_yT = yT.next()
            P.dma("sp", lambda e, yt=yt, tt=tt: e.dma_start(out=yt[:], in_=y0_s[tt * 128:(tt + 1) * 128, :]),
                  t_yt, True)
            P.dma("sp", lambda e, xt=xt, tt=tt: e.dma_start(out=xt[:], in_=x_d[tt * 128:(tt + 1) * 128, :]),
                  t_xt, True)
            for kc in range(8):
                P.op("pe", lambda e, yt=yt, kc=kc: e.transpose(
                    ytp[:, kc * 128:(kc + 1) * 128], yt[:, kc * 128:(kc + 1) * 128], ident_b[:]),
                    reads=[t_yt, t_const], writes=[t_ytp])
            P.op("act", lambda e, yTt=yTt: e.activation(
                out=yTt[:, 0:4, :], in_=ytp[:, 0:512].rearrange("p (a b) -> p a b", b=128), func=AF.Identity),
                reads=[t_ytp], writes=[t_yT])
            P.op("act", lambda e, yTt=yTt: e.activation(
                out=yTt[:, 4:8, :], in_=ytp[:, 512:1024].rearrange("p (a b) -> p a b", b=128), func=AF.Identity),
                reads=[t_ytp], writes=[t_yT])
            for hf in range(2):
                for kc in range(8):
                    P.op("pe", lambda e, yTt=yTt, hf=hf, kc=kc: e.matmul(
                        po[hf][:], lhsT=yTt[:, kc, :], rhs=wo[:, kc, hf * 512:(hf + 1) * 512],
                        start=(kc == 0), stop=(kc == 7)), reads=[t_yT, t_wo], writes=[t_po[hf]])
                P.op("dve", lambda e, x1=x1, hf=hf: e.tensor_tensor(
                    out=x1[:, hf * 512:(hf + 1) * 512], in0=po[hf][:], in1=gate_bc[0][:, hf * 512:(hf + 1) * 512],
                    op=ALU.mult), reads=[t_po[hf], t_gbc[0]], writes=[t_x1])
            P.op("dve", lambda e, x1=x1, xt=xt: e.tensor_tensor(out=x1[:], in0=x1[:], in1=xt[:], op=ALU.add),
                 reads=[t_xt, t_x1], writes=[t_x1])
            P.dma("pool", lambda e, x1=x1, tt=tt: e.dma_start(out=x1_s[tt * 128:(tt + 1) * 128, :], in_=x1[:]),
                  t_x1, False)
            return x1, t_x1

        x1s, xns = {}, {}
        for tt in range(NT + 2):
            lists = []
            if tt < NT:
                P.capture()
                x1s[tt] = stage_X(tt)
                lists.append(P.end_capture())
            if 0 <= tt - 1 < NT:
                P.capture()
                x1, t_x1 = x1s.pop(tt - 1)
                xns[tt - 1] = norm_a(x1[:], t_x1, R)
                lists.append(P.end_capture())
            if 0 <= tt - 2 < NT:
                P.capture()
                norm_b(1, tt - 2, *xns.pop(tt - 2), R)
                lists.append(P.end_capture())
            P.replay_merged(lists)
        P.end_phase()
    if stop_after == 4:
        return finish()

    with ExitStack() as es:
        frow = [sb(es, "frow5_%d" % i, [128, S], BF16) for i in range(2)]
        t_frow = [P.tok("frow5_%d" % i, dma=True) for i in range(2)]
        raw = Rot([(sb(es, "raw%d" % i, [128, 515], F32), P.tok("raw"), P.tok("rawc")) for i in range(3)])
        acc = Rot([(sb(es, "acc%d" % i, [128, 512], F32), P.tok("acc")) for i in range(3)])
        convw = sb(es, "convw", [128, 64], F32)
        convb = sb(es, "convb", [128, 16], F32)
        gb_bc = sb(es, "gb_bc", [128, 16], F32)
        t_cv = P.tok("cv", dma=True)
        P.dma("sp", lambda e: e.dma_start(out=convw[:], in_=convwT_d), t_cv, True)
        P.dma("sp", lambda e: e.dma_start(out=convb[:], in_=convbT_d), t_cv, True)
        P.dma("sp", lambda e: e.dma_start(out=gb_bc[:], in_=bass.AP(gateb_d.tensor, 0, [[0, 128], [1, 16]])), t_cv, True)
        tstb = Rot([(sb(es, "tstb5_%d" % i, [128, 4, 512], BF16), P.tok("tstb5_%d" % i, dma=True)) for i in range(2)])
        tsto = Rot([(sb(es, "tsto%d" % i, [128, 4, 256], F32), P.tok("tsto%d" % i, dma=True)) for i in range(2)])
        z1 = Rot([(sb(es, "z1_%d" % i, [128, 256], F32), P.tok("z1")) for i in range(2)])
        gtmp = Rot([(sb(es, "gtmp%d" % i, [128, 24], F32), P.tok("gtmp")) for i in range(2)])
        for kh in range(4):
            P.dma("pool", lambda e, kh=kh: e.dma_start(
                out=wo1[:, kh * 4:(kh + 1) * 4, :], in_=wob_d[kh * 512:(kh + 1) * 512, :].rearrange(
                    "(kc p) n -> p kc n", p=128)), t_wo1, True)
        hg_bc5 = sb(es, "hg_bc5", [128, 2048], F32)
        P.dma("sp", lambda e: e.dma_start(out=hg_bc5[:], in_=bass.AP(headg_d.tensor, 0, [[0, 128], [1, 2048]])),
              t_cv, True)
        sgt = Rot([(sb(es, "sgt%d" % i, [128, 512], F32), P.tok("sgt")) for i in range(2)])
        st5 = {"prev": None}

        def fm_sink5(blk, qt, pa, t_pa):
            rw, t_rw, t_rwc = raw.next()
            ac, t_ac = acc.next()
            fr, tf = frow[blk % 2], t_frow[blk % 2]
            P.op("act", lambda e: e.activation(out=rw[:, 3:515], in_=pa[:], func=AF.Identity),
                 reads=[t_pa], writes=[t_rw])
            P.op("act", lambda e: e.activation(out=ac[:], in_=pa[:], func=AF.Identity,
                                               scale=convw[:, blk * 4 + 3:blk * 4 + 4]),
                 reads=[t_pa, t_cv], writes=[t_ac])
            if st5.get("pend"):
                st5.pop("pend")()
            if qt == 0:
                P.op("dve", lambda e: e.memset(rw[:, 0:3], 0.0), writes=[t_rwc])
            else:
                prw, t_prw = st5["prev"]
                P.op("dve", lambda e: e.tensor_copy(rw[:, 0:3], prw[:, 512:515]), reads=[t_prw], writes=[t_rwc])
            st5["prev"] = (rw, t_rw)
            for j in range(0, 3):
                P.op("dve", lambda e, j=j: e.scalar_tensor_tensor(
                    out=ac[:], in0=rw[:, j:j + 512], scalar=convw[:, blk * 4 + j:blk * 4 + j + 1], in1=ac[:],
                    op0=ALU.mult, op1=ALU.add), reads=[t_rw, t_rwc, t_cv, t_ac], writes=[t_ac])

            def pend():
                P.op("act", lambda e: e.activation(out=fr[:, qt * 512:(qt + 1) * 512], in_=ac[:], func=AF.Silu,
                                                   bias=convb[:, blk:blk + 1]),
                     reads=[t_ac, t_cv], writes=[tf])
                if qt == NQ - 1:
                    P.dma("sp", lambda e: e.dma_start(out=qkT1_s[blk], in_=fr[:]), tf, False)

            st5["pend"] = pend

        tm5 = {}

        def tm_sink5(meta, tt, pa, t_pa):
            if st5.get("pend"):
                st5.pop("pend")()
            if meta == "g":
                gt, t_gt = gtmp.next()
                P.op("dve", lambda e: e.tensor_tensor(out=ipre[:, tt, :], in0=pa[:, 0:8], in1=gb_bc[:, 0:8],
                                                      op=ALU.add), reads=[t_pa, t_cv], writes=[t_ipre])
                P.op("dve", lambda e: e.tensor_tensor(out=gt[:, 0:8], in0=pa[:, 8:16], in1=gb_bc[:, 8:16],
                                                      op=ALU.add), reads=[t_pa, t_cv], writes=[t_gt])
                P.op("act", lambda e: e.activation(out=gt[:, 8:16], in_=gt[:, 0:8], func=AF.Exp, scale=-1.0),
                     reads=[t_gt], writes=[t_gt])
                P.op("act", lambda e: e.activation(out=gt[:, 16:24], in_=gt[:, 8:16], func=AF.Ln,
                                                   bias=cvec[:, 1:2]), reads=[t_gt, t_cvec], writes=[t_gt])
                P.op("dve", lambda e: e.tensor_scalar(out=logf[:, tt, :], in0=gt[:, 16:24], scalar1=-1.0,
                                                      scalar2=None, op0=ALU.mult), reads=[t_gt], writes=[t_logf])
                return
            kind, i = meta
            if tt % 4 == 0:
                tm5["buf"] = tstb.next() if kind == "v" else tsto.next()
            st, t_s = tm5["buf"]
            if kind == "v":
                P.op("dve", lambda e: e.tensor_copy(st[:, tt % 4, :], pa[:]), reads=[t_pa], writes=[t_s])
                if tt % 4 == 3:
                    dst = vtok1_s[(tt - 3) * 128:(tt + 1) * 128, i * 512:(i + 1) * 512].rearrange(
                        "(t p) c -> p t c", p=128)
                    P.dma("sp", lambda e: e.dma_start(out=dst, in_=st[:]), t_s, False)
            else:
                zz, t_zz = z1.next()
                sg_, t_sg_ = sgt.next()
                P.op("act", lambda e: e.activation(out=sg_[:], in_=pa[:], func=AF.Sigmoid),
                     reads=[t_pa], writes=[t_sg_])
                P.op("dve", lambda e: e.tensor_tensor(out=zz[:], in0=pa[:, 256:512], in1=sg_[:, 256:512], op=ALU.mult),
                     reads=[t_pa, t_sg_], writes=[t_zz])
                P.op("dve", lambda e: e.tensor_tensor(out=st[:, tt % 4, :], in0=sg_[:, 0:256], in1=zz[:], op=ALU.mult),
                     reads=[t_sg_, t_zz], writes=[t_s])
                P.op("dve", lambda e: e.tensor_tensor(out=st[:, tt % 4, :], in0=st[:, tt % 4, :],
                                                      in1=hg_bc5[:, i * 256:(i + 1) * 256], op=ALU.mult),
                     reads=[t_cv], writes=[t_s])
                if tt % 4 == 3:
                    dst = ogz_s[(tt - 3) * 128:(tt + 1) * 128, i * 256:(i + 1) * 256].rearrange(
                        "(t p) c -> p t c", p=128)
                    P.dma("sp", lambda e: e.dma_start(out=dst, in_=st[:]), t_s, False)

        groups = ([("fm", i * 512, 512, i * 4) for i in range(4)]
                  + [("tm", 2048 + i * 512, 512, ("v", i)) for i in range(4)]
                  + [("tm", 4096 + i * 512, 512, ("ogz", i)) for i in range(8)]
                  + [("tm", 8192, 16, "g")])
        inproj(es, wb_d, groups, fm_sink5, tm_sink5)
        P.end_phase()
    es_h.close()
    if stop_after == 5:
        return finish()

    with ExitStack() as es:
        R = norm_res(es, full=False)
        t_c6 = P.tok("c6", dma=True)
        mask01 = sb(es, "mask01", [128, 128], F32)
        fing_bc = sb(es, "fing_bc", [128, D], F32)
        P.dma("sp", lambda e: e.dma_start(out=mask01[:], in_=cd["mask01"]), t_c6, True)
        P.dma("sp", lambda e: e.dma_start(out=fing_bc[:], in_=bass.AP(fing_d.tensor, 0, [[0, 128], [1, D]])), t_c6, True)
        Ct = sb(es, "Ct", [128, 8, 257], F32)
        Cbf = sb(es, "Cbf", [128, 8, 257], BF16)
        t_Ct = [P.tok("Ct_%d" % h) for h in range(8)]
        t_Cbf = [P.tok("Cbf_%d" % h) for h in range(8)]
        qkr = Rot([(sb(es, "qk%d" % i, [128, 16, 128], BF16), P.tok("qk%d" % i, dma=True)) for i in range(3)])
        Var = [(sb(es, "Va%d" % i, [128, 8, 257], BF16), P.tok("Va%d" % i, dma=True)) for i in range(3)]
        for va, t_va in Var:
            P.op("pool", lambda e, va=va: e.memset(va[:], 1.0), writes=[t_va])
        Var = Rot(Var)
        ogzr = Rot([(sb(es, "ogzt%d" % i, [128, 2048], F32), P.tok("ogzt%d" % i, dma=True)) for i in range(3)])
        x1r = Rot([(sb(es, "x1r%d" % i, [128, D], F32), P.tok("x1r%d" % i, dma=True)) for i in range(5)])
        outr = Rot([(sb(es, "outt%d" % i, [128, D], F32), P.tok("outt%d" % i, dma=True)) for i in range(3)])
        numr = Rot([(sb(es, "numsb%d" % i, [128, 8, 257], F32), P.tok("numsb")) for i in range(2)])
        junk6 = sb(es, "junk6", [128, 256], F32)
        t_junk6 = P.tok("junk6")
        yT6 = sb(es, "yT6", [128, 16, 128], BF16)
        x2 = sb(es, "x2", [128, D], F32)
        t_yT6, t_x2 = P.tok("yT6"), P.tok("x2")
        smr = Rot([(sb(es, "sm6_%d" % i, [128, 64], F32), P.tok("sm6")) for i in range(4)])
        sm2r = Rot([(sb(es, "sm6b_%d" % i, [128, 80], F32), P.tok("sm6b")) for i in range(2)])
        Sm8 = sb(es, "Sm8", [128, 8, 128], BF16)
        Kt8 = sb(es, "Kt8", [128, 8, 128], BF16)
        t_Sm8 = [P.tok("Sm8_%d" % h) for h in range(8)]
        t_Kt8 = [P.tok("Kt8_%d" % h) for h in range(8)]
        po = [ps(es, "po6_%d" % i, [128, 512], F32) for i in range(2)]
        t_po = [P.tok("po6_0"), P.tok("po6_1")]
        ptrk = ps(es, "ptrk", [128, D], BF16)
        ptry = ps(es, "ptry", [128, D], BF16)
        t_ptrk1 = P.tok("ptrk")
        t_ptrk = [t_ptrk1] * 8
        t_ptry = P.tok("ptry")
        pS2 = [ps(es, "pS2_%d" % i, [128, 4, 128], F32) for i in range(2)]
        t_pSb = [P.tok("pSb0"), P.tok("pSb1")]
        t_pS = [t_pSb[h // 4] for h in range(8)]
        pnd_ = ps(es, "pnd", [128, 512], F32)
        pnd2 = [pnd_, pnd_]
        pdC = ps(es, "pdC", [128, 512], F32)
        t_pnd_ = P.tok("pnd")
        t_pnd2, t_pdC = [t_pnd_, t_pnd_], P.tok("pdC")
        t_pbg = t_pdC
        pbg = pdC[:, 384:400]
        chunk = {}

        loaded = {}
        pending_stores = []

        def stage_L(n):
            qk, t_qk = qkr.next()
            va, t_va = Var.next()
            og, t_og = ogzr.next()
            x1, t_x1 = x1r.next()
            P.dma("sp", lambda e: e.dma_start(
                out=qk[:], in_=qkT1_s[:, :, n * 128:(n + 1) * 128].rearrange("b p t -> p b t")), t_qk, True)
            P.dma("sp", lambda e: e.dma_start(
                out=va[:, :, 0:256], in_=vtok1_s[n * 128:(n + 1) * 128, :].rearrange("p (h v) -> p h v", v=256)),
                t_va, True)
            P.dma("sp", lambda e: e.dma_start(out=x1[:], in_=x1_s[n * 128:(n + 1) * 128, :]), t_x1, True)
            P.dma("sp", lambda e: e.dma_start(out=og[:], in_=ogz_s[n * 128:(n + 1) * 128, :]), t_og, True)
            loaded[n] = (qk, t_qk, va, t_va, og, t_og, x1, t_x1)

        def stage_H(n):
            qk, t_qk, va, t_va, og, t_og, x1, t_x1 = loaded.pop(n)
            P.op("pe", lambda e: e.matmul(pdC[:, 384:392], lhsT=mask01[:], rhs=logf[:, n, :], start=True, stop=True),
                 reads=[t_c6, t_logf], writes=[t_pbg])
            P.op("pe", lambda e: e.matmul(pdC[:, 392:400], lhsT=ones_f[:], rhs=logf[:, n, :], start=True, stop=True),
                 reads=[t_const, t_logf], writes=[t_pbg])
            sm, t_sm = smr.next()
            P.op("act", lambda e: e.activation(out=sm[:, 32:48], in_=pdC[:, 384:400], func=AF.Identity),
                 reads=[t_pbg], writes=[t_sm])
            P.op("dve", lambda e: e.tensor_tensor(out=sm[:, 0:8], in0=ipre[:, n, :], in1=sm[:, 32:40],
                                                  op=ALU.subtract), reads=[t_ipre, t_sm], writes=[t_sm])
            P.op("act", lambda e: e.activation(out=sm[:, 8:16], in_=sm[:, 0:8], func=AF.Exp, bias=cvec[:, 2:3]),
                 reads=[t_sm, t_cvec], writes=[t_sm])
            P.op("act", lambda e: e.activation(out=sm[:, 16:32], in_=sm[:, 32:48], func=AF.Exp),
                 reads=[t_sm], writes=[t_sm])
            nums, t_nums = numr.next()
            last = (n == NT - 1)
            prev = chunk.get(n - 1)
            for h in range(8):
                P.op("pe", lambda e, h=h: e.matmul(pS2[h // 4][:, h % 4, :], lhsT=qk[:, 8 + h, :], rhs=qk[:, h, :],
                                                   start=True, stop=True), reads=[t_qk], writes=[t_pS[h]])
            if not last:
                for h in range(8):
                    P.op("pe", lambda e, h=h: e.transpose(ptrk[:, h * 128:(h + 1) * 128], qk[:, 8 + h, :], ident_b[:]),
                         reads=[t_qk, t_const], writes=[t_ptrk[h]])
            if prev is not None:
                psm, t_psm = prev
                for h in range(8):
                    P.op("act", lambda e, h=h: e.activation(
                        out=Cbf[:, h, :], in_=Ct[:, h, :], func=AF.Identity, scale=psm[:, 24 + h:25 + h]),
                        reads=[t_Ct[h], t_psm], writes=[t_Cbf[h]])
            for h in range(8):
                P.op("dve", lambda e, h=h: e.scalar_tensor_tensor(
                    out=Sm8[:, h, :], in0=pS2[h // 4][:, h % 4, :], scalar=sm[:, 8 + h:9 + h], in1=mask01[:],
                    op0=ALU.mult, op1=ALU.mult), reads=[t_pS[h], t_sm, t_c6], writes=[t_Sm8[h]])
                if not last:
                    P.op("act", lambda e, h=h: e.activation(
                        out=Kt8[:, h, :], in_=ptrk[:, h * 128:(h + 1) * 128], func=AF.Identity,
                        scale=sm[:, 8 + h:9 + h]), reads=[t_ptrk[h], t_sm], writes=[t_Kt8[h]])
            for h in range(8):
                pnd, t_pnd = pnd2[h % 2], t_pnd2[h % 2]
                P.op("pe", lambda e, h=h, pnd=pnd: e.matmul(pnd[:, 0:257], lhsT=Sm8[:, h, :], rhs=va[:, h, :],
                                                            start=True, stop=(n == 0)),
                     reads=[t_Sm8[h], t_va], writes=[t_pnd])
                if n > 0:
                    P.op("pe", lambda e, h=h, pnd=pnd: e.matmul(pnd[:, 0:257], lhsT=qk[:, h, :], rhs=Cbf[:, h, :],
                                                                start=False, stop=True),
                         reads=[t_qk, t_Cbf[h]], writes=[t_pnd])
                P.op("act", lambda e, h=h, pnd=pnd: e.activation(out=nums[:, h, :], in_=pnd[:, 0:257],
                                                                 func=AF.Identity),
                     reads=[t_pnd], writes=[t_nums])
                if not last:
                    P.op("pe", lambda e, h=h: e.matmul(pdC[:, 0:257], lhsT=Kt8[:, h, :], rhs=va[:, h, :],
                                                       start=True, stop=True),
                         reads=[t_Kt8[h], t_va], writes=[t_pdC])
                    if n == 0:
                        P.op("dve", lambda e, h=h: e.tensor_copy(Ct[:, h, :], pdC[:, 0:257]),
                             reads=[t_pdC], writes=[t_Ct[h]])
                    else:
                        psm, t_psm = prev
                        P.op("dve", lambda e, h=h: e.scalar_tensor_tensor(
                            out=Ct[:, h, :], in0=Ct[:, h, :], scalar=psm[:, 24 + h:25 + h], in1=pdC[:, 0:257],
                            op0=ALU.mult, op1=ALU.add), reads=[t_pdC, t_psm, t_Ct[h], t_Cbf[h]], writes=[t_Ct[h]])
            chunk[n] = (sm, t_sm)
            return (n, sm, t_sm, nums, t_nums, og, t_og, x1, t_x1)

        ybfr = Rot([(sb(es, "ybf6r%d" % i, [128, 2048], BF16), P.tok("ybf6r")) for i in range(2)])

        def stage_E1(ctx):
            n, sm, t_sm, nums, t_nums, og, t_og, x1, t_x1 = ctx
            s2_, t_s2_ = sm2r.next()
            for h in range(8):
                P.op("act", lambda e, h=h: e.activation(out=junk6[:], in_=nums[:, h, 0:256], func=AF.Square,
                                                        accum_out=s2_[:, h:h + 1]),
                     reads=[t_nums], writes=[t_junk6, t_s2_])
            P.op("dve", lambda e: e.tensor_tensor(out=s2_[:, 8:16], in0=nums[:, :, 256], in1=sm[:, 16:24], op=ALU.mult),
                 reads=[t_nums, t_sm], writes=[t_s2_])
            P.op("dve", lambda e: e.tensor_scalar(out=s2_[:, 16:24], in0=s2_[:, 8:16], scalar1=-1.0, scalar2=None,
                                                  op0=ALU.mult), reads=[t_s2_], writes=[t_s2_])
            P.op("dve", lambda e: e.tensor_tensor(out=s2_[:, 16:24], in0=s2_[:, 16:24], in1=s2_[:, 8:16], op=ALU.max),
                 reads=[t_s2_], writes=[t_s2_])
            P.op("dve", lambda e: e.tensor_scalar(out=s2_[:, 16:24], in0=s2_[:, 16:24], scalar1=1.0, scalar2=None,
                                                  op0=ALU.max), reads=[t_s2_], writes=[t_s2_])
            P.op("dve", lambda e: e.reciprocal(out=s2_[:, 24:32], in_=s2_[:, 16:24]), reads=[t_s2_], writes=[t_s2_])
            P.op("dve", lambda e: e.tensor_tensor(out=s2_[:, 32:40], in0=s2_[:, 24:32], in1=sm[:, 16:24], op=ALU.mult),
                 reads=[t_s2_, t_sm], writes=[t_s2_])
            P.op("dve", lambda e: e.tensor_tensor(out=s2_[:, 40:48], in0=s2_[:, 32:40], in1=s2_[:, 32:40], op=ALU.mult),
                 reads=[t_s2_], writes=[t_s2_])
            P.op("dve", lambda e: e.tensor_tensor(out=s2_[:, 40:48], in0=s2_[:, 40:48], in1=s2_[:, 0:8], op=ALU.mult),
                 reads=[t_s2_], writes=[t_s2_])
            P.op("act", lambda e: e.activation(out=s2_[:, 48:56], in_=s2_[:, 40:48], func=AF.Ln,
                                               scale=1.0 / 256, bias=cvec[:, 0:1]),
                 reads=[t_s2_, t_cvec], writes=[t_s2_])
            P.op("act", lambda e: e.activation(out=s2_[:, 56:64], in_=s2_[:, 48:56], func=AF.Exp, scale=-0.5),
                 reads=[t_s2_], writes=[t_s2_])
            P.op("dve", lambda e: e.tensor_tensor(out=s2_[:, 64:72], in0=s2_[:, 32:40], in1=s2_[:, 56:64], op=ALU.mult),
                 reads=[t_s2_], writes=[t_s2_])
            yb, t_yb = ybfr.next()
            for h in range(8):
                P.op("dve", lambda e, h=h: e.scalar_tensor_tensor(
                    out=yb[:, h * 256:(h + 1) * 256], in0=nums[:, h, 0:256], scalar=s2_[:, 64 + h:65 + h],
                    in1=og[:, h * 256:(h + 1) * 256], op0=ALU.mult, op1=ALU.mult),
                    reads=[t_nums, t_s2_, t_og], writes=[t_yb])
            return (n, x1, t_x1, yb, t_yb)

        def stage_E2(c2):
            n, x1, t_x1, yb, t_yb = c2
            for r8 in range(2):
                for kc in range(8):
                    P.op("pe", lambda e, r8=r8, kc=kc: e.transpose(
                        ptry[:, kc * 128:(kc + 1) * 128], yb[:, (r8 * 8 + kc) * 128:(r8 * 8 + kc + 1) * 128],
                        ident_b[:]), reads=[t_yb, t_const], writes=[t_ptry])
                P.op("act", lambda e, r8=r8: e.activation(
                    out=yT6[:, r8 * 8:(r8 + 1) * 8, :], in_=ptry[:].rearrange("p (a b) -> p a b", b=128),
                    func=AF.Identity), reads=[t_ptry], writes=[t_yT6])
            for hf in range(2):
                for kc in range(16):
                    P.op("pe", lambda e, hf=hf, kc=kc: e.matmul(
                        po[hf][:], lhsT=yT6[:, kc, :], rhs=wo1[:, kc, hf * 512:(hf + 1) * 512],
                        start=(kc == 0), stop=(kc == 15)), reads=[t_yT6, t_wo1], writes=[t_po[hf]])
                P.op("dve", lambda e, hf=hf: e.tensor_tensor(
                    out=x2[:, hf * 512:(hf + 1) * 512], in0=po[hf][:], in1=gate_bc[1][:, hf * 512:(hf + 1) * 512],
                    op=ALU.mult), reads=[t_po[hf], t_gbc[1]], writes=[t_x2])
            P.op("dve", lambda e: e.tensor_tensor(out=x2[:], in0=x2[:], in1=x1[:], op=ALU.add),
                 reads=[t_x1, t_x2], writes=[t_x2])
            st, t_st = rms_stats(x2[:], t_x2, R)
            ot, t_ot = outr.next()
            P.op("dve", lambda e: e.scalar_tensor_tensor(
                out=ot[:], in0=x2[:], scalar=st[:, 2:3], in1=fing_bc[:], op0=ALU.mult, op1=ALU.mult),
                reads=[t_x2, t_st, t_c6], writes=[t_ot])
            pending_stores.append((n, ot, t_ot))

        def flush_stores():
            while pending_stores:
                n_, ot, t_ot = pending_stores.pop(0)
                P.dma("sp", lambda e, n_=n_, ot=ot: e.dma_start(out=out_d[n_ * 128:(n_ + 1) * 128, :], in_=ot[:]),
                      t_ot, False)

        cH, cE = {}, {}
        stage_L(0)
        for n in range(NT + 2):
            if n + 1 < NT:
                stage_L(n + 1)
            flush_stores()
            lists = []
            if n < NT:
                P.capture()
                cH[n] = stage_H(n)
                lists.append(P.end_capture())
            if 0 <= n - 2 < NT:
                P.capture()
                stage_E2(cE.pop(n - 2))
                lists.append(P.end_capture())
            if 0 <= n - 1 < NT:
                P.capture()
                cE[n - 1] = stage_E1(cH.pop(n - 1))
                lists.append(P.end_capture())
            P.replay_merged(lists)
        flush_stores()
        P.end_phase()
    es_g.close()
    return nc


def prep_inputs(inputs):
    f = np.float32
    g = lambda k: np.asarray(inputs[k], dtype=f)
    sh = {}
    sh["ada_w"] = np.ascontiguousarray(g("ada_w"))
    sh["adabT"] = np.ascontiguousarray(g("ada_b").reshape(2, 24, 128).transpose(2, 0, 1).reshape(128, 48))
    sh["ngT"] = np.ascontiguousarray(g("norm_g").reshape(2, 8, 128).transpose(2, 0, 1).reshape(128, 16))
    sh["fing"] = np.ascontiguousarray(g("final_g").reshape(1, D))
    sh["wa"] = np.ascontiguousarray(g("a_w_in")[0][:, WA_PERM])
    sh["cmp_w1"] = np.ascontiguousarray(g("a_cmp_w1")[0])
    sh["peT"] = np.ascontiguousarray(g("a_cmp_pe")[0].transpose(2, 0, 1).reshape(64, 64))
    sh["b1T"] = np.ascontiguousarray(g("a_cmp_b1")[0].reshape(2, 2, 128).transpose(2, 0, 1).reshape(128, 4))
    sh["cmp_w2"] = np.ascontiguousarray(g("a_cmp_w2")[0])
    sh["b2T"] = np.ascontiguousarray(g("a_cmp_b2")[0].T)
    sh["a_w_out"] = np.ascontiguousarray(g("a_w_out")[0])
    sh["wb"] = np.ascontiguousarray(g("b_w_in")[0][:, WB_PERM])
    sh["convwT"] = np.ascontiguousarray(g("b_conv_w")[0].reshape(4, 16, 128).transpose(2, 1, 0).reshape(128, 64))
    sh["convbT"] = np.ascontiguousarray(g("b_conv_b")[0].reshape(16, 128).T)
    sh["gateb"] = np.ascontiguousarray(g("b_gate_b")[0].reshape(1, 16))
    sh["headg"] = np.ascontiguousarray(g("b_head_g")[0].reshape(1, 2048))
    sh["b_w_out"] = np.ascontiguousarray(g("b_w_out")[0])
    for k, v in make_consts().items():
        sh["k_" + k] = v
    per = []
    x = g("x")
    c = g("c")
    for b in range(8):
        m = dict(sh)
        m["x"] = np.ascontiguousarray(x[b])
        m["cT"] = np.ascontiguousarray(c[b].reshape(8, 128).T)
        per.append(m)
    return per


def kernel(**inputs):
    nc = build_program()
    in_maps = prep_inputs(inputs)
    res = run_bass_kernel_spmd(nc, in_maps, core_ids=list(range(8)))
    return np.stack([np.asarray(r["out"], dtype=np.float32) for r in res.results], axis=0)
```
